# Optimizing a Trainium2 kernel written in Bass

```python
import math
import jax, jax.numpy as jnp
from jax import lax
import numpy as np

D_MODEL = 1024
BATCH = 8
SEQ = 2048
DEPTH = 2

HEAD_DIM = 64
N_HEADS = D_MODEL // HEAD_DIM
D_FF = 4 * D_MODEL
DECAY_LORA = 64
AAA_LORA = 64
GATE_LORA = 160
N_SHIFT_MIX = 6
BLOCK_Q = 128
N_A_LAYERS = DEPTH // 2
N_B_LAYERS = DEPTH - N_A_LAYERS
NORM_EPS = 1e-6
GN_EPS = 64e-5

kernel_name = "rwkv7_fox_yoco_hybrid"


def rmsnorm(x, g, eps=NORM_EPS):
    xf = x.astype(jnp.float32)
    y = xf * lax.rsqrt(jnp.mean(xf * xf, axis=-1, keepdims=True) + eps)
    return (y * g.astype(jnp.float32)).astype(x.dtype)


def to_heads(t):
    b, s, _ = t.shape
    return t.reshape(b, s, N_HEADS, HEAD_DIM)


def rwkv7_time_mix(h, mu, w_rkv, w0, w1, w2, a0, a1, a2, g1, g2,
                   k_k, k_a, r_k, lnx_w, lnx_b, w_o):
    B, T, D = h.shape
    f32 = jnp.float32
    x_prev = jnp.pad(h[:, :-1], ((0, 0), (1, 0), (0, 0)))
    xx = x_prev - h
    xs = h[None] + xx[None] * mu[:, None, None, :]
    x_r, x_w, x_k, x_v, x_a, x_g = xs[0], xs[1], xs[2], xs[3], xs[4], xs[5]
    rkv = jnp.einsum('pbtd,pde->pbte', jnp.stack([x_r, x_k, x_v]), w_rkv)
    r, k, v = rkv[0], rkv[1], rkv[2]
    w = -jax.nn.softplus(-(w0 + jnp.tanh(x_w @ w1) @ w2)) - 0.5
    decay = jnp.exp(-jnp.exp(w.astype(f32)))
    a = jax.nn.sigmoid(a0 + (x_a @ a1) @ a2)
    g = jax.nn.sigmoid(x_g @ g1) @ g2
    kk = to_heads((k * k_k).astype(f32))
    kk = kk / jnp.maximum(jnp.linalg.norm(kk, axis=-1, keepdims=True), 1e-12)
    k = k * (1.0 + (a - 1.0) * k_a)

    r_h = to_heads(r).astype(f32)
    k_h = to_heads(k).astype(f32)
    v_h = to_heads(v).astype(f32)
    a_h = to_heads(a).astype(f32)
    w_h = to_heads(decay)
    b_h = kk * a_h

    def step(S, inp):
        r_t, w_t, k_t, v_t, kk_t, b_t = inp
        sa = jnp.einsum('bhvk,bhk->bhv', S, -kk_t)
        S = (S * w_t[:, :, None, :] + sa[..., None] * b_t[:, :, None, :]
             + v_t[..., None] * k_t[:, :, None, :])
        y_t = jnp.einsum('bhvk,bhk->bhv', S, r_t)
        return S, y_t

    tm = lambda t: jnp.moveaxis(t, 1, 0)
    S0 = jnp.zeros((B, N_HEADS, HEAD_DIM, HEAD_DIM), f32)
    _, y = lax.scan(step, S0, (tm(r_h), tm(w_h), tm(k_h), tm(v_h), tm(kk), tm(b_h)))
    y = jnp.moveaxis(y, 0, 1)

    mean = jnp.mean(y, axis=-1, keepdims=True)
    var = jnp.mean(jnp.square(y - mean), axis=-1, keepdims=True)
    y = (y - mean) * lax.rsqrt(var + GN_EPS)
    y = y.reshape(B, T, D) * lnx_w.astype(f32) + lnx_b.astype(f32)
    bonus = jnp.sum(r_h * k_h * r_k.astype(f32), axis=-1, keepdims=True) * v_h
    y = (y + bonus.reshape(B, T, D)).astype(h.dtype) * g
    return y @ w_o


def shared_kv(x, kv_norm_g, kv_w, kv_f_bias, k_norm_g):
    D = x.shape[-1]
    h = rmsnorm(x, kv_norm_g)
    proj = h @ kv_w
    k = to_heads(proj[..., :D])
    v = to_heads(proj[..., D:2 * D])
    f_logit = (proj[..., 2 * D:] + kv_f_bias).astype(jnp.float32)
    k = rmsnorm(k, k_norm_g)
    log_f = jax.nn.log_sigmoid(f_logit)
    c = jnp.cumsum(jnp.transpose(log_f, (0, 2, 1)), axis=-1)
    return jnp.transpose(k, (0, 2, 1, 3)), jnp.transpose(v, (0, 2, 1, 3)), c


def forgetting_attention(q, k, v, c):
    T = q.shape[2]
    scale = 1.0 / math.sqrt(HEAD_DIM)
    outs = []
    for s in range(0, T, BLOCK_Q):
        e = s + BLOCK_Q
        logits = jnp.einsum('bhqd,bhkd->bhqk', q[:, :, s:e], k[:, :, :e]).astype(jnp.float32) * scale
        logits = logits + (c[:, :, s:e, None] - c[:, :, None, :e])
        causal = jnp.arange(s, e)[:, None] >= jnp.arange(e)[None, :]
        logits = jnp.where(causal, logits, -jnp.inf)
        p = jax.nn.softmax(logits, axis=-1).astype(v.dtype)
        outs.append(jnp.einsum('bhqk,bhkd->bhqd', p, v[:, :, :e]))
    return jnp.concatenate(outs, axis=2)


def fox_layer(h, w_q, q_norm_g, w_o, k_sh, v_sh, c_sh):
    B, T, D = h.shape
    q = rmsnorm(to_heads(h @ w_q), q_norm_g)
    q = jnp.transpose(q, (0, 2, 1, 3))
    o = forgetting_attention(q, k_sh, v_sh, c_sh)
    o = jnp.transpose(o, (0, 2, 1, 3)).reshape(B, T, D)
    return o @ w_o


def sq_relu_mlp(h, w_in, w_out):
    return jnp.square(jax.nn.relu(h @ w_in)) @ w_out


def setup_inputs(seed: int = 0) -> dict:
    key = jax.random.key(seed)
    ks = iter(jax.random.split(key, 40))
    D, H, N, F = D_MODEL, N_HEADS, HEAD_DIM, D_FF
    nA, nB = N_A_LAYERS, N_B_LAYERS
    nrm = lambda shape, s: s * jax.random.normal(next(ks), shape, jnp.float32)
    gain = lambda shape: 1.0 + nrm(shape, 0.05)
    decay_base = jnp.linspace(-6.5, -1.5, D, dtype=jnp.float32)
    inp = {}
    inp["x"] = jax.random.normal(next(ks), (BATCH, SEQ, D), jnp.float32)
    inp["rwkv_norm_g"] = gain((nA, D))
    inp["rwkv_mu"] = jax.random.uniform(next(ks), (nA, N_SHIFT_MIX, D), jnp.float32)
    inp["rwkv_w_rkv"] = nrm((nA, 3, D, D), D ** -0.5)
    inp["rwkv_w0"] = decay_base[None] + nrm((nA, D), 0.1)
    inp["rwkv_w1"] = nrm((nA, D, DECAY_LORA), D ** -0.5)
    inp["rwkv_w2"] = nrm((nA, DECAY_LORA, D), 0.1 * DECAY_LORA ** -0.5)
    inp["rwkv_a0"] = nrm((nA, D), 0.1)
    inp["rwkv_a1"] = nrm((nA, D, AAA_LORA), D ** -0.5)
    inp["rwkv_a2"] = nrm((nA, AAA_LORA, D), 0.1 * AAA_LORA ** -0.5)
    inp["rwkv_g1"] = nrm((nA, D, GATE_LORA), D ** -0.5)
    inp["rwkv_g2"] = nrm((nA, GATE_LORA, D), GATE_LORA ** -0.5)
    inp["rwkv_k_k"] = 0.85 + nrm((nA, D), 0.05)
    inp["rwkv_k_a"] = 1.0 + nrm((nA, D), 0.05)
    inp["rwkv_r_k"] = nrm((nA, H, N), 0.1)
    inp["rwkv_lnx_w"] = gain((nA, D))
    inp["rwkv_lnx_b"] = nrm((nA, D), 0.02)
    inp["rwkv_w_o"] = nrm((nA, D, D), 0.5 * D ** -0.5)
    inp["kv_norm_g"] = gain((D,))
    inp["kv_w"] = nrm((D, 2 * D + H), D ** -0.5)
    inp["kv_f_bias"] = jax.random.uniform(next(ks), (H,), jnp.float32, 0.5, 3.0)
    inp["k_norm_g"] = gain((H, N))
    inp["attn_norm_g"] = gain((nB, D))
    inp["attn_w_q"] = nrm((nB, D, D), D ** -0.5)
    inp["q_norm_g"] = gain((nB, H, N))
    inp["attn_w_o"] = nrm((nB, D, D), 0.5 * D ** -0.5)
    inp["mlp_norm_g"] = gain((DEPTH, D))
    inp["mlp_w_in"] = nrm((DEPTH, D, F), D ** -0.5)
    inp["mlp_w_out"] = nrm((DEPTH, F, D), 0.5 * F ** -0.5)
    return inp


def reference(x, rwkv_norm_g, rwkv_mu, rwkv_w_rkv, rwkv_w0, rwkv_w1, rwkv_w2,
              rwkv_a0, rwkv_a1, rwkv_a2, rwkv_g1, rwkv_g2, rwkv_k_k, rwkv_k_a,
              rwkv_r_k, rwkv_lnx_w, rwkv_lnx_b, rwkv_w_o,
              kv_norm_g, kv_w, kv_f_bias, k_norm_g,
              attn_norm_g, attn_w_q, q_norm_g, attn_w_o,
              mlp_norm_g, mlp_w_in, mlp_w_out):
    k_sh = v_sh = c_sh = None
    for layer in range(DEPTH):
        if layer < N_A_LAYERS:
            i = layer
            h = rmsnorm(x, rwkv_norm_g[i])
            x = x + rwkv7_time_mix(h, rwkv_mu[i], rwkv_w_rkv[i], rwkv_w0[i], rwkv_w1[i],
                                   rwkv_w2[i], rwkv_a0[i], rwkv_a1[i], rwkv_a2[i],
                                   rwkv_g1[i], rwkv_g2[i], rwkv_k_k[i], rwkv_k_a[i],
                                   rwkv_r_k[i], rwkv_lnx_w[i], rwkv_lnx_b[i], rwkv_w_o[i])
        else:
            j = layer - N_A_LAYERS
            h = rmsnorm(x, attn_norm_g[j])
            x = x + fox_layer(h, attn_w_q[j], q_norm_g[j], attn_w_o[j], k_sh, v_sh, c_sh)
        x = x + sq_relu_mlp(rmsnorm(x, mlp_norm_g[layer]), mlp_w_in[layer], mlp_w_out[layer])
        if layer == N_A_LAYERS - 1:
            k_sh, v_sh, c_sh = shared_kv(x, kv_norm_g, kv_w, kv_f_bias, k_norm_g)
    return x
```

```python
import numpy as np
import concourse.bass as bass
import concourse.mybir as mybir
from concourse.bass_utils import run_bass_kernel_spmd

F32 = mybir.dt.float32
BF16 = mybir.dt.bfloat16
AF = mybir.ActivationFunctionType
ALU = mybir.AluOpType
AX = mybir.AxisListType

D = 1024
T = 2048
H = 16
N = 64
NB = 16
FF = 4096
C_DEC = 0.6065306597126334
NORM_EPS = 1e-6
GN_EPS = 64e-5


DBG = 0


class _Stop(Exception):
    pass


def ck(k):
    if DBG == k:
        raise _Stop()


class Unit:
    __slots__ = ("w", "r", "name", "excl")

    def __init__(self, name="", excl=False):
        self.w = None
        self.r = {}
        self.name = name
        self.excl = excl


def units(n, name=""):
    return [Unit(f"{name}{i}") for i in range(n)]


class KB:
    SEM_ROLL = 3000

    def __init__(self, nc, n_dma_sems=6, same_engine_sync=True):
        self.nc = nc
        self.eng = {"pe": nc.tensor, "act": nc.scalar, "dve": nc.vector,
                    "pool": nc.gpsimd, "sp": nc.sync}
        self.q = {e: [] for e in self.eng}
        self.nsem = 0
        self.sem = {e: self._newsem(e) for e in self.eng}
        self.cnt = {e: 0 for e in self.eng}
        self.seen = {e: {} for e in self.eng}
        self.same = same_engine_sync
        self.dq = {}
        for e in ("sp", "act", "pool"):
            self.dq[e] = {"sems": [self._newsem(f"d{e}") for _ in range(n_dma_sems)],
                          "cnt": [0] * n_dma_sems, "i": 0}
        self.sb_off = 16512
        self.sb_top = 229344
        self.ninst = 0
        self.ntens = 0

    def _newsem(self, tag):
        self.nsem += 1
        return self.nc.alloc_semaphore(f"sem_{tag}_{self.nsem}")

    def sb(self, name, shape, dtype, off=None):
        esz = 2 if dtype == BF16 else 4
        n = int(np.prod(shape[1:])) * esz
        if off is None:
            off = self.sb_off
            self.sb_off = (off + n + 31) // 32 * 32
        assert off + n <= self.sb_top, (name, off, n, self.sb_top)
        self.ntens += 1
        return self.nc.alloc_sbuf_tensor_at(f"{name}_{self.ntens}", list(shape), dtype, offset=off)

    def _collect(self, e, reads, writes):
        deps = {}

        def add(tok):
            s, v = tok
            k = id(s)
            if k not in deps or deps[k][1] < v:
                deps[k] = (s, v)
        own_sem = self.sem[e]
        for u in reads:
            if u.w is not None:
                add(u.w)
            if u.excl:
                for tok in u.r.values():
                    if tok[0] is not own_sem:
                        add(tok)
        for u in writes:
            if u.w is not None:
                add(u.w)
            for tok in u.r.values():
                add(tok)
        waits = []
        own = id(self.sem[e])
        for k, (s, v) in deps.items():
            if k == own and (e == "pe" or not self.same):
                continue
            if self.seen[e].get(k, 0) < v:
                waits.append((s, v))
                self.seen[e][k] = v
        return waits

    def _mark(self, tok, reads, writes):
        s = tok[0]
        for u in writes:
            u.w = tok
            u.r = {}
        for u in reads:
            if u.w is tok:
                continue
            u.r[id(s)] = tok

    def op(self, e, fn, reads=(), writes=()):
        waits = self._collect(e, reads, writes)
        if self.cnt[e] >= self.SEM_ROLL:
            self.sem[e] = self._newsem(e)
            self.cnt[e] = 0
        self.cnt[e] += 1
        tok = (self.sem[e], self.cnt[e])
        self.q[e].append((waits, fn, tok[0], 1))
        self._mark(tok, reads, writes)
        self.ninst += 1
        return tok

    def dma(self, e, out, in_, reads=(), writes=(), **kw):
        d = self.dq[e]
        i = d["i"]
        d["i"] = (i + 1) % len(d["sems"])
        s = d["sems"][i]
        waits = self._collect(e, reads, writes)
        if d["cnt"][i] > 0 and self.seen[e].get(id(s), 0) < d["cnt"][i]:
            waits.append((s, d["cnt"][i]))
            self.seen[e][id(s)] = d["cnt"][i]
        if d["cnt"][i] >= 16 * 180:
            s = self._newsem(f"d{e}")
            d["sems"][i] = s
            d["cnt"][i] = 0
        d["cnt"][i] += 16
        tok = (s, d["cnt"][i])
        self.q[e].append((waits, lambda E: E.dma_start(out=out, in_=in_, **kw), s, 16))
        self._mark(tok, reads, writes)
        self.ninst += 1
        return tok

    def final_wait(self, e, us):
        deps = {}
        for u in us:
            if u.w is not None:
                s, v = u.w
                if id(s) not in deps or deps[id(s)][1] < v:
                    deps[id(s)] = (s, v)
        self.q[e].append((list(deps.values()), None, None, 0))

    def fence(self):
        toks = []
        for e in self.eng:
            if self.cnt[e] > 0:
                toks.append((self.sem[e], self.cnt[e]))
        for e, d in self.dq.items():
            for s, c in zip(d["sems"], d["cnt"]):
                if c > 0:
                    toks.append((s, c))
        for e in self.eng:
            waits = []
            for s, v in toks:
                if self.seen[e].get(id(s), 0) < v:
                    waits.append((s, v))
                    self.seen[e][id(s)] = v
            if waits:
                self.q[e].append((waits, None, None, 0))

    def build(self):
        nc = self.nc
        with nc.Block() as block:
            def mk(ename):
                def body(E):
                    for waits, fn, s, inc in self.q[ename]:
                        for ws, wv in waits:
                            E.wait_ge(ws, wv)
                        if fn is not None:
                            fn(E).then_inc(s, inc)
                return body
            block.tensor(mk("pe"))
            block.scalar(mk("act"))
            block.vector(mk("dve"))
            block.gpsimd(mk("pool"))
            block.sync(mk("sp"))


class G:
    def __init__(self, kb):
        self.kb = kb

    def mm(self, out, lhsT, rhs, start, stop, r, w):
        self.kb.op("pe", lambda E: E.matmul(out, lhsT=lhsT, rhs=rhs, start=start, stop=stop), r, w)

    def tr(self, out, in_, ident, r, w):
        self.kb.op("pe", lambda E: E.transpose(out=out, in_=in_, identity=ident), r, w)

    def tt(self, e, out, in0, in1, op, r, w):
        self.kb.op(e, lambda E: E.tensor_tensor(out=out, in0=in0, in1=in1, op=op), r, w)

    def ts(self, e, out, in0, s1, s2, op0, op1, r, w):
        if op1 is None:
            self.kb.op(e, lambda E: E.tensor_scalar(out=out, in0=in0, scalar1=s1, scalar2=None, op0=op0), r, w)
        else:
            self.kb.op(e, lambda E: E.tensor_scalar(out=out, in0=in0, scalar1=s1, scalar2=s2, op0=op0, op1=op1), r, w)

    def stt(self, out, in0, scalar, in1, op0, op1, r, w):
        self.kb.op("dve", lambda E: E.scalar_tensor_tensor(out=out, in0=in0, scalar=scalar, in1=in1, op0=op0, op1=op1), r, w)

    def act(self, out, in_, func, r, w, bias=None, scale=None, accum_out=None):
        kw = {}
        if bias is not None:
            kw["bias"] = bias
        if scale is not None:
            kw["scale"] = scale
        if accum_out is not None:
            kw["accum_out"] = accum_out
        self.kb.op("act", lambda E: E.activation(out=out, in_=in_, func=func, **kw), r, w)

    def cp(self, e, out, in_, r, w):
        if e == "act":
            self.kb.op(e, lambda E: E.activation(out=out, in_=in_, func=AF.Copy), r, w)
        else:
            self.kb.op(e, lambda E: E.tensor_copy(out=out, in_=in_), r, w)

    def red(self, out, in_, r, w):
        self.kb.op("dve", lambda E: E.tensor_reduce(out=out, in_=in_, axis=AX.X, op=ALU.add), r, w)

    def memset(self, e, ap, val, w):
        self.kb.op(e, lambda E: E.memset(ap, val), (), w)


def build_nc(stop_after=99):
    nc = bass.Bass("TRN2", target_bir_lowering=False)
    dt = nc.dram_tensor

    def inp(name, shape):
        return dt(name, list(shape), F32, kind="ExternalInput").ap()

    x_d = inp("x", [T, D])
    rwkv_norm_g = inp("rwkv_norm_g", [D])
    rwkv_mu = inp("rwkv_mu", [6, D])
    w_rkv = inp("rwkv_w_rkv", [3, D, D])
    w0_d = inp("rwkv_w0", [D])
    w1_d = inp("rwkv_w1", [D, 64])
    w2_d = inp("rwkv_w2", [64, D])
    a0_d = inp("rwkv_a0", [D])
    a1_d = inp("rwkv_a1", [D, 64])
    a2_d = inp("rwkv_a2", [64, D])
    g1_d = inp("rwkv_g1", [D, 160])
    g2_d = inp("rwkv_g2", [160, D])
    kk_d = inp("rwkv_k_k", [D])
    ka_d = inp("rwkv_k_a", [D])
    rk_d = inp("rwkv_r_k", [D])
    lnw_d = inp("rwkv_lnx_w", [D])
    lnb_d = inp("rwkv_lnx_b", [D])
    wo_d = inp("rwkv_w_o", [D, D])
    kv_norm_g = inp("kv_norm_g", [D])
    kv_w = inp("kv_w", [D, 2 * D + H])
    kv_f_bias = inp("kv_f_bias", [H])
    k_norm_g = inp("k_norm_g", [D])
    attn_norm_g = inp("attn_norm_g", [D])
    attn_w_q = inp("attn_w_q", [D, D])
    q_norm_g = inp("q_norm_g", [D])
    attn_w_o = inp("attn_w_o", [D, D])
    mlp_norm_g = inp("mlp_norm_g", [2, D])
    mlp_w_in = inp("mlp_w_in", [2, D, FF])
    mlp_w_out = inp("mlp_w_out", [2, FF, D])
    out_d = dt("out", [T, D], F32, kind="ExternalOutput").ap()
    x1_d = dt("x1_scratch", [T, D], F32).ap()
    kT_d = dt("kT_scratch", [H, 70, T], BF16).ap()
    qT_d = dt("qT_scratch", [H, 70, T], BF16).ap()

    kb = KB(nc)
    g = G(kb)
    NC = ALU

    PS = [nc.alloc_psum_tensor(f"ps{i}", [128, 512], F32) for i in range(8)]
    PSU = [Unit(f"ps{i}", excl=True) for i in range(8)]
    pT = PS[2][:].bitcast(BF16)
    pTU = [PSU[2], PSU[2]]

    ident_f = kb.sb("ident_f", [128, 128], F32)
    ident_b = kb.sb("ident_b", [128, 128], BF16)
    tri_f = kb.sb("tri_f", [128, 128], F32)
    ones_f = kb.sb("ones_f", [128, 128], F32)
    maskE = kb.sb("maskE", [128, 256], BF16)
    mask_ls = kb.sb("mask_ls", [128, 128], BF16)
    tri_b = kb.sb("tri_b", [128, 128], BF16)
    ones_b = kb.sb("ones_b", [128, 128], BF16)
    maskneg = kb.sb("maskneg", [128, 128], F32)
    ctmp = kb.sb("ctmp", [128, 128], F32)
    cU = Unit("consts")
    ctU = Unit("ctmp")

    def sel(out, in_, cm, step, base, r, w):
        kb.op("pool", lambda E: E.affine_select(out=out, in_=in_, pattern=[[step, 128]], compare_op=ALU.is_ge,
                                                fill=0.0, base=base, channel_multiplier=cm), r, w)
    g.memset("pool", ones_f[:], 1.0, [cU])
    sel(ctmp[:], ones_f[:], 1, -1, 0, [cU], [ctU])
    sel(ident_f[:], ctmp[:], -1, 1, 0, [ctU], [cU])
    g.cp("pool", ident_b[:], ident_f[:], [cU], [cU])
    sel(tri_f[:], ones_f[:], -1, 1, 0, [cU], [cU])
    g.cp("pool", tri_b[:], tri_f[:], [cU], [cU])
    g.cp("pool", ones_b[:], ones_f[:], [cU], [cU])
    g.cp("pool", maskE[:, 128:256], tri_f[:], [cU], [cU])
    sel(ctmp[:], ones_f[:], -1, 1, -1, [cU], [ctU])
    g.cp("pool", maskE[:, 0:128], ctmp[:], [ctU], [cU])
    sel(ctmp[:], ones_f[:], 1, -1, -1, [cU, ctU], [ctU])
    g.cp("pool", mask_ls[:], ctmp[:], [ctU], [cU])
    g.ts("pool", maskneg[:], tri_f[:], -1.0, 1e4, ALU.add, ALU.mult, [cU], [cU])

    persist_off = kb.sb_off

    def load_featcols(vecs, name):
        rows = 8 * len(vecs)
        stage = kb.sb(name + "_st", [64, 128], F32)
        dstt = kb.sb(name, [128, rows], F32)
        su, du = Unit(name + "_st"), Unit(name)
        for vi, vec in enumerate(vecs):
            kb.dma("sp", stage[vi * 8:(vi + 1) * 8, :], vec.rearrange("(c p) -> c p", p=128), writes=[su])
        g.tr(PS[2][:, 0:rows], stage[0:rows, :], ident_f[0:rows, 0:rows], [su, cU], [PSU[2]])
        g.cp("dve", dstt[:], PS[2][:, 0:rows], [PSU[2]], [du])
        return dstt, du

    def rstd_of(xin_ap, junk_ap, ss, rstd, rU, wU_junk, wU_s):
        g.act(junk_ap, xin_ap, AF.Square, rU, [wU_junk, wU_s], accum_out=ss)
        g.ts("dve", rstd, ss, 1.0 / D, NORM_EPS, ALU.mult, ALU.add, [wU_s], [wU_s])
        g.act(rstd, rstd, AF.Ln, [wU_s], [wU_s])
        g.act(rstd, rstd, AF.Exp, [wU_s], [wU_s], scale=-0.5)

    def phase_rwkv():
        kb.sb_off = persist_off
        W3 = [kb.sb(f"W{p}", [128, 8, D], BF16) for p in range(3)]
        Wo = kb.sb("Wo", [128, 8, D], BF16)
        W1 = kb.sb("W1", [128, 8, 64], BF16)
        A1 = kb.sb("A1", [128, 8, 64], BF16)
        G1 = kb.sb("G1", [128, 8, 160], BF16)
        W2 = kb.sb("W2", [64, D], BF16)
        A2 = kb.sb("A2", [64, D], BF16)
        G2a = kb.sb("G2a", [128, D], BF16)
        G2b = kb.sb("G2b", [32, D], BF16)
        gmu, gmuU = load_featcols([rwkv_mu[m] for m in range(6)] + [rwkv_norm_g], "gmu")
        muT = gmu[:, 0:48].rearrange("p (m c) -> p m c", m=6)
        gT = gmu[:, 48:56]
        w0r = kb.sb("w0r", [128, D], F32)
        a0r = kb.sb("a0r", [128, D], BF16)
        kkr_ = kb.sb("kkrow", [128, D], BF16)
        kar_ = kb.sb("karow", [128, D], BF16)
        rkr_ = kb.sb("rkrow", [128, D], BF16)
        lnwr = kb.sb("lnwrow", [128, D], BF16)
        lnbr = kb.sb("lnbrow", [128, D], BF16)
        wU = Unit("rw_weights")
        rowf = kb.sb("tA", [128, D], F32)
        tA = rowf
        rowU = Unit("tA")

        for p in range(3):
            for c in range(8):
                kb.dma("pool", W3[p][:, c, :], w_rkv[p, c * 128:(c + 1) * 128, :], writes=[wU])
        for c in range(8):
            kb.dma("pool", Wo[:, c, :], wo_d[c * 128:(c + 1) * 128, :], writes=[wU])
        kb.dma("pool", W1[:], w1_d.rearrange("(c p) e -> p c e", p=128), writes=[wU])
        kb.dma("pool", A1[:], a1_d.rearrange("(c p) e -> p c e", p=128), writes=[wU])
        kb.dma("pool", G1[:], g1_d.rearrange("(c p) e -> p c e", p=128), writes=[wU])
        kb.dma("pool", W2[:], w2_d, writes=[wU])
        kb.dma("pool", A2[:], a2_d, writes=[wU])
        kb.dma("pool", G2a[:], g2_d[0:128, :], writes=[wU])
        kb.dma("pool", G2b[:], g2_d[128:160, :], writes=[wU])
        kb.dma("sp", w0r[:], w0_d.partition_broadcast(128), writes=[wU])
        for src, dst in ((a0_d, a0r), (kk_d, kkr_), (ka_d, kar_), (rk_d, rkr_), (lnw_d, lnwr), (lnb_d, lnbr)):
            kb.dma("sp", rowf[:], src.partition_broadcast(128), writes=[rowU])
            g.cp("dve", dst[:], rowf[:], [rowU], [wU])
        for c in range(8):
            for Wt in (W3[0], W3[1], W3[2], W1, A1, G1):
                g.act(Wt[:, c, :], Wt[:, c, :], AF.Copy, [wU, gmuU], [wU], scale=gT[:, c:c + 1])

        ck(1)
        xin = kb.sb("xin", [128, D], F32)
        r_sb = kb.sb("r_sb", [128, D], F32)
        k_sb = kb.sb("k_sb", [128, D], F32)
        a_sb = kb.sb("a_sb", [128, D], F32)
        sg = kb.sb("sg", [128, D], F32)
        tB = kb.sb("tB", [128, D], F32)
        sgh = kb.sb("sgh", [128, D], BF16)
        sgl = kb.sb("sgl", [128, D], BF16)
        kkt = kb.sb("kkt", [128, D], F32)
        kp = kb.sb("kp", [128, D], F32)
        b_sb = kb.sb("b_sb", [128, D], F32)
        Ea = kb.sb("Ea", [128, D], F32)
        Eb = kb.sb("Eb", [128, D], F32)
        y_sb = kb.sb("y_sb", [128, D], F32)
        xn = kb.sb("xn", [128, D], BF16)
        v_bf = kb.sb("v_bf", [128, D], BF16)
        g_bf = kb.sb("g_bf", [128, D], BF16)
        At = kb.sb("At", [128, D], BF16)
        Bt = kb.sb("Bt", [128, D], BF16)
        Kt = kb.sb("Kt", [128, D], BF16)
        Rt = kb.sb("Rt", [128, D], BF16)
        Bh = kb.sb("Bh", [128, D], BF16)
        Kh = kb.sb("Kh", [128, D], BF16)
        yg = xn
        hTx = [kb.sb(f"hTx{i}", [128, 8, 129], BF16) for i in range(2)]
        xxT = kb.sb("xxT", [128, 8, 128], BF16)
        xp = [kb.sb(f"xp{i}", [128, 8, 128], BF16) for i in range(6)]
        ygT = kb.sb("ygT", [128, 8, 128], BF16)
        twT = kb.sb("twT", [64, 128], BF16)
        t1T = kb.sb("t1T", [64, 128], BF16)
        sgT1 = kb.sb("sgT1", [128, 128], BF16)
        sgT2 = kb.sb("sgT2", [32, 128], BF16)
        small = kb.sb("small", [128, 16 * 12], F32)
        ss = small[:, 0:1]
        rstd = small[:, 1:2]
        ss16 = small[:, 16:32]
        rn16 = small[:, 32:48]
        bs16 = small[:, 48:64]
        s1 = small[:, 64:80]
        s2 = small[:, 80:96]
        mean = small[:, 96:112]
        var = small[:, 112:128]
        rs = small[:, 128:144]
        m2 = small[:, 144:160]
        PCf = kb.sb("PCf", [64, 16], F32)
        S_f = kb.sb("S_f", [64, D], F32)
        S_bf = [kb.sb(f"S_bf{i}", [64, D], BF16) for i in range(2)]
        FM = [kb.sb(f"FM{i}", [64, 512], BF16) for i in range(2)]
        E1t = [kb.sb(f"E1t{i}", [128, 256], BF16) for i in range(2)]
        E2t = [kb.sb(f"E2t{i}", [128, 256], BF16) for i in range(2)]
        Aj = [kb.sb(f"Aj{i}", [128, 128], BF16) for i in range(2)]
        Bj = [kb.sb(f"Bj{i}", [128, 128], BF16) for i in range(2)]
        Nj = [kb.sb(f"Nj{i}", [128, 128], BF16) for i in range(2)]
        GT = [kb.sb(f"GT{i}", [64, 64], BF16) for i in range(2)]
        QhT = [kb.sb(f"QhT{i}", [64, 128], BF16) for i in range(2)]

        U = {n: Unit(n) for n in ["xin", "r", "k", "a", "sg", "tA", "tB", "cum", "kk", "kp", "b", "Ea", "Eb", "y",
                                  "xn", "v", "g", "At", "Bt", "Kt", "Rt", "Bh", "Kh", "yg", "xxT", "ygT", "twT",
                                  "t1T", "sgT1", "sgT2", "ss", "ss16", "bs16", "gn", "PCf", "x1d", "sgh", "sgl"]}
        U["tA"] = rowU
        U["yg"] = U["xn"]
        hU = units(2, "hTx")
        xpU = units(6, "xp")
        SfU = units(16, "Sf")
        SbU = [units(16, "Sb0_"), units(16, "Sb1_")]
        FMU = units(2, "FM")
        E1U = units(2, "E1t")
        E2U = units(2, "E2t")
        AjU = units(2, "Aj")
        BjU = units(2, "Bj")
        NjU = units(2, "Nj")
        GTU = units(2, "GT")
        QhU = units(2, "QhT")
        UAr = [[PSU[3]] * 4, [PSU[5]] * 4]
        UBr = [[PSU[4]] * 4, [PSU[6]] * 4]
        YrU = [PSU[7]] * 2
        HrU = [PSU[7]] * 2
        LU = PSU[7]
        L2U = PSU[7]
        UA = [PS[3], PS[5]]
        UB = [PS[4], PS[6]]
        P7 = PS[7]
        pbig = [0]

        def bigps():
            i = pbig[0]
            pbig[0] ^= 1
            return PS[i], PSU[i]

        g.memset("dve", S_f[:], 0.0, SfU)
        g.memset("pool", S_bf[0][:], 0.0, SbU[0])
        g.memset("pool", hTx[0][:, :, 0:1], 0.0, [hU[0]])

        for b in range(NB):
            cur = b % 2
            nxt = 1 - cur
            hc, hUc = hTx[cur], hU[cur]
            kb.dma("sp", xin[:], x_d[b * 128:(b + 1) * 128, :], reads=[], writes=[U["xin"]])
            rstd_of(xin[:], xn[:], ss, rstd, [U["xin"]], U["xn"], U["ss"])
            g.ts("dve", xn[:], xin[:], rstd, None, ALU.mult, None, [U["xin"], U["ss"]], [U["xn"]])
            for c in range(8):
                g.tr(pT[:, c * 128:(c + 1) * 128], xn[:, c * 128:(c + 1) * 128], ident_b[:], [U["xn"], cU], [pTU[c // 4]])
            g.cp("act", hc[:, :, 1:129], pT.rearrange("p (c t) -> p c t", c=8), pTU, [hUc])
            g.cp("pool", hTx[nxt][:, :, 0:1], hc[:, :, 128:129], [hUc], [hU[nxt]])
            g.tt("dve", xxT[:], hc[:, :, 0:128], hc[:, :, 1:129], ALU.subtract, [hUc], [U["xxT"]])
            for p in range(6):
                for c in range(8):
                    g.stt(xp[p][:, c, :], xxT[:, c, :], muT[:, p, c:c + 1], hc[:, c, 1:129], ALU.mult, ALU.add,
                          [U["xxT"], hUc, wU, gmuU], [xpU[p]])

            ck(2)

            def proj(xpi, Wt, evac):
                for half in range(2):
                    ps, psu = bigps()
                    for c in range(8):
                        g.mm(ps[:], xp[xpi][:, c, :], Wt[:, c, half * 512:(half + 1) * 512], c == 0, c == 7,
                             [xpU[xpi], wU], [psu])
                    evac(ps, psu, half)
            hs = lambda half: slice(half * 512, (half + 1) * 512)
            proj(0, W3[0], lambda ps, psu, half: g.cp("act", r_sb[:, hs(half)], ps[:], [psu], [U["r"]]))
            proj(2, W3[1], lambda ps, psu, half: g.cp("act", k_sb[:, hs(half)], ps[:], [psu], [U["k"]]))
            proj(3, W3[2], lambda ps, psu, half: g.cp("act", v_bf[:, hs(half)], ps[:], [psu], [U["v"]]))
            ck(3)
            for c in range(8):
                g.mm(P7[0:64, 256:384], W1[:, c, :], xp[1][:, c, :], c == 0, c == 7, [xpU[1], wU], [LU])
            g.act(twT[:], P7[0:64, 256:384], AF.Tanh, [LU], [U["twT"]])
            for half in range(2):
                ps, psu = bigps()
                g.mm(ps[:], twT[:], W2[:, hs(half)], True, True, [U["twT"], wU], [psu])
                g.tt("dve", tA[:, hs(half)], ps[:], w0r[:, hs(half)], ALU.add, [psu, wU], [U["tA"]])
            g.act(sg[:], tA[:], AF.Sigmoid, [U["tA"]], [U["sg"]])
            for c in range(8):
                g.mm(P7[0:64, 256:384], A1[:, c, :], xp[4][:, c, :], c == 0, c == 7, [xpU[4], wU], [LU])
            g.cp("act", t1T[:], P7[0:64, 256:384], [LU], [U["t1T"]])
            for half in range(2):
                ps, psu = bigps()
                g.mm(ps[:], t1T[:], A2[:, hs(half)], True, True, [U["t1T"], wU], [psu])
                g.tt("dve", tB[:, hs(half)], ps[:], a0r[:, hs(half)], ALU.add, [psu, wU], [U["tB"]])
            g.act(a_sb[:], tB[:], AF.Sigmoid, [U["tB"]], [U["a"]])
            for c in range(8):
                g.mm(P7[:, 256:384], G1[:, c, 0:128], xp[5][:, c, :], c == 0, c == 7, [xpU[5], wU], [LU])
            for c in range(8):
                g.mm(P7[0:32, 384:512], G1[:, c, 128:160], xp[5][:, c, :], c == 0, c == 7, [xpU[5], wU], [L2U])
            g.act(sgT1[:], P7[:, 256:384], AF.Sigmoid, [LU], [U["sgT1"]])
            g.act(sgT2[:], P7[0:32, 384:512], AF.Sigmoid, [L2U], [U["sgT2"]])
            for half in range(2):
                ps, psu = bigps()
                g.mm(ps[:], sgT1[:], G2a[:, hs(half)], True, False, [U["sgT1"], wU], [psu])
                g.mm(ps[:], sgT2[:], G2b[:, hs(half)], False, True, [U["sgT2"], wU], [psu])
                g.cp("act", g_bf[:, hs(half)], ps[:], [psu], [U["g"]])
            ck(4)
            g.cp("act", sgh[:], sg[:], [U["sg"]], [U["sgh"]])
            g.tt("pool", tA[:], sg[:], sgh[:], ALU.subtract, [U["sg"], U["sgh"]], [U["tA"]])
            g.cp("act", sgl[:], tA[:], [U["tA"]], [U["sgl"]])

            def trimm(maskT, half):
                ps, psu = bigps()
                g.mm(ps[:], maskT, sgh[:, hs(half)], True, False, [U["sgh"], cU], [psu])
                g.mm(ps[:], maskT, sgl[:, hs(half)], False, True, [U["sgl"], cU], [psu])
                return ps, psu
            for half in range(2):
                ps, psu = trimm(tri_b[:], half)
                g.act(Ea[:, hs(half)], ps[:], AF.Exp, [psu], [U["Ea"]], scale=-C_DEC)
                g.act(Eb[:, hs(half)], ps[:], AF.Exp, [psu], [U["Eb"]], scale=C_DEC)
            ck(41)
            ps, psu = bigps()
            for h in range(H):
                g.mm(ps[0:64, h:h + 1], sgh[:, h * 64:(h + 1) * 64], tri_b[:, 127:128], True, False, [U["sgh"], cU], [psu])
                g.mm(ps[0:64, h:h + 1], sgl[:, h * 64:(h + 1) * 64], tri_b[:, 127:128], False, True, [U["sgl"], cU], [psu])
            g.act(PCf[:], ps[0:64, 0:16], AF.Exp, [psu], [U["PCf"]], scale=-C_DEC)
            ck(5)
            g.tt("pool", kkt[:], k_sb[:], kkr_[:], ALU.mult, [U["k"], wU], [U["kk"]])
            g.tt("pool", tA[:], kkt[:], kkt[:], ALU.mult, [U["kk"]], [U["tA"]])
            g.red(ss16, tA[:].rearrange("p (h n) -> p h n", h=H), [U["tA"]], [U["ss16"]])
            g.ts("dve", rn16, ss16, 1e-24, None, ALU.max, None, [U["ss16"]], [U["ss16"]])
            g.act(rn16, rn16, AF.Ln, [U["ss16"]], [U["ss16"]])
            g.act(rn16, rn16, AF.Exp, [U["ss16"]], [U["ss16"]], scale=-0.5)
            v3 = lambda t_: t_[:].rearrange("p (h n) -> p h n", h=H)
            bc = lambda s_: s_.unsqueeze(2).broadcast_to([128, H, N])
            g.tt("dve", v3(kkt), v3(kkt), bc(rn16), ALU.mult, [U["kk"], U["ss16"]], [U["kk"]])
            g.tt("pool", Rt[:], r_sb[:], Ea[:], ALU.mult, [U["r"], U["Ea"]], [U["Rt"]])
            g.stt(tA[:], a_sb[:], -1.0, kar_[:], ALU.add, ALU.mult, [U["a"], wU], [U["tA"]])
            g.stt(kp[:], tA[:], 1.0, k_sb[:], ALU.add, ALU.mult, [U["tA"], U["k"]], [U["kp"]])
            g.tt("pool", b_sb[:], kkt[:], a_sb[:], ALU.mult, [U["kk"], U["a"]], [U["b"]])
            g.tt("pool", Bt[:], b_sb[:], Eb[:], ALU.mult, [U["b"], U["Eb"]], [U["Bt"]])
            g.tt("dve", Kt[:], kp[:], Eb[:], ALU.mult, [U["kp"], U["Eb"]], [U["Kt"]])
            for half in range(2):
                ps, psu = trimm(mask_ls[:], half)
                g.act(Eb[:, hs(half)], ps[:], AF.Exp, [psu], [U["Eb"]], scale=-C_DEC)
            for half in range(2):
                ps, psu = trimm(maskE[:, 0:128], half)
                g.act(Ea[:, hs(half)], ps[:], AF.Exp, [psu], [U["Ea"]], scale=-C_DEC)
            g.stt(At[:], kkt[:], -1.0, Ea[:], ALU.mult, ALU.mult, [U["kk"], U["Ea"]], [U["At"]])
            g.tt("dve", Bh[:], b_sb[:], Eb[:], ALU.mult, [U["b"], U["Eb"]], [U["Bh"]])
            g.tt("pool", Kh[:], kp[:], Eb[:], ALU.mult, [U["kp"], U["Eb"]], [U["Kh"]])
            g.tt("pool", tB[:], r_sb[:], kp[:], ALU.mult, [U["r"], U["kp"]], [U["tB"]])
            g.tt("pool", tB[:], tB[:], rkr_[:], ALU.mult, [U["tB"], wU], [U["tB"]])
            g.red(bs16, v3(tB), [U["tB"]], [U["bs16"]])

            ck(6)
            for h in range(H):
                pr = h % 2
                hsl = slice(h * 64, (h + 1) * 64)
                fm, fmu = FM[pr], FMU[pr]
                ua, ub = UA[pr], UB[pr]
                uar, ubr = UAr[pr], UBr[pr]
                pTh = pT[0:64, pr * 512:(pr + 1) * 512]
                for qi, (src, su) in enumerate(((At, U["At"]), (Rt, U["Rt"]), (Bt, U["Bt"]), (Kt, U["Kt"]))):
                    g.tr(pT[0:64, pr * 512 + qi * 128: pr * 512 + (qi + 1) * 128], src[:, hsl], ident_b[:], [su, cU], [pTU[pr]])
                g.cp("act", fm[:], pTh, [pTU[pr]], [fmu])
                AT_, RT_, BT_, KT_ = fm[:, 0:128], fm[:, 128:256], fm[:, 256:384], fm[:, 384:512]
                g.mm(ua[:, 0:256], BT_, fm[:, 0:256], True, True, [fmu], [uar[0], uar[1]])
                g.mm(ua[:, 256:512], KT_, fm[:, 0:256], True, True, [fmu], [uar[2], uar[3]])
                g.mm(ub[:, 0:128], AT_, BT_, True, True, [fmu], [ubr[0]])
                e1, e2 = E1t[pr], E2t[pr]
                g.tt("dve", e1[:], ua[:, 0:256], maskE[:], ALU.mult, [uar[0], uar[1], cU], [E1U[pr]])
                g.tt("dve", e2[:], ua[:, 256:512], maskE[:], ALU.mult, [uar[2], uar[3], cU], [E2U[pr]])
                g.tt("dve", Bj[0][:], ub[:, 0:128], mask_ls[:], ALU.mult, [ubr[0], cU], [BjU[0]])
                MrbT, LakT, MrkT = e1[:, 128:256], e2[:, 0:128], e2[:, 128:256]
                g.mm(ub[:, 128:192], AT_, ident_b[0:64, 0:64], True, True, [fmu, cU], [ubr[1]])
                g.mm(ub[:, 192:256], LakT, v_bf[:, hsl], True, True, [E2U[pr], U["v"]], [ubr[1]])
                g.cp("act", Nj[0][:], ub[:, 128:256], [ubr[1]], [NjU[0]])
                A_cur, A_u = e1[:, 0:128], E1U[pr]
                B_cur, B_u = Bj[0][:], BjU[0]
                for j in range(7):
                    ni, no = j % 2, (j + 1) % 2
                    g.mm(ub[:, 256 + ni * 128: 384 + ni * 128], A_cur, Nj[ni][:], True, True, [A_u, NjU[ni]], [ubr[2 + ni]])
                    g.tt("dve", Nj[no][:], ub[:, 256 + ni * 128: 384 + ni * 128], Nj[ni][:], ALU.add,
                         [ubr[2 + ni], NjU[ni]], [NjU[no]])
                    if j < 6:
                        jo = j % 2
                        g.mm(ua[:, jo * 256: jo * 256 + 128], B_cur, A_cur, True, True, [A_u, B_u], [uar[jo * 2]])
                        g.mm(ua[:, jo * 256 + 128: jo * 256 + 256], A_cur, B_cur, True, True, [A_u, B_u], [uar[jo * 2 + 1]])
                        bo = (j + 1) % 2
                        g.cp("act", Aj[jo][:], ua[:, jo * 256: jo * 256 + 128], [uar[jo * 2]], [AjU[jo]])
                        g.cp("act", Bj[bo][:], ua[:, jo * 256 + 128: jo * 256 + 256], [uar[jo * 2 + 1]], [BjU[bo]])
                        A_cur, A_u = Aj[jo][:], AjU[jo]
                        B_cur, B_u = Bj[bo][:], BjU[bo]
                XZ, XZu = Nj[1], NjU[1]
                X_, Z_ = XZ[:, 0:64], XZ[:, 64:128]
                g.mm(ub[0:64, 0:64], X_, Bh[:, hsl], True, True, [XZu, U["Bh"]], [ubr[0]])
                g.cp("act", GT[pr][:], ub[0:64, 0:64], [ubr[0]], [GTU[pr]])
                g.mm(ub[0:64, 128:256], X_, MrbT, True, True, [XZu, E1U[pr]], [ubr[1]])
                g.tt("dve", QhT[pr][:], ub[0:64, 128:256], RT_, ALU.add, [ubr[1], fmu], [QhU[pr]])
                Sb_cur, Sb_cu = S_bf[cur][:, hsl], SbU[cur][h]
                Yr = P7[:, pr * 64:(pr + 1) * 64]
                g.mm(Yr, MrbT, Z_, True, False, [E1U[pr], XZu], [YrU[pr]])
                g.mm(Yr, MrkT, v_bf[:, hsl], False, False, [E2U[pr], U["v"]], [YrU[pr]])
                g.mm(Yr, QhT[pr][:], Sb_cur, False, True, [QhU[pr], Sb_cu], [YrU[pr]])
                g.cp("act", y_sb[:, hsl], Yr, [YrU[pr]], [U["y"]])
                Hr = P7[0:64, 128 + pr * 64: 192 + pr * 64]
                g.mm(Hr, Bh[:, hsl], Z_, True, False, [U["Bh"], XZu], [HrU[pr]])
                g.mm(Hr, Kh[:, hsl], v_bf[:, hsl], False, False, [U["Kh"], U["v"]], [HrU[pr]])
                g.mm(Hr, GT[pr][:], Sb_cur, False, True, [GTU[pr], Sb_cu], [HrU[pr]])
                g.stt(S_f[:, hsl], S_f[:, hsl], PCf[:, h:h + 1], Hr, ALU.mult, ALU.add,
                      [SfU[h], U["PCf"], HrU[pr]], [SfU[h]])
                g.cp("pool", S_bf[nxt][:, hsl], S_f[:, hsl], [SfU[h]], [SbU[nxt][h]])
                ck(7)

            ck(8)
            g.red(s1, v3(y_sb), [U["y"]], [U["gn"]])
            g.tt("pool", tA[:], y_sb[:], y_sb[:], ALU.mult, [U["y"]], [U["tA"]])
            g.red(s2, v3(tA), [U["tA"]], [U["gn"]])
            g.ts("dve", mean, s1, 1.0 / N, None, ALU.mult, None, [U["gn"]], [U["gn"]])
            g.tt("dve", m2, mean, mean, ALU.mult, [U["gn"]], [U["gn"]])
            g.stt(var, s2, 1.0 / N, m2, ALU.mult, ALU.subtract, [U["gn"]], [U["gn"]])
            g.ts("dve", var, var, GN_EPS, None, ALU.add, None, [U["gn"]], [U["gn"]])
            g.act(var, var, AF.Sqrt, [U["gn"]], [U["gn"]])
            kb.op("dve", lambda E: E.reciprocal(out=rs, in_=var), [U["gn"]], [U["gn"]])
            g.tt("dve", v3(tA), v3(y_sb), bc(mean), ALU.subtract, [U["y"], U["gn"]], [U["tA"]])
            g.tt("dve", v3(tA), v3(tA), bc(rs), ALU.mult, [U["tA"], U["gn"]], [U["tA"]])
            g.tt("pool", tA[:], tA[:], lnwr[:], ALU.mult, [U["tA"], wU], [U["tA"]])
            g.tt("pool", tA[:], tA[:], lnbr[:], ALU.add, [U["tA"], wU], [U["tA"]])
            g.tt("dve", v3(tB), v3(v_bf), bc(bs16), ALU.mult, [U["v"], U["bs16"]], [U["tB"]])
            g.tt("pool", tA[:], tA[:], tB[:], ALU.add, [U["tA"], U["tB"]], [U["tA"]])
            g.tt("pool", yg[:], tA[:], g_bf[:], ALU.mult, [U["tA"], U["g"]], [U["yg"]])
            for c in range(8):
                g.tr(pT[:, c * 128:(c + 1) * 128], yg[:, c * 128:(c + 1) * 128], ident_b[:], [U["yg"], cU], [pTU[c // 4]])
            g.cp("act", ygT[:], pT.rearrange("p (c t) -> p c t", c=8), pTU, [U["ygT"]])
            for half in range(2):
                ps, psu = bigps()
                for c in range(8):
                    g.mm(ps[:], ygT[:, c, :], Wo[:, c, hs(half)], c == 0, c == 7, [U["ygT"], wU], [psu])
                g.tt("dve", xin[:, hs(half)], ps[:], xin[:, hs(half)], ALU.add, [psu, U["xin"]], [U["xin"]])
            kb.dma("sp", x1_d[b * 128:(b + 1) * 128, :], xin[:], reads=[U["xin"]], writes=[U["x1d"]])
        return U["x1d"]

    try:
        x1U = phase_rwkv()
    except _Stop:
        x1U = Unit("x1dummy")
    kb.fence()

    kb.sb_off = persist_off
    x_sb = kb.sb("x_sb", [128, NB, D], F32)
    xU = units(NB, "x")
    resid_off = kb.sb_off
    for b in range(NB):
        kb.dma("sp", x_sb[:, b, :], x1_d[b * 128:(b + 1) * 128, :], reads=[x1U], writes=[xU[b]])

    outU = Unit("out")

    def dump_x():
        for b in range(NB):
            kb.dma("sp", out_d[b * 128:(b + 1) * 128, :], x_sb[:, b, :], reads=[xU[b]], writes=[outU])
        kb.final_wait("sp", [outU])
        kb.fence()
        kb.build()
        return nc

    if stop_after <= 1:
        return dump_x()

    def phase_mlp(layer, final):
        kb.sb_off = resid_off
        hnT = kb.sb("hnT", [128, 8, T], BF16)
        hid = kb.sb("hid", [128, 4, T], BF16)
        WI = [kb.sb(f"WI{i}", [128, 8, 512], BF16) for i in range(2)]
        WO = [kb.sb(f"WO{i}", [128, 4, D], BF16) for i in range(2)]
        gT, gU = load_featcols([mlp_norm_g[layer]], f"mgT{layer}")
        xn = kb.sb("mxn", [128, D], BF16)
        junk = kb.sb("mjunk", [128, D], BF16)
        rl = [kb.sb(f"mrl{i}", [128, 512], F32) for i in range(2)]
        small = kb.sb("msmall", [128, 4], F32)
        hnU = units(NB, "hnT")
        NT = T // 512
        hidU = [units(NT, f"hid{fc}_") for fc in range(4)]
        WIU = units(2, "WI")
        WOU = units(2, "WO")
        xnU = Unit("mxn")
        jU = Unit("mjunk")
        sU = Unit("msmall")
        rlU = units(2, "mrl")

        def load_w(e):
            i = e % 2
            for c in range(8):
                kb.dma("pool", WI[i][:, c, :], mlp_w_in[layer, c * 128:(c + 1) * 128, e * 512:(e + 1) * 512], writes=[WIU[i]])
            for fc in range(4):
                kb.dma("pool", WO[i][:, fc, :], mlp_w_out[layer, e * 512 + fc * 128: e * 512 + (fc + 1) * 128, :], writes=[WOU[i]])
        load_w(0)
        for b in range(NB):
            rstd_of(x_sb[:, b, :], junk[:], small[:, 0:1], small[:, 1:2], [xU[b]], jU, sU)
            g.ts("dve", xn[:], x_sb[:, b, :], small[:, 1:2], None, ALU.mult, None, [xU[b], sU], [xnU])
            for c in range(8):
                g.tr(pT[:, c * 128:(c + 1) * 128], xn[:, c * 128:(c + 1) * 128], ident_b[:], [xnU, cU], [pTU[c // 4]])
            for c in range(8):
                if c % 2:
                    g.ts("dve", hnT[:, c, b * 128:(b + 1) * 128], pT[:, c * 128:(c + 1) * 128], gT[:, c:c + 1], None, ALU.mult, None,
                         [pTU[c // 4], gU], [hnU[b]])
                else:
                    g.act(hnT[:, c, b * 128:(b + 1) * 128], pT[:, c * 128:(c + 1) * 128], AF.Copy, [pTU[c // 4], gU], [hnU[b]],
                          scale=gT[:, c:c + 1])
        pb = [0]
        for e in range(8):
            i = e % 2
            if e + 1 < 8:
                load_w(e + 1)
            for fc in range(4):
                for tt in range(NT):
                    pi = pb[0]
                    pb[0] ^= 1
                    ps, psu = PS[pi], PSU[pi]
                    for c in range(8):
                        g.mm(ps[:], WI[i][:, c, fc * 128:(fc + 1) * 128], hnT[:, c, tt * 512:(tt + 1) * 512], c == 0, c == 7,
                             [WIU[i]] + hnU[tt * 4:(tt + 1) * 4], [psu])
                    ri = (fc * 4 + tt) % 2
                    g.act(rl[ri][:], ps[:], AF.Relu, [psu], [rlU[ri]])
                    g.tt("pool", hid[:, fc, tt * 512:(tt + 1) * 512], rl[ri][:], rl[ri][:], ALU.mult, [rlU[ri]], [hidU[fc][tt]])
            for b in range(NB):
                for half in range(2):
                    pi = 3 + pb[0]
                    pb[0] ^= 1
                    ps, psu = PS[pi], PSU[pi]
                    for fc in range(4):
                        g.mm(ps[:], hid[:, fc, b * 128:(b + 1) * 128], WO[i][:, fc, half * 512:(half + 1) * 512], fc == 0, fc == 3,
                             [hidU[fc][b // 4], WOU[i]], [psu])
                    g.tt("dve", x_sb[:, b, half * 512:(half + 1) * 512], ps[:], x_sb[:, b, half * 512:(half + 1) * 512], ALU.add,
                         [psu, xU[b]], [xU[b]])
                if final and e == 7:
                    kb.dma("sp", out_d[b * 128:(b + 1) * 128, :], x_sb[:, b, :], reads=[xU[b]], writes=[outU])

    phase_mlp(0, False)
    kb.fence()
    if stop_after <= 2:
        return dump_x()

    kb.sb_off = resid_off
    V_all = kb.sb("V_all", [128, NB, H, 65], BF16)
    VU = units(NB, "V")
    oU = units(NB, "o")
    attn_off = kb.sb_off

    def phase_qkv():
        kb.sb_off = attn_off
        KVW = kb.sb("KVW", [128, 8, 2 * D + H], BF16)
        WQ = kb.sb("WQ", [128, 8, D], BF16)
        gkq, gkqU = load_featcols([kv_norm_g, attn_norm_g], "gkq")
        gTk = gkq[:, 0:8]
        gTq = gkq[:, 8:16]
        kgr = kb.sb("kgr", [128, D], F32)
        qgr = kb.sb("qgr", [128, D], F32)
        fbr = kb.sb("fbr", [128, H], F32)
        xn = kb.sb("qxn", [128, D], BF16)
        junk = kb.sb("qjunk", [128, D], BF16)
        hT = kb.sb("qhT", [128, 8, 128], BF16)
        kq = kb.sb("kq_sb", [128, D], F32)
        sq = kb.sb("sq_sb", [128, D], F32)
        k_aug = kb.sb("k_aug", [128, H, 70], BF16)
        q_aug = kb.sb("q_aug", [128, H, 70], BF16)
        kTb = kb.sb("kTb", [70, H, 128], BF16)
        qTb = kb.sb("qTb", [70, H, 128], BF16)
        small = kb.sb("qsmall", [128, 16 * 12], F32)
        carry = kb.sb("carry", [128, H], F32)
        cbf = kb.sb("cbf", [128, 3, H], BF16)
        lfb = kb.sb("lfb", [128, 3, H], BF16)
        ss, rstd = small[:, 0:1], small[:, 1:2]
        ss16, rn16, lf, cc, r1 = small[:, 16:32], small[:, 32:48], small[:, 48:64], small[:, 64:80], small[:, 80:96]
        wU = Unit("qkv_w")
        U = {n: Unit(n) for n in ["xn", "junk", "hT", "kq", "sq", "k_aug", "q_aug", "kTb", "qTb", "s", "ss16", "lf", "cc",
                                  "carry", "cbf", "kTd", "qTd", "lfb"]}
        for c in range(8):
            kb.dma("pool", KVW[:, c, 0:1024], kv_w[c * 128:(c + 1) * 128, 0:1024], writes=[wU])
            kb.dma("pool", KVW[:, c, 1024:2 * D + H], kv_w[c * 128:(c + 1) * 128, 1024:2 * D + H], writes=[wU])
            kb.dma("pool", WQ[:, c, :], attn_w_q[c * 128:(c + 1) * 128, :], writes=[wU])
        kb.dma("sp", kgr[:], k_norm_g.partition_broadcast(128), writes=[wU])
        kb.dma("sp", qgr[:], q_norm_g.partition_broadcast(128), writes=[wU])
        kb.dma("sp", fbr[:], kv_f_bias.partition_broadcast(128), writes=[wU])
        for c in range(8):
            g.act(KVW[:, c, :], KVW[:, c, :], AF.Copy, [wU, gkqU], [wU], scale=gTk[:, c:c + 1])
            g.act(WQ[:, c, :], WQ[:, c, :], AF.Copy, [wU, gkqU], [wU], scale=gTq[:, c:c + 1])
        g.ts("dve", qgr[:], qgr[:], 0.125, None, ALU.mult, None, [wU], [wU])
        ck(10)
        g.memset("pool", carry[:], 0.0, [U["carry"]])
        g.memset("pool", k_aug[:], 1.0, [U["k_aug"]])
        g.memset("pool", q_aug[:], 1.0, [U["q_aug"]])
        g.memset("pool", V_all[:], 1.0, VU)
        v3 = lambda ap: ap.rearrange("p (h n) -> p h n", h=H)
        bc = lambda s_: s_.unsqueeze(2).broadcast_to([128, H, N])
        pbig = [0]

        def bigps():
            i = pbig[0]
            pbig[0] ^= 1
            return PS[i], PSU[i]

        def headnorm(dst_aug, gain_row):
            g.tt("pool", sq[:], kq[:], kq[:], ALU.mult, [U["kq"]], [U["sq"]])
            g.red(ss16, v3(sq[:]), [U["sq"]], [U["ss16"]])
            g.ts("dve", rn16, ss16, 1.0 / N, NORM_EPS, ALU.mult, ALU.add, [U["ss16"]], [U["ss16"]])
            g.act(rn16, rn16, AF.Ln, [U["ss16"]], [U["ss16"]])
            g.act(rn16, rn16, AF.Exp, [U["ss16"]], [U["ss16"]], scale=-0.5)
            g.tt("dve", v3(sq[:]), v3(kq[:]), bc(rn16), ALU.mult, [U["kq"], U["ss16"]], [U["sq"]])
            return lambda du: g.tt("pool", dst_aug[:, :, 0:64], v3(sq[:]), v3(gain_row[:]), ALU.mult, [U["sq"], wU], [du])

        for b in range(NB):
            rstd_of(x_sb[:, b, :], junk[:], ss, rstd, [xU[b]], U["junk"], U["s"])
            g.ts("dve", xn[:], x_sb[:, b, :], rstd, None, ALU.mult, None, [xU[b], U["s"]], [U["xn"]])
            for c in range(8):
                g.tr(pT[:, c * 128:(c + 1) * 128], xn[:, c * 128:(c + 1) * 128], ident_b[:], [U["xn"], cU], [pTU[c // 4]])
            g.cp("act", hT[:], pT.rearrange("p (c t) -> p c t", c=8), pTU, [U["hT"]])
            for half in range(2):
                ps, psu = bigps()
                for c in range(8):
                    g.mm(ps[:], hT[:, c, :], KVW[:, c, half * 512:(half + 1) * 512], c == 0, c == 7, [U["hT"], wU], [psu])
                g.cp("act", kq[:, half * 512:(half + 1) * 512], ps[:], [psu], [U["kq"]])
            headnorm(k_aug, kgr)(U["k_aug"])
            ck(11)
            for half in range(2):
                ps, psu = bigps()
                for c in range(8):
                    g.mm(ps[:], hT[:, c, :], KVW[:, c, D + half * 512: D + (half + 1) * 512], c == 0, c == 7, [U["hT"], wU], [psu])
                g.cp("act", V_all[:, b, half * 8:(half + 1) * 8, 0:64], ps[:].rearrange("p (h n) -> p h n", h=8), [psu], [VU[b]])
            ps, psu = bigps()
            for c in range(8):
                g.mm(ps[:, 0:16], hT[:, c, :], KVW[:, c, 2 * D:2 * D + H], c == 0, c == 7, [U["hT"], wU], [psu])
            g.tt("dve", lf, ps[:, 0:16], fbr[:], ALU.add, [psu, wU], [U["lf"]])
            g.act(lf, lf, AF.Sigmoid, [U["lf"]], [U["lf"]])
            g.act(lf, lf, AF.Ln, [U["lf"]], [U["lf"]])
            g.cp("dve", lfb[:, 0, :], lf, [U["lf"]], [U["lfb"]])
            g.tt("dve", r1, lf, lfb[:, 0, :], ALU.subtract, [U["lf"], U["lfb"]], [U["cc"]])
            g.cp("dve", lfb[:, 1, :], r1, [U["cc"]], [U["lfb"]])
            g.tt("dve", r1, r1, lfb[:, 1, :], ALU.subtract, [U["cc"], U["lfb"]], [U["cc"]])
            g.cp("dve", lfb[:, 2, :], r1, [U["cc"]], [U["lfb"]])
            ps, psu = bigps()
            for pc in range(3):
                g.mm(ps[:, 0:16], tri_b[:], lfb[:, pc, :], pc == 0, pc == 2, [U["lfb"], cU], [psu])
            ps2, psu2 = bigps()
            for pc in range(3):
                g.mm(ps2[:, 0:16], ones_b[:], lfb[:, pc, :], pc == 0, pc == 2, [U["lfb"], cU], [psu2])
            g.tt("dve", cc, ps[:, 0:16], carry[:], ALU.add, [psu, U["carry"]], [U["cc"]])
            g.tt("dve", carry[:], ps2[:, 0:16], carry[:], ALU.add, [psu2, U["carry"]], [U["carry"]])
            g.cp("dve", cbf[:, 0, :], cc, [U["cc"]], [U["cbf"]])
            g.tt("dve", r1, cc, cbf[:, 0, :], ALU.subtract, [U["cc"], U["cbf"]], [U["cc"]])
            g.cp("dve", cbf[:, 1, :], r1, [U["cc"]], [U["cbf"]])
            g.tt("dve", r1, r1, cbf[:, 1, :], ALU.subtract, [U["cc"], U["cbf"]], [U["cc"]])
            g.cp("dve", cbf[:, 2, :], r1, [U["cc"]], [U["cbf"]])
            g.ts("dve", k_aug[:, :, 67:70], cbf[:].rearrange("p s h -> p h s"), -1.0, None, ALU.mult, None, [U["cbf"]], [U["k_aug"]])
            g.cp("dve", q_aug[:, :, 64:67], cbf[:].rearrange("p s h -> p h s"), [U["cbf"]], [U["q_aug"]])
            ck(12)
            for half in range(2):
                ps, psu = bigps()
                for c in range(8):
                    g.mm(ps[:], hT[:, c, :], WQ[:, c, half * 512:(half + 1) * 512], c == 0, c == 7, [U["hT"], wU], [psu])
                g.cp("act", kq[:, half * 512:(half + 1) * 512], ps[:], [psu], [U["kq"]])
            headnorm(q_aug, qgr)(U["q_aug"])
            ck(13)
            for aug, au, Tb, Tu, dst, du in ((k_aug, U["k_aug"], kTb, U["kTb"], kT_d, U["kTd"]),
                                             (q_aug, U["q_aug"], qTb, U["qTb"], qT_d, U["qTd"])):
                for hh in range(2):
                    for h8 in range(8):
                        h = hh * 8 + h8
                        g.tr(pT[0:70, h8 * 128:(h8 + 1) * 128], aug[:, h, :], ident_b[:], [au, cU], [pTU[h8 // 4]])
                    g.cp("act", Tb[:, hh * 8:(hh + 1) * 8, :], pT[0:70, :].rearrange("p (h t) -> p h t", h=8), pTU, [Tu])
                    if hh == 0:
                        ck(14)
                kb.dma("sp", dst[:, :, b * 128:(b + 1) * 128].rearrange("h r t -> r h t"), Tb[:], reads=[Tu], writes=[du])
                ck(15)
        return U["kTd"], U["qTd"]

    try:
        kTdU, qTdU = phase_qkv()
    except _Stop:
        kb.fence()
        return dump_x()
    kb.fence()

    def phase_attn():
        kb.sb_off = attn_off
        o_all = kb.sb("o_all", [128, NB, D], BF16)
        WOa = kb.sb("WOa", [128, 8, D], BF16)
        kTh = [kb.sb(f"kTh{i}", [70, T], BF16) for i in range(2)]
        qTh = [kb.sb(f"qTh{i}", [70, T], BF16) for i in range(2)]
        PT = [kb.sb(f"PT{i}", [128, 4, 128], BF16) for i in range(2)]
        tmpS = [kb.sb(f"tmpS{i}", [128, 128], F32) for i in range(2)]
        rec = kb.sb("rec", [128, 4], F32)
        oT = kb.sb("oT", [128, 8, 128], BF16)
        wU = Unit("woa")
        khU, qhU, PTU, tSU = units(2, "kTh"), units(2, "qTh"), units(2, "PT"), units(2, "tmpS")
        recU, oTU = Unit("rec"), Unit("oT")
        for c in range(8):
            kb.dma("pool", WOa[:, c, :], attn_w_o[c * 128:(c + 1) * 128, :], writes=[wU])
        psS = [PS[3], PS[4]]
        psSU = [PSU[3], PSU[4]]
        psO = [PS[5], PS[6]]
        psOU = [PSU[5], PSU[6]]
        gi = 0
        oi = 0
        for h in range(H):
            hb = h % 2
            kb.dma("sp", kTh[hb][:], kT_d[h], reads=[kTdU], writes=[khU[hb]])
            kb.dma("sp", qTh[hb][:], qT_d[h], reads=[qTdU], writes=[qhU[hb]])
            for i in range(NB):
                po, pou = psO[oi % 2], psOU[oi % 2]
                oi += 1
                for j0 in range(0, i + 1, 4):
                    js = list(range(j0, min(j0 + 4, i + 1)))
                    ps, psu = psS[gi % 2], psSU[gi % 2]
                    pt, ptu = PT[gi % 2], PTU[gi % 2]
                    ts_, tsu = tmpS[gi % 2], tSU[gi % 2]
                    gi += 1
                    for jj, j in enumerate(js):
                        g.mm(ps[:, jj * 128:(jj + 1) * 128], kTh[hb][:, j * 128:(j + 1) * 128], qTh[hb][:, i * 128:(i + 1) * 128],
                             True, True, [khU[hb], qhU[hb]], [psu])
                    nfull = len(js) - (1 if js[-1] == i else 0)
                    isdiag = js[-1] == i
                    extra = []
                    if isdiag:
                        jj = len(js) - 1
                        g.tt("dve", ts_[:], ps[:, jj * 128:(jj + 1) * 128], maskneg[:], ALU.add, [psu, cU], [tsu])
                        extra = [tsu]
                    if nfull > 0:
                        g.act(pt[:, 0:nfull, :], ps[:, 0:nfull * 128].rearrange("p (j t) -> p j t", j=nfull), AF.Exp, [psu] + extra, [ptu])
                    if isdiag:
                        g.act(pt[:, jj, :], ts_[:], AF.Exp, [tsu], [ptu])
                    if h == 0 and i == 3:
                        ck(32)
                    for jj, j in enumerate(js):
                        g.mm(po[:, 0:65], pt[:, jj, :], V_all[:, j, h, :], j == 0, j == i, [ptu, VU[j]], [pou])
                ck(16)
                kb.op("dve", lambda E, po=po: E.reciprocal(out=rec[:, 0:1], in_=po[:, 64:65]), [pou], [recU])
                g.ts("dve", o_all[:, i, h * 64:(h + 1) * 64], po[:, 0:64], rec[:, 0:1], None, ALU.mult, None, [pou, recU], [oU[i]])
                ck(18 + i if h == 0 else -1)
        ck(17)
        pb = 0
        for b in range(NB):
            for c in range(8):
                g.tr(pT[:, c * 128:(c + 1) * 128], o_all[:, b, c * 128:(c + 1) * 128], ident_b[:], [oU[b], cU], [pTU[c // 4]])
            g.cp("act", oT[:], pT.rearrange("p (c t) -> p c t", c=8), pTU, [oTU])
            for half in range(2):
                ps, psu = PS[pb], PSU[pb]
                pb ^= 1
                for c in range(8):
                    g.mm(ps[:], oT[:, c, :], WOa[:, c, half * 512:(half + 1) * 512], c == 0, c == 7, [oTU, wU], [psu])
                g.tt("dve", x_sb[:, b, half * 512:(half + 1) * 512], ps[:], x_sb[:, b, half * 512:(half + 1) * 512], ALU.add,
                     [psu, xU[b]], [xU[b]])

    try:
        phase_attn()
    except _Stop:
        pass
    kb.fence()
    if stop_after <= 3:
        return dump_x()

    phase_mlp(1, True)
    kb.final_wait("sp", [outU])
    kb.fence()
    kb.build()
    return nc


_NC_CACHE = {}


def _prep(inputs):
    f = lambda a: np.ascontiguousarray(np.asarray(a, dtype=np.float32))
    common = {
        "rwkv_norm_g": f(inputs["rwkv_norm_g"][0]),
        "rwkv_mu": f(inputs["rwkv_mu"][0]),
        "rwkv_w_rkv": f(inputs["rwkv_w_rkv"][0]),
        "rwkv_w0": f(inputs["rwkv_w0"][0]),
        "rwkv_w1": f(inputs["rwkv_w1"][0]),
        "rwkv_w2": f(inputs["rwkv_w2"][0]),
        "rwkv_a0": f(inputs["rwkv_a0"][0]),
        "rwkv_a1": f(inputs["rwkv_a1"][0]),
        "rwkv_a2": f(inputs["rwkv_a2"][0]),
        "rwkv_g1": f(inputs["rwkv_g1"][0]),
        "rwkv_g2": f(inputs["rwkv_g2"][0]),
        "rwkv_k_k": f(inputs["rwkv_k_k"][0]),
        "rwkv_k_a": f(inputs["rwkv_k_a"][0]),
        "rwkv_r_k": f(inputs["rwkv_r_k"][0]).reshape(D),
        "rwkv_lnx_w": f(inputs["rwkv_lnx_w"][0]),
        "rwkv_lnx_b": f(inputs["rwkv_lnx_b"][0]),
        "rwkv_w_o": f(inputs["rwkv_w_o"][0]),
        "kv_norm_g": f(inputs["kv_norm_g"]),
        "kv_w": f(inputs["kv_w"]),
        "kv_f_bias": f(inputs["kv_f_bias"]),
        "k_norm_g": f(inputs["k_norm_g"]).reshape(D),
        "attn_norm_g": f(inputs["attn_norm_g"][0]),
        "attn_w_q": f(inputs["attn_w_q"][0]),
        "q_norm_g": f(inputs["q_norm_g"][0]).reshape(D),
        "attn_w_o": f(inputs["attn_w_o"][0]),
        "mlp_norm_g": f(inputs["mlp_norm_g"]),
        "mlp_w_in": f(inputs["mlp_w_in"]),
        "mlp_w_out": f(inputs["mlp_w_out"]),
    }
    x = f(inputs["x"])
    return [dict(common, x=x[b]) for b in range(8)]


def kernel(_stop_after=99, **inputs):
    if _stop_after not in _NC_CACHE:
        _NC_CACHE[_stop_after] = build_nc(_stop_after)
    nc = _NC_CACHE[_stop_after]
    in_maps = _prep(inputs)
    res = run_bass_kernel_spmd(nc, in_maps, core_ids=list(range(8)))
    return np.stack([np.asarray(r["out"], dtype=np.float32) for r in res.results], axis=0)
```

```python
import numpy as np
import concourse.bass as bass
import concourse.mybir as mybir
from concourse.bass_utils import run_bass_kernel_spmd

F32 = mybir.dt.float32
BF16 = mybir.dt.bfloat16
AF = mybir.ActivationFunctionType
ALU = mybir.AluOpType
AX = mybir.AxisListType

D = 1024
T = 2048
H = 16
N = 64
NB = 16
FF = 4096
C_DEC = 0.6065306597126334
NORM_EPS = 1e-6
GN_EPS = 64e-5


DBG = 0


class _Stop(Exception):
    pass


def ck(k):
    if DBG == k:
        raise _Stop()


class Unit:
    __slots__ = ("w", "r", "name", "excl")

    def __init__(self, name="", excl=False):
        self.w = None
        self.r = {}
        self.name = name
        self.excl = excl


def units(n, name=""):
    return [Unit(f"{name}{i}") for i in range(n)]


class KB:
    SEM_ROLL = 3000

    def __init__(self, nc, n_dma_sems=6, same_engine_sync=True):
        self.nc = nc
        self.eng = {"pe": nc.tensor, "act": nc.scalar, "dve": nc.vector,
                    "pool": nc.gpsimd, "sp": nc.sync}
        self.q = {e: [] for e in self.eng}
        self.nsem = 0
        self.sem = {e: self._newsem(e) for e in self.eng}
        self.cnt = {e: 0 for e in self.eng}
        self.seen = {e: {} for e in self.eng}
        self.same = same_engine_sync
        self.dq = {}
        for e in ("sp", "act", "pool"):
            self.dq[e] = {"sems": [self._newsem(f"d{e}") for _ in range(n_dma_sems)],
                          "cnt": [0] * n_dma_sems, "i": 0}
        self.sb_off = 16512
        self.sb_top = 229344
        self.ninst = 0
        self.ntens = 0

    def _newsem(self, tag):
        self.nsem += 1
        return self.nc.alloc_semaphore(f"sem_{tag}_{self.nsem}")

    def sb(self, name, shape, dtype, off=None):
        esz = 2 if dtype == BF16 else 4
        n = int(np.prod(shape[1:])) * esz
        if off is None:
            off = self.sb_off
            self.sb_off = (off + n + 31) // 32 * 32
        assert off + n <= self.sb_top, (name, off, n, self.sb_top)
        self.ntens += 1
        return self.nc.alloc_sbuf_tensor_at(f"{name}_{self.ntens}", list(shape), dtype, offset=off)

    def _collect(self, e, reads, writes):
        deps = {}

        def add(tok):
            s, v = tok
            k = id(s)
            if k not in deps or deps[k][1] < v:
                deps[k] = (s, v)
        own_sem = self.sem[e]
        for u in reads:
            if u.w is not None:
                add(u.w)
            if u.excl:
                for tok in u.r.values():
                    if tok[0] is not own_sem:
                        add(tok)
        for u in writes:
            if u.w is not None:
                add(u.w)
            for tok in u.r.values():
                add(tok)
        waits = []
        own = id(self.sem[e])
        for k, (s, v) in deps.items():
            if k == own and (e == "pe" or not self.same):
                continue
            if self.seen[e].get(k, 0) < v:
                waits.append((s, v))
                self.seen[e][k] = v
        return waits

    def _mark(self, tok, reads, writes):
        s = tok[0]
        for u in writes:
            u.w = tok
            u.r = {}
        for u in reads:
            if u.w is tok:
                continue
            u.r[id(s)] = tok

    def op(self, e, fn, reads=(), writes=()):
        waits = self._collect(e, reads, writes)
        if self.cnt[e] >= self.SEM_ROLL:
            self.sem[e] = self._newsem(e)
            self.cnt[e] = 0
        self.cnt[e] += 1
        tok = (self.sem[e], self.cnt[e])
        self.q[e].append((waits, fn, tok[0], 1))
        self._mark(tok, reads, writes)
        self.ninst += 1
        return tok

    def dma(self, e, out, in_, reads=(), writes=(), **kw):
        d = self.dq[e]
        i = d["i"]
        d["i"] = (i + 1) % len(d["sems"])
        s = d["sems"][i]
        waits = self._collect(e, reads, writes)
        if d["cnt"][i] > 0 and self.seen[e].get(id(s), 0) < d["cnt"][i]:
            waits.append((s, d["cnt"][i]))
            self.seen[e][id(s)] = d["cnt"][i]
        if d["cnt"][i] >= 16 * 180:
            s = self._newsem(f"d{e}")
            d["sems"][i] = s
            d["cnt"][i] = 0
        d["cnt"][i] += 16
        tok = (s, d["cnt"][i])
        self.q[e].append((waits, lambda E: E.dma_start(out=out, in_=in_, **kw), s, 16))
        self._mark(tok, reads, writes)
        self.ninst += 1
        return tok

    def gate(self, e, us):
        waits = self._collect(e, (), us)
        if waits:
            self.q[e].append((waits, None, None, 0))

    def final_wait(self, e, us):
        deps = {}
        for u in us:
            if u.w is not None:
                s, v = u.w
                if id(s) not in deps or deps[id(s)][1] < v:
                    deps[id(s)] = (s, v)
        self.q[e].append((list(deps.values()), None, None, 0))

    def fence(self):
        toks = []
        for e in self.eng:
            if self.cnt[e] > 0:
                toks.append((self.sem[e], self.cnt[e]))
        for e, d in self.dq.items():
            for s, c in zip(d["sems"], d["cnt"]):
                if c > 0:
                    toks.append((s, c))
        for e in self.eng:
            waits = []
            for s, v in toks:
                if self.seen[e].get(id(s), 0) < v:
                    waits.append((s, v))
                    self.seen[e][id(s)] = v
            if waits:
                self.q[e].append((waits, None, None, 0))

    def build(self):
        nc = self.nc
        with nc.Block() as block:
            def mk(ename):
                def body(E):
                    for waits, fn, s, inc in self.q[ename]:
                        for ws, wv in waits:
                            E.wait_ge(ws, wv)
                        if fn is not None:
                            fn(E).then_inc(s, inc)
                return body
            block.tensor(mk("pe"))
            block.scalar(mk("act"))
            block.vector(mk("dve"))
            block.gpsimd(mk("pool"))
            block.sync(mk("sp"))


class G:
    def __init__(self, kb):
        self.kb = kb

    def mm(self, out, lhsT, rhs, start, stop, r, w):
        self.kb.op("pe", lambda E: E.matmul(out, lhsT=lhsT, rhs=rhs, start=start, stop=stop), r, w)

    def tr(self, out, in_, ident, r, w):
        self.kb.op("pe", lambda E: E.transpose(out=out, in_=in_, identity=ident), r, w)

    def tt(self, e, out, in0, in1, op, r, w):
        self.kb.op(e, lambda E: E.tensor_tensor(out=out, in0=in0, in1=in1, op=op), r, w)

    def ts(self, e, out, in0, s1, s2, op0, op1, r, w):
        if op1 is None:
            self.kb.op(e, lambda E: E.tensor_scalar(out=out, in0=in0, scalar1=s1, scalar2=None, op0=op0), r, w)
        else:
            self.kb.op(e, lambda E: E.tensor_scalar(out=out, in0=in0, scalar1=s1, scalar2=s2, op0=op0, op1=op1), r, w)

    def stt(self, out, in0, scalar, in1, op0, op1, r, w):
        self.kb.op("dve", lambda E: E.scalar_tensor_tensor(out=out, in0=in0, scalar=scalar, in1=in1, op0=op0, op1=op1), r, w)

    def act(self, out, in_, func, r, w, bias=None, scale=None, accum_out=None):
        kw = {}
        if bias is not None:
            kw["bias"] = bias
        if scale is not None:
            kw["scale"] = scale
        if accum_out is not None:
            kw["accum_out"] = accum_out
        self.kb.op("act", lambda E: E.activation(out=out, in_=in_, func=func, **kw), r, w)

    def cp(self, e, out, in_, r, w):
        if e == "act":
            self.kb.op(e, lambda E: E.activation(out=out, in_=in_, func=AF.Copy), r, w)
        else:
            self.kb.op(e, lambda E: E.tensor_copy(out=out, in_=in_), r, w)

    def red(self, out, in_, r, w):
        self.kb.op("dve", lambda E: E.tensor_reduce(out=out, in_=in_, axis=AX.X, op=ALU.add), r, w)

    def memset(self, e, ap, val, w):
        self.kb.op(e, lambda E: E.memset(ap, val), (), w)


def build_nc(stop_after=99):
    nc = bass.Bass("TRN2", target_bir_lowering=False)
    dt = nc.dram_tensor

    def inp(name, shape):
        return dt(name, list(shape), F32, kind="ExternalInput").ap()

    x_d = inp("x", [T, D])
    rwkv_norm_g = inp("rwkv_norm_g", [D])
    rwkv_mu = inp("rwkv_mu", [6, D])
    w_rkv = inp("rwkv_w_rkv", [3, D, D])
    w0_d = inp("rwkv_w0", [D])
    w1_d = inp("rwkv_w1", [D, 64])
    w2_d = inp("rwkv_w2", [64, D])
    a0_d = inp("rwkv_a0", [D])
    a1_d = inp("rwkv_a1", [D, 64])
    a2_d = inp("rwkv_a2", [64, D])
    g1_d = inp("rwkv_g1", [D, 160])
    g2_d = inp("rwkv_g2", [160, D])
    kk_d = inp("rwkv_k_k", [D])
    ka_d = inp("rwkv_k_a", [D])
    rk_d = inp("rwkv_r_k", [D])
    lnw_d = inp("rwkv_lnx_w", [D])
    lnb_d = inp("rwkv_lnx_b", [D])
    wo_d = inp("rwkv_w_o", [D, D])
    kv_norm_g = inp("kv_norm_g", [D])
    kv_w = inp("kv_w", [D, 2 * D + H])
    kv_f_bias = inp("kv_f_bias", [H])
    k_norm_g = inp("k_norm_g", [D])
    attn_norm_g = inp("attn_norm_g", [D])
    attn_w_q = inp("attn_w_q", [D, D])
    q_norm_g = inp("q_norm_g", [D])
    attn_w_o = inp("attn_w_o", [D, D])
    mlp_norm_g = inp("mlp_norm_g", [2, D])
    mlp_w_in = inp("mlp_w_in", [2, D, FF])
    mlp_w_out = inp("mlp_w_out", [2, FF, D])
    out_d = dt("out", [T, D], F32, kind="ExternalOutput").ap()
    x1_d = dt("x1_scratch", [T, D], F32).ap()
    kT_d = dt("kT_scratch", [H, 70, T], BF16).ap()
    qT_d = dt("qT_scratch", [H, 70, T], BF16).ap()

    kb = KB(nc)
    g = G(kb)
    NC = ALU

    PS = [nc.alloc_psum_tensor(f"ps{i}", [128, 512], F32) for i in range(8)]
    PSU = [Unit(f"ps{i}", excl=True) for i in range(8)]
    pT = PS[2][:].bitcast(BF16)
    pTU = [PSU[2], PSU[2]]

    ident_f = kb.sb("ident_f", [128, 128], F32)
    ident_b = kb.sb("ident_b", [128, 128], BF16)
    tri_f = kb.sb("tri_f", [128, 128], F32)
    ones_f = kb.sb("ones_f", [128, 128], F32)
    maskE = kb.sb("maskE", [128, 256], BF16)
    mask_ls = kb.sb("mask_ls", [128, 128], BF16)
    tri_b = kb.sb("tri_b", [128, 128], BF16)
    ones_b = kb.sb("ones_b", [128, 128], BF16)
    maskneg_b = kb.sb("maskneg_b", [128, 128], BF16)
    maskE2 = kb.sb("maskE2", [128, 512], BF16)
    mask2 = kb.sb("mask2", [128, 256], BF16)
    maskneg = kb.sb("maskneg", [128, 128], F32)
    ctmp = kb.sb("ctmp", [128, 128], F32)
    cU = Unit("consts")
    ctU = Unit("ctmp")

    def sel(out, in_, cm, step, base, r, w):
        kb.op("pool", lambda E: E.affine_select(out=out, in_=in_, pattern=[[step, 128]], compare_op=ALU.is_ge,
                                                fill=0.0, base=base, channel_multiplier=cm), r, w)
    g.memset("pool", ones_f[:], 1.0, [cU])
    sel(ctmp[:], ones_f[:], 1, -1, 0, [cU], [ctU])
    sel(ident_f[:], ctmp[:], -1, 1, 0, [ctU], [cU])
    g.cp("pool", ident_b[:], ident_f[:], [cU], [cU])
    sel(tri_f[:], ones_f[:], -1, 1, 0, [cU], [cU])
    g.cp("pool", tri_b[:], tri_f[:], [cU], [cU])
    g.cp("pool", ones_b[:], ones_f[:], [cU], [cU])
    g.cp("pool", maskE[:, 128:256], tri_f[:], [cU], [cU])
    sel(ctmp[:], ones_f[:], -1, 1, -1, [cU], [ctU])
    g.cp("pool", maskE[:, 0:128], ctmp[:], [ctU], [cU])
    sel(ctmp[:], ones_f[:], 1, -1, -1, [cU, ctU], [ctU])
    g.cp("pool", mask_ls[:], ctmp[:], [ctU], [cU])
    g.ts("pool", maskneg[:], tri_f[:], -1.0, 1e4, ALU.add, ALU.mult, [cU], [cU])
    g.cp("pool", maskneg_b[:], maskneg[:], [cU], [cU])
    g.cp("pool", maskE2[:, 0:256], maskE[:], [cU], [cU])
    g.cp("pool", maskE2[:, 256:512], maskE[:], [cU], [cU])
    g.cp("pool", mask2[:, 0:128], mask_ls[:], [cU], [cU])
    g.cp("pool", mask2[:, 128:256], ones_f[:], [cU], [cU])

    persist_off = kb.sb_off

    def load_featcols(vecs, name):
        rows = 8 * len(vecs)
        stage = kb.sb(name + "_st", [64, 128], F32)
        dstt = kb.sb(name, [128, rows], F32)
        su, du = Unit(name + "_st"), Unit(name)
        for vi, vec in enumerate(vecs):
            kb.dma("sp", stage[vi * 8:(vi + 1) * 8, :], vec.rearrange("(c p) -> c p", p=128), writes=[su])
        g.tr(PS[2][:, 0:rows], stage[0:rows, :], ident_f[0:rows, 0:rows], [su, cU], [PSU[2]])
        g.cp("dve", dstt[:], PS[2][:, 0:rows], [PSU[2]], [du])
        return dstt, du

    def rstd_of(xin_ap, junk_ap, ss, rstd, rU, wU_junk, wU_s):
        g.act(junk_ap, xin_ap, AF.Square, rU, [wU_junk, wU_s], accum_out=ss)
        g.ts("dve", rstd, ss, 1.0 / D, NORM_EPS, ALU.mult, ALU.add, [wU_s], [wU_s])
        g.act(rstd, rstd, AF.Ln, [wU_s], [wU_s])
        g.act(rstd, rstd, AF.Exp, [wU_s], [wU_s], scale=-0.5)

    def phase_rwkv():
        kb.sb_off = persist_off
        W3 = [kb.sb(f"W{p}", [128, 8, D], BF16) for p in range(3)]
        Wo = kb.sb("Wo", [128, 8, D], BF16)
        W1 = kb.sb("W1", [128, 8, 64], BF16)
        A1 = kb.sb("A1", [128, 8, 64], BF16)
        G1 = kb.sb("G1", [128, 8, 160], BF16)
        W2 = kb.sb("W2", [64, D], BF16)
        A2 = kb.sb("A2", [64, D], BF16)
        G2a = kb.sb("G2a", [128, D], BF16)
        G2b = kb.sb("G2b", [32, D], BF16)
        gmu, gmuU = load_featcols([rwkv_mu[m] for m in range(6)] + [rwkv_norm_g], "gmu")
        muT = gmu[:, 0:48].rearrange("p (m c) -> p m c", m=6)
        gT = gmu[:, 48:56]
        w0r = kb.sb("w0r", [128, D], F32)
        a0r = kb.sb("a0r", [128, D], BF16)
        kkr_ = kb.sb("kkrow", [128, D], BF16)
        kar_ = kb.sb("karow", [128, D], BF16)
        rkr_ = kb.sb("rkrow", [128, D], BF16)
        lnwr = kb.sb("lnwrow", [128, D], BF16)
        lnbr = kb.sb("lnbrow", [128, D], BF16)
        wU = Unit("rw_weights")
        rowf = kb.sb("tA", [128, D], F32)
        tA = rowf
        rowU = Unit("tA")

        for p in range(3):
            for c in range(8):
                kb.dma("pool", W3[p][:, c, :], w_rkv[p, c * 128:(c + 1) * 128, :], writes=[wU])
        for c in range(8):
            kb.dma("pool", Wo[:, c, :], wo_d[c * 128:(c + 1) * 128, :], writes=[wU])
        kb.dma("pool", W1[:], w1_d.rearrange("(c p) e -> p c e", p=128), writes=[wU])
        kb.dma("pool", A1[:], a1_d.rearrange("(c p) e -> p c e", p=128), writes=[wU])
        kb.dma("pool", G1[:], g1_d.rearrange("(c p) e -> p c e", p=128), writes=[wU])
        kb.dma("pool", W2[:], w2_d, writes=[wU])
        kb.dma("pool", A2[:], a2_d, writes=[wU])
        kb.dma("pool", G2a[:], g2_d[0:128, :], writes=[wU])
        kb.dma("pool", G2b[:], g2_d[128:160, :], writes=[wU])
        kb.dma("sp", w0r[:], w0_d.partition_broadcast(128), writes=[wU])
        for src, dst in ((a0_d, a0r), (kk_d, kkr_), (ka_d, kar_), (rk_d, rkr_), (lnw_d, lnwr), (lnb_d, lnbr)):
            kb.dma("sp", rowf[:], src.partition_broadcast(128), writes=[rowU])
            g.cp("dve", dst[:], rowf[:], [rowU], [wU])
        for c in range(8):
            for Wt in (W3[0], W3[1], W3[2], W1, A1, G1):
                g.act(Wt[:, c, :], Wt[:, c, :], AF.Copy, [wU, gmuU], [wU], scale=gT[:, c:c + 1])

        ck(1)
        xin = kb.sb("xin", [128, D], F32)
        r_sb = kb.sb("r_sb", [128, D], F32)
        k_sb = kb.sb("k_sb", [128, D], F32)
        a_sb = kb.sb("a_sb", [128, D], F32)
        sg = kb.sb("sg", [128, D], F32)
        sgh = kb.sb("sgh", [128, D], BF16)
        sgl = kb.sb("sgl", [128, D], BF16)
        kkt = kb.sb("kkt", [128, D], F32)
        kp = kb.sb("kp", [128, D], F32)
        b_sb = kb.sb("b_sb", [128, D], F32)
        Ea_off = kb.sb_off
        Ea = kb.sb("Ea", [128, D], F32)
        Eb_off = kb.sb_off
        Eb = kb.sb("Eb", [128, D], F32)
        y_sb = kb.sb("y_sb", [128, D], F32)
        xn = kb.sb("xn", [128, D], BF16)
        v_bf = kb.sb("v_bf", [128, D], BF16)
        g_bf = kb.sb("g_bf", [128, D], BF16)
        At = kb.sb("At", [128, D], BF16)
        Bt = kb.sb("Bt", [128, D], BF16)
        Kt = kb.sb("Kt", [128, D], BF16)
        Rt = kb.sb("Rt", [128, D], BF16)
        Bh = kb.sb("Bh", [128, D], BF16)
        Kh = kb.sb("Kh", [128, D], BF16)
        yg = xn
        hTx = [kb.sb(f"hTx{i}", [128, 8, 129], BF16) for i in range(2)]
        xxT = kb.sb("xxT", [128, 8, 128], BF16)
        xp = [kb.sb(f"xp{i}", [128, 8, 128], BF16) for i in range(6)]
        ygT = xxT
        twT = kb.sb("twT", [64, 128], BF16)
        t1T = kb.sb("t1T", [64, 128], BF16)
        sgT1 = kb.sb("sgT1", [128, 128], BF16)
        sgT2 = kb.sb("sgT2", [32, 128], BF16)
        small = kb.sb("small", [128, 16 * 12], F32)
        ss = small[:, 0:1]
        rstd = small[:, 1:2]
        ss16 = small[:, 16:32]
        rn16 = small[:, 32:48]
        bs16 = small[:, 48:64]
        s1 = small[:, 64:80]
        s2 = small[:, 80:96]
        mean = small[:, 96:112]
        var = small[:, 112:128]
        rs = small[:, 128:144]
        m2 = small[:, 144:160]
        PCf = kb.sb("PCf", [64, 16], F32)
        S_f = kb.sb("S_f", [64, D], F32)
        S_bf = [kb.sb(f"S_bf{i}", [64, D], BF16) for i in range(2)]
        G4 = 4
        FM = [kb.sb(f"FM{i}", [64, 512], BF16) for i in range(G4)]
        E12 = [kb.sb(f"E12_{i}", [128, 512], BF16, off=Eb_off + i * 1024) for i in range(G4)]
        BN0 = [kb.sb(f"BN0_{i}", [128, 256], BF16) for i in range(G4)]
        AB = [[kb.sb(f"AB{i}_{k}", [128, 256], BF16, off=Ea_off + (i * 2 + k) * 512) for k in range(2)] for i in range(G4)]
        Nj = [[kb.sb(f"Nj{i}_{k}", [128, 128], BF16) for k in range(2)] for i in range(G4)]
        GT = [kb.sb(f"GT{i}", [64, 64], BF16) for i in range(G4)]
        QhT = [kb.sb(f"QhT{i}", [64, 128], BF16) for i in range(G4)]

        U = {n: Unit(n) for n in ["xin", "r", "k", "a", "sg", "tA", "tB", "cum", "kk", "kp", "b", "Ea", "Eb", "y",
                                  "xn", "v", "g", "At", "Bt", "Kt", "Rt", "Bh", "Kh", "yg", "xxT", "ygT", "twT",
                                  "t1T", "sgT1", "sgT2", "ss", "ss16", "bs16", "gn", "PCf", "x1d", "sgh", "sgl"]}
        U["tA"] = rowU
        U["yg"] = U["xn"]
        U["ygT"] = U["xxT"]
        hU = units(2, "hTx")
        xpU = units(6, "xp")
        SfU = units(16, "Sf")
        SbU = [units(16, "Sb0_"), units(16, "Sb1_")]
        FMU = units(G4, "FM")
        E12U = units(G4, "E12")
        BN0U = units(G4, "BN0")
        ABU = [units(2, f"AB{i}_") for i in range(G4)]
        NjU = [units(2, f"Nj{i}_") for i in range(G4)]
        GTU = units(G4, "GT")
        QhU = units(G4, "QhT")
        LU = PSU[7]
        L2U = PSU[7]
        P7 = PS[7]
        pbig = [0]

        def bigps():
            i = pbig[0]
            pbig[0] ^= 1
            return PS[i], PSU[i]

        g.memset("dve", S_f[:], 0.0, SfU)
        g.memset("pool", S_bf[0][:], 0.0, SbU[0])
        g.memset("pool", hTx[0][:, :, 0:1], 0.0, [hU[0]])

        for b in range(NB):
            cur = b % 2
            nxt = 1 - cur
            hc, hUc = hTx[cur], hU[cur]
            kb.dma("sp", xin[:], x_d[b * 128:(b + 1) * 128, :], reads=[], writes=[U["xin"]])
            rstd_of(xin[:], xn[:], ss, rstd, [U["xin"]], U["xn"], U["ss"])
            g.ts("dve", xn[:], xin[:], rstd, None, ALU.mult, None, [U["xin"], U["ss"]], [U["xn"]])
            for c in range(8):
                g.tr(pT[:, c * 128:(c + 1) * 128], xn[:, c * 128:(c + 1) * 128], ident_b[:], [U["xn"], cU], [pTU[c // 4]])
            g.cp("act", hc[:, :, 1:129], pT.rearrange("p (c t) -> p c t", c=8), pTU, [hUc])
            g.cp("pool", hTx[nxt][:, :, 0:1], hc[:, :, 128:129], [hUc], [hU[nxt]])
            g.tt("dve", xxT[:], hc[:, :, 0:128], hc[:, :, 1:129], ALU.subtract, [hUc], [U["xxT"]])
            for p in range(6):
                for c in range(8):
                    g.stt(xp[p][:, c, :], xxT[:, c, :], muT[:, p, c:c + 1], hc[:, c, 1:129], ALU.mult, ALU.add,
                          [U["xxT"], hUc, wU, gmuU], [xpU[p]])

            ck(2)

            def proj(xpi, Wt, evac):
                for half in range(2):
                    ps, psu = bigps()
                    for c in range(8):
                        g.mm(ps[:], xp[xpi][:, c, :], Wt[:, c, half * 512:(half + 1) * 512], c == 0, c == 7,
                             [xpU[xpi], wU], [psu])
                    evac(ps, psu, half)
            hs = lambda half: slice(half * 512, (half + 1) * 512)
            proj(0, W3[0], lambda ps, psu, half: g.cp("act", r_sb[:, hs(half)], ps[:], [psu], [U["r"]]))
            proj(2, W3[1], lambda ps, psu, half: g.cp("act", k_sb[:, hs(half)], ps[:], [psu], [U["k"]]))
            proj(3, W3[2], lambda ps, psu, half: g.cp("act", v_bf[:, hs(half)], ps[:], [psu], [U["v"]]))
            ck(3)
            for c in range(8):
                g.mm(P7[0:64, 256:384], W1[:, c, :], xp[1][:, c, :], c == 0, c == 7, [xpU[1], wU], [LU])
            g.act(twT[:], P7[0:64, 256:384], AF.Tanh, [LU], [U["twT"]])
            for half in range(2):
                ps, psu = bigps()
                g.mm(ps[:], twT[:], W2[:, hs(half)], True, True, [U["twT"], wU], [psu])
                g.tt("dve", tA[:, hs(half)], ps[:], w0r[:, hs(half)], ALU.add, [psu, wU], [U["tA"]])
            g.act(sg[:], tA[:], AF.Sigmoid, [U["tA"]], [U["sg"]])
            for c in range(8):
                g.mm(P7[0:64, 256:384], A1[:, c, :], xp[4][:, c, :], c == 0, c == 7, [xpU[4], wU], [LU])
            g.cp("act", t1T[:], P7[0:64, 256:384], [LU], [U["t1T"]])
            for half in range(2):
                ps, psu = bigps()
                g.mm(ps[:], t1T[:], A2[:, hs(half)], True, True, [U["t1T"], wU], [psu])
                g.tt("dve", a_sb[:, hs(half)], ps[:], a0r[:, hs(half)], ALU.add, [psu, wU], [U["a"]])
            g.act(a_sb[:], a_sb[:], AF.Sigmoid, [U["a"]], [U["a"]])
            for c in range(8):
                g.mm(P7[:, 256:384], G1[:, c, 0:128], xp[5][:, c, :], c == 0, c == 7, [xpU[5], wU], [LU])
            for c in range(8):
                g.mm(P7[0:32, 384:512], G1[:, c, 128:160], xp[5][:, c, :], c == 0, c == 7, [xpU[5], wU], [L2U])
            g.act(sgT1[:], P7[:, 256:384], AF.Sigmoid, [LU], [U["sgT1"]])
            g.act(sgT2[:], P7[0:32, 384:512], AF.Sigmoid, [L2U], [U["sgT2"]])
            for half in range(2):
                ps, psu = bigps()
                g.mm(ps[:], sgT1[:], G2a[:, hs(half)], True, False, [U["sgT1"], wU], [psu])
                g.mm(ps[:], sgT2[:], G2b[:, hs(half)], False, True, [U["sgT2"], wU], [psu])
                g.cp("act", g_bf[:, hs(half)], ps[:], [psu], [U["g"]])
            ck(4)
            g.cp("act", sgh[:], sg[:], [U["sg"]], [U["sgh"]])
            g.tt("pool", tA[:], sg[:], sgh[:], ALU.subtract, [U["sg"], U["sgh"]], [U["tA"]])
            g.cp("act", sgl[:], tA[:], [U["tA"]], [U["sgl"]])

            def trimm(maskT, half):
                ps, psu = bigps()
                g.mm(ps[:], maskT, sgh[:, hs(half)], True, False, [U["sgh"], cU], [psu])
                g.mm(ps[:], maskT, sgl[:, hs(half)], False, True, [U["sgl"], cU], [psu])
                return ps, psu
            kb.gate("act", [uu for q_ in range(G4) for uu in (E12U[q_], ABU[q_][0], ABU[q_][1])])
            for half in range(2):
                ps, psu = trimm(tri_b[:], half)
                g.act(Ea[:, hs(half)], ps[:], AF.Exp, [psu], [U["Ea"]], scale=-C_DEC)
                g.act(Eb[:, hs(half)], ps[:], AF.Exp, [psu], [U["Eb"]], scale=C_DEC)
            ck(41)
            ps, psu = bigps()
            for h in range(H):
                g.mm(ps[0:64, h:h + 1], sgh[:, h * 64:(h + 1) * 64], tri_b[:, 127:128], True, False, [U["sgh"], cU], [psu])
                g.mm(ps[0:64, h:h + 1], sgl[:, h * 64:(h + 1) * 64], tri_b[:, 127:128], False, True, [U["sgl"], cU], [psu])
            g.act(PCf[:], ps[0:64, 0:16], AF.Exp, [psu], [U["PCf"]], scale=-C_DEC)
            ck(5)
            g.tt("pool", kkt[:], k_sb[:], kkr_[:], ALU.mult, [U["k"], wU], [U["kk"]])
            g.tt("pool", tA[:], kkt[:], kkt[:], ALU.mult, [U["kk"]], [U["tA"]])
            g.red(ss16, tA[:].rearrange("p (h n) -> p h n", h=H), [U["tA"]], [U["ss16"]])
            g.ts("dve", rn16, ss16, 1e-24, None, ALU.max, None, [U["ss16"]], [U["ss16"]])
            g.act(rn16, rn16, AF.Ln, [U["ss16"]], [U["ss16"]])
            g.act(rn16, rn16, AF.Exp, [U["ss16"]], [U["ss16"]], scale=-0.5)
            v3 = lambda t_: t_[:].rearrange("p (h n) -> p h n", h=H)
            bc = lambda s_: s_.unsqueeze(2).broadcast_to([128, H, N])
            g.tt("dve", v3(kkt), v3(kkt), bc(rn16), ALU.mult, [U["kk"], U["ss16"]], [U["kk"]])
            g.tt("pool", Rt[:], r_sb[:], Ea[:], ALU.mult, [U["r"], U["Ea"]], [U["Rt"]])
            g.stt(tA[:], a_sb[:], -1.0, kar_[:], ALU.add, ALU.mult, [U["a"], wU], [U["tA"]])
            g.stt(kp[:], tA[:], 1.0, k_sb[:], ALU.add, ALU.mult, [U["tA"], U["k"]], [U["kp"]])
            g.tt("pool", b_sb[:], kkt[:], a_sb[:], ALU.mult, [U["kk"], U["a"]], [U["b"]])
            g.tt("pool", Bt[:], b_sb[:], Eb[:], ALU.mult, [U["b"], U["Eb"]], [U["Bt"]])
            g.tt("dve", Kt[:], kp[:], Eb[:], ALU.mult, [U["kp"], U["Eb"]], [U["Kt"]])
            for half in range(2):
                ps, psu = trimm(mask_ls[:], half)
                g.act(Eb[:, hs(half)], ps[:], AF.Exp, [psu], [U["Eb"]], scale=-C_DEC)
            for half in range(2):
                ps, psu = trimm(maskE[:, 0:128], half)
                g.act(Ea[:, hs(half)], ps[:], AF.Exp, [psu], [U["Ea"]], scale=-C_DEC)
            g.stt(At[:], kkt[:], -1.0, Ea[:], ALU.mult, ALU.mult, [U["kk"], U["Ea"]], [U["At"]])
            g.tt("dve", Bh[:], b_sb[:], Eb[:], ALU.mult, [U["b"], U["Eb"]], [U["Bh"]])
            g.tt("pool", Kh[:], kp[:], Eb[:], ALU.mult, [U["kp"], U["Eb"]], [U["Kh"]])
            g.tt("pool", tA[:], r_sb[:], kp[:], ALU.mult, [U["r"], U["kp"]], [U["tA"]])
            g.tt("pool", tA[:], tA[:], rkr_[:], ALU.mult, [U["tA"], wU], [U["tA"]])
            g.red(bs16, v3(tA), [U["tA"]], [U["bs16"]])

            ck(6)
            kb.gate("dve", [U["Ea"], U["Eb"]])
            kb.gate("act", [U["Ea"], U["Eb"]])
            for h0 in range(0, H, G4):
                hs4 = list(range(h0, h0 + G4))

                def ctx(h):
                    q = h - h0
                    return q, PS[3 + q], PSU[3 + q], slice(h * 64, (h + 1) * 64)
                for h in hs4:
                    q, bk, bu, hsl = ctx(h)
                    bkb = bk[:].bitcast(BF16)
                    for qi, (src, su) in enumerate(((At, U["At"]), (Rt, U["Rt"]), (Bt, U["Bt"]), (Kt, U["Kt"]))):
                        g.tr(bkb[0:64, qi * 128:(qi + 1) * 128], src[:, hsl], ident_b[:], [su, cU], [bu])
                    g.cp("act", FM[q][:], bkb[0:64, 0:512], [bu], [FMU[q]])
                for h in hs4:
                    q, bk, bu, hsl = ctx(h)
                    fm = FM[q]
                    g.mm(bk[:, 0:256], fm[:, 256:384], fm[:, 0:256], True, True, [FMU[q]], [bu])
                    g.mm(bk[:, 256:512], fm[:, 384:512], fm[:, 0:256], True, True, [FMU[q]], [bu])
                    g.tt("dve", E12[q][:], bk[:], maskE2[:], ALU.mult, [bu, cU], [E12U[q]])
                for h in hs4:
                    q, bk, bu, hsl = ctx(h)
                    fm = FM[q]
                    g.mm(bk[:, 0:128], fm[:, 0:128], fm[:, 256:384], True, True, [FMU[q]], [bu])
                    g.mm(bk[:, 128:192], fm[:, 0:128], ident_b[0:64, 0:64], True, True, [FMU[q], cU], [bu])
                    g.mm(bk[:, 192:256], E12[q][:, 256:384], v_bf[:, hsl], True, True, [E12U[q], U["v"]], [bu])
                    g.tt("dve", BN0[q][:], bk[:, 0:256], mask2[:], ALU.mult, [bu, cU], [BN0U[q]])
                cur_ab = {}
                for h in hs4:
                    q = h - h0
                    cur_ab[q] = (E12[q][:, 0:128], BN0[q][:, 0:128], [E12U[q], BN0U[q]], BN0[q][:, 128:256], [BN0U[q]])
                for j in range(7):
                    for h in hs4:
                        q, bk, bu, hsl = ctx(h)
                        A_cur, B_cur, abu, N_cur, nu = cur_ab[q]
                        no = (j + 1) % 2
                        g.mm(bk[:, 0:128], A_cur, N_cur, True, True, abu + nu, [bu])
                        if j < 6:
                            g.mm(bk[:, 128:256], B_cur, A_cur, True, True, abu, [bu])
                            g.mm(bk[:, 256:384], A_cur, B_cur, True, True, abu, [bu])
                        g.tt("dve", Nj[q][no][:], bk[:, 0:128], N_cur, ALU.add, [bu] + nu, [NjU[q][no]])
                        if j < 6:
                            g.cp("act", AB[q][no][:], bk[:, 128:384], [bu], [ABU[q][no]])
                            cur_ab[q] = (AB[q][no][:, 0:128], AB[q][no][:, 128:256], [ABU[q][no]], Nj[q][no][:], [NjU[q][no]])
                        else:
                            cur_ab[q] = (None, None, None, Nj[q][no][:], [NjU[q][no]])
                for h in hs4:
                    q, bk, bu, hsl = ctx(h)
                    XZ, XZu = cur_ab[q][3], cur_ab[q][4]
                    X_ = XZ[:, 0:64]
                    g.mm(bk[0:64, 0:64], X_, Bh[:, hsl], True, True, XZu + [U["Bh"]], [bu])
                    g.mm(bk[0:64, 128:256], X_, E12[q][:, 128:256], True, True, XZu + [E12U[q]], [bu])
                    g.cp("act", GT[q][:], bk[0:64, 0:64], [bu], [GTU[q]])
                    g.tt("dve", QhT[q][:], bk[0:64, 128:256], FM[q][:, 128:256], ALU.add, [bu, FMU[q]], [QhU[q]])
                for h in hs4:
                    q, bk, bu, hsl = ctx(h)
                    XZ, XZu = cur_ab[q][3], cur_ab[q][4]
                    Z_ = XZ[:, 64:128]
                    MrbT, MrkT = E12[q][:, 128:256], E12[q][:, 384:512]
                    Sb_cur, Sb_cu = S_bf[cur][:, hsl], SbU[cur][h]
                    Yr = bk[:, 256:320]
                    g.mm(Yr, MrbT, Z_, True, False, [E12U[q]] + XZu, [bu])
                    g.mm(Yr, MrkT, v_bf[:, hsl], False, False, [E12U[q], U["v"]], [bu])
                    g.mm(Yr, QhT[q][:], Sb_cur, False, True, [QhU[q], Sb_cu], [bu])
                    Hr = bk[0:64, 320:384]
                    g.mm(Hr, Bh[:, hsl], Z_, True, False, [U["Bh"]] + XZu, [bu])
                    g.mm(Hr, Kh[:, hsl], v_bf[:, hsl], False, False, [U["Kh"], U["v"]], [bu])
                    g.mm(Hr, GT[q][:], Sb_cur, False, True, [GTU[q], Sb_cu], [bu])
                    g.cp("act", y_sb[:, hsl], Yr, [bu], [U["y"]])
                    g.stt(S_f[:, hsl], S_f[:, hsl], PCf[:, h:h + 1], Hr, ALU.mult, ALU.add,
                          [SfU[h], U["PCf"], bu], [SfU[h]])
                    g.cp("pool", S_bf[nxt][:, hsl], S_f[:, hsl], [SfU[h]], [SbU[nxt][h]])
                ck(7)

            ck(8)
            g.red(s1, v3(y_sb), [U["y"]], [U["gn"]])
            g.tt("pool", tA[:], y_sb[:], y_sb[:], ALU.mult, [U["y"]], [U["tA"]])
            g.red(s2, v3(tA), [U["tA"]], [U["gn"]])
            g.ts("dve", mean, s1, 1.0 / N, None, ALU.mult, None, [U["gn"]], [U["gn"]])
            g.tt("dve", m2, mean, mean, ALU.mult, [U["gn"]], [U["gn"]])
            g.stt(var, s2, 1.0 / N, m2, ALU.mult, ALU.subtract, [U["gn"]], [U["gn"]])
            g.ts("dve", var, var, GN_EPS, None, ALU.add, None, [U["gn"]], [U["gn"]])
            g.act(var, var, AF.Sqrt, [U["gn"]], [U["gn"]])
            kb.op("dve", lambda E: E.reciprocal(out=rs, in_=var), [U["gn"]], [U["gn"]])
            g.tt("dve", v3(tA), v3(y_sb), bc(mean), ALU.subtract, [U["y"], U["gn"]], [U["tA"]])
            g.tt("dve", v3(tA), v3(tA), bc(rs), ALU.mult, [U["tA"], U["gn"]], [U["tA"]])
            g.tt("pool", tA[:], tA[:], lnwr[:], ALU.mult, [U["tA"], wU], [U["tA"]])
            g.tt("pool", tA[:], tA[:], lnbr[:], ALU.add, [U["tA"], wU], [U["tA"]])
            g.tt("dve", v3(kp), v3(v_bf), bc(bs16), ALU.mult, [U["v"], U["bs16"]], [U["kp"]])
            g.tt("pool", tA[:], tA[:], kp[:], ALU.add, [U["tA"], U["kp"]], [U["tA"]])
            g.tt("pool", yg[:], tA[:], g_bf[:], ALU.mult, [U["tA"], U["g"]], [U["yg"]])
            for c in range(8):
                g.tr(pT[:, c * 128:(c + 1) * 128], yg[:, c * 128:(c + 1) * 128], ident_b[:], [U["yg"], cU], [pTU[c // 4]])
            g.cp("act", ygT[:], pT.rearrange("p (c t) -> p c t", c=8), pTU, [U["ygT"]])
            for half in range(2):
                ps, psu = bigps()
                for c in range(8):
                    g.mm(ps[:], ygT[:, c, :], Wo[:, c, hs(half)], c == 0, c == 7, [U["ygT"], wU], [psu])
                g.tt("dve", xin[:, hs(half)], ps[:], xin[:, hs(half)], ALU.add, [psu, U["xin"]], [U["xin"]])
            kb.dma("sp", x1_d[b * 128:(b + 1) * 128, :], xin[:], reads=[U["xin"]], writes=[U["x1d"]])
        return U["x1d"]

    try:
        x1U = phase_rwkv()
    except _Stop:
        x1U = Unit("x1dummy")
    kb.fence()

    kb.sb_off = persist_off
    x_sb = kb.sb("x_sb", [128, NB, D], F32)
    xU = units(NB, "x")
    resid_off = kb.sb_off
    for b in range(NB):
        kb.dma("sp", x_sb[:, b, :], x1_d[b * 128:(b + 1) * 128, :], reads=[x1U], writes=[xU[b]])

    outU = Unit("out")

    def dump_x():
        for b in range(NB):
            kb.dma("sp", out_d[b * 128:(b + 1) * 128, :], x_sb[:, b, :], reads=[xU[b]], writes=[outU])
        kb.final_wait("sp", [outU])
        kb.fence()
        kb.build()
        return nc

    if stop_after <= 1:
        return dump_x()

    def phase_mlp(layer, final):
        kb.sb_off = resid_off
        hnT = kb.sb("hnT", [128, 8, T], BF16)
        hid = kb.sb("hid", [128, 4, T], BF16)
        WI = [kb.sb(f"WI{i}", [128, 8, 512], BF16) for i in range(2)]
        WO = [kb.sb(f"WO{i}", [128, 4, D], BF16) for i in range(2)]
        gT, gU = load_featcols([mlp_norm_g[layer]], f"mgT{layer}")
        xn = kb.sb("mxn", [128, D], BF16)
        junk = kb.sb("mjunk", [128, D], BF16)
        rl = [kb.sb(f"mrl{i}", [128, 512], F32) for i in range(2)]
        small = kb.sb("msmall", [128, 4], F32)
        hnU = units(NB, "hnT")
        NT = T // 512
        hidU = [units(NT, f"hid{fc}_") for fc in range(4)]
        WIU = units(2, "WI")
        WOU = units(2, "WO")
        xnU = Unit("mxn")
        jU = Unit("mjunk")
        sU = Unit("msmall")
        rlU = units(2, "mrl")

        def load_w(e):
            i = e % 2
            for c in range(8):
                kb.dma("pool", WI[i][:, c, :], mlp_w_in[layer, c * 128:(c + 1) * 128, e * 512:(e + 1) * 512], writes=[WIU[i]])
            for fc in range(4):
                kb.dma("pool", WO[i][:, fc, :], mlp_w_out[layer, e * 512 + fc * 128: e * 512 + (fc + 1) * 128, :], writes=[WOU[i]])
        load_w(0)
        for b in range(NB):
            rstd_of(x_sb[:, b, :], junk[:], small[:, 0:1], small[:, 1:2], [xU[b]], jU, sU)
            g.ts("dve", xn[:], x_sb[:, b, :], small[:, 1:2], None, ALU.mult, None, [xU[b], sU], [xnU])
            for c in range(8):
                g.tr(pT[:, c * 128:(c + 1) * 128], xn[:, c * 128:(c + 1) * 128], ident_b[:], [xnU, cU], [pTU[c // 4]])
            for c in range(8):
                if c % 2:
                    g.ts("dve", hnT[:, c, b * 128:(b + 1) * 128], pT[:, c * 128:(c + 1) * 128], gT[:, c:c + 1], None, ALU.mult, None,
                         [pTU[c // 4], gU], [hnU[b]])
                else:
                    g.act(hnT[:, c, b * 128:(b + 1) * 128], pT[:, c * 128:(c + 1) * 128], AF.Copy, [pTU[c // 4], gU], [hnU[b]],
                          scale=gT[:, c:c + 1])
        pb = [0]
        for e in range(8):
            i = e % 2
            if e + 1 < 8:
                load_w(e + 1)
            for fc in range(4):
                for tt in range(NT):
                    pi = pb[0]
                    pb[0] ^= 1
                    ps, psu = PS[pi], PSU[pi]
                    for c in range(8):
                        g.mm(ps[:], WI[i][:, c, fc * 128:(fc + 1) * 128], hnT[:, c, tt * 512:(tt + 1) * 512], c == 0, c == 7,
                             [WIU[i]] + hnU[tt * 4:(tt + 1) * 4], [psu])
                    ri = (fc * 4 + tt) % 2
                    g.act(rl[ri][:], ps[:], AF.Relu, [psu], [rlU[ri]])
                    g.tt("pool", hid[:, fc, tt * 512:(tt + 1) * 512], rl[ri][:], rl[ri][:], ALU.mult, [rlU[ri]], [hidU[fc][tt]])
            for b in range(NB):
                for half in range(2):
                    pi = 3 + pb[0]
                    pb[0] ^= 1
                    ps, psu = PS[pi], PSU[pi]
                    for fc in range(4):
                        g.mm(ps[:], hid[:, fc, b * 128:(b + 1) * 128], WO[i][:, fc, half * 512:(half + 1) * 512], fc == 0, fc == 3,
                             [hidU[fc][b // 4], WOU[i]], [psu])
                    g.tt("dve", x_sb[:, b, half * 512:(half + 1) * 512], ps[:], x_sb[:, b, half * 512:(half + 1) * 512], ALU.add,
                         [psu, xU[b]], [xU[b]])
                if final and e == 7:
                    kb.dma("sp", out_d[b * 128:(b + 1) * 128, :], x_sb[:, b, :], reads=[xU[b]], writes=[outU])

    phase_mlp(0, False)
    kb.fence()
    if stop_after <= 2:
        return dump_x()

    kb.sb_off = resid_off
    V_all = kb.sb("V_all", [128, NB, H, 65], BF16)
    VU = units(NB, "V")
    oU = units(NB, "o")
    attn_off = kb.sb_off

    def phase_qkv():
        kb.sb_off = attn_off
        KVW = kb.sb("KVW", [128, 8, 2 * D + H], BF16)
        WQ = kb.sb("WQ", [128, 8, D], BF16)
        gkq, gkqU = load_featcols([kv_norm_g, attn_norm_g], "gkq")
        gTk = gkq[:, 0:8]
        gTq = gkq[:, 8:16]
        kgr = kb.sb("kgr", [128, D], F32)
        qgr = kb.sb("qgr", [128, D], F32)
        fbr = kb.sb("fbr", [128, H], F32)
        xn = kb.sb("qxn", [128, D], BF16)
        junk = kb.sb("qjunk", [128, D], BF16)
        hT = kb.sb("qhT", [128, 8, 128], BF16)
        kq = kb.sb("kq_sb", [128, D], F32)
        sq = kb.sb("sq_sb", [128, D], F32)
        k_aug = kb.sb("k_aug", [128, H, 70], BF16)
        q_aug = kb.sb("q_aug", [128, H, 70], BF16)
        kTb = kb.sb("kTb", [70, H, 128], BF16)
        qTb = kb.sb("qTb", [70, H, 128], BF16)
        small = kb.sb("qsmall", [128, 16 * 12], F32)
        carry = kb.sb("carry", [128, H], F32)
        cbf = kb.sb("cbf", [128, 3, H], BF16)
        lfb = kb.sb("lfb", [128, 3, H], BF16)
        ss, rstd = small[:, 0:1], small[:, 1:2]
        ss16, rn16, lf, cc, r1 = small[:, 16:32], small[:, 32:48], small[:, 48:64], small[:, 64:80], small[:, 80:96]
        wU = Unit("qkv_w")
        U = {n: Unit(n) for n in ["xn", "junk", "hT", "kq", "sq", "k_aug", "q_aug", "kTb", "qTb", "s", "ss16", "lf", "cc",
                                  "carry", "cbf", "kTd", "qTd", "lfb"]}
        for c in range(8):
            kb.dma("pool", KVW[:, c, 0:1024], kv_w[c * 128:(c + 1) * 128, 0:1024], writes=[wU])
            kb.dma("pool", KVW[:, c, 1024:2 * D + H], kv_w[c * 128:(c + 1) * 128, 1024:2 * D + H], writes=[wU])
            kb.dma("pool", WQ[:, c, :], attn_w_q[c * 128:(c + 1) * 128, :], writes=[wU])
        kb.dma("sp", kgr[:], k_norm_g.partition_broadcast(128), writes=[wU])
        kb.dma("sp", qgr[:], q_norm_g.partition_broadcast(128), writes=[wU])
        kb.dma("sp", fbr[:], kv_f_bias.partition_broadcast(128), writes=[wU])
        for c in range(8):
            g.act(KVW[:, c, :], KVW[:, c, :], AF.Copy, [wU, gkqU], [wU], scale=gTk[:, c:c + 1])
            g.act(WQ[:, c, :], WQ[:, c, :], AF.Copy, [wU, gkqU], [wU], scale=gTq[:, c:c + 1])
        g.ts("dve", qgr[:], qgr[:], 0.125, None, ALU.mult, None, [wU], [wU])
        ck(10)
        g.memset("pool", carry[:], 0.0, [U["carry"]])
        g.memset("pool", k_aug[:], 1.0, [U["k_aug"]])
        g.memset("pool", q_aug[:], 1.0, [U["q_aug"]])
        g.memset("pool", V_all[:], 1.0, VU)
        v3 = lambda ap: ap.rearrange("p (h n) -> p h n", h=H)
        bc = lambda s_: s_.unsqueeze(2).broadcast_to([128, H, N])
        pbig = [0]

        def bigps():
            i = pbig[0]
            pbig[0] ^= 1
            return PS[i], PSU[i]

        def headnorm(dst_aug, gain_row):
            g.tt("pool", sq[:], kq[:], kq[:], ALU.mult, [U["kq"]], [U["sq"]])
            g.red(ss16, v3(sq[:]), [U["sq"]], [U["ss16"]])
            g.ts("dve", rn16, ss16, 1.0 / N, NORM_EPS, ALU.mult, ALU.add, [U["ss16"]], [U["ss16"]])
            g.act(rn16, rn16, AF.Ln, [U["ss16"]], [U["ss16"]])
            g.act(rn16, rn16, AF.Exp, [U["ss16"]], [U["ss16"]], scale=-0.5)
            g.tt("dve", v3(sq[:]), v3(kq[:]), bc(rn16), ALU.mult, [U["kq"], U["ss16"]], [U["sq"]])
            return lambda du: g.tt("pool", dst_aug[:, :, 0:64], v3(sq[:]), v3(gain_row[:]), ALU.mult, [U["sq"], wU], [du])

        for b in range(NB):
            rstd_of(x_sb[:, b, :], junk[:], ss, rstd, [xU[b]], U["junk"], U["s"])
            g.ts("dve", xn[:], x_sb[:, b, :], rstd, None, ALU.mult, None, [xU[b], U["s"]], [U["xn"]])
            for c in range(8):
                g.tr(pT[:, c * 128:(c + 1) * 128], xn[:, c * 128:(c + 1) * 128], ident_b[:], [U["xn"], cU], [pTU[c // 4]])
            g.cp("act", hT[:], pT.rearrange("p (c t) -> p c t", c=8), pTU, [U["hT"]])
            for half in range(2):
                ps, psu = bigps()
                for c in range(8):
                    g.mm(ps[:], hT[:, c, :], KVW[:, c, half * 512:(half + 1) * 512], c == 0, c == 7, [U["hT"], wU], [psu])
                g.cp("act", kq[:, half * 512:(half + 1) * 512], ps[:], [psu], [U["kq"]])
            headnorm(k_aug, kgr)(U["k_aug"])
            ck(11)
            for half in range(2):
                ps, psu = bigps()
                for c in range(8):
                    g.mm(ps[:], hT[:, c, :], KVW[:, c, D + half * 512: D + (half + 1) * 512], c == 0, c == 7, [U["hT"], wU], [psu])
                g.cp("act", V_all[:, b, half * 8:(half + 1) * 8, 0:64], ps[:].rearrange("p (h n) -> p h n", h=8), [psu], [VU[b]])
            ps, psu = bigps()
            for c in range(8):
                g.mm(ps[:, 0:16], hT[:, c, :], KVW[:, c, 2 * D:2 * D + H], c == 0, c == 7, [U["hT"], wU], [psu])
            g.tt("dve", lf, ps[:, 0:16], fbr[:], ALU.add, [psu, wU], [U["lf"]])
            g.act(lf, lf, AF.Sigmoid, [U["lf"]], [U["lf"]])
            g.act(lf, lf, AF.Ln, [U["lf"]], [U["lf"]])
            g.cp("dve", lfb[:, 0, :], lf, [U["lf"]], [U["lfb"]])
            g.tt("dve", r1, lf, lfb[:, 0, :], ALU.subtract, [U["lf"], U["lfb"]], [U["cc"]])
            g.cp("dve", lfb[:, 1, :], r1, [U["cc"]], [U["lfb"]])
            g.tt("dve", r1, r1, lfb[:, 1, :], ALU.subtract, [U["cc"], U["lfb"]], [U["cc"]])
            g.cp("dve", lfb[:, 2, :], r1, [U["cc"]], [U["lfb"]])
            ps, psu = bigps()
            for pc in range(3):
                g.mm(ps[:, 0:16], tri_b[:], lfb[:, pc, :], pc == 0, pc == 2, [U["lfb"], cU], [psu])
            ps2, psu2 = bigps()
            for pc in range(3):
                g.mm(ps2[:, 0:16], ones_b[:], lfb[:, pc, :], pc == 0, pc == 2, [U["lfb"], cU], [psu2])
            g.tt("dve", cc, ps[:, 0:16], carry[:], ALU.add, [psu, U["carry"]], [U["cc"]])
            g.tt("dve", carry[:], ps2[:, 0:16], carry[:], ALU.add, [psu2, U["carry"]], [U["carry"]])
            g.cp("dve", cbf[:, 0, :], cc, [U["cc"]], [U["cbf"]])
            g.tt("dve", r1, cc, cbf[:, 0, :], ALU.subtract, [U["cc"], U["cbf"]], [U["cc"]])
            g.cp("dve", cbf[:, 1, :], r1, [U["cc"]], [U["cbf"]])
            g.tt("dve", r1, r1, cbf[:, 1, :], ALU.subtract, [U["cc"], U["cbf"]], [U["cc"]])
            g.cp("dve", cbf[:, 2, :], r1, [U["cc"]], [U["cbf"]])
            g.ts("dve", k_aug[:, :, 67:70], cbf[:].rearrange("p s h -> p h s"), -1.0, None, ALU.mult, None, [U["cbf"]], [U["k_aug"]])
            g.cp("dve", q_aug[:, :, 64:67], cbf[:].rearrange("p s h -> p h s"), [U["cbf"]], [U["q_aug"]])
            ck(12)
            for half in range(2):
                ps, psu = bigps()
                for c in range(8):
                    g.mm(ps[:], hT[:, c, :], WQ[:, c, half * 512:(half + 1) * 512], c == 0, c == 7, [U["hT"], wU], [psu])
                g.cp("act", kq[:, half * 512:(half + 1) * 512], ps[:], [psu], [U["kq"]])
            headnorm(q_aug, qgr)(U["q_aug"])
            ck(13)
            for aug, au, Tb, Tu, dst, du in ((k_aug, U["k_aug"], kTb, U["kTb"], kT_d, U["kTd"]),
                                             (q_aug, U["q_aug"], qTb, U["qTb"], qT_d, U["qTd"])):
                for hh in range(2):
                    for h8 in range(8):
                        h = hh * 8 + h8
                        g.tr(pT[0:70, h8 * 128:(h8 + 1) * 128], aug[:, h, :], ident_b[:], [au, cU], [pTU[h8 // 4]])
                    g.cp("act", Tb[:, hh * 8:(hh + 1) * 8, :], pT[0:70, :].rearrange("p (h t) -> p h t", h=8), pTU, [Tu])
                    if hh == 0:
                        ck(14)
                kb.dma("sp", dst[:, :, b * 128:(b + 1) * 128].rearrange("h r t -> r h t"), Tb[:], reads=[Tu], writes=[du])
                ck(15)
        return U["kTd"], U["qTd"]

    try:
        kTdU, qTdU = phase_qkv()
    except _Stop:
        kb.fence()
        return dump_x()
    kb.fence()

    def phase_attn():
        kb.sb_off = attn_off
        o_all = kb.sb("o_all", [128, NB, D], BF16)
        WOa = kb.sb("WOa", [128, 8, D], BF16)
        kTh = [kb.sb(f"kTh{i}", [70, T], BF16) for i in range(2)]
        qTh = [kb.sb(f"qTh{i}", [70, T], BF16) for i in range(2)]
        rec = kb.sb("rec", [128, 4], F32)
        oT = kb.sb("oT", [128, 8, 128], BF16)
        wU = Unit("woa")
        khU, qhU = units(2, "kTh"), units(2, "qTh")
        recU, oTU = Unit("rec"), Unit("oT")
        for c in range(8):
            kb.dma("pool", WOa[:, c, :], attn_w_o[c * 128:(c + 1) * 128, :], writes=[wU])
        NBUF = 3
        psS = [PS[3], PS[4], PS[7]]
        psSU = [PSU[3], PSU[4], PSU[7]]
        psO = [PS[5], PS[6]]
        psOU = [PSU[5], PSU[6]]
        PT = [kb.sb(f"PTb{i}", [128, 4, 128], BF16) for i in range(NBUF)]
        PTU = units(NBUF, "PTb")
        groups = []
        oi = 0
        for h in range(H):
            for i in range(NB):
                for j0 in range(0, i + 1, 4):
                    groups.append((h, i, list(range(j0, min(j0 + 4, i + 1))), oi % 2))
                oi += 1
        loaded = set()

        def load_head(h):
            if h < H and h not in loaded:
                loaded.add(h)
                hb = h % 2
                kb.dma("sp", kTh[hb][:], kT_d[h], reads=[kTdU], writes=[khU[hb]])
                kb.dma("sp", qTh[hb][:], qT_d[h], reads=[qTdU], writes=[qhU[hb]])

        def emit_qk(gidx):
            h, i, js, _ = groups[gidx]
            hb = h % 2
            if i == 0:
                load_head(h)
                load_head(h + 1)
            ps, psu = psS[gidx % NBUF], psSU[gidx % NBUF]
            for jj, j in enumerate(js):
                diag = j == i
                g.mm(ps[:, jj * 128:(jj + 1) * 128], kTh[hb][:, j * 128:(j + 1) * 128], qTh[hb][:, i * 128:(i + 1) * 128],
                     True, not diag, [khU[hb], qhU[hb]], [psu])
                if diag:
                    g.mm(ps[:, jj * 128:(jj + 1) * 128], ident_b[:], maskneg_b[:], False, True, [cU], [psu])

        def emit_rest(gidx):
            h, i, js, ob = groups[gidx]
            ps, psu = psS[gidx % NBUF], psSU[gidx % NBUF]
            pt, ptu = PT[gidx % NBUF], PTU[gidx % NBUF]
            po, pou = psO[ob], psOU[ob]
            n = len(js)
            g.act(pt[:, 0:n, :], ps[:, 0:n * 128].rearrange("p (j t) -> p j t", j=n), AF.Exp, [psu], [ptu])
            for jj, j in enumerate(js):
                g.mm(po[:, 0:65], pt[:, jj, :], V_all[:, j, h, :], j == 0, j == i, [ptu, VU[j]], [pou])
            if js[-1] == i:
                kb.op("dve", lambda E, po=po: E.reciprocal(out=rec[:, 0:1], in_=po[:, 64:65]), [pou], [recU])
                g.ts("dve", o_all[:, i, h * 64:(h + 1) * 64], po[:, 0:64], rec[:, 0:1], None, ALU.mult, None, [pou, recU], [oU[i]])

        emit_qk(0)
        for gidx in range(len(groups)):
            if gidx + 1 < len(groups):
                emit_qk(gidx + 1)
            emit_rest(gidx)
        ck(17)
        pb = 0
        for b in range(NB):
            for c in range(8):
                g.tr(pT[:, c * 128:(c + 1) * 128], o_all[:, b, c * 128:(c + 1) * 128], ident_b[:], [oU[b], cU], [pTU[c // 4]])
            g.cp("act", oT[:], pT.rearrange("p (c t) -> p c t", c=8), pTU, [oTU])
            for half in range(2):
                ps, psu = PS[pb], PSU[pb]
                pb ^= 1
                for c in range(8):
                    g.mm(ps[:], oT[:, c, :], WOa[:, c, half * 512:(half + 1) * 512], c == 0, c == 7, [oTU, wU], [psu])
                g.tt("dve", x_sb[:, b, half * 512:(half + 1) * 512], ps[:], x_sb[:, b, half * 512:(half + 1) * 512], ALU.add,
                     [psu, xU[b]], [xU[b]])

    try:
        phase_attn()
    except _Stop:
        pass
    kb.fence()
    if stop_after <= 3:
        return dump_x()

    phase_mlp(1, True)
    kb.final_wait("sp", [outU])
    kb.fence()
    kb.build()
    return nc


_NC_CACHE = {}


def _prep(inputs):
    f = lambda a: np.ascontiguousarray(np.asarray(a, dtype=np.float32))
    common = {
        "rwkv_norm_g": f(inputs["rwkv_norm_g"][0]),
        "rwkv_mu": f(inputs["rwkv_mu"][0]),
        "rwkv_w_rkv": f(inputs["rwkv_w_rkv"][0]),
        "rwkv_w0": f(inputs["rwkv_w0"][0]),
        "rwkv_w1": f(inputs["rwkv_w1"][0]),
        "rwkv_w2": f(inputs["rwkv_w2"][0]),
        "rwkv_a0": f(inputs["rwkv_a0"][0]),
        "rwkv_a1": f(inputs["rwkv_a1"][0]),
        "rwkv_a2": f(inputs["rwkv_a2"][0]),
        "rwkv_g1": f(inputs["rwkv_g1"][0]),
        "rwkv_g2": f(inputs["rwkv_g2"][0]),
        "rwkv_k_k": f(inputs["rwkv_k_k"][0]),
        "rwkv_k_a": f(inputs["rwkv_k_a"][0]),
        "rwkv_r_k": f(inputs["rwkv_r_k"][0]).reshape(D),
        "rwkv_lnx_w": f(inputs["rwkv_lnx_w"][0]),
        "rwkv_lnx_b": f(inputs["rwkv_lnx_b"][0]),
        "rwkv_w_o": f(inputs["rwkv_w_o"][0]),
        "kv_norm_g": f(inputs["kv_norm_g"]),
        "kv_w": f(inputs["kv_w"]),
        "kv_f_bias": f(inputs["kv_f_bias"]),
        "k_norm_g": f(inputs["k_norm_g"]).reshape(D),
        "attn_norm_g": f(inputs["attn_norm_g"][0]),
        "attn_w_q": f(inputs["attn_w_q"][0]),
        "q_norm_g": f(inputs["q_norm_g"][0]).reshape(D),
        "attn_w_o": f(inputs["attn_w_o"][0]),
        "mlp_norm_g": f(inputs["mlp_norm_g"]),
        "mlp_w_in": f(inputs["mlp_w_in"]),
        "mlp_w_out": f(inputs["mlp_w_out"]),
    }
    x = f(inputs["x"])
    return [dict(common, x=x[b]) for b in range(8)]


def kernel(_stop_after=99, **inputs):
    if _stop_after not in _NC_CACHE:
        _NC_CACHE[_stop_after] = build_nc(_stop_after)
    nc = _NC_CACHE[_stop_after]
    in_maps = _prep(inputs)
    res = run_bass_kernel_spmd(nc, in_maps, core_ids=list(range(8)))
    return np.stack([np.asarray(r["out"], dtype=np.float32) for r in res.results], axis=0)
```

```python
import numpy as np
import concourse.bass as bass
import concourse.mybir as mybir
from concourse.bass_utils import run_bass_kernel_spmd

F32 = mybir.dt.float32
BF16 = mybir.dt.bfloat16
AF = mybir.ActivationFunctionType
ALU = mybir.AluOpType
AX = mybir.AxisListType

D = 1024
T = 2048
H = 16
N = 64
NB = 16
FF = 4096
C_DEC = 0.6065306597126334
NORM_EPS = 1e-6
GN_EPS = 64e-5


DBG = 0


class _Stop(Exception):
    pass


def ck(k):
    if DBG == k:
        raise _Stop()


class Unit:
    __slots__ = ("w", "r", "name", "excl")

    def __init__(self, name="", excl=False):
        self.w = None
        self.r = {}
        self.name = name
        self.excl = excl


def units(n, name=""):
    return [Unit(f"{name}{i}") for i in range(n)]


class KB:
    SEM_ROLL = 3000

    def __init__(self, nc, n_dma_sems=6, same_engine_sync=True):
        self.nc = nc
        self.eng = {"pe": nc.tensor, "act": nc.scalar, "dve": nc.vector,
                    "pool": nc.gpsimd, "sp": nc.sync}
        self.q = {e: [] for e in self.eng}
        self.nsem = 0
        self.sem = {e: self._newsem(e) for e in self.eng}
        self.cnt = {e: 0 for e in self.eng}
        self.seen = {e: {} for e in self.eng}
        self.same = same_engine_sync
        self.dq = {}
        for e in ("sp", "act", "pool"):
            self.dq[e] = {"sems": [self._newsem(f"d{e}") for _ in range(n_dma_sems)],
                          "cnt": [0] * n_dma_sems, "i": 0}
        self.sb_off = 16512
        self.sb_top = 229344
        self.ninst = 0
        self.ntens = 0

    def _newsem(self, tag):
        self.nsem += 1
        return self.nc.alloc_semaphore(f"sem_{tag}_{self.nsem}")

    def sb(self, name, shape, dtype, off=None):
        esz = 2 if dtype == BF16 else 4
        n = int(np.prod(shape[1:])) * esz
        if off is None:
            off = self.sb_off
            self.sb_off = (off + n + 31) // 32 * 32
        assert off + n <= self.sb_top, (name, off, n, self.sb_top)
        self.ntens += 1
        return self.nc.alloc_sbuf_tensor_at(f"{name}_{self.ntens}", list(shape), dtype, offset=off)

    def _collect(self, e, reads, writes):
        deps = {}

        def add(tok):
            s, v = tok
            k = id(s)
            if k not in deps or deps[k][1] < v:
                deps[k] = (s, v)
        own_sem = self.sem[e]
        for u in reads:
            if u.w is not None:
                add(u.w)
            if u.excl:
                for tok in u.r.values():
                    if tok[0] is not own_sem:
                        add(tok)
        for u in writes:
            if u.w is not None:
                add(u.w)
            for tok in u.r.values():
                add(tok)
        waits = []
        own = id(self.sem[e])
        for k, (s, v) in deps.items():
            if k == own and (e == "pe" or not self.same):
                continue
            if self.seen[e].get(k, 0) < v:
                waits.append((s, v))
                self.seen[e][k] = v
        return waits

    def _mark(self, tok, reads, writes):
        s = tok[0]
        for u in writes:
            u.w = tok
            u.r = {}
        for u in reads:
            if u.w is tok:
                continue
            u.r[id(s)] = tok

    def op(self, e, fn, reads=(), writes=()):
        waits = self._collect(e, reads, writes)
        if self.cnt[e] >= self.SEM_ROLL:
            self.sem[e] = self._newsem(e)
            self.cnt[e] = 0
        self.cnt[e] += 1
        tok = (self.sem[e], self.cnt[e])
        self.q[e].append((waits, fn, tok[0], 1))
        self._mark(tok, reads, writes)
        self.ninst += 1
        return tok

    def dma(self, e, out, in_, reads=(), writes=(), **kw):
        d = self.dq[e]
        i = d["i"]
        d["i"] = (i + 1) % len(d["sems"])
        s = d["sems"][i]
        waits = self._collect(e, reads, writes)
        if d["cnt"][i] > 0 and self.seen[e].get(id(s), 0) < d["cnt"][i]:
            waits.append((s, d["cnt"][i]))
            self.seen[e][id(s)] = d["cnt"][i]
        if d["cnt"][i] >= 16 * 180:
            s = self._newsem(f"d{e}")
            d["sems"][i] = s
            d["cnt"][i] = 0
        d["cnt"][i] += 16
        tok = (s, d["cnt"][i])
        self.q[e].append((waits, lambda E: E.dma_start(out=out, in_=in_, **kw), s, 16))
        self._mark(tok, reads, writes)
        self.ninst += 1
        return tok

    def gate(self, e, us):
        waits = self._collect(e, (), us)
        if waits:
            self.q[e].append((waits, None, None, 0))

    def final_wait(self, e, us):
        deps = {}
        for u in us:
            if u.w is not None:
                s, v = u.w
                if id(s) not in deps or deps[id(s)][1] < v:
                    deps[id(s)] = (s, v)
        self.q[e].append((list(deps.values()), None, None, 0))

    def fence(self):
        toks = []
        for e in self.eng:
            if self.cnt[e] > 0:
                toks.append((self.sem[e], self.cnt[e]))
        for e, d in self.dq.items():
            for s, c in zip(d["sems"], d["cnt"]):
                if c > 0:
                    toks.append((s, c))
        for e in self.eng:
            waits = []
            for s, v in toks:
                if self.seen[e].get(id(s), 0) < v:
                    waits.append((s, v))
                    self.seen[e][id(s)] = v
            if waits:
                self.q[e].append((waits, None, None, 0))

    def build(self):
        nc = self.nc
        with nc.Block() as block:
            def mk(ename):
                def body(E):
                    for waits, fn, s, inc in self.q[ename]:
                        for ws, wv in waits:
                            E.wait_ge(ws, wv)
                        if fn is not None:
                            fn(E).then_inc(s, inc)
                return body
            block.tensor(mk("pe"))
            block.scalar(mk("act"))
            block.vector(mk("dve"))
            block.gpsimd(mk("pool"))
            block.sync(mk("sp"))


class G:
    def __init__(self, kb):
        self.kb = kb

    def mm(self, out, lhsT, rhs, start, stop, r, w):
        self.kb.op("pe", lambda E: E.matmul(out, lhsT=lhsT, rhs=rhs, start=start, stop=stop), r, w)

    def tr(self, out, in_, ident, r, w):
        self.kb.op("pe", lambda E: E.transpose(out=out, in_=in_, identity=ident), r, w)

    def tt(self, e, out, in0, in1, op, r, w):
        self.kb.op(e, lambda E: E.tensor_tensor(out=out, in0=in0, in1=in1, op=op), r, w)

    def ts(self, e, out, in0, s1, s2, op0, op1, r, w):
        if op1 is None:
            self.kb.op(e, lambda E: E.tensor_scalar(out=out, in0=in0, scalar1=s1, scalar2=None, op0=op0), r, w)
        else:
            self.kb.op(e, lambda E: E.tensor_scalar(out=out, in0=in0, scalar1=s1, scalar2=s2, op0=op0, op1=op1), r, w)

    def stt(self, out, in0, scalar, in1, op0, op1, r, w):
        self.kb.op("dve", lambda E: E.scalar_tensor_tensor(out=out, in0=in0, scalar=scalar, in1=in1, op0=op0, op1=op1), r, w)

    def act(self, out, in_, func, r, w, bias=None, scale=None, accum_out=None):
        kw = {}
        if bias is not None:
            kw["bias"] = bias
        if scale is not None:
            kw["scale"] = scale
        if accum_out is not None:
            kw["accum_out"] = accum_out
        self.kb.op("act", lambda E: E.activation(out=out, in_=in_, func=func, **kw), r, w)

    def cp(self, e, out, in_, r, w):
        if e == "act":
            self.kb.op(e, lambda E: E.activation(out=out, in_=in_, func=AF.Copy), r, w)
        else:
            self.kb.op(e, lambda E: E.tensor_copy(out=out, in_=in_), r, w)

    def red(self, out, in_, r, w):
        self.kb.op("dve", lambda E: E.tensor_reduce(out=out, in_=in_, axis=AX.X, op=ALU.add), r, w)

    def memset(self, e, ap, val, w):
        self.kb.op(e, lambda E: E.memset(ap, val), (), w)


def build_nc(stop_after=99):
    nc = bass.Bass("TRN2", target_bir_lowering=False)
    dt = nc.dram_tensor

    def inp(name, shape):
        return dt(name, list(shape), F32, kind="ExternalInput").ap()

    x_d = inp("x", [T, D])
    rwkv_norm_g = inp("rwkv_norm_g", [D])
    rwkv_mu = inp("rwkv_mu", [6, D])
    w_rkv = inp("rwkv_w_rkv", [3, D, D])
    w0_d = inp("rwkv_w0", [D])
    w1_d = inp("rwkv_w1", [D, 64])
    w2_d = inp("rwkv_w2", [64, D])
    a0_d = inp("rwkv_a0", [D])
    a1_d = inp("rwkv_a1", [D, 64])
    a2_d = inp("rwkv_a2", [64, D])
    g1_d = inp("rwkv_g1", [D, 160])
    g2_d = inp("rwkv_g2", [160, D])
    kk_d = inp("rwkv_k_k", [D])
    ka_d = inp("rwkv_k_a", [D])
    rk_d = inp("rwkv_r_k", [D])
    lnw_d = inp("rwkv_lnx_w", [D])
    lnb_d = inp("rwkv_lnx_b", [D])
    wo_d = inp("rwkv_w_o", [D, D])
    kv_norm_g = inp("kv_norm_g", [D])
    kv_w = inp("kv_w", [D, 2 * D + H])
    kv_f_bias = inp("kv_f_bias", [H])
    k_norm_g = inp("k_norm_g", [D])
    attn_norm_g = inp("attn_norm_g", [D])
    attn_w_q = inp("attn_w_q", [D, D])
    q_norm_g = inp("q_norm_g", [D])
    attn_w_o = inp("attn_w_o", [D, D])
    mlp_norm_g = inp("mlp_norm_g", [2, D])
    mlp_w_in = inp("mlp_w_in", [2, D, FF])
    mlp_w_out = inp("mlp_w_out", [2, FF, D])
    out_d = dt("out", [T, D], F32, kind="ExternalOutput").ap()
    x1_d = dt("x1_scratch", [T, D], F32).ap()
    kT_d = dt("kT_scratch", [H, 70, T], BF16).ap()
    qT_d = dt("qT_scratch", [H, 70, T], BF16).ap()

    kb = KB(nc)
    g = G(kb)
    NC = ALU

    PS = [nc.alloc_psum_tensor(f"ps{i}", [128, 512], F32) for i in range(8)]
    PSU = [Unit(f"ps{i}", excl=True) for i in range(8)]
    pT = PS[2][:].bitcast(BF16)
    pTU = [PSU[2], PSU[2]]

    ident_f = kb.sb("ident_f", [128, 128], F32)
    ident_b = kb.sb("ident_b", [128, 128], BF16)
    tri_f = kb.sb("tri_f", [128, 128], F32)
    ones_f = kb.sb("ones_f", [128, 128], F32)
    maskE = kb.sb("maskE", [128, 256], BF16)
    mask_ls = kb.sb("mask_ls", [128, 128], BF16)
    tri_b = kb.sb("tri_b", [128, 128], BF16)
    ones_b = kb.sb("ones_b", [128, 128], BF16)
    maskneg_b = kb.sb("maskneg_b", [128, 128], BF16)
    maskE2 = kb.sb("maskE2", [128, 512], BF16)
    mask2 = kb.sb("mask2", [128, 256], BF16)
    maskneg = kb.sb("maskneg", [128, 128], F32)
    ctmp = kb.sb("ctmp", [128, 128], F32)
    cU = Unit("consts")
    ctU = Unit("ctmp")

    def sel(out, in_, cm, step, base, r, w):
        kb.op("pool", lambda E: E.affine_select(out=out, in_=in_, pattern=[[step, 128]], compare_op=ALU.is_ge,
                                                fill=0.0, base=base, channel_multiplier=cm), r, w)
    g.memset("pool", ones_f[:], 1.0, [cU])
    sel(ctmp[:], ones_f[:], 1, -1, 0, [cU], [ctU])
    sel(ident_f[:], ctmp[:], -1, 1, 0, [ctU], [cU])
    g.cp("pool", ident_b[:], ident_f[:], [cU], [cU])
    sel(tri_f[:], ones_f[:], -1, 1, 0, [cU], [cU])
    g.cp("pool", tri_b[:], tri_f[:], [cU], [cU])
    g.cp("pool", ones_b[:], ones_f[:], [cU], [cU])
    g.cp("pool", maskE[:, 128:256], tri_f[:], [cU], [cU])
    sel(ctmp[:], ones_f[:], -1, 1, -1, [cU], [ctU])
    g.cp("pool", maskE[:, 0:128], ctmp[:], [ctU], [cU])
    sel(ctmp[:], ones_f[:], 1, -1, -1, [cU, ctU], [ctU])
    g.cp("pool", mask_ls[:], ctmp[:], [ctU], [cU])
    g.ts("pool", maskneg[:], tri_f[:], -1.0, 1e4, ALU.add, ALU.mult, [cU], [cU])
    g.cp("pool", maskneg_b[:], maskneg[:], [cU], [cU])
    g.cp("pool", maskE2[:, 0:256], maskE[:], [cU], [cU])
    g.cp("pool", maskE2[:, 256:512], maskE[:], [cU], [cU])
    g.cp("pool", mask2[:, 0:128], mask_ls[:], [cU], [cU])
    g.cp("pool", mask2[:, 128:256], ones_f[:], [cU], [cU])

    persist_off = kb.sb_off

    def load_featcols(vecs, name):
        rows = 8 * len(vecs)
        stage = kb.sb(name + "_st", [64, 128], F32)
        dstt = kb.sb(name, [128, rows], F32)
        su, du = Unit(name + "_st"), Unit(name)
        for vi, vec in enumerate(vecs):
            kb.dma("sp", stage[vi * 8:(vi + 1) * 8, :], vec.rearrange("(c p) -> c p", p=128), writes=[su])
        g.tr(PS[2][:, 0:rows], stage[0:rows, :], ident_f[0:rows, 0:rows], [su, cU], [PSU[2]])
        g.cp("dve", dstt[:], PS[2][:, 0:rows], [PSU[2]], [du])
        return dstt, du

    def rstd_of(xin_ap, junk_ap, ss, rstd, rU, wU_junk, wU_s):
        g.act(junk_ap, xin_ap, AF.Square, rU, [wU_junk, wU_s], accum_out=ss)
        g.ts("dve", rstd, ss, 1.0 / D, NORM_EPS, ALU.mult, ALU.add, [wU_s], [wU_s])
        g.act(rstd, rstd, AF.Ln, [wU_s], [wU_s])
        g.act(rstd, rstd, AF.Exp, [wU_s], [wU_s], scale=-0.5)

    def phase_rwkv():
        kb.sb_off = persist_off
        W3 = [kb.sb(f"W{p}", [128, 8, D], BF16) for p in range(3)]
        Wo = kb.sb("Wo", [128, 8, D], BF16)
        W1 = kb.sb("W1", [128, 8, 64], BF16)
        A1 = kb.sb("A1", [128, 8, 64], BF16)
        G1 = kb.sb("G1", [128, 8, 160], BF16)
        W2 = kb.sb("W2", [64, D], BF16)
        A2 = kb.sb("A2", [64, D], BF16)
        G2a = kb.sb("G2a", [128, D], BF16)
        G2b = kb.sb("G2b", [32, D], BF16)
        gmu, gmuU = load_featcols([rwkv_mu[m] for m in range(6)] + [rwkv_norm_g], "gmu")
        muT = gmu[:, 0:48].rearrange("p (m c) -> p m c", m=6)
        gT = gmu[:, 48:56]
        w0r = kb.sb("w0r", [128, D], F32)
        a0r = kb.sb("a0r", [128, D], BF16)
        kkr_ = kb.sb("kkrow", [128, D], BF16)
        kar_ = kb.sb("karow", [128, D], BF16)
        rkr_ = kb.sb("rkrow", [128, D], BF16)
        lnwr = kb.sb("lnwrow", [128, D], BF16)
        lnbr = kb.sb("lnbrow", [128, D], BF16)
        wU = Unit("rw_weights")
        rowf = kb.sb("tA", [128, D], F32)
        tA = rowf
        rowU = Unit("tA")

        for p in range(3):
            for c in range(8):
                kb.dma("pool", W3[p][:, c, :], w_rkv[p, c * 128:(c + 1) * 128, :], writes=[wU])
        for c in range(8):
            kb.dma("pool", Wo[:, c, :], wo_d[c * 128:(c + 1) * 128, :], writes=[wU])
        kb.dma("pool", W1[:], w1_d.rearrange("(c p) e -> p c e", p=128), writes=[wU])
        kb.dma("pool", A1[:], a1_d.rearrange("(c p) e -> p c e", p=128), writes=[wU])
        kb.dma("pool", G1[:], g1_d.rearrange("(c p) e -> p c e", p=128), writes=[wU])
        kb.dma("pool", W2[:], w2_d, writes=[wU])
        kb.dma("pool", A2[:], a2_d, writes=[wU])
        kb.dma("pool", G2a[:], g2_d[0:128, :], writes=[wU])
        kb.dma("pool", G2b[:], g2_d[128:160, :], writes=[wU])
        kb.dma("sp", w0r[:], w0_d.partition_broadcast(128), writes=[wU])
        for src, dst in ((a0_d, a0r), (kk_d, kkr_), (ka_d, kar_), (rk_d, rkr_), (lnw_d, lnwr), (lnb_d, lnbr)):
            kb.dma("sp", rowf[:], src.partition_broadcast(128), writes=[rowU])
            g.cp("dve", dst[:], rowf[:], [rowU], [wU])
        for c in range(8):
            for Wt in (W3[0], W3[1], W3[2], W1, A1, G1):
                g.act(Wt[:, c, :], Wt[:, c, :], AF.Copy, [wU, gmuU], [wU], scale=gT[:, c:c + 1])

        ck(1)
        xin = kb.sb("xin", [128, D], F32)
        r_sb = kb.sb("r_sb", [128, D], F32)
        k_sb = kb.sb("k_sb", [128, D], F32)
        a_sb = kb.sb("a_sb", [128, D], F32)
        sg = kb.sb("sg", [128, D], F32)
        sgh = kb.sb("sgh", [128, D], BF16)
        sgl = kb.sb("sgl", [128, D], BF16)
        kkt = kb.sb("kkt", [128, D], F32)
        kp = kb.sb("kp", [128, D], F32)
        b_sb = kb.sb("b_sb", [128, D], F32)
        Ea_off = kb.sb_off
        Ea = kb.sb("Ea", [128, D], F32)
        Eb_off = kb.sb_off
        Eb = kb.sb("Eb", [128, D], F32)
        y_sb = kb.sb("y_sb", [128, D], F32)
        xn = kb.sb("xn", [128, D], BF16)
        v_bf = kb.sb("v_bf", [128, D], BF16)
        g_bf = kb.sb("g_bf", [128, D], BF16)
        At = kb.sb("At", [128, D], BF16)
        Bt = kb.sb("Bt", [128, D], BF16)
        Kt = kb.sb("Kt", [128, D], BF16)
        Rt = kb.sb("Rt", [128, D], BF16)
        Bh = kb.sb("Bh", [128, D], BF16)
        Kh = kb.sb("Kh", [128, D], BF16)
        yg = xn
        hTx = [kb.sb(f"hTx{i}", [128, 8, 129], BF16) for i in range(2)]
        xxT = kb.sb("xxT", [128, 8, 128], BF16)
        xp = [kb.sb(f"xp{i}", [128, 8, 128], BF16) for i in range(6)]
        ygT = xxT
        twT = kb.sb("twT", [64, 128], BF16)
        t1T = kb.sb("t1T", [64, 128], BF16)
        sgT1 = kb.sb("sgT1", [128, 128], BF16)
        sgT2 = kb.sb("sgT2", [32, 128], BF16)
        small = kb.sb("small", [128, 16 * 12], F32)
        ss = small[:, 0:1]
        rstd = small[:, 1:2]
        ss16 = small[:, 16:32]
        rn16 = small[:, 32:48]
        bs16 = small[:, 48:64]
        s1 = small[:, 64:80]
        s2 = small[:, 80:96]
        mean = small[:, 96:112]
        var = small[:, 112:128]
        rs = small[:, 128:144]
        m2 = small[:, 144:160]
        PCf = kb.sb("PCf", [64, 16], F32)
        S_f = kb.sb("S_f", [64, D], F32)
        S_bf = [kb.sb(f"S_bf{i}", [64, D], BF16) for i in range(2)]
        G4 = 4
        FM = [kb.sb(f"FM{i}", [64, 512], BF16) for i in range(G4)]
        E12 = [kb.sb(f"E12_{i}", [128, 512], BF16, off=Eb_off + i * 1024) for i in range(G4)]
        BN0 = [kb.sb(f"BN0_{i}", [128, 256], BF16) for i in range(G4)]
        AB = [[kb.sb(f"AB{i}_{k}", [128, 256], BF16, off=Ea_off + (i * 2 + k) * 512) for k in range(2)] for i in range(G4)]
        Nj = [[kb.sb(f"Nj{i}_{k}", [128, 128], BF16) for k in range(2)] for i in range(G4)]
        GT = [kb.sb(f"GT{i}", [64, 64], BF16) for i in range(G4)]
        QhT = [kb.sb(f"QhT{i}", [64, 128], BF16) for i in range(G4)]

        U = {n: Unit(n) for n in ["xin", "r", "k", "a", "sg", "tA", "tB", "cum", "kk", "kp", "b", "Ea", "Eb", "y",
                                  "xn", "v", "g", "At", "Bt", "Kt", "Rt", "Bh", "Kh", "yg", "xxT", "ygT", "twT",
                                  "t1T", "sgT1", "sgT2", "ss", "ss16", "bs16", "gn", "PCf", "x1d", "sgh", "sgl"]}
        U["tA"] = rowU
        U["yg"] = U["xn"]
        U["ygT"] = U["xxT"]
        hU = units(2, "hTx")
        xpU = units(6, "xp")
        SfU = units(16, "Sf")
        SbU = [units(16, "Sb0_"), units(16, "Sb1_")]
        FMU = units(G4, "FM")
        E12U = units(G4, "E12")
        BN0U = units(G4, "BN0")
        ABU = [units(2, f"AB{i}_") for i in range(G4)]
        NjU = [units(2, f"Nj{i}_") for i in range(G4)]
        GTU = units(G4, "GT")
        QhU = units(G4, "QhT")
        LU = PSU[7]
        L2U = PSU[7]
        P7 = PS[7]
        pbig = [0]

        def bigps():
            i = pbig[0]
            pbig[0] ^= 1
            return PS[i], PSU[i]

        g.memset("dve", S_f[:], 0.0, SfU)
        g.memset("pool", S_bf[0][:], 0.0, SbU[0])
        g.memset("pool", hTx[0][:, :, 0:1], 0.0, [hU[0]])

        for b in range(NB):
            cur = b % 2
            nxt = 1 - cur
            hc, hUc = hTx[cur], hU[cur]
            kb.dma("sp", xin[:], x_d[b * 128:(b + 1) * 128, :], reads=[], writes=[U["xin"]])
            rstd_of(xin[:], xn[:], ss, rstd, [U["xin"]], U["xn"], U["ss"])
            g.ts("dve", xn[:], xin[:], rstd, None, ALU.mult, None, [U["xin"], U["ss"]], [U["xn"]])
            for c in range(8):
                g.tr(pT[:, c * 128:(c + 1) * 128], xn[:, c * 128:(c + 1) * 128], ident_b[:], [U["xn"], cU], [pTU[c // 4]])
            g.cp("act", hc[:, :, 1:129], pT.rearrange("p (c t) -> p c t", c=8), pTU, [hUc])
            g.cp("pool", hTx[nxt][:, :, 0:1], hc[:, :, 128:129], [hUc], [hU[nxt]])
            g.tt("dve", xxT[:], hc[:, :, 0:128], hc[:, :, 1:129], ALU.subtract, [hUc], [U["xxT"]])
            for p in range(6):
                for c in range(8):
                    g.stt(xp[p][:, c, :], xxT[:, c, :], muT[:, p, c:c + 1], hc[:, c, 1:129], ALU.mult, ALU.add,
                          [U["xxT"], hUc, wU, gmuU], [xpU[p]])

            ck(2)

            def proj(xpi, Wt, evac):
                for half in range(2):
                    ps, psu = bigps()
                    for c in range(8):
                        g.mm(ps[:], xp[xpi][:, c, :], Wt[:, c, half * 512:(half + 1) * 512], c == 0, c == 7,
                             [xpU[xpi], wU], [psu])
                    evac(ps, psu, half)
            hs = lambda half: slice(half * 512, (half + 1) * 512)
            proj(0, W3[0], lambda ps, psu, half: g.cp("act", r_sb[:, hs(half)], ps[:], [psu], [U["r"]]))
            proj(2, W3[1], lambda ps, psu, half: g.cp("act", k_sb[:, hs(half)], ps[:], [psu], [U["k"]]))
            proj(3, W3[2], lambda ps, psu, half: g.cp("act", v_bf[:, hs(half)], ps[:], [psu], [U["v"]]))
            ck(3)
            for c in range(8):
                g.mm(P7[0:64, 256:384], W1[:, c, :], xp[1][:, c, :], c == 0, c == 7, [xpU[1], wU], [LU])
            g.act(twT[:], P7[0:64, 256:384], AF.Tanh, [LU], [U["twT"]])
            for half in range(2):
                ps, psu = bigps()
                g.mm(ps[:], twT[:], W2[:, hs(half)], True, True, [U["twT"], wU], [psu])
                g.tt("dve", tA[:, hs(half)], ps[:], w0r[:, hs(half)], ALU.add, [psu, wU], [U["tA"]])
            g.act(sg[:], tA[:], AF.Sigmoid, [U["tA"]], [U["sg"]])
            for c in range(8):
                g.mm(P7[0:64, 256:384], A1[:, c, :], xp[4][:, c, :], c == 0, c == 7, [xpU[4], wU], [LU])
            g.cp("act", t1T[:], P7[0:64, 256:384], [LU], [U["t1T"]])
            for half in range(2):
                ps, psu = bigps()
                g.mm(ps[:], t1T[:], A2[:, hs(half)], True, True, [U["t1T"], wU], [psu])
                g.tt("dve", a_sb[:, hs(half)], ps[:], a0r[:, hs(half)], ALU.add, [psu, wU], [U["a"]])
            g.act(a_sb[:], a_sb[:], AF.Sigmoid, [U["a"]], [U["a"]])
            for c in range(8):
                g.mm(P7[:, 256:384], G1[:, c, 0:128], xp[5][:, c, :], c == 0, c == 7, [xpU[5], wU], [LU])
            for c in range(8):
                g.mm(P7[0:32, 384:512], G1[:, c, 128:160], xp[5][:, c, :], c == 0, c == 7, [xpU[5], wU], [L2U])
            g.act(sgT1[:], P7[:, 256:384], AF.Sigmoid, [LU], [U["sgT1"]])
            g.act(sgT2[:], P7[0:32, 384:512], AF.Sigmoid, [L2U], [U["sgT2"]])
            for half in range(2):
                ps, psu = bigps()
                g.mm(ps[:], sgT1[:], G2a[:, hs(half)], True, False, [U["sgT1"], wU], [psu])
                g.mm(ps[:], sgT2[:], G2b[:, hs(half)], False, True, [U["sgT2"], wU], [psu])
                g.cp("act", g_bf[:, hs(half)], ps[:], [psu], [U["g"]])
            ck(4)
            g.cp("act", sgh[:], sg[:], [U["sg"]], [U["sgh"]])
            g.tt("pool", tA[:], sg[:], sgh[:], ALU.subtract, [U["sg"], U["sgh"]], [U["tA"]])
            g.cp("act", sgl[:], tA[:], [U["tA"]], [U["sgl"]])

            def trimm(maskT, half):
                ps, psu = bigps()
                g.mm(ps[:], maskT, sgh[:, hs(half)], True, False, [U["sgh"], cU], [psu])
                g.mm(ps[:], maskT, sgl[:, hs(half)], False, True, [U["sgl"], cU], [psu])
                return ps, psu
            kb.gate("act", [uu for q_ in range(G4) for uu in (E12U[q_], ABU[q_][0], ABU[q_][1])])
            for half in range(2):
                ps, psu = trimm(tri_b[:], half)
                g.act(Ea[:, hs(half)], ps[:], AF.Exp, [psu], [U["Ea"]], scale=-C_DEC)
                g.act(Eb[:, hs(half)], ps[:], AF.Exp, [psu], [U["Eb"]], scale=C_DEC)
            ck(41)
            ps, psu = bigps()
            for h in range(H):
                g.mm(ps[0:64, h:h + 1], sgh[:, h * 64:(h + 1) * 64], tri_b[:, 127:128], True, False, [U["sgh"], cU], [psu])
                g.mm(ps[0:64, h:h + 1], sgl[:, h * 64:(h + 1) * 64], tri_b[:, 127:128], False, True, [U["sgl"], cU], [psu])
            g.act(PCf[:], ps[0:64, 0:16], AF.Exp, [psu], [U["PCf"]], scale=-C_DEC)
            ck(5)
            g.tt("pool", kkt[:], k_sb[:], kkr_[:], ALU.mult, [U["k"], wU], [U["kk"]])
            g.tt("pool", tA[:], kkt[:], kkt[:], ALU.mult, [U["kk"]], [U["tA"]])
            g.red(ss16, tA[:].rearrange("p (h n) -> p h n", h=H), [U["tA"]], [U["ss16"]])
            g.ts("dve", rn16, ss16, 1e-24, None, ALU.max, None, [U["ss16"]], [U["ss16"]])
            g.act(rn16, rn16, AF.Ln, [U["ss16"]], [U["ss16"]])
            g.act(rn16, rn16, AF.Exp, [U["ss16"]], [U["ss16"]], scale=-0.5)
            v3 = lambda t_: t_[:].rearrange("p (h n) -> p h n", h=H)
            bc = lambda s_: s_.unsqueeze(2).broadcast_to([128, H, N])
            g.tt("dve", v3(kkt), v3(kkt), bc(rn16), ALU.mult, [U["kk"], U["ss16"]], [U["kk"]])
            g.tt("pool", Rt[:], r_sb[:], Ea[:], ALU.mult, [U["r"], U["Ea"]], [U["Rt"]])
            g.stt(tA[:], a_sb[:], -1.0, kar_[:], ALU.add, ALU.mult, [U["a"], wU], [U["tA"]])
            g.stt(kp[:], tA[:], 1.0, k_sb[:], ALU.add, ALU.mult, [U["tA"], U["k"]], [U["kp"]])
            g.tt("pool", b_sb[:], kkt[:], a_sb[:], ALU.mult, [U["kk"], U["a"]], [U["b"]])
            g.tt("pool", Bt[:], b_sb[:], Eb[:], ALU.mult, [U["b"], U["Eb"]], [U["Bt"]])
            g.tt("dve", Kt[:], kp[:], Eb[:], ALU.mult, [U["kp"], U["Eb"]], [U["Kt"]])
            for half in range(2):
                ps, psu = trimm(mask_ls[:], half)
                g.act(Eb[:, hs(half)], ps[:], AF.Exp, [psu], [U["Eb"]], scale=-C_DEC)
            for half in range(2):
                ps, psu = trimm(maskE[:, 0:128], half)
                g.act(Ea[:, hs(half)], ps[:], AF.Exp, [psu], [U["Ea"]], scale=-C_DEC)
            g.stt(At[:], kkt[:], -1.0, Ea[:], ALU.mult, ALU.mult, [U["kk"], U["Ea"]], [U["At"]])
            g.tt("dve", Bh[:], b_sb[:], Eb[:], ALU.mult, [U["b"], U["Eb"]], [U["Bh"]])
            g.tt("pool", Kh[:], kp[:], Eb[:], ALU.mult, [U["kp"], U["Eb"]], [U["Kh"]])
            g.tt("pool", tA[:], r_sb[:], kp[:], ALU.mult, [U["r"], U["kp"]], [U["tA"]])
            g.tt("pool", tA[:], tA[:], rkr_[:], ALU.mult, [U["tA"], wU], [U["tA"]])
            g.red(bs16, v3(tA), [U["tA"]], [U["bs16"]])

            ck(6)
            kb.gate("dve", [U["Ea"], U["Eb"]])
            kb.gate("act", [U["Ea"], U["Eb"]])
            for h0 in range(0, H, G4):
                hs4 = list(range(h0, h0 + G4))

                def ctx(h):
                    q = h - h0
                    return q, PS[3 + q], PSU[3 + q], slice(h * 64, (h + 1) * 64)
                for h in hs4:
                    q, bk, bu, hsl = ctx(h)
                    bkb = bk[:].bitcast(BF16)
                    for qi, (src, su) in enumerate(((At, U["At"]), (Rt, U["Rt"]), (Bt, U["Bt"]), (Kt, U["Kt"]))):
                        g.tr(bkb[0:64, qi * 128:(qi + 1) * 128], src[:, hsl], ident_b[:], [su, cU], [bu])
                    g.cp("act", FM[q][:], bkb[0:64, 0:512], [bu], [FMU[q]])
                for h in hs4:
                    q, bk, bu, hsl = ctx(h)
                    fm = FM[q]
                    g.mm(bk[:, 0:256], fm[:, 256:384], fm[:, 0:256], True, True, [FMU[q]], [bu])
                    g.mm(bk[:, 256:512], fm[:, 384:512], fm[:, 0:256], True, True, [FMU[q]], [bu])
                    g.tt("dve", E12[q][:], bk[:], maskE2[:], ALU.mult, [bu, cU], [E12U[q]])
                for h in hs4:
                    q, bk, bu, hsl = ctx(h)
                    fm = FM[q]
                    g.mm(bk[:, 0:128], fm[:, 0:128], fm[:, 256:384], True, True, [FMU[q]], [bu])
                    g.mm(bk[:, 128:192], fm[:, 0:128], ident_b[0:64, 0:64], True, True, [FMU[q], cU], [bu])
                    g.mm(bk[:, 192:256], E12[q][:, 256:384], v_bf[:, hsl], True, True, [E12U[q], U["v"]], [bu])
                    g.tt("dve", BN0[q][:], bk[:, 0:256], mask2[:], ALU.mult, [bu, cU], [BN0U[q]])
                cur_ab = {}
                for h in hs4:
                    q = h - h0
                    cur_ab[q] = (E12[q][:, 0:128], BN0[q][:, 0:128], [E12U[q], BN0U[q]], BN0[q][:, 128:256], [BN0U[q]])
                for j in range(7):
                    for h in hs4:
                        q, bk, bu, hsl = ctx(h)
                        A_cur, B_cur, abu, N_cur, nu = cur_ab[q]
                        no = (j + 1) % 2
                        g.mm(bk[:, 0:128], A_cur, N_cur, True, True, abu + nu, [bu])
                        if j < 6:
                            g.mm(bk[:, 128:256], B_cur, A_cur, True, True, abu, [bu])
                            g.mm(bk[:, 256:384], A_cur, B_cur, True, True, abu, [bu])
                        g.tt("dve", Nj[q][no][:], bk[:, 0:128], N_cur, ALU.add, [bu] + nu, [NjU[q][no]])
                        if j < 6:
                            g.cp("act", AB[q][no][:], bk[:, 128:384], [bu], [ABU[q][no]])
                            cur_ab[q] = (AB[q][no][:, 0:128], AB[q][no][:, 128:256], [ABU[q][no]], Nj[q][no][:], [NjU[q][no]])
                        else:
                            cur_ab[q] = (None, None, None, Nj[q][no][:], [NjU[q][no]])
                for h in hs4:
                    q, bk, bu, hsl = ctx(h)
                    XZ, XZu = cur_ab[q][3], cur_ab[q][4]
                    X_ = XZ[:, 0:64]
                    g.mm(bk[0:64, 0:64], X_, Bh[:, hsl], True, True, XZu + [U["Bh"]], [bu])
                    g.mm(bk[0:64, 128:256], X_, E12[q][:, 128:256], True, True, XZu + [E12U[q]], [bu])
                    g.cp("act", GT[q][:], bk[0:64, 0:64], [bu], [GTU[q]])
                    g.tt("dve", QhT[q][:], bk[0:64, 128:256], FM[q][:, 128:256], ALU.add, [bu, FMU[q]], [QhU[q]])
                for h in hs4:
                    q, bk, bu, hsl = ctx(h)
                    XZ, XZu = cur_ab[q][3], cur_ab[q][4]
                    Z_ = XZ[:, 64:128]
                    MrbT, MrkT = E12[q][:, 128:256], E12[q][:, 384:512]
                    Sb_cur, Sb_cu = S_bf[cur][:, hsl], SbU[cur][h]
                    Yr = bk[:, 256:320]
                    g.mm(Yr, MrbT, Z_, True, False, [E12U[q]] + XZu, [bu])
                    g.mm(Yr, MrkT, v_bf[:, hsl], False, False, [E12U[q], U["v"]], [bu])
                    g.mm(Yr, QhT[q][:], Sb_cur, False, True, [QhU[q], Sb_cu], [bu])
                    Hr = bk[0:64, 320:384]
                    g.mm(Hr, Bh[:, hsl], Z_, True, False, [U["Bh"]] + XZu, [bu])
                    g.mm(Hr, Kh[:, hsl], v_bf[:, hsl], False, False, [U["Kh"], U["v"]], [bu])
                    g.mm(Hr, GT[q][:], Sb_cur, False, True, [GTU[q], Sb_cu], [bu])
                    g.cp("act", y_sb[:, hsl], Yr, [bu], [U["y"]])
                    g.stt(S_f[:, hsl], S_f[:, hsl], PCf[:, h:h + 1], Hr, ALU.mult, ALU.add,
                          [SfU[h], U["PCf"], bu], [SfU[h]])
                    g.cp("pool", S_bf[nxt][:, hsl], S_f[:, hsl], [SfU[h]], [SbU[nxt][h]])
                ck(7)

            ck(8)
            g.red(s1, v3(y_sb), [U["y"]], [U["gn"]])
            g.tt("pool", tA[:], y_sb[:], y_sb[:], ALU.mult, [U["y"]], [U["tA"]])
            g.red(s2, v3(tA), [U["tA"]], [U["gn"]])
            g.ts("dve", mean, s1, 1.0 / N, None, ALU.mult, None, [U["gn"]], [U["gn"]])
            g.tt("dve", m2, mean, mean, ALU.mult, [U["gn"]], [U["gn"]])
            g.stt(var, s2, 1.0 / N, m2, ALU.mult, ALU.subtract, [U["gn"]], [U["gn"]])
            g.ts("dve", var, var, GN_EPS, None, ALU.add, None, [U["gn"]], [U["gn"]])
            g.act(var, var, AF.Sqrt, [U["gn"]], [U["gn"]])
            kb.op("dve", lambda E: E.reciprocal(out=rs, in_=var), [U["gn"]], [U["gn"]])
            g.tt("dve", v3(tA), v3(y_sb), bc(mean), ALU.subtract, [U["y"], U["gn"]], [U["tA"]])
            g.tt("dve", v3(tA), v3(tA), bc(rs), ALU.mult, [U["tA"], U["gn"]], [U["tA"]])
            g.tt("pool", tA[:], tA[:], lnwr[:], ALU.mult, [U["tA"], wU], [U["tA"]])
            g.tt("pool", tA[:], tA[:], lnbr[:], ALU.add, [U["tA"], wU], [U["tA"]])
            g.tt("dve", v3(kp), v3(v_bf), bc(bs16), ALU.mult, [U["v"], U["bs16"]], [U["kp"]])
            g.tt("pool", tA[:], tA[:], kp[:], ALU.add, [U["tA"], U["kp"]], [U["tA"]])
            g.tt("pool", yg[:], tA[:], g_bf[:], ALU.mult, [U["tA"], U["g"]], [U["yg"]])
            for c in range(8):
                g.tr(pT[:, c * 128:(c + 1) * 128], yg[:, c * 128:(c + 1) * 128], ident_b[:], [U["yg"], cU], [pTU[c // 4]])
            g.cp("act", ygT[:], pT.rearrange("p (c t) -> p c t", c=8), pTU, [U["ygT"]])
            for half in range(2):
                ps, psu = bigps()
                for c in range(8):
                    g.mm(ps[:], ygT[:, c, :], Wo[:, c, hs(half)], c == 0, c == 7, [U["ygT"], wU], [psu])
                g.tt("dve", xin[:, hs(half)], ps[:], xin[:, hs(half)], ALU.add, [psu, U["xin"]], [U["xin"]])
            kb.dma("sp", x1_d[b * 128:(b + 1) * 128, :], xin[:], reads=[U["xin"]], writes=[U["x1d"]])
        return U["x1d"]

    try:
        x1U = phase_rwkv()
    except _Stop:
        x1U = Unit("x1dummy")
    kb.fence()

    kb.sb_off = persist_off
    x_sb = kb.sb("x_sb", [128, NB, D], F32)
    xU = units(NB, "x")
    resid_off = kb.sb_off
    for b in range(NB):
        kb.dma("sp", x_sb[:, b, :], x1_d[b * 128:(b + 1) * 128, :], reads=[x1U], writes=[xU[b]])

    outU = Unit("out")

    def dump_x():
        for b in range(NB):
            kb.dma("sp", out_d[b * 128:(b + 1) * 128, :], x_sb[:, b, :], reads=[xU[b]], writes=[outU])
        kb.final_wait("sp", [outU])
        kb.fence()
        kb.build()
        return nc

    if stop_after <= 1:
        return dump_x()

    SB_TOP = 229344
    KVW_OFF = SB_TOP - 8 * (2 * D + H) * 2
    WQ_OFF = KVW_OFF - 8 * D * 2
    MLPW_OFF = SB_TOP - 4 * 8192

    def mlp_weights(layer, off=None):
        if off is None:
            WI = [kb.sb(f"WI{layer}_{i}", [128, 8, 512], BF16) for i in range(2)]
            WO = [kb.sb(f"WO{layer}_{i}", [128, 4, D], BF16) for i in range(2)]
        else:
            WI = [kb.sb(f"WI{layer}_{i}", [128, 8, 512], BF16, off=off + i * 8192) for i in range(2)]
            WO = [kb.sb(f"WO{layer}_{i}", [128, 4, D], BF16, off=off + 16384 + i * 8192) for i in range(2)]
        WIU, WOU = units(2, "WI"), units(2, "WO")
        done = set()

        def load_w(e):
            if e in done or e >= 8:
                return
            done.add(e)
            i = e % 2
            for c in range(8):
                kb.dma("pool", WI[i][:, c, :], mlp_w_in[layer, c * 128:(c + 1) * 128, e * 512:(e + 1) * 512], writes=[WIU[i]])
            for fc in range(4):
                kb.dma("pool", WO[i][:, fc, :], mlp_w_out[layer, e * 512 + fc * 128: e * 512 + (fc + 1) * 128, :], writes=[WOU[i]])
        return WI, WO, WIU, WOU, load_w

    def phase_mlp(layer, final, wts=None, mid_hook=None):
        kb.sb_off = resid_off
        hnT = kb.sb("hnT", [128, 8, T], BF16)
        hid = kb.sb("hid", [128, 4, T], BF16)
        WI, WO, WIU, WOU, load_w = wts if wts is not None else mlp_weights(layer)
        gT, gU = load_featcols([mlp_norm_g[layer]], f"mgT{layer}")
        xn = kb.sb("mxn", [128, D], BF16)
        junk = kb.sb("mjunk", [128, D], BF16)
        rl = [kb.sb(f"mrl{i}", [128, 512], F32) for i in range(2)]
        small = kb.sb("msmall", [128, 32], F32)
        ss16a, rstd16 = small[:, 0:NB], small[:, 16:16 + NB]
        hnU = units(NB, "hnT")
        NT = T // 512
        hidU = [units(NT, f"hid{fc}_") for fc in range(4)]
        xnU = Unit("mxn")
        jU = Unit("mjunk")
        sU = Unit("msmall")
        rlU = units(2, "mrl")
        load_w(0)
        for b in range(NB):
            g.act(junk[:], x_sb[:, b, :], AF.Square, [xU[b]], [jU, sU], accum_out=ss16a[:, b:b + 1])
        g.ts("dve", rstd16, ss16a, 1.0 / D, NORM_EPS, ALU.mult, ALU.add, [sU], [sU])
        g.act(rstd16, rstd16, AF.Ln, [sU], [sU])
        g.act(rstd16, rstd16, AF.Exp, [sU], [sU], scale=-0.5)

        def pre(b):
            g.ts("dve", xn[:], x_sb[:, b, :], rstd16[:, b:b + 1], None, ALU.mult, None, [xU[b], sU], [xnU])
            for c in range(8):
                g.tr(pT[:, c * 128:(c + 1) * 128], xn[:, c * 128:(c + 1) * 128], ident_b[:], [xnU, cU], [PSU[2]])
            for c in range(8):
                if c % 2:
                    g.ts("dve", hnT[:, c, b * 128:(b + 1) * 128], pT[:, c * 128:(c + 1) * 128], gT[:, c:c + 1], None, ALU.mult, None,
                         [PSU[2], gU], [hnU[b]])
                else:
                    g.act(hnT[:, c, b * 128:(b + 1) * 128], pT[:, c * 128:(c + 1) * 128], AF.Copy, [PSU[2], gU], [hnU[b]],
                          scale=gT[:, c:c + 1])
        pb = [0]

        def hidden(e, fc, tt):
            i = e % 2
            pi = pb[0]
            pb[0] ^= 1
            ps, psu = PS[pi], PSU[pi]
            for c in range(8):
                g.mm(ps[:], WI[i][:, c, fc * 128:(fc + 1) * 128], hnT[:, c, tt * 512:(tt + 1) * 512], c == 0, c == 7,
                     [WIU[i]] + hnU[tt * 4:(tt + 1) * 4], [psu])
            ri = (fc * 4 + tt) % 2
            g.act(rl[ri][:], ps[:], AF.Relu, [psu], [rlU[ri]])
            g.tt("pool", hid[:, fc, tt * 512:(tt + 1) * 512], rl[ri][:], rl[ri][:], ALU.mult, [rlU[ri]], [hidU[fc][tt]])

        for b in range(4):
            pre(b)
        for e in range(8):
            i = e % 2
            load_w(e + 1)
            if e == 2 and mid_hook is not None:
                mid_hook()
            if e == 0:
                for tt in range(NT):
                    if tt + 1 < NT:
                        for b in range(4 * (tt + 1), 4 * (tt + 2)):
                            pre(b)
                    for fc in range(4):
                        hidden(e, fc, tt)
            else:
                for fc in range(4):
                    for tt in range(NT):
                        hidden(e, fc, tt)
            for b in range(NB):
                for half in range(2):
                    pi = 3 + pb[0]
                    pb[0] ^= 1
                    ps, psu = PS[pi], PSU[pi]
                    for fc in range(4):
                        g.mm(ps[:], hid[:, fc, b * 128:(b + 1) * 128], WO[i][:, fc, half * 512:(half + 1) * 512], fc == 0, fc == 3,
                             [hidU[fc][b // 4], WOU[i]], [psu])
                    g.tt("dve", x_sb[:, b, half * 512:(half + 1) * 512], ps[:], x_sb[:, b, half * 512:(half + 1) * 512], ALU.add,
                         [psu, xU[b]], [xU[b]])
                if final and e == 7:
                    kb.dma("sp", out_d[b * 128:(b + 1) * 128, :], x_sb[:, b, :], reads=[xU[b]], writes=[outU])
        assert kb.sb_off <= (WQ_OFF if layer == 0 else MLPW_OFF), kb.sb_off

    KVW = kb.sb("KVW", [128, 8, 2 * D + H], BF16, off=KVW_OFF)
    WQ = kb.sb("WQ", [128, 8, D], BF16, off=WQ_OFF)
    qkvwU = Unit("qkv_w")

    def prefetch_qkv_w():
        for c in range(8):
            kb.dma("pool", KVW[:, c, 0:1024], kv_w[c * 128:(c + 1) * 128, 0:1024], writes=[qkvwU])
            kb.dma("pool", KVW[:, c, 1024:2 * D + H], kv_w[c * 128:(c + 1) * 128, 1024:2 * D + H], writes=[qkvwU])
            kb.dma("pool", WQ[:, c, :], attn_w_q[c * 128:(c + 1) * 128, :], writes=[qkvwU])

    phase_mlp(0, False, mid_hook=prefetch_qkv_w)
    kb.fence()
    if stop_after <= 2:
        return dump_x()

    kb.sb_off = resid_off
    V_all = kb.sb("V_all", [128, NB, H, 65], BF16)
    VU = units(NB, "V")
    oU = units(NB, "o")
    attn_off = kb.sb_off

    def phase_qkv():
        kb.sb_off = attn_off
        gkq, gkqU = load_featcols([kv_norm_g, attn_norm_g], "gkq")
        gTk = gkq[:, 0:8]
        gTq = gkq[:, 8:16]
        kgr = kb.sb("kgr", [128, D], F32)
        qgr = kb.sb("qgr", [128, D], F32)
        fbr = kb.sb("fbr", [128, H], F32)
        xn = kb.sb("qxn", [128, D], BF16)
        junk = kb.sb("qjunk", [128, D], BF16)
        hT = [kb.sb(f"qhT{i}", [128, 8, 128], BF16) for i in range(2)]
        k_sb = kb.sb("k_sb", [128, D], F32)
        q_sb = kb.sb("q_sb", [128, D], F32)
        sqk = kb.sb("sqk", [128, D], F32)
        sqq = kb.sb("sqq", [128, D], F32)
        k_aug = kb.sb("k_aug", [128, H, 70], BF16)
        q_aug = kb.sb("q_aug", [128, H, 70], BF16)
        kTb = kb.sb("kTb", [70, H, 128], BF16)
        qTb = kb.sb("qTb", [70, H, 128], BF16)
        small = kb.sb("qsmall", [128, 16 * 12], F32)
        carry = kb.sb("carry", [128, H], F32)
        cbf = kb.sb("cbf", [128, 3, H], BF16)
        lfb = kb.sb("lfb", [128, 3, H], BF16)
        ss16a, rstd16 = small[:, 0:NB], small[:, 16:16 + NB]
        lf, cc, r1 = small[:, 48:64], small[:, 64:80], small[:, 80:96]
        ssk, rnk, ssq, rnq = small[:, 96:112], small[:, 112:128], small[:, 128:144], small[:, 144:160]
        wU = qkvwU
        U = {n: Unit(n) for n in ["xn", "junk", "k", "q", "sqk", "sqq", "k_aug", "q_aug", "kTb", "qTb", "s", "ssk", "ssq",
                                  "lf", "cc", "carry", "cbf", "kTd", "qTd", "lfb"]}
        hTU = units(2, "qhT")
        kb.dma("sp", kgr[:], k_norm_g.partition_broadcast(128), writes=[wU])
        kb.dma("sp", qgr[:], q_norm_g.partition_broadcast(128), writes=[wU])
        kb.dma("sp", fbr[:], kv_f_bias.partition_broadcast(128), writes=[wU])
        for c in range(8):
            g.act(KVW[:, c, :], KVW[:, c, :], AF.Copy, [wU, gkqU], [wU], scale=gTk[:, c:c + 1])
            g.ts("dve", WQ[:, c, :], WQ[:, c, :], gTq[:, c:c + 1], None, ALU.mult, None, [wU, gkqU], [wU])
        g.ts("dve", qgr[:], qgr[:], 0.125, None, ALU.mult, None, [wU], [wU])
        ck(10)
        g.memset("pool", carry[:], 0.0, [U["carry"]])
        g.memset("pool", k_aug[:], 1.0, [U["k_aug"]])
        g.memset("pool", q_aug[:], 1.0, [U["q_aug"]])
        g.memset("pool", V_all[:], 1.0, VU)
        v3 = lambda ap: ap.rearrange("p (h n) -> p h n", h=H)
        bc = lambda s_: s_.unsqueeze(2).broadcast_to([128, H, N])
        for b in range(NB):
            g.act(junk[:], x_sb[:, b, :], AF.Square, [xU[b]], [U["junk"], U["s"]], accum_out=ss16a[:, b:b + 1])
        g.ts("dve", rstd16, ss16a, 1.0 / D, NORM_EPS, ALU.mult, ALU.add, [U["s"]], [U["s"]])
        g.act(rstd16, rstd16, AF.Ln, [U["s"]], [U["s"]])
        g.act(rstd16, rstd16, AF.Exp, [U["s"]], [U["s"]], scale=-0.5)

        def stageA(b):
            g.ts("dve", xn[:], x_sb[:, b, :], rstd16[:, b:b + 1], None, ALU.mult, None, [xU[b], U["s"]], [U["xn"]])
            for c in range(8):
                g.tr(pT[:, c * 128:(c + 1) * 128], xn[:, c * 128:(c + 1) * 128], ident_b[:], [U["xn"], cU], [PSU[2]])
            g.cp("act", hT[b % 2][:], pT.rearrange("p (c t) -> p c t", c=8), [PSU[2]], [hTU[b % 2]])

        def stageB(b):
            h_, hu = hT[b % 2], hTU[b % 2]
            for half in range(2):
                for c in range(8):
                    g.mm(PS[half][:], h_[:, c, :], KVW[:, c, half * 512:(half + 1) * 512], c == 0, c == 7, [hu, wU], [PSU[half]])
            for half in range(2):
                for c in range(8):
                    g.mm(PS[3 + half][:], h_[:, c, :], KVW[:, c, D + half * 512: D + (half + 1) * 512], c == 0, c == 7,
                         [hu, wU], [PSU[3 + half]])
            for half in range(2):
                for c in range(8):
                    g.mm(PS[5 + half][:], h_[:, c, :], WQ[:, c, half * 512:(half + 1) * 512], c == 0, c == 7, [hu, wU], [PSU[5 + half]])
            for c in range(8):
                g.mm(PS[7][:, 0:16], h_[:, c, :], KVW[:, c, 2 * D:2 * D + H], c == 0, c == 7, [hu, wU], [PSU[7]])

        def stageC1(b):
            for half in range(2):
                g.cp("act", k_sb[:, half * 512:(half + 1) * 512], PS[half][:], [PSU[half]], [U["k"]])
            for half in range(2):
                g.cp("act", V_all[:, b, half * 8:(half + 1) * 8, 0:64], PS[3 + half][:].rearrange("p (h n) -> p h n", h=8),
                     [PSU[3 + half]], [VU[b]])
            for half in range(2):
                g.cp("act", q_sb[:, half * 512:(half + 1) * 512], PS[5 + half][:], [PSU[5 + half]], [U["q"]])
            g.tt("dve", lf, PS[7][:, 0:16], fbr[:], ALU.add, [PSU[7], wU], [U["lf"]])

        def headnorm(src, su, sq, squ, ss_, rn_, ssu, dst_aug, gain_row, du):
            g.tt("pool", sq[:], src[:], src[:], ALU.mult, [su], [squ])
            g.red(ss_, v3(sq[:]), [squ], [ssu])
            g.ts("dve", rn_, ss_, 1.0 / N, NORM_EPS, ALU.mult, ALU.add, [ssu], [ssu])
            g.act(rn_, rn_, AF.Ln, [ssu], [ssu])
            g.act(rn_, rn_, AF.Exp, [ssu], [ssu], scale=-0.5)
            g.tt("dve", v3(sq[:]), v3(src[:]), bc(rn_), ALU.mult, [su, ssu], [squ])
            g.tt("pool", dst_aug[:, :, 0:64], v3(sq[:]), v3(gain_row[:]), ALU.mult, [squ, wU], [du])

        def stageC2(b):
            headnorm(k_sb, U["k"], sqk, U["sqk"], ssk, rnk, U["ssk"], k_aug, kgr, U["k_aug"])
            headnorm(q_sb, U["q"], sqq, U["sqq"], ssq, rnq, U["ssq"], q_aug, qgr, U["q_aug"])
            g.act(lf, lf, AF.Sigmoid, [U["lf"]], [U["lf"]])
            g.act(lf, lf, AF.Ln, [U["lf"]], [U["lf"]])
            g.cp("dve", lfb[:, 0, :], lf, [U["lf"]], [U["lfb"]])
            g.tt("dve", r1, lf, lfb[:, 0, :], ALU.subtract, [U["lf"], U["lfb"]], [U["cc"]])
            g.cp("dve", lfb[:, 1, :], r1, [U["cc"]], [U["lfb"]])
            g.tt("dve", r1, r1, lfb[:, 1, :], ALU.subtract, [U["cc"], U["lfb"]], [U["cc"]])
            g.cp("dve", lfb[:, 2, :], r1, [U["cc"]], [U["lfb"]])
            for pc in range(3):
                g.mm(PS[2][:, 0:16], tri_b[:], lfb[:, pc, :], pc == 0, pc == 2, [U["lfb"], cU], [PSU[2]])
            for pc in range(3):
                g.mm(PS[2][:, 16:32], ones_b[:], lfb[:, pc, :], pc == 0, pc == 2, [U["lfb"], cU], [PSU[2]])
            g.tt("dve", cc, PS[2][:, 0:16], carry[:], ALU.add, [PSU[2], U["carry"]], [U["cc"]])
            g.tt("dve", carry[:], PS[2][:, 16:32], carry[:], ALU.add, [PSU[2], U["carry"]], [U["carry"]])
            g.cp("dve", cbf[:, 0, :], cc, [U["cc"]], [U["cbf"]])
            g.tt("dve", r1, cc, cbf[:, 0, :], ALU.subtract, [U["cc"], U["cbf"]], [U["cc"]])
            g.cp("dve", cbf[:, 1, :], r1, [U["cc"]], [U["cbf"]])
            g.tt("dve", r1, r1, cbf[:, 1, :], ALU.subtract, [U["cc"], U["cbf"]], [U["cc"]])
            g.cp("dve", cbf[:, 2, :], r1, [U["cc"]], [U["cbf"]])
            g.ts("dve", k_aug[:, :, 67:70], cbf[:].rearrange("p s h -> p h s"), -1.0, None, ALU.mult, None, [U["cbf"]], [U["k_aug"]])
            g.cp("dve", q_aug[:, :, 64:67], cbf[:].rearrange("p s h -> p h s"), [U["cbf"]], [U["q_aug"]])
            for aug, au, Tb, Tu, dst, du in ((k_aug, U["k_aug"], kTb, U["kTb"], kT_d, U["kTd"]),
                                             (q_aug, U["q_aug"], qTb, U["qTb"], qT_d, U["qTd"])):
                for hh in range(2):
                    for h8 in range(8):
                        h = hh * 8 + h8
                        g.tr(pT[0:70, h8 * 128:(h8 + 1) * 128], aug[:, h, :], ident_b[:], [au, cU], [PSU[2]])
                    g.cp("act", Tb[:, hh * 8:(hh + 1) * 8, :], pT[0:70, :].rearrange("p (h t) -> p h t", h=8), [PSU[2]], [Tu])
                kb.dma("sp", dst[:, :, b * 128:(b + 1) * 128].rearrange("h r t -> r h t"), Tb[:], reads=[Tu], writes=[du])

        assert kb.sb_off <= WQ_OFF, kb.sb_off
        stageA(0)
        stageB(0)
        for b in range(NB):
            if b + 1 < NB:
                stageA(b + 1)
            stageC1(b)
            if b + 1 < NB:
                stageB(b + 1)
            stageC2(b)
            ck(15 if b == 0 else -1)
        return U["kTd"], U["qTd"]

    try:
        kTdU, qTdU = phase_qkv()
    except _Stop:
        kb.fence()
        return dump_x()
    kb.fence()

    def phase_attn():
        kb.sb_off = attn_off
        o_all = kb.sb("o_all", [128, NB, D], BF16)
        WOa = kb.sb("WOa", [128, 8, D], BF16)
        kTh = [kb.sb(f"kTh{i}", [70, T], BF16) for i in range(2)]
        qTh = [kb.sb(f"qTh{i}", [70, T], BF16) for i in range(2)]
        rec = kb.sb("rec", [128, 4], F32)
        oT = kb.sb("oT", [128, 8, 128], BF16)
        wU = Unit("woa")
        khU, qhU = units(2, "kTh"), units(2, "qTh")
        recU, oTU = Unit("rec"), Unit("oT")
        for c in range(8):
            kb.dma("pool", WOa[:, c, :], attn_w_o[c * 128:(c + 1) * 128, :], writes=[wU])
        mlp1_w[4](0)
        assert kb.sb_off + 3 * 1024 + 64 <= MLPW_OFF, kb.sb_off
        NBUF = 3
        psS = [PS[3], PS[4], PS[7]]
        psSU = [PSU[3], PSU[4], PSU[7]]
        psO = [PS[5], PS[6]]
        psOU = [PSU[5], PSU[6]]
        PT = [kb.sb(f"PTb{i}", [128, 4, 128], BF16) for i in range(NBUF)]
        PTU = units(NBUF, "PTb")
        groups = []
        oi = 0
        for h in range(H):
            for i in range(NB):
                for j0 in range(0, i + 1, 4):
                    groups.append((h, i, list(range(j0, min(j0 + 4, i + 1))), oi % 2))
                oi += 1
        loaded = set()

        def load_head(h):
            if h < H and h not in loaded:
                loaded.add(h)
                hb = h % 2
                kb.dma("sp", kTh[hb][:], kT_d[h], reads=[kTdU], writes=[khU[hb]])
                kb.dma("sp", qTh[hb][:], qT_d[h], reads=[qTdU], writes=[qhU[hb]])

        def emit_qk(gidx):
            h, i, js, _ = groups[gidx]
            hb = h % 2
            if i == 0:
                load_head(h)
                load_head(h + 1)
            ps, psu = psS[gidx % NBUF], psSU[gidx % NBUF]
            for jj, j in enumerate(js):
                diag = j == i
                g.mm(ps[:, jj * 128:(jj + 1) * 128], kTh[hb][:, j * 128:(j + 1) * 128], qTh[hb][:, i * 128:(i + 1) * 128],
                     True, not diag, [khU[hb], qhU[hb]], [psu])
                if diag:
                    g.mm(ps[:, jj * 128:(jj + 1) * 128], ident_b[:], maskneg_b[:], False, True, [cU], [psu])

        def emit_rest(gidx):
            h, i, js, ob = groups[gidx]
            ps, psu = psS[gidx % NBUF], psSU[gidx % NBUF]
            pt, ptu = PT[gidx % NBUF], PTU[gidx % NBUF]
            po, pou = psO[ob], psOU[ob]
            n = len(js)
            g.act(pt[:, 0:n, :], ps[:, 0:n * 128].rearrange("p (j t) -> p j t", j=n), AF.Exp, [psu], [ptu])
            for jj, j in enumerate(js):
                g.mm(po[:, 0:65], pt[:, jj, :], V_all[:, j, h, :], j == 0, j == i, [ptu, VU[j]], [pou])
            if js[-1] == i:
                kb.op("dve", lambda E, po=po: E.reciprocal(out=rec[:, 0:1], in_=po[:, 64:65]), [pou], [recU])
                g.ts("dve", o_all[:, i, h * 64:(h + 1) * 64], po[:, 0:64], rec[:, 0:1], None, ALU.mult, None, [pou, recU], [oU[i]])

        emit_qk(0)
        for gidx in range(len(groups)):
            if gidx + 1 < len(groups):
                emit_qk(gidx + 1)
            emit_rest(gidx)
        ck(17)
        pb = 0
        for b in range(NB):
            for c in range(8):
                g.tr(pT[:, c * 128:(c + 1) * 128], o_all[:, b, c * 128:(c + 1) * 128], ident_b[:], [oU[b], cU], [pTU[c // 4]])
            g.cp("act", oT[:], pT.rearrange("p (c t) -> p c t", c=8), pTU, [oTU])
            for half in range(2):
                ps, psu = PS[pb], PSU[pb]
                pb ^= 1
                for c in range(8):
                    g.mm(ps[:], oT[:, c, :], WOa[:, c, half * 512:(half + 1) * 512], c == 0, c == 7, [oTU, wU], [psu])
                g.tt("dve", x_sb[:, b, half * 512:(half + 1) * 512], ps[:], x_sb[:, b, half * 512:(half + 1) * 512], ALU.add,
                     [psu, xU[b]], [xU[b]])

    mlp1_w = mlp_weights(1, off=MLPW_OFF)
    try:
        phase_attn()
    except _Stop:
        pass
    kb.fence()
    if stop_after <= 3:
        return dump_x()

    phase_mlp(1, True, wts=mlp1_w)
    kb.final_wait("sp", [outU])
    kb.fence()
    kb.build()
    return nc


_NC_CACHE = {}


def _prep(inputs):
    f = lambda a: np.ascontiguousarray(np.asarray(a, dtype=np.float32))
    common = {
        "rwkv_norm_g": f(inputs["rwkv_norm_g"][0]),
        "rwkv_mu": f(inputs["rwkv_mu"][0]),
        "rwkv_w_rkv": f(inputs["rwkv_w_rkv"][0]),
        "rwkv_w0": f(inputs["rwkv_w0"][0]),
        "rwkv_w1": f(inputs["rwkv_w1"][0]),
        "rwkv_w2": f(inputs["rwkv_w2"][0]),
        "rwkv_a0": f(inputs["rwkv_a0"][0]),
        "rwkv_a1": f(inputs["rwkv_a1"][0]),
        "rwkv_a2": f(inputs["rwkv_a2"][0]),
        "rwkv_g1": f(inputs["rwkv_g1"][0]),
        "rwkv_g2": f(inputs["rwkv_g2"][0]),
        "rwkv_k_k": f(inputs["rwkv_k_k"][0]),
        "rwkv_k_a": f(inputs["rwkv_k_a"][0]),
        "rwkv_r_k": f(inputs["rwkv_r_k"][0]).reshape(D),
        "rwkv_lnx_w": f(inputs["rwkv_lnx_w"][0]),
        "rwkv_lnx_b": f(inputs["rwkv_lnx_b"][0]),
        "rwkv_w_o": f(inputs["rwkv_w_o"][0]),
        "kv_norm_g": f(inputs["kv_norm_g"]),
        "kv_w": f(inputs["kv_w"]),
        "kv_f_bias": f(inputs["kv_f_bias"]),
        "k_norm_g": f(inputs["k_norm_g"]).reshape(D),
        "attn_norm_g": f(inputs["attn_norm_g"][0]),
        "attn_w_q": f(inputs["attn_w_q"][0]),
        "q_norm_g": f(inputs["q_norm_g"][0]).reshape(D),
        "attn_w_o": f(inputs["attn_w_o"][0]),
        "mlp_norm_g": f(inputs["mlp_norm_g"]),
        "mlp_w_in": f(inputs["mlp_w_in"]),
        "mlp_w_out": f(inputs["mlp_w_out"]),
    }
    x = f(inputs["x"])
    return [dict(common, x=x[b]) for b in range(8)]


def kernel(_stop_after=99, **inputs):
    if _stop_after not in _NC_CACHE:
        _NC_CACHE[_stop_after] = build_nc(_stop_after)
    nc = _NC_CACHE[_stop_after]
    in_maps = _prep(inputs)
    res = run_bass_kernel_spmd(nc, in_maps, core_ids=list(range(8)))
    return np.stack([np.asarray(r["out"], dtype=np.float32) for r in res.results], axis=0)
```

```python
import numpy as np
import concourse.bass as bass
import concourse.mybir as mybir
from concourse.bass_utils import run_bass_kernel_spmd

F32 = mybir.dt.float32
BF16 = mybir.dt.bfloat16
AF = mybir.ActivationFunctionType
ALU = mybir.AluOpType
AX = mybir.AxisListType

D = 1024
T = 2048
H = 16
N = 64
NB = 16
FF = 4096
C_DEC = 0.6065306597126334
NORM_EPS = 1e-6
GN_EPS = 64e-5


DBG = 0


class _Stop(Exception):
    pass


def ck(k):
    if DBG == k:
        raise _Stop()


class Unit:
    __slots__ = ("w", "r", "name", "excl", "ws", "base", "mopen")

    def __init__(self, name="", excl=False):
        self.w = None
        self.r = {}
        self.ws = []
        self.base = []
        self.mopen = False
        self.name = name
        self.excl = excl


def units(n, name=""):
    return [Unit(f"{name}{i}") for i in range(n)]


class KB:
    SEM_ROLL = 3000

    def __init__(self, nc, n_dma_sems=6, same_engine_sync=True):
        self.nc = nc
        self.eng = {"pe": nc.tensor, "act": nc.scalar, "dve": nc.vector,
                    "pool": nc.gpsimd, "sp": nc.sync}
        self.q = {e: [] for e in self.eng}
        self.nsem = 0
        self.sem = {e: self._newsem(e) for e in self.eng}
        self.cnt = {e: 0 for e in self.eng}
        self.seen = {e: {} for e in self.eng}
        self.same = same_engine_sync
        self.dq = {}
        for e in ("sp", "act", "pool"):
            self.dq[e] = {"sems": [self._newsem(f"d{e}") for _ in range(n_dma_sems)],
                          "cnt": [0] * n_dma_sems, "i": 0}
        self.sb_off = 16512
        self.sb_top = 229344
        self.ninst = 0
        self.ntens = 0

    def _newsem(self, tag):
        self.nsem += 1
        return self.nc.alloc_semaphore(f"sem_{tag}_{self.nsem}")

    def sb(self, name, shape, dtype, off=None):
        esz = 2 if dtype == BF16 else 4
        n = int(np.prod(shape[1:])) * esz
        if off is None:
            off = self.sb_off
            self.sb_off = (off + n + 31) // 32 * 32
        assert off + n <= self.sb_top, (name, off, n, self.sb_top)
        self.ntens += 1
        return self.nc.alloc_sbuf_tensor_at(f"{name}_{self.ntens}", list(shape), dtype, offset=off)

    def _collect(self, e, reads, writes, multi=False):
        deps = {}

        def add(tok):
            s, v = tok
            k = id(s)
            if k not in deps or deps[k][1] < v:
                deps[k] = (s, v)
        own_sem = self.sem[e]
        for u in reads:
            if u.w is not None:
                add(u.w)
            for tok in u.ws:
                add(tok)
            if u.excl:
                for tok in u.r.values():
                    if tok[0] is not own_sem:
                        add(tok)
        for u in writes:
            if multi and u.mopen:
                for tok in u.base:
                    add(tok)
                continue
            base = []
            if u.w is not None:
                base.append(u.w)
            base.extend(u.ws)
            base.extend(u.r.values())
            for tok in base:
                add(tok)
            if multi:
                u.base = base
        waits = []
        own = id(self.sem[e])
        for k, (s, v) in deps.items():
            if k == own and (e == "pe" or not self.same):
                continue
            if self.seen[e].get(k, 0) < v:
                waits.append((s, v))
                self.seen[e][k] = v
        return waits

    def _mark(self, tok, reads, writes, multi=False):
        s = tok[0]
        for u in writes:
            if multi and u.mopen:
                u.ws.append(tok)
                continue
            u.w = tok
            u.ws = []
            u.r = {}
            u.mopen = multi
        for u in reads:
            if u.w is tok:
                continue
            u.mopen = False
            u.r[id(s)] = tok

    def op(self, e, fn, reads=(), writes=()):
        waits = self._collect(e, reads, writes)
        if self.cnt[e] >= self.SEM_ROLL:
            self.sem[e] = self._newsem(e)
            self.cnt[e] = 0
        self.cnt[e] += 1
        tok = (self.sem[e], self.cnt[e])
        self.q[e].append((waits, fn, tok[0], 1))
        self._mark(tok, reads, writes)
        self.ninst += 1
        return tok

    def dma(self, e, out, in_, reads=(), writes=(), multi=False, **kw):
        d = self.dq[e]
        i = d["i"]
        d["i"] = (i + 1) % len(d["sems"])
        s = d["sems"][i]
        waits = self._collect(e, reads, writes, multi=multi)
        if d["cnt"][i] > 0 and self.seen[e].get(id(s), 0) < d["cnt"][i]:
            waits.append((s, d["cnt"][i]))
            self.seen[e][id(s)] = d["cnt"][i]
        if d["cnt"][i] >= 16 * 180:
            s = self._newsem(f"d{e}")
            d["sems"][i] = s
            d["cnt"][i] = 0
        d["cnt"][i] += 16
        tok = (s, d["cnt"][i])
        self.q[e].append((waits, lambda E: E.dma_start(out=out, in_=in_, **kw), s, 16))
        self._mark(tok, reads, writes, multi=multi)
        self.ninst += 1
        return tok

    def gate(self, e, us):
        waits = self._collect(e, (), us)
        if waits:
            self.q[e].append((waits, None, None, 0))

    def final_wait(self, e, us):
        deps = {}
        for u in us:
            if u.w is not None:
                s, v = u.w
                if id(s) not in deps or deps[id(s)][1] < v:
                    deps[id(s)] = (s, v)
        self.q[e].append((list(deps.values()), None, None, 0))

    def fence(self):
        toks = []
        for e in self.eng:
            if self.cnt[e] > 0:
                toks.append((self.sem[e], self.cnt[e]))
        for e, d in self.dq.items():
            for s, c in zip(d["sems"], d["cnt"]):
                if c > 0:
                    toks.append((s, c))
        for e in self.eng:
            waits = []
            for s, v in toks:
                if self.seen[e].get(id(s), 0) < v:
                    waits.append((s, v))
                    self.seen[e][id(s)] = v
            if waits:
                self.q[e].append((waits, None, None, 0))

    def build(self):
        nc = self.nc
        with nc.Block() as block:
            def mk(ename):
                def body(E):
                    for waits, fn, s, inc in self.q[ename]:
                        for ws, wv in waits:
                            E.wait_ge(ws, wv)
                        if fn is not None:
                            fn(E).then_inc(s, inc)
                return body
            block.tensor(mk("pe"))
            block.scalar(mk("act"))
            block.vector(mk("dve"))
            block.gpsimd(mk("pool"))
            block.sync(mk("sp"))


class G:
    def __init__(self, kb):
        self.kb = kb

    def mm(self, out, lhsT, rhs, start, stop, r, w):
        self.kb.op("pe", lambda E: E.matmul(out, lhsT=lhsT, rhs=rhs, start=start, stop=stop), r, w)

    def tr(self, out, in_, ident, r, w):
        self.kb.op("pe", lambda E: E.transpose(out=out, in_=in_, identity=ident), r, w)

    def tt(self, e, out, in0, in1, op, r, w):
        self.kb.op(e, lambda E: E.tensor_tensor(out=out, in0=in0, in1=in1, op=op), r, w)

    def ts(self, e, out, in0, s1, s2, op0, op1, r, w):
        if op1 is None:
            self.kb.op(e, lambda E: E.tensor_scalar(out=out, in0=in0, scalar1=s1, scalar2=None, op0=op0), r, w)
        else:
            self.kb.op(e, lambda E: E.tensor_scalar(out=out, in0=in0, scalar1=s1, scalar2=s2, op0=op0, op1=op1), r, w)

    def stt(self, out, in0, scalar, in1, op0, op1, r, w):
        self.kb.op("dve", lambda E: E.scalar_tensor_tensor(out=out, in0=in0, scalar=scalar, in1=in1, op0=op0, op1=op1), r, w)

    def act(self, out, in_, func, r, w, bias=None, scale=None, accum_out=None):
        kw = {}
        if bias is not None:
            kw["bias"] = bias
        if scale is not None:
            kw["scale"] = scale
        if accum_out is not None:
            kw["accum_out"] = accum_out
        self.kb.op("act", lambda E: E.activation(out=out, in_=in_, func=func, **kw), r, w)

    def cp(self, e, out, in_, r, w):
        if e == "act":
            self.kb.op(e, lambda E: E.activation(out=out, in_=in_, func=AF.Copy), r, w)
        else:
            self.kb.op(e, lambda E: E.tensor_copy(out=out, in_=in_), r, w)

    def red(self, out, in_, r, w):
        self.kb.op("dve", lambda E: E.tensor_reduce(out=out, in_=in_, axis=AX.X, op=ALU.add), r, w)

    def memset(self, e, ap, val, w):
        self.kb.op(e, lambda E: E.memset(ap, val), (), w)


def build_nc(stop_after=99):
    nc = bass.Bass("TRN2", target_bir_lowering=False)
    dt = nc.dram_tensor

    def inp(name, shape):
        return dt(name, list(shape), F32, kind="ExternalInput").ap()

    x_d = inp("x", [T, D])
    rwkv_norm_g = inp("rwkv_norm_g", [D])
    rwkv_mu = inp("rwkv_mu", [6, D])
    w_rkv = inp("rwkv_w_rkv", [3, D, D])
    w0_d = inp("rwkv_w0", [D])
    w1_d = inp("rwkv_w1", [D, 64])
    w2_d = inp("rwkv_w2", [64, D])
    a0_d = inp("rwkv_a0", [D])
    a1_d = inp("rwkv_a1", [D, 64])
    a2_d = inp("rwkv_a2", [64, D])
    g1_d = inp("rwkv_g1", [D, 160])
    g2_d = inp("rwkv_g2", [160, D])
    kk_d = inp("rwkv_k_k", [D])
    ka_d = inp("rwkv_k_a", [D])
    rk_d = inp("rwkv_r_k", [D])
    lnw_d = inp("rwkv_lnx_w", [D])
    lnb_d = inp("rwkv_lnx_b", [D])
    wo_d = inp("rwkv_w_o", [D, D])
    kv_norm_g = inp("kv_norm_g", [D])
    kv_w = inp("kv_w", [D, 2 * D + H])
    kv_f_bias = inp("kv_f_bias", [H])
    k_norm_g = inp("k_norm_g", [D])
    attn_norm_g = inp("attn_norm_g", [D])
    attn_w_q = inp("attn_w_q", [D, D])
    q_norm_g = inp("q_norm_g", [D])
    attn_w_o = inp("attn_w_o", [D, D])
    mlp_norm_g = inp("mlp_norm_g", [2, D])
    mlp_w_in = inp("mlp_w_in", [2, D, FF])
    mlp_w_out = inp("mlp_w_out", [2, FF, D])
    out_d = dt("out", [T, D], F32, kind="ExternalOutput").ap()
    x1_d = dt("x1_scratch", [T, D], F32).ap()
    kT_d = dt("kT_scratch", [H, 70, T], BF16).ap()
    qT_d = dt("qT_scratch", [H, 70, T], BF16).ap()

    kb = KB(nc)
    g = G(kb)
    NC = ALU

    PS = [nc.alloc_psum_tensor(f"ps{i}", [128, 512], F32) for i in range(8)]
    PSU = [Unit(f"ps{i}", excl=True) for i in range(8)]
    pT = PS[2][:].bitcast(BF16)
    pTU = [PSU[2], PSU[2]]

    ident_f = kb.sb("ident_f", [128, 128], F32)
    ident_b = kb.sb("ident_b", [128, 128], BF16)
    tri_f = kb.sb("tri_f", [128, 128], F32)
    ones_f = kb.sb("ones_f", [128, 128], F32)
    maskE = kb.sb("maskE", [128, 256], BF16)
    mask_ls = kb.sb("mask_ls", [128, 128], BF16)
    tri_b = kb.sb("tri_b", [128, 128], BF16)
    ones_b = kb.sb("ones_b", [128, 128], BF16)
    maskneg_b = kb.sb("maskneg_b", [128, 128], BF16)
    maskE2 = kb.sb("maskE2", [128, 512], BF16)
    mask2 = kb.sb("mask2", [128, 256], BF16)
    maskneg = kb.sb("maskneg", [128, 128], F32)
    ctmp = kb.sb("ctmp", [128, 128], F32)
    cU = Unit("consts")
    ctU = Unit("ctmp")

    def sel(out, in_, cm, step, base, r, w):
        kb.op("pool", lambda E: E.affine_select(out=out, in_=in_, pattern=[[step, 128]], compare_op=ALU.is_ge,
                                                fill=0.0, base=base, channel_multiplier=cm), r, w)
    g.memset("pool", ones_f[:], 1.0, [cU])
    sel(ctmp[:], ones_f[:], 1, -1, 0, [cU], [ctU])
    sel(ident_f[:], ctmp[:], -1, 1, 0, [ctU], [cU])
    g.cp("pool", ident_b[:], ident_f[:], [cU], [cU])
    sel(tri_f[:], ones_f[:], -1, 1, 0, [cU], [cU])
    g.cp("pool", tri_b[:], tri_f[:], [cU], [cU])
    g.cp("pool", ones_b[:], ones_f[:], [cU], [cU])
    g.cp("pool", maskE[:, 128:256], tri_f[:], [cU], [cU])
    sel(ctmp[:], ones_f[:], -1, 1, -1, [cU], [ctU])
    g.cp("pool", maskE[:, 0:128], ctmp[:], [ctU], [cU])
    sel(ctmp[:], ones_f[:], 1, -1, -1, [cU, ctU], [ctU])
    g.cp("pool", mask_ls[:], ctmp[:], [ctU], [cU])
    g.ts("pool", maskneg[:], tri_f[:], -1.0, 1e4, ALU.add, ALU.mult, [cU], [cU])
    g.cp("pool", maskneg_b[:], maskneg[:], [cU], [cU])
    g.cp("pool", maskE2[:, 0:256], maskE[:], [cU], [cU])
    g.cp("pool", maskE2[:, 256:512], maskE[:], [cU], [cU])
    g.cp("pool", mask2[:, 0:128], mask_ls[:], [cU], [cU])
    g.cp("pool", mask2[:, 128:256], ones_f[:], [cU], [cU])

    persist_off = kb.sb_off

    def load_featcols(vecs, name):
        rows = 8 * len(vecs)
        stage = kb.sb(name + "_st", [64, 128], F32)
        dstt = kb.sb(name, [128, rows], F32)
        su, du = Unit(name + "_st"), Unit(name)
        for vi, vec in enumerate(vecs):
            kb.dma("sp", stage[vi * 8:(vi + 1) * 8, :], vec.rearrange("(c p) -> c p", p=128), writes=[su])
        g.tr(PS[2][:, 0:rows], stage[0:rows, :], ident_f[0:rows, 0:rows], [su, cU], [PSU[2]])
        g.cp("dve", dstt[:], PS[2][:, 0:rows], [PSU[2]], [du])
        return dstt, du

    def rstd_of(xin_ap, junk_ap, ss, rstd, rU, wU_junk, wU_s):
        wj = list(wU_junk) if isinstance(wU_junk, (list, tuple)) else [wU_junk]
        g.act(junk_ap, xin_ap, AF.Square, rU, wj + [wU_s], accum_out=ss)
        g.ts("dve", rstd, ss, 1.0 / D, NORM_EPS, ALU.mult, ALU.add, [wU_s], [wU_s])
        g.act(rstd, rstd, AF.Ln, [wU_s], [wU_s])
        g.act(rstd, rstd, AF.Exp, [wU_s], [wU_s], scale=-0.5)

    def phase_rwkv():
        kb.sb_off = persist_off
        W3 = [kb.sb(f"W{p}", [128, 8, D], BF16) for p in range(3)]
        Wo = kb.sb("Wo", [128, 8, D], BF16)
        W1 = kb.sb("W1", [128, 8, 64], BF16)
        A1 = kb.sb("A1", [128, 8, 64], BF16)
        G1 = kb.sb("G1", [128, 8, 160], BF16)
        W2 = kb.sb("W2", [64, D], BF16)
        A2 = kb.sb("A2", [64, D], BF16)
        G2a = kb.sb("G2a", [128, D], BF16)
        G2b = kb.sb("G2b", [32, D], BF16)
        gmu, gmuU = load_featcols([rwkv_mu[m] for m in range(6)] + [rwkv_norm_g], "gmu")
        muT = gmu[:, 0:48].rearrange("p (m c) -> p m c", m=6)
        gT = gmu[:, 48:56]
        w0r = kb.sb("w0r", [128, D], F32)
        a0r = kb.sb("a0r", [128, D], BF16)
        kkr_ = kb.sb("kkrow", [128, D], BF16)
        kar_ = kb.sb("karow", [128, D], BF16)
        rkr_ = kb.sb("rkrow", [128, D], BF16)
        lnwr = kb.sb("lnwrow", [128, D], BF16)
        lnbr = kb.sb("lnbrow", [128, D], BF16)
        wU = Unit("rw_weights")
        rowf = kb.sb("tA", [128, D], F32)
        tA = rowf
        rowU = Unit("tA")

        for p in range(3):
            for c in range(8):
                kb.dma("pool", W3[p][:, c, :], w_rkv[p, c * 128:(c + 1) * 128, :], writes=[wU], multi=True)
        for c in range(8):
            kb.dma("pool", Wo[:, c, :], wo_d[c * 128:(c + 1) * 128, :], writes=[wU], multi=True)
        kb.dma("pool", W1[:], w1_d.rearrange("(c p) e -> p c e", p=128), writes=[wU], multi=True)
        kb.dma("pool", A1[:], a1_d.rearrange("(c p) e -> p c e", p=128), writes=[wU], multi=True)
        kb.dma("pool", G1[:], g1_d.rearrange("(c p) e -> p c e", p=128), writes=[wU], multi=True)
        kb.dma("pool", W2[:], w2_d, writes=[wU], multi=True)
        kb.dma("pool", A2[:], a2_d, writes=[wU], multi=True)
        kb.dma("pool", G2a[:], g2_d[0:128, :], writes=[wU], multi=True)
        kb.dma("pool", G2b[:], g2_d[128:160, :], writes=[wU], multi=True)
        kb.dma("sp", w0r[:], w0_d.partition_broadcast(128), writes=[wU])
        for src, dst in ((a0_d, a0r), (kk_d, kkr_), (ka_d, kar_), (rk_d, rkr_), (lnw_d, lnwr), (lnb_d, lnbr)):
            kb.dma("sp", rowf[:], src.partition_broadcast(128), writes=[rowU])
            g.cp("dve", dst[:], rowf[:], [rowU], [wU])
        for c in range(8):
            for Wt in (W3[0], W3[1], W3[2], W1, A1, G1):
                g.act(Wt[:, c, :], Wt[:, c, :], AF.Copy, [wU, gmuU], [wU], scale=gT[:, c:c + 1])

        ck(1)
        xin = kb.sb("xin", [128, D], F32)
        r_sb = kb.sb("r_sb", [128, D], F32)
        k_sb = kb.sb("k_sb", [128, D], F32)
        a_sb = kb.sb("a_sb", [128, D], F32)
        sg = kb.sb("sg", [128, D], F32)
        sgh = kb.sb("sgh", [128, D], BF16)
        sgl = kb.sb("sgl", [128, D], BF16)
        kkt = kb.sb("kkt", [128, D], F32)
        kp = kb.sb("kp", [128, D], F32)
        b_sb = kb.sb("b_sb", [128, D], F32)
        Ea_off = kb.sb_off
        Ea = kb.sb("Ea", [128, D], F32)
        Eb_off = kb.sb_off
        Eb = kb.sb("Eb", [128, D], F32)
        y_sb = kb.sb("y_sb", [128, D], F32)
        xn = kb.sb("xn", [128, D], BF16)
        v_bf = kb.sb("v_bf", [128, D], BF16)
        g_bf = kb.sb("g_bf", [128, D], BF16)
        At = kb.sb("At", [128, D], BF16)
        Bt = kb.sb("Bt", [128, D], BF16)
        Kt = kb.sb("Kt", [128, D], BF16)
        Rt = kb.sb("Rt", [128, D], BF16)
        Bh = kb.sb("Bh", [128, D], BF16)
        Kh = kb.sb("Kh", [128, D], BF16)
        yg = xn
        hTx = [kb.sb(f"hTx{i}", [128, 8, 129], BF16) for i in range(2)]
        xxT = kb.sb("xxT", [128, 8, 128], BF16)
        xp = [kb.sb(f"xp{i}", [128, 8, 128], BF16) for i in range(6)]
        ygT = xxT
        twT = kb.sb("twT", [64, 128], BF16)
        t1T = kb.sb("t1T", [64, 128], BF16)
        sgT1 = kb.sb("sgT1", [128, 128], BF16)
        sgT2 = kb.sb("sgT2", [32, 128], BF16)
        small = kb.sb("small", [128, 16 * 12], F32)
        ss = small[:, 0:1]
        rstd = small[:, 1:2]
        ss16 = small[:, 16:32]
        rn16 = small[:, 32:48]
        bs16 = small[:, 48:64]
        s1 = small[:, 64:80]
        s2 = small[:, 80:96]
        mean = small[:, 96:112]
        var = small[:, 112:128]
        rs = small[:, 128:144]
        m2 = small[:, 144:160]
        PCf = kb.sb("PCf", [64, 16], F32)
        S_f = kb.sb("S_f", [64, D], F32)
        S_bf = [kb.sb(f"S_bf{i}", [64, D], BF16) for i in range(2)]
        G4 = 4
        FM = [kb.sb(f"FM{i}", [64, 512], BF16) for i in range(G4)]
        E12 = [kb.sb(f"E12_{i}", [128, 512], BF16, off=Eb_off + i * 1024) for i in range(G4)]
        BN0 = [kb.sb(f"BN0_{i}", [128, 256], BF16) for i in range(G4)]
        AB = [[kb.sb(f"AB{i}_{k}", [128, 256], BF16, off=Ea_off + (i * 2 + k) * 512) for k in range(2)] for i in range(G4)]
        Nj = [[kb.sb(f"Nj{i}_{k}", [128, 128], BF16) for k in range(2)] for i in range(G4)]
        GT = [kb.sb(f"GT{i}", [64, 64], BF16) for i in range(G4)]
        QhT = [kb.sb(f"QhT{i}", [64, 128], BF16) for i in range(G4)]

        U = {n: Unit(n) for n in ["xin", "r", "k", "a", "sg", "tA", "tB", "cum", "kk", "kp", "b", "Ea", "Eb", "y",
                                  "xn", "v", "g", "At", "Bt", "Kt", "Rt", "Bh", "Kh", "yg", "xxT", "ygT", "twT",
                                  "t1T", "sgT1", "sgT2", "ss", "ss16", "bs16", "gn", "PCf", "x1d", "sgh", "sgl"]}
        U["tA"] = rowU
        U["yg"] = U["xn"]
        U["ygT"] = U["xxT"]
        hU = units(2, "hTx")
        xpU = units(6, "xp")
        SfU = units(16, "Sf")
        SbU = [units(16, "Sb0_"), units(16, "Sb1_")]
        FMU = units(G4, "FM")
        E12U = units(G4, "E12")
        BN0U = units(G4, "BN0")
        ABU = [units(2, f"AB{i}_") for i in range(G4)]
        NjU = [units(2, f"Nj{i}_") for i in range(G4)]
        GTU = units(G4, "GT")
        QhU = units(G4, "QhT")
        LU = PSU[7]
        L2U = PSU[7]
        P7 = PS[7]
        pbig = [0]
        BIGB = [0, 1, 3, 4, 5, 6]

        def bigps():
            i = BIGB[pbig[0] % 6]
            pbig[0] += 1
            return PS[i], PSU[i]

        HS = 10
        cols = (slice(0, HS * 64), slice(HS * 64, D))
        hds = (slice(0, HS), slice(HS, H))
        nh = (HS, H - HS)
        ENG = ("dve", "pool")
        UH = {n: (Unit(n + "_D"), Unit(n + "_P")) for n in ["kk", "tA", "kp", "b", "Rt", "Bt", "Kt", "At", "Bh", "Kh", "y", "yg"]}
        xnU = list(UH["yg"])

        def uh(n, h):
            return UH[n][0 if h < HS else 1]

        def R(spec, i):
            return [UH[x][i] if isinstance(x, str) else x for x in spec]

        def v3h(t_, i):
            return t_[:, cols[i]].rearrange("p (h n) -> p h n", n=N)

        def bch(s_, i):
            return s_[:, hds[i]].unsqueeze(2).broadcast_to([128, nh[i], N])

        def tt2(out, in0, in1, op, r, w):
            for i in (0, 1):
                g.tt(ENG[i], out[:, cols[i]], in0[:, cols[i]], in1[:, cols[i]], op, R(r, i), R(w, i))

        def tt2b(out, in0, s16, op, r, w):
            for i in (0, 1):
                g.tt(ENG[i], v3h(out, i), v3h(in0, i), bch(s16, i), op, R(r, i), R(w, i))

        def stt2(out, in0, scalar, in1, op0, op1, r, w):
            g.stt(out[:, cols[0]], in0[:, cols[0]], scalar, in1[:, cols[0]], op0, op1, R(r, 0), R(w, 0))
            neutral = 1.0 if op1 == ALU.mult else 0.0
            kb.op("pool", lambda E: E.tensor_scalar(out=out[:, cols[1]], in0=in0[:, cols[1]], scalar1=scalar, scalar2=neutral,
                                                    op0=op0, op1=(ALU.mult if op1 == ALU.mult else ALU.add)), R(r, 1), R(w, 1))
            g.tt("pool", out[:, cols[1]], out[:, cols[1]], in1[:, cols[1]], op1, R(r, 1) + R(w, 1), R(w, 1))

        g.memset("dve", S_f[:], 0.0, SfU)
        g.memset("pool", S_bf[0][:], 0.0, SbU[0])
        g.memset("pool", hTx[0][:, :, 0:1], 0.0, [hU[0]])

        for b in range(NB):
            cur = b % 2
            nxt = 1 - cur
            hc, hUc = hTx[cur], hU[cur]
            kb.dma("sp", xin[:], x_d[b * 128:(b + 1) * 128, :], reads=[], writes=[U["xin"]])
            rstd_of(xin[:], xn[:], ss, rstd, [U["xin"]], xnU, U["ss"])
            g.ts("dve", xn[:], xin[:], rstd, None, ALU.mult, None, [U["xin"], U["ss"]], xnU)
            for c in range(8):
                g.tr(pT[:, c * 128:(c + 1) * 128], xn[:, c * 128:(c + 1) * 128], ident_b[:], xnU + [cU], [pTU[c // 4]])
            g.cp("act", hc[:, :, 1:129], pT.rearrange("p (c t) -> p c t", c=8), pTU, [hUc])
            g.cp("pool", hTx[nxt][:, :, 0:1], hc[:, :, 128:129], [hUc], [hU[nxt]])
            g.tt("dve", xxT[:], hc[:, :, 0:128], hc[:, :, 1:129], ALU.subtract, [hUc], [U["xxT"]])
            xnv = xn[:].rearrange("p (c t) -> p c t", c=8)
            for p in range(6):
                mu_bc = muT[:, p, :].unsqueeze(2).broadcast_to([128, 8, 128])
                g.tt("dve", xnv, xxT[:], mu_bc, ALU.mult, [U["xxT"], gmuU], xnU)
                g.tt("dve", xp[p][:], xnv, hc[:, :, 1:129], ALU.add, xnU + [hUc], [xpU[p]])
            ck(2)

            def proj(xpi, Wt, evac):
                for half in range(2):
                    ps, psu = bigps()
                    for c in range(8):
                        g.mm(ps[:], xp[xpi][:, c, :], Wt[:, c, half * 512:(half + 1) * 512], c == 0, c == 7,
                             [xpU[xpi], wU], [psu])
                    evac(ps, psu, half)
            hs = lambda half: slice(half * 512, (half + 1) * 512)
            proj(0, W3[0], lambda ps, psu, half: g.cp("act", r_sb[:, hs(half)], ps[:], [psu], [U["r"]]))
            proj(2, W3[1], lambda ps, psu, half: g.cp("act", k_sb[:, hs(half)], ps[:], [psu], [U["k"]]))
            proj(3, W3[2], lambda ps, psu, half: g.cp("act", v_bf[:, hs(half)], ps[:], [psu], [U["v"]]))
            ck(3)
            for c in range(8):
                g.mm(P7[0:64, 256:384], W1[:, c, :], xp[1][:, c, :], c == 0, c == 7, [xpU[1], wU], [LU])
            g.act(twT[:], P7[0:64, 256:384], AF.Tanh, [LU], [U["twT"]])
            for half in range(2):
                ps, psu = bigps()
                g.mm(ps[:], twT[:], W2[:, hs(half)], True, True, [U["twT"], wU], [psu])
                g.tt("dve", tA[:, hs(half)], ps[:], w0r[:, hs(half)], ALU.add, [psu, wU], [rowU] + list(UH["tA"]))
            g.act(sg[:], tA[:], AF.Sigmoid, [rowU] + list(UH["tA"]), [U["sg"]])
            for c in range(8):
                g.mm(P7[0:64, 256:384], A1[:, c, :], xp[4][:, c, :], c == 0, c == 7, [xpU[4], wU], [LU])
            g.cp("act", t1T[:], P7[0:64, 256:384], [LU], [U["t1T"]])
            for half in range(2):
                ps, psu = bigps()
                g.mm(ps[:], t1T[:], A2[:, hs(half)], True, True, [U["t1T"], wU], [psu])
                g.tt("dve", a_sb[:, hs(half)], ps[:], a0r[:, hs(half)], ALU.add, [psu, wU], [U["a"]])
            g.act(a_sb[:], a_sb[:], AF.Sigmoid, [U["a"]], [U["a"]])
            for c in range(8):
                g.mm(P7[:, 256:384], G1[:, c, 0:128], xp[5][:, c, :], c == 0, c == 7, [xpU[5], wU], [LU])
            for c in range(8):
                g.mm(P7[0:32, 384:512], G1[:, c, 128:160], xp[5][:, c, :], c == 0, c == 7, [xpU[5], wU], [L2U])
            g.act(sgT1[:], P7[:, 256:384], AF.Sigmoid, [LU], [U["sgT1"]])
            g.act(sgT2[:], P7[0:32, 384:512], AF.Sigmoid, [L2U], [U["sgT2"]])
            for half in range(2):
                ps, psu = bigps()
                g.mm(ps[:], sgT1[:], G2a[:, hs(half)], True, False, [U["sgT1"], wU], [psu])
                g.mm(ps[:], sgT2[:], G2b[:, hs(half)], False, True, [U["sgT2"], wU], [psu])
                g.cp("act", g_bf[:, hs(half)], ps[:], [psu], [U["g"]])
            ck(4)
            g.cp("act", sgh[:], sg[:], [U["sg"]], [U["sgh"]])
            g.tt("pool", tA[:], sg[:], sgh[:], ALU.subtract, [U["sg"], U["sgh"]], [rowU] + list(UH["tA"]))
            g.cp("act", sgl[:], tA[:], [rowU] + list(UH["tA"]), [U["sgl"]])

            def trimm(maskT, half):
                ps, psu = bigps()
                g.mm(ps[:], maskT, sgh[:, hs(half)], True, False, [U["sgh"], cU], [psu])
                g.mm(ps[:], maskT, sgl[:, hs(half)], False, True, [U["sgl"], cU], [psu])
                return ps, psu
            kb.gate("act", [uu for q_ in range(G4) for uu in (E12U[q_], ABU[q_][0], ABU[q_][1])])
            for half in range(2):
                ps, psu = trimm(tri_b[:], half)
                g.act(Ea[:, hs(half)], ps[:], AF.Exp, [psu], [U["Ea"]], scale=-C_DEC)
                g.act(Eb[:, hs(half)], ps[:], AF.Exp, [psu], [U["Eb"]], scale=C_DEC)
            ck(41)
            ps, psu = bigps()
            for h in range(H):
                g.mm(ps[0:64, h:h + 1], sgh[:, h * 64:(h + 1) * 64], tri_b[:, 127:128], True, False, [U["sgh"], cU], [psu])
                g.mm(ps[0:64, h:h + 1], sgl[:, h * 64:(h + 1) * 64], tri_b[:, 127:128], False, True, [U["sgl"], cU], [psu])
            g.act(PCf[:], ps[0:64, 0:16], AF.Exp, [psu], [U["PCf"]], scale=-C_DEC)
            ck(5)
            v3 = lambda t_: t_[:].rearrange("p (h n) -> p h n", h=H)
            bc = lambda s_: s_.unsqueeze(2).broadcast_to([128, H, N])
            tt2(kkt, k_sb, kkr_, ALU.mult, [U["k"], wU], ["kk"])
            tt2(tA, kkt, kkt, ALU.mult, ["kk"], ["tA"])
            g.red(ss16, v3(tA), list(UH["tA"]), [U["ss16"]])
            g.ts("dve", rn16, ss16, 1e-24, None, ALU.max, None, [U["ss16"]], [U["ss16"]])
            g.act(rn16, rn16, AF.Ln, [U["ss16"]], [U["ss16"]])
            g.act(rn16, rn16, AF.Exp, [U["ss16"]], [U["ss16"]], scale=-0.5)
            tt2b(kkt, kkt, rn16, ALU.mult, ["kk", U["ss16"]], ["kk"])
            tt2(Rt, r_sb, Ea, ALU.mult, [U["r"], U["Ea"]], ["Rt"])
            stt2(tA, a_sb, -1.0, kar_, ALU.add, ALU.mult, [U["a"], wU], ["tA"])
            stt2(kp, tA, 1.0, k_sb, ALU.add, ALU.mult, ["tA", U["k"]], ["kp"])
            tt2(b_sb, kkt, a_sb, ALU.mult, ["kk", U["a"]], ["b"])
            tt2(Bt, b_sb, Eb, ALU.mult, ["b", U["Eb"]], ["Bt"])
            tt2(Kt, kp, Eb, ALU.mult, ["kp", U["Eb"]], ["Kt"])
            for half in range(2):
                ps, psu = trimm(mask_ls[:], half)
                g.act(Eb[:, hs(half)], ps[:], AF.Exp, [psu], [U["Eb"]], scale=-C_DEC)
            for half in range(2):
                ps, psu = trimm(maskE[:, 0:128], half)
                g.act(Ea[:, hs(half)], ps[:], AF.Exp, [psu], [U["Ea"]], scale=-C_DEC)
            stt2(At, kkt, -1.0, Ea, ALU.mult, ALU.mult, ["kk", U["Ea"]], ["At"])
            tt2(Bh, b_sb, Eb, ALU.mult, ["b", U["Eb"]], ["Bh"])
            tt2(Kh, kp, Eb, ALU.mult, ["kp", U["Eb"]], ["Kh"])
            tt2(tA, r_sb, kp, ALU.mult, [U["r"], "kp"], ["tA"])
            tt2(tA, tA, rkr_, ALU.mult, ["tA", wU], ["tA"])
            g.red(bs16, v3(tA), list(UH["tA"]), [U["bs16"]])
            tt2b(kkt, v_bf, bs16, ALU.mult, [U["v"], U["bs16"], "kk"], ["kk"])
            tt2(kkt, kkt, lnbr, ALU.add, ["kk", wU], ["kk"])

            ck(6)
            kb.gate("dve", [U["Ea"], U["Eb"]])
            kb.gate("act", [U["Ea"], U["Eb"]])
            for h0 in range(0, H, G4):
                hs4 = list(range(h0, h0 + G4))

                def ctx(h):
                    q = h - h0
                    return q, PS[3 + q], PSU[3 + q], slice(h * 64, (h + 1) * 64)
                for h in hs4:
                    q, bk, bu, hsl = ctx(h)
                    bkb = bk[:].bitcast(BF16)
                    for qi, (src, su) in enumerate(((At, uh("At", h)), (Rt, uh("Rt", h)), (Bt, uh("Bt", h)), (Kt, uh("Kt", h)))):
                        g.tr(bkb[0:64, qi * 128:(qi + 1) * 128], src[:, hsl], ident_b[:], [su, cU], [bu])
                    g.cp("act", FM[q][:], bkb[0:64, 0:512], [bu], [FMU[q]])
                for h in hs4:
                    q, bk, bu, hsl = ctx(h)
                    fm = FM[q]
                    g.mm(bk[:, 0:256], fm[:, 256:384], fm[:, 0:256], True, True, [FMU[q]], [bu])
                    g.mm(bk[:, 256:512], fm[:, 384:512], fm[:, 0:256], True, True, [FMU[q]], [bu])
                    g.tt("dve", E12[q][:], bk[:], maskE2[:], ALU.mult, [bu, cU], [E12U[q]])
                for h in hs4:
                    q, bk, bu, hsl = ctx(h)
                    fm = FM[q]
                    g.mm(bk[:, 0:128], fm[:, 0:128], fm[:, 256:384], True, True, [FMU[q]], [bu])
                    g.mm(bk[:, 128:192], fm[:, 0:128], ident_b[0:64, 0:64], True, True, [FMU[q], cU], [bu])
                    g.mm(bk[:, 192:256], E12[q][:, 256:384], v_bf[:, hsl], True, True, [E12U[q], U["v"]], [bu])
                    g.tt("dve", BN0[q][:], bk[:, 0:256], mask2[:], ALU.mult, [bu, cU], [BN0U[q]])
                cur_ab = {}
                for h in hs4:
                    q = h - h0
                    cur_ab[q] = (E12[q][:, 0:128], BN0[q][:, 0:128], [E12U[q], BN0U[q]], BN0[q][:, 128:256], [BN0U[q]])
                for j in range(7):
                    for h in hs4:
                        q, bk, bu, hsl = ctx(h)
                        A_cur, B_cur, abu, N_cur, nu = cur_ab[q]
                        no = (j + 1) % 2
                        g.mm(bk[:, 0:128], A_cur, N_cur, True, True, abu + nu, [bu])
                        if j < 6:
                            g.mm(bk[:, 128:256], B_cur, A_cur, True, True, abu, [bu])
                            g.mm(bk[:, 256:384], A_cur, B_cur, True, True, abu, [bu])
                        g.tt("dve", Nj[q][no][:], bk[:, 0:128], N_cur, ALU.add, [bu] + nu, [NjU[q][no]])
                        if j < 6:
                            g.cp("act", AB[q][no][:], bk[:, 128:384], [bu], [ABU[q][no]])
                            cur_ab[q] = (AB[q][no][:, 0:128], AB[q][no][:, 128:256], [ABU[q][no]], Nj[q][no][:], [NjU[q][no]])
                        else:
                            cur_ab[q] = (None, None, None, Nj[q][no][:], [NjU[q][no]])
                for h in hs4:
                    q, bk, bu, hsl = ctx(h)
                    XZ, XZu = cur_ab[q][3], cur_ab[q][4]
                    X_ = XZ[:, 0:64]
                    g.mm(bk[0:64, 0:64], X_, Bh[:, hsl], True, True, XZu + [uh("Bh", h)], [bu])
                    g.mm(bk[0:64, 128:256], X_, E12[q][:, 128:256], True, True, XZu + [E12U[q]], [bu])
                    g.cp("act", GT[q][:], bk[0:64, 0:64], [bu], [GTU[q]])
                    g.tt("dve", QhT[q][:], bk[0:64, 128:256], FM[q][:, 128:256], ALU.add, [bu, FMU[q]], [QhU[q]])
                for h in hs4:
                    q, bk, bu, hsl = ctx(h)
                    XZ, XZu = cur_ab[q][3], cur_ab[q][4]
                    Z_ = XZ[:, 64:128]
                    MrbT, MrkT = E12[q][:, 128:256], E12[q][:, 384:512]
                    Sb_cur, Sb_cu = S_bf[cur][:, hsl], SbU[cur][h]
                    Yr = bk[:, 256:320]
                    g.mm(Yr, MrbT, Z_, True, False, [E12U[q]] + XZu, [bu])
                    g.mm(Yr, MrkT, v_bf[:, hsl], False, False, [E12U[q], U["v"]], [bu])
                    g.mm(Yr, QhT[q][:], Sb_cur, False, True, [QhU[q], Sb_cu], [bu])
                    Hr = bk[0:64, 320:384]
                    g.mm(Hr, Bh[:, hsl], Z_, True, False, [uh("Bh", h)] + XZu, [bu])
                    g.mm(Hr, Kh[:, hsl], v_bf[:, hsl], False, False, [uh("Kh", h), U["v"]], [bu])
                    g.mm(Hr, GT[q][:], Sb_cur, False, True, [GTU[q], Sb_cu], [bu])
                    g.cp("act", y_sb[:, hsl], Yr, [bu], [uh("y", h)])
                    g.stt(S_f[:, hsl], S_f[:, hsl], PCf[:, h:h + 1], Hr, ALU.mult, ALU.add,
                          [SfU[h], U["PCf"], bu], [SfU[h]])
                    g.cp("pool", S_bf[nxt][:, hsl], S_f[:, hsl], [SfU[h]], [SbU[nxt][h]])
                ck(7)

            ck(8)
            yU = list(UH["y"])
            g.red(s1, v3(y_sb), yU, [U["gn"]])
            tt2(tA, y_sb, y_sb, ALU.mult, ["y"], ["tA"])
            g.red(s2, v3(tA), list(UH["tA"]), [U["gn"]])
            g.ts("dve", mean, s1, 1.0 / N, None, ALU.mult, None, [U["gn"]], [U["gn"]])
            g.tt("dve", m2, mean, mean, ALU.mult, [U["gn"]], [U["gn"]])
            g.stt(var, s2, 1.0 / N, m2, ALU.mult, ALU.subtract, [U["gn"]], [U["gn"]])
            g.ts("dve", var, var, GN_EPS, None, ALU.add, None, [U["gn"]], [U["gn"]])
            g.act(rs, var, AF.Ln, [U["gn"]], [U["gn"]])
            g.act(rs, rs, AF.Exp, [U["gn"]], [U["gn"]], scale=-0.5)
            g.stt(m2, mean, -1.0, rs, ALU.mult, ALU.mult, [U["gn"]], [U["gn"]])
            tt2b(tA, y_sb, rs, ALU.mult, ["y", U["gn"]], ["tA"])
            tt2b(tA, tA, m2, ALU.add, ["tA", U["gn"]], ["tA"])
            tt2(tA, tA, lnwr, ALU.mult, ["tA", wU], ["tA"])
            tt2(tA, tA, kkt, ALU.add, ["tA", "kk"], ["tA"])
            tt2(yg, tA, g_bf, ALU.mult, ["tA", U["g"]], ["yg"])
            for c in range(8):
                g.tr(pT[:, c * 128:(c + 1) * 128], yg[:, c * 128:(c + 1) * 128], ident_b[:], xnU + [cU], [pTU[c // 4]])
            g.cp("act", ygT[:], pT.rearrange("p (c t) -> p c t", c=8), pTU, [U["ygT"]])
            for half in range(2):
                ps, psu = bigps()
                for c in range(8):
                    g.mm(ps[:], ygT[:, c, :], Wo[:, c, hs(half)], c == 0, c == 7, [U["ygT"], wU], [psu])
                g.tt("dve", xin[:, hs(half)], ps[:], xin[:, hs(half)], ALU.add, [psu, U["xin"]], [U["xin"]])
            kb.dma("sp", x1_d[b * 128:(b + 1) * 128, :], xin[:], reads=[U["xin"]], writes=[U["x1d"]])
        return U["x1d"]

    try:
        x1U = phase_rwkv()
    except _Stop:
        x1U = Unit("x1dummy")
    kb.fence()

    kb.sb_off = persist_off
    x_sb = kb.sb("x_sb", [128, NB, D], F32)
    xU = units(NB, "x")
    resid_off = kb.sb_off
    for b in range(NB):
        kb.dma("sp", x_sb[:, b, :], x1_d[b * 128:(b + 1) * 128, :], reads=[x1U], writes=[xU[b]])

    outU = Unit("out")

    def dump_x():
        for b in range(NB):
            kb.dma("sp", out_d[b * 128:(b + 1) * 128, :], x_sb[:, b, :], reads=[xU[b]], writes=[outU])
        kb.final_wait("sp", [outU])
        kb.fence()
        kb.build()
        return nc

    if stop_after <= 1:
        return dump_x()

    SB_TOP = 229344
    KVW_OFF = SB_TOP - 8 * (2 * D + H) * 2
    WQ_OFF = KVW_OFF - 8 * D * 2
    MLPW_OFF = SB_TOP - 4 * 8192

    def mlp_weights(layer, off=None):
        if off is None:
            WI = [kb.sb(f"WI{layer}_{i}", [128, 8, 512], BF16) for i in range(2)]
            WO = [kb.sb(f"WO{layer}_{i}", [128, 4, D], BF16) for i in range(2)]
        else:
            WI = [kb.sb(f"WI{layer}_{i}", [128, 8, 512], BF16, off=off + i * 8192) for i in range(2)]
            WO = [kb.sb(f"WO{layer}_{i}", [128, 4, D], BF16, off=off + 16384 + i * 8192) for i in range(2)]
        WIU, WOU = units(2, "WI"), units(2, "WO")
        done = set()

        def load_w(e):
            if e in done or e >= 8:
                return
            done.add(e)
            i = e % 2
            for c in range(8):
                kb.dma("pool", WI[i][:, c, :], mlp_w_in[layer, c * 128:(c + 1) * 128, e * 512:(e + 1) * 512], writes=[WIU[i]], multi=True)
            for fc in range(4):
                kb.dma("pool", WO[i][:, fc, :], mlp_w_out[layer, e * 512 + fc * 128: e * 512 + (fc + 1) * 128, :], writes=[WOU[i]], multi=True)
        return WI, WO, WIU, WOU, load_w

    def phase_mlp(layer, final, wts=None, mid_hook=None):
        kb.sb_off = resid_off
        hnT = kb.sb("hnT", [128, 8, T], BF16)
        hid = kb.sb("hid", [128, 4, T], BF16)
        WI, WO, WIU, WOU, load_w = wts if wts is not None else mlp_weights(layer)
        gT, gU = load_featcols([mlp_norm_g[layer]], f"mgT{layer}")
        xn = kb.sb("mxn", [128, D], BF16)
        junk = kb.sb("mjunk", [128, D], BF16)
        rl = [kb.sb(f"mrl{i}", [128, 512], F32) for i in range(2)]
        small = kb.sb("msmall", [128, 32], F32)
        ss16a, rstd16 = small[:, 0:NB], small[:, 16:16 + NB]
        hnU = units(NB, "hnT")
        NT = T // 512
        hidU = [units(NT, f"hid{fc}_") for fc in range(4)]
        xnU = Unit("mxn")
        jU = Unit("mjunk")
        sU = Unit("msmall")
        rlU = units(2, "mrl")
        load_w(0)
        for b in range(NB):
            g.act(junk[:], x_sb[:, b, :], AF.Square, [xU[b]], [jU, sU], accum_out=ss16a[:, b:b + 1])
        g.ts("dve", rstd16, ss16a, 1.0 / D, NORM_EPS, ALU.mult, ALU.add, [sU], [sU])
        g.act(rstd16, rstd16, AF.Ln, [sU], [sU])
        g.act(rstd16, rstd16, AF.Exp, [sU], [sU], scale=-0.5)

        def pre(b):
            g.ts("dve", xn[:], x_sb[:, b, :], rstd16[:, b:b + 1], None, ALU.mult, None, [xU[b], sU], [xnU])
            for c in range(8):
                g.tr(pT[:, c * 128:(c + 1) * 128], xn[:, c * 128:(c + 1) * 128], ident_b[:], [xnU, cU], [PSU[2]])
            for c in range(8):
                if c % 2:
                    g.ts("dve", hnT[:, c, b * 128:(b + 1) * 128], pT[:, c * 128:(c + 1) * 128], gT[:, c:c + 1], None, ALU.mult, None,
                         [PSU[2], gU], [hnU[b]])
                else:
                    g.act(hnT[:, c, b * 128:(b + 1) * 128], pT[:, c * 128:(c + 1) * 128], AF.Copy, [PSU[2], gU], [hnU[b]],
                          scale=gT[:, c:c + 1])
        pb = [0]

        def hidden(e, fc, tt):
            i = e % 2
            pi = pb[0]
            pb[0] ^= 1
            ps, psu = PS[pi], PSU[pi]
            for c in range(8):
                g.mm(ps[:], WI[i][:, c, fc * 128:(fc + 1) * 128], hnT[:, c, tt * 512:(tt + 1) * 512], c == 0, c == 7,
                     [WIU[i]] + hnU[tt * 4:(tt + 1) * 4], [psu])
            ri = (fc * 4 + tt) % 2
            g.act(rl[ri][:], ps[:], AF.Relu, [psu], [rlU[ri]])
            g.tt("pool", hid[:, fc, tt * 512:(tt + 1) * 512], rl[ri][:], rl[ri][:], ALU.mult, [rlU[ri]], [hidU[fc][tt]])

        for b in range(4):
            pre(b)
        for e in range(8):
            i = e % 2
            load_w(e + 1)
            if mid_hook is not None:
                mid_hook(e)
            if e == 0:
                for tt in range(NT):
                    if tt + 1 < NT:
                        for b in range(4 * (tt + 1), 4 * (tt + 2)):
                            pre(b)
                    for fc in range(4):
                        hidden(e, fc, tt)
            else:
                for fc in range(4):
                    for tt in range(NT):
                        hidden(e, fc, tt)
            for b in range(NB):
                for half in range(2):
                    pi = 3 + pb[0]
                    pb[0] ^= 1
                    ps, psu = PS[pi], PSU[pi]
                    for fc in range(4):
                        g.mm(ps[:], hid[:, fc, b * 128:(b + 1) * 128], WO[i][:, fc, half * 512:(half + 1) * 512], fc == 0, fc == 3,
                             [hidU[fc][b // 4], WOU[i]], [psu])
                    g.tt("dve", x_sb[:, b, half * 512:(half + 1) * 512], ps[:], x_sb[:, b, half * 512:(half + 1) * 512], ALU.add,
                         [psu, xU[b]], [xU[b]])
                if final and e == 7:
                    kb.dma("sp", out_d[b * 128:(b + 1) * 128, :], x_sb[:, b, :], reads=[xU[b]], writes=[outU])
        assert kb.sb_off <= (WQ_OFF if layer == 0 else MLPW_OFF), kb.sb_off

    KVW = kb.sb("KVW", [128, 8, 2 * D + H], BF16, off=KVW_OFF)
    WQ = kb.sb("WQ", [128, 8, D], BF16, off=WQ_OFF)
    qkvwU = Unit("qkv_w")

    def prefetch_qkv_w(e):
        todo = []
        for c in range(8):
            todo.append((KVW[:, c, 0:1024], kv_w[c * 128:(c + 1) * 128, 0:1024]))
            todo.append((KVW[:, c, 1024:2 * D + H], kv_w[c * 128:(c + 1) * 128, 1024:2 * D + H]))
            todo.append((WQ[:, c, :], attn_w_q[c * 128:(c + 1) * 128, :]))
        per = 4
        if 1 <= e <= 6:
            for o, i_ in todo[(e - 1) * per:(e - 1) * per + per]:
                kb.dma("pool", o, i_, writes=[qkvwU], multi=True)

    phase_mlp(0, False, mid_hook=prefetch_qkv_w)
    kb.fence()
    if stop_after <= 2:
        return dump_x()

    kb.sb_off = resid_off
    V_all = kb.sb("V_all", [128, NB, H, 65], BF16)
    VU = units(NB, "V")
    oU = units(NB, "o")
    attn_off = kb.sb_off

    def phase_qkv():
        kb.sb_off = attn_off
        gkq, gkqU = load_featcols([kv_norm_g, attn_norm_g], "gkq")
        gTk = gkq[:, 0:8]
        gTq = gkq[:, 8:16]
        kgr = kb.sb("kgr", [128, D], F32)
        qgr = kb.sb("qgr", [128, D], F32)
        fbr = kb.sb("fbr", [128, H], F32)
        xn = kb.sb("qxn", [128, D], BF16)
        junk = kb.sb("qjunk", [128, D], BF16)
        hT = [kb.sb(f"qhT{i}", [128, 8, 128], BF16) for i in range(2)]
        k_sb = kb.sb("k_sb", [128, D], F32)
        q_sb = kb.sb("q_sb", [128, D], F32)
        sqk = kb.sb("sqk", [128, D], F32)
        sqq = kb.sb("sqq", [128, D], F32)
        k_aug = kb.sb("k_aug", [128, H, 70], BF16)
        q_aug = kb.sb("q_aug", [128, H, 70], BF16)
        kTb = kb.sb("kTb", [70, H, 128], BF16)
        qTb = kb.sb("qTb", [70, H, 128], BF16)
        small = kb.sb("qsmall", [128, 16 * 12], F32)
        carry = kb.sb("carry", [128, H], F32)
        cbf = kb.sb("cbf", [128, 3, H], BF16)
        lfb = kb.sb("lfb", [128, 3, H], BF16)
        ss16a, rstd16 = small[:, 0:NB], small[:, 16:16 + NB]
        lf, cc, r1 = small[:, 48:64], small[:, 64:80], small[:, 80:96]
        ssk, rnk, ssq, rnq = small[:, 96:112], small[:, 112:128], small[:, 128:144], small[:, 144:160]
        wU = qkvwU
        U = {n: Unit(n) for n in ["xn", "junk", "k", "q", "sqk", "sqq", "k_aug", "q_aug", "kTb", "qTb", "s", "ssk", "ssq",
                                  "lf", "cc", "carry", "cbf", "kTd", "qTd", "lfb"]}
        hTU = units(2, "qhT")
        kb.dma("sp", kgr[:], k_norm_g.partition_broadcast(128), writes=[wU])
        kb.dma("sp", qgr[:], q_norm_g.partition_broadcast(128), writes=[wU])
        kb.dma("sp", fbr[:], kv_f_bias.partition_broadcast(128), writes=[wU])
        for c in range(8):
            g.act(KVW[:, c, :], KVW[:, c, :], AF.Copy, [wU, gkqU], [wU], scale=gTk[:, c:c + 1])
            g.ts("dve", WQ[:, c, :], WQ[:, c, :], gTq[:, c:c + 1], None, ALU.mult, None, [wU, gkqU], [wU])
        g.ts("dve", qgr[:], qgr[:], 0.125, None, ALU.mult, None, [wU], [wU])
        ck(10)
        g.memset("pool", carry[:], 0.0, [U["carry"]])
        g.memset("pool", k_aug[:], 1.0, [U["k_aug"]])
        g.memset("pool", q_aug[:], 1.0, [U["q_aug"]])
        g.memset("pool", V_all[:], 1.0, VU)
        v3 = lambda ap: ap.rearrange("p (h n) -> p h n", h=H)
        bc = lambda s_: s_.unsqueeze(2).broadcast_to([128, H, N])
        for b in range(NB):
            g.act(junk[:], x_sb[:, b, :], AF.Square, [xU[b]], [U["junk"], U["s"]], accum_out=ss16a[:, b:b + 1])
        g.ts("dve", rstd16, ss16a, 1.0 / D, NORM_EPS, ALU.mult, ALU.add, [U["s"]], [U["s"]])
        g.act(rstd16, rstd16, AF.Ln, [U["s"]], [U["s"]])
        g.act(rstd16, rstd16, AF.Exp, [U["s"]], [U["s"]], scale=-0.5)

        def stageA(b):
            g.ts("dve", xn[:], x_sb[:, b, :], rstd16[:, b:b + 1], None, ALU.mult, None, [xU[b], U["s"]], [U["xn"]])
            for c in range(8):
                g.tr(pT[:, c * 128:(c + 1) * 128], xn[:, c * 128:(c + 1) * 128], ident_b[:], [U["xn"], cU], [PSU[2]])
            g.cp("act", hT[b % 2][:], pT.rearrange("p (c t) -> p c t", c=8), [PSU[2]], [hTU[b % 2]])

        def stageBv(b):
            h_, hu = hT[b % 2], hTU[b % 2]
            for half in range(2):
                for c in range(8):
                    g.mm(PS[3 + half][:], h_[:, c, :], KVW[:, c, D + half * 512: D + (half + 1) * 512], c == 0, c == 7,
                         [hu, wU], [PSU[3 + half]])

        def stageB(b):
            h_, hu = hT[b % 2], hTU[b % 2]
            for half in range(2):
                for c in range(8):
                    g.mm(PS[half][:], h_[:, c, :], KVW[:, c, half * 512:(half + 1) * 512], c == 0, c == 7, [hu, wU], [PSU[half]])
            for half in range(2):
                for c in range(8):
                    g.mm(PS[5 + half][:], h_[:, c, :], WQ[:, c, half * 512:(half + 1) * 512], c == 0, c == 7, [hu, wU], [PSU[5 + half]])
            for c in range(8):
                g.mm(PS[7][:, 0:16], h_[:, c, :], KVW[:, c, 2 * D:2 * D + H], c == 0, c == 7, [hu, wU], [PSU[7]])

        def stageC1(b):
            for half in range(2):
                g.cp("act", k_sb[:, half * 512:(half + 1) * 512], PS[half][:], [PSU[half]], [U["k"]])
            for half in range(2):
                g.cp("act", V_all[:, b, half * 8:(half + 1) * 8, 0:64], PS[3 + half][:].rearrange("p (h n) -> p h n", h=8),
                     [PSU[3 + half]], [VU[b]])
            for half in range(2):
                g.cp("act", q_sb[:, half * 512:(half + 1) * 512], PS[5 + half][:], [PSU[5 + half]], [U["q"]])
            g.tt("dve", lf, PS[7][:, 0:16], fbr[:], ALU.add, [PSU[7], wU], [U["lf"]])

        def headnorm(src, su, sq, squ, ss_, rn_, ssu, dst_aug, gain_row, du):
            g.tt("pool", sq[:], src[:], src[:], ALU.mult, [su], [squ])
            g.red(ss_, v3(sq[:]), [squ], [ssu])
            g.ts("dve", rn_, ss_, 1.0 / N, NORM_EPS, ALU.mult, ALU.add, [ssu], [ssu])
            g.act(rn_, rn_, AF.Ln, [ssu], [ssu])
            g.act(rn_, rn_, AF.Exp, [ssu], [ssu], scale=-0.5)
            g.tt("dve", v3(sq[:]), v3(src[:]), bc(rn_), ALU.mult, [su, ssu], [squ])
            g.tt("pool", dst_aug[:, :, 0:64], v3(sq[:]), v3(gain_row[:]), ALU.mult, [squ, wU], [du])

        def stageC2(b):
            g.act(lf, lf, AF.Exp, [U["lf"]], [U["lf"]], scale=-1.0)
            g.act(lf, lf, AF.Ln, [U["lf"]], [U["lf"]], bias=1.0)
            g.ts("dve", lf, lf, -1.0, None, ALU.mult, None, [U["lf"]], [U["lf"]])
            g.cp("dve", lfb[:, 0, :], lf, [U["lf"]], [U["lfb"]])
            g.tt("dve", r1, lf, lfb[:, 0, :], ALU.subtract, [U["lf"], U["lfb"]], [U["cc"]])
            g.cp("dve", lfb[:, 1, :], r1, [U["cc"]], [U["lfb"]])
            g.tt("dve", r1, r1, lfb[:, 1, :], ALU.subtract, [U["cc"], U["lfb"]], [U["cc"]])
            g.cp("dve", lfb[:, 2, :], r1, [U["cc"]], [U["lfb"]])
            for pc in range(3):
                g.mm(PS[2][:, 0:16], tri_b[:], lfb[:, pc, :], pc == 0, pc == 2, [U["lfb"], cU], [PSU[2]])
            for pc in range(3):
                g.mm(PS[2][:, 16:32], ones_b[:], lfb[:, pc, :], pc == 0, pc == 2, [U["lfb"], cU], [PSU[2]])
            g.tt("dve", cc, PS[2][:, 0:16], carry[:], ALU.add, [PSU[2], U["carry"]], [U["cc"]])
            g.tt("dve", carry[:], PS[2][:, 16:32], carry[:], ALU.add, [PSU[2], U["carry"]], [U["carry"]])
            g.cp("dve", cbf[:, 0, :], cc, [U["cc"]], [U["cbf"]])
            g.tt("dve", r1, cc, cbf[:, 0, :], ALU.subtract, [U["cc"], U["cbf"]], [U["cc"]])
            g.cp("dve", cbf[:, 1, :], r1, [U["cc"]], [U["cbf"]])
            g.tt("dve", r1, r1, cbf[:, 1, :], ALU.subtract, [U["cc"], U["cbf"]], [U["cc"]])
            g.cp("dve", cbf[:, 2, :], r1, [U["cc"]], [U["cbf"]])
            headnorm(k_sb, U["k"], sqk, U["sqk"], ssk, rnk, U["ssk"], k_aug, kgr, U["k_aug"])
            headnorm(q_sb, U["q"], sqq, U["sqq"], ssq, rnq, U["ssq"], q_aug, qgr, U["q_aug"])
            g.ts("dve", k_aug[:, :, 67:70], cbf[:].rearrange("p s h -> p h s"), -1.0, None, ALU.mult, None, [U["cbf"]], [U["k_aug"]])
            g.cp("dve", q_aug[:, :, 64:67], cbf[:].rearrange("p s h -> p h s"), [U["cbf"]], [U["q_aug"]])
            tbank = [2, 3, 4, 2]
            gi_ = 0
            for aug, au, Tb, Tu, dst, du in ((k_aug, U["k_aug"], kTb, U["kTb"], kT_d, U["kTd"]),
                                             (q_aug, U["q_aug"], qTb, U["qTb"], qT_d, U["qTd"])):
                for hh in range(2):
                    bkI = tbank[gi_]
                    gi_ += 1
                    pTx = PS[bkI][:].bitcast(BF16)
                    for h8 in range(8):
                        h = hh * 8 + h8
                        g.tr(pTx[0:70, h8 * 128:(h8 + 1) * 128], aug[:, h, :], ident_b[:], [au, cU], [PSU[bkI]])
                    g.cp("act" if hh == 0 else "dve", Tb[:, hh * 8:(hh + 1) * 8, :], pTx[0:70, :].rearrange("p (h t) -> p h t", h=8),
                         [PSU[bkI]], [Tu])
                kb.dma("sp", dst[:, :, b * 128:(b + 1) * 128].rearrange("h r t -> r h t"), Tb[:], reads=[Tu], writes=[du])

        assert kb.sb_off <= WQ_OFF, kb.sb_off
        stageA(0)
        stageB(0)
        stageBv(0)
        for b in range(NB):
            if b + 1 < NB:
                stageA(b + 1)
            stageC1(b)
            if b + 1 < NB:
                stageB(b + 1)
            stageC2(b)
            if b + 1 < NB:
                stageBv(b + 1)
            ck(15 if b == 0 else -1)
        return U["kTd"], U["qTd"]

    try:
        kTdU, qTdU = phase_qkv()
    except _Stop:
        kb.fence()
        return dump_x()
    kb.fence()

    def phase_attn():
        kb.sb_off = attn_off
        o_all = kb.sb("o_all", [128, NB, D], BF16)
        WOa = kb.sb("WOa", [128, 8, D], BF16)
        kTh = [kb.sb(f"kTh{i}", [70, T], BF16) for i in range(2)]
        qTh = [kb.sb(f"qTh{i}", [70, T], BF16) for i in range(2)]
        rec = kb.sb("rec", [128, 4], F32)
        oT = kb.sb("oT", [128, 8, 128], BF16)
        wU = Unit("woa")
        khU, qhU = units(2, "kTh"), units(2, "qTh")
        recU, oTU = Unit("rec"), Unit("oT")
        for c in range(8):
            kb.dma("pool", WOa[:, c, :], attn_w_o[c * 128:(c + 1) * 128, :], writes=[wU], multi=True)
        mlp1_w[4](0)
        assert kb.sb_off + 3 * 1024 + 64 <= MLPW_OFF, kb.sb_off
        NBUF = 3
        psS = [PS[3], PS[4], PS[7]]
        psSU = [PSU[3], PSU[4], PSU[7]]
        psO = [PS[5], PS[6]]
        psOU = [PSU[5], PSU[6]]
        PT = [kb.sb(f"PTb{i}", [128, 4, 128], BF16) for i in range(NBUF)]
        PTU = units(NBUF, "PTb")
        groups = []
        oi = 0
        for h in range(H):
            for i in range(NB):
                for j0 in range(0, i + 1, 4):
                    groups.append((h, i, list(range(j0, min(j0 + 4, i + 1))), oi % 2))
                oi += 1
        loaded = set()

        def load_head(h):
            if h < H and h not in loaded:
                loaded.add(h)
                hb = h % 2
                kb.dma("sp", kTh[hb][:], kT_d[h], reads=[kTdU], writes=[khU[hb]])
                kb.dma("sp", qTh[hb][:], qT_d[h], reads=[qTdU], writes=[qhU[hb]])

        def emit_qk(gidx):
            h, i, js, _ = groups[gidx]
            hb = h % 2
            if i == 0:
                load_head(h)
                load_head(h + 1)
            ps, psu = psS[gidx % NBUF], psSU[gidx % NBUF]
            for jj, j in enumerate(js):
                diag = j == i
                g.mm(ps[:, jj * 128:(jj + 1) * 128], kTh[hb][:, j * 128:(j + 1) * 128], qTh[hb][:, i * 128:(i + 1) * 128],
                     True, not diag, [khU[hb], qhU[hb]], [psu])
                if diag:
                    g.mm(ps[:, jj * 128:(jj + 1) * 128], ident_b[:], maskneg_b[:], False, True, [cU], [psu])

        def emit_rest(gidx):
            h, i, js, ob = groups[gidx]
            ps, psu = psS[gidx % NBUF], psSU[gidx % NBUF]
            pt, ptu = PT[gidx % NBUF], PTU[gidx % NBUF]
            po, pou = psO[ob], psOU[ob]
            n = len(js)
            g.act(pt[:, 0:n, :], ps[:, 0:n * 128].rearrange("p (j t) -> p j t", j=n), AF.Exp, [psu], [ptu])
            for jj, j in enumerate(js):
                g.mm(po[:, 0:65], pt[:, jj, :], V_all[:, j, h, :], j == 0, j == i, [ptu, VU[j]], [pou])
            if js[-1] == i:
                kb.op("dve", lambda E, po=po: E.reciprocal(out=rec[:, 0:1], in_=po[:, 64:65]), [pou], [recU])
                g.ts("dve", o_all[:, i, h * 64:(h + 1) * 64], po[:, 0:64], rec[:, 0:1], None, ALU.mult, None, [pou, recU], [oU[i]])

        emit_qk(0)
        for gidx in range(len(groups)):
            if gidx + 1 < len(groups):
                emit_qk(gidx + 1)
            emit_rest(gidx)
        ck(17)
        pb = 0
        for b in range(NB):
            for c in range(8):
                g.tr(pT[:, c * 128:(c + 1) * 128], o_all[:, b, c * 128:(c + 1) * 128], ident_b[:], [oU[b], cU], [pTU[c // 4]])
            g.cp("act", oT[:], pT.rearrange("p (c t) -> p c t", c=8), pTU, [oTU])
            for half in range(2):
                ps, psu = PS[pb], PSU[pb]
                pb ^= 1
                for c in range(8):
                    g.mm(ps[:], oT[:, c, :], WOa[:, c, half * 512:(half + 1) * 512], c == 0, c == 7, [oTU, wU], [psu])
                g.tt("dve", x_sb[:, b, half * 512:(half + 1) * 512], ps[:], x_sb[:, b, half * 512:(half + 1) * 512], ALU.add,
                     [psu, xU[b]], [xU[b]])

    mlp1_w = mlp_weights(1, off=MLPW_OFF)
    try:
        phase_attn()
    except _Stop:
        pass
    kb.fence()
    if stop_after <= 3:
        return dump_x()

    phase_mlp(1, True, wts=mlp1_w)
    kb.final_wait("sp", [outU])
    kb.fence()
    kb.build()
    return nc


_NC_CACHE = {}


def _prep(inputs):
    f = lambda a: np.ascontiguousarray(np.asarray(a, dtype=np.float32))
    common = {
        "rwkv_norm_g": f(inputs["rwkv_norm_g"][0]),
        "rwkv_mu": f(inputs["rwkv_mu"][0]),
        "rwkv_w_rkv": f(inputs["rwkv_w_rkv"][0]),
        "rwkv_w0": f(inputs["rwkv_w0"][0]),
        "rwkv_w1": f(inputs["rwkv_w1"][0]),
        "rwkv_w2": f(inputs["rwkv_w2"][0]),
        "rwkv_a0": f(inputs["rwkv_a0"][0]),
        "rwkv_a1": f(inputs["rwkv_a1"][0]),
        "rwkv_a2": f(inputs["rwkv_a2"][0]),
        "rwkv_g1": f(inputs["rwkv_g1"][0]),
        "rwkv_g2": f(inputs["rwkv_g2"][0]),
        "rwkv_k_k": f(inputs["rwkv_k_k"][0]),
        "rwkv_k_a": f(inputs["rwkv_k_a"][0]),
        "rwkv_r_k": f(inputs["rwkv_r_k"][0]).reshape(D),
        "rwkv_lnx_w": f(inputs["rwkv_lnx_w"][0]),
        "rwkv_lnx_b": f(inputs["rwkv_lnx_b"][0]),
        "rwkv_w_o": f(inputs["rwkv_w_o"][0]),
        "kv_norm_g": f(inputs["kv_norm_g"]),
        "kv_w": f(inputs["kv_w"]),
        "kv_f_bias": f(inputs["kv_f_bias"]),
        "k_norm_g": f(inputs["k_norm_g"]).reshape(D),
        "attn_norm_g": f(inputs["attn_norm_g"][0]),
        "attn_w_q": f(inputs["attn_w_q"][0]),
        "q_norm_g": f(inputs["q_norm_g"][0]).reshape(D),
        "attn_w_o": f(inputs["attn_w_o"][0]),
        "mlp_norm_g": f(inputs["mlp_norm_g"]),
        "mlp_w_in": f(inputs["mlp_w_in"]),
        "mlp_w_out": f(inputs["mlp_w_out"]),
    }
    x = f(inputs["x"])
    return [dict(common, x=x[b]) for b in range(8)]


def kernel(_stop_after=99, **inputs):
    if _stop_after not in _NC_CACHE:
        _NC_CACHE[_stop_after] = build_nc(_stop_after)
    nc = _NC_CACHE[_stop_after]
    in_maps = _prep(inputs)
    res = run_bass_kernel_spmd(nc, in_maps, core_ids=list(range(8)))
    return np.stack([np.asarray(r["out"], dtype=np.float32) for r in res.results], axis=0)
```

```python
import numpy as np
import concourse.bass as bass
import concourse.mybir as mybir
from concourse.bass_utils import run_bass_kernel_spmd

F32 = mybir.dt.float32
BF16 = mybir.dt.bfloat16
AF = mybir.ActivationFunctionType
ALU = mybir.AluOpType
AX = mybir.AxisListType

D = 1024
T = 2048
H = 16
N = 64
NB = 16
FF = 4096
C_DEC = 0.6065306597126334
NORM_EPS = 1e-6
GN_EPS = 64e-5


DBG = 0


class _Stop(Exception):
    pass


def ck(k):
    if DBG == k:
        raise _Stop()


class Unit:
    __slots__ = ("w", "r", "name", "excl", "ws", "base", "mopen")

    def __init__(self, name="", excl=False):
        self.w = None
        self.r = {}
        self.ws = []
        self.base = []
        self.mopen = False
        self.name = name
        self.excl = excl


def units(n, name=""):
    return [Unit(f"{name}{i}") for i in range(n)]


class KB:
    SEM_ROLL = 3000

    def __init__(self, nc, n_dma_sems=6, same_engine_sync=True):
        self.nc = nc
        self.eng = {"pe": nc.tensor, "act": nc.scalar, "dve": nc.vector,
                    "pool": nc.gpsimd, "sp": nc.sync}
        self.q = {e: [] for e in self.eng}
        self.nsem = 0
        self.sem = {e: self._newsem(e) for e in self.eng}
        self.cnt = {e: 0 for e in self.eng}
        self.seen = {e: {} for e in self.eng}
        self.same = same_engine_sync
        self.dq = {}
        for e in ("sp", "act", "pool"):
            self.dq[e] = {"sems": [self._newsem(f"d{e}") for _ in range(n_dma_sems)],
                          "cnt": [0] * n_dma_sems, "i": 0}
        self.sb_off = 16512
        self.sb_top = 229344
        self.ninst = 0
        self.ntens = 0

    def _newsem(self, tag):
        self.nsem += 1
        return self.nc.alloc_semaphore(f"sem_{tag}_{self.nsem}")

    def sb(self, name, shape, dtype, off=None):
        esz = 2 if dtype == BF16 else 4
        n = int(np.prod(shape[1:])) * esz
        if off is None:
            off = self.sb_off
            self.sb_off = (off + n + 31) // 32 * 32
        assert off + n <= self.sb_top, (name, off, n, self.sb_top)
        self.ntens += 1
        return self.nc.alloc_sbuf_tensor_at(f"{name}_{self.ntens}", list(shape), dtype, offset=off)

    def _collect(self, e, reads, writes, multi=False):
        deps = {}

        def add(tok):
            s, v = tok
            k = id(s)
            if k not in deps or deps[k][1] < v:
                deps[k] = (s, v)
        own_sem = self.sem[e]
        for u in reads:
            if u.w is not None:
                add(u.w)
            for tok in u.ws:
                add(tok)
            if u.excl:
                for tok in u.r.values():
                    if tok[0] is not own_sem:
                        add(tok)
        for u in writes:
            if multi and u.mopen:
                for tok in u.base:
                    add(tok)
                continue
            base = []
            if u.w is not None:
                base.append(u.w)
            base.extend(u.ws)
            base.extend(u.r.values())
            for tok in base:
                add(tok)
            if multi:
                u.base = base
        waits = []
        own = id(self.sem[e])
        for k, (s, v) in deps.items():
            if k == own and (e == "pe" or not self.same):
                continue
            if self.seen[e].get(k, 0) < v:
                waits.append((s, v))
                self.seen[e][k] = v
        return waits

    def _mark(self, tok, reads, writes, multi=False):
        s = tok[0]
        for u in writes:
            if multi and u.mopen:
                u.ws.append(tok)
                continue
            u.w = tok
            u.ws = []
            u.r = {}
            u.mopen = multi
        for u in reads:
            if u.w is tok:
                continue
            u.mopen = False
            u.r[id(s)] = tok

    def op(self, e, fn, reads=(), writes=()):
        waits = self._collect(e, reads, writes)
        if self.cnt[e] >= self.SEM_ROLL:
            self.sem[e] = self._newsem(e)
            self.cnt[e] = 0
        self.cnt[e] += 1
        tok = (self.sem[e], self.cnt[e])
        self.q[e].append((waits, fn, tok[0], 1))
        self._mark(tok, reads, writes)
        self.ninst += 1
        return tok

    def dma(self, e, out, in_, reads=(), writes=(), multi=False, **kw):
        d = self.dq[e]
        i = d["i"]
        d["i"] = (i + 1) % len(d["sems"])
        s = d["sems"][i]
        waits = self._collect(e, reads, writes, multi=multi)
        if d["cnt"][i] > 0 and self.seen[e].get(id(s), 0) < d["cnt"][i]:
            waits.append((s, d["cnt"][i]))
            self.seen[e][id(s)] = d["cnt"][i]
        if d["cnt"][i] >= 16 * 180:
            s = self._newsem(f"d{e}")
            d["sems"][i] = s
            d["cnt"][i] = 0
        d["cnt"][i] += 16
        tok = (s, d["cnt"][i])
        self.q[e].append((waits, lambda E: E.dma_start(out=out, in_=in_, **kw), s, 16))
        self._mark(tok, reads, writes, multi=multi)
        self.ninst += 1
        return tok

    def gate(self, e, us):
        waits = self._collect(e, (), us)
        if waits:
            self.q[e].append((waits, None, None, 0))

    def final_wait(self, e, us):
        deps = {}
        for u in us:
            if u.w is not None:
                s, v = u.w
                if id(s) not in deps or deps[id(s)][1] < v:
                    deps[id(s)] = (s, v)
        self.q[e].append((list(deps.values()), None, None, 0))

    def fence(self):
        toks = []
        for e in self.eng:
            if self.cnt[e] > 0:
                toks.append((self.sem[e], self.cnt[e]))
        for e, d in self.dq.items():
            for s, c in zip(d["sems"], d["cnt"]):
                if c > 0:
                    toks.append((s, c))
        for e in self.eng:
            waits = []
            for s, v in toks:
                if self.seen[e].get(id(s), 0) < v:
                    waits.append((s, v))
                    self.seen[e][id(s)] = v
            if waits:
                self.q[e].append((waits, None, None, 0))

    def build(self):
        nc = self.nc
        with nc.Block() as block:
            def mk(ename):
                def body(E):
                    for waits, fn, s, inc in self.q[ename]:
                        for ws, wv in waits:
                            E.wait_ge(ws, wv)
                        if fn is not None:
                            fn(E).then_inc(s, inc)
                return body
            block.tensor(mk("pe"))
            block.scalar(mk("act"))
            block.vector(mk("dve"))
            block.gpsimd(mk("pool"))
            block.sync(mk("sp"))


class G:
    def __init__(self, kb):
        self.kb = kb

    def mm(self, out, lhsT, rhs, start, stop, r, w):
        self.kb.op("pe", lambda E: E.matmul(out, lhsT=lhsT, rhs=rhs, start=start, stop=stop), r, w)

    def tr(self, out, in_, ident, r, w):
        self.kb.op("pe", lambda E: E.transpose(out=out, in_=in_, identity=ident), r, w)

    def tt(self, e, out, in0, in1, op, r, w):
        self.kb.op(e, lambda E: E.tensor_tensor(out=out, in0=in0, in1=in1, op=op), r, w)

    def ts(self, e, out, in0, s1, s2, op0, op1, r, w):
        if op1 is None:
            self.kb.op(e, lambda E: E.tensor_scalar(out=out, in0=in0, scalar1=s1, scalar2=None, op0=op0), r, w)
        else:
            self.kb.op(e, lambda E: E.tensor_scalar(out=out, in0=in0, scalar1=s1, scalar2=s2, op0=op0, op1=op1), r, w)

    def stt(self, out, in0, scalar, in1, op0, op1, r, w):
        self.kb.op("dve", lambda E: E.scalar_tensor_tensor(out=out, in0=in0, scalar=scalar, in1=in1, op0=op0, op1=op1), r, w)

    def act(self, out, in_, func, r, w, bias=None, scale=None, accum_out=None):
        kw = {}
        if bias is not None:
            kw["bias"] = bias
        if scale is not None:
            kw["scale"] = scale
        if accum_out is not None:
            kw["accum_out"] = accum_out
        self.kb.op("act", lambda E: E.activation(out=out, in_=in_, func=func, **kw), r, w)

    def cp(self, e, out, in_, r, w):
        if e == "act":
            self.kb.op(e, lambda E: E.activation(out=out, in_=in_, func=AF.Copy), r, w)
        else:
            self.kb.op(e, lambda E: E.tensor_copy(out=out, in_=in_), r, w)

    def red(self, out, in_, r, w):
        self.kb.op("dve", lambda E: E.tensor_reduce(out=out, in_=in_, axis=AX.X, op=ALU.add), r, w)

    def memset(self, e, ap, val, w):
        self.kb.op(e, lambda E: E.memset(ap, val), (), w)


def build_nc(stop_after=99):
    nc = bass.Bass("TRN2", target_bir_lowering=False)
    dt = nc.dram_tensor

    def inp(name, shape):
        return dt(name, list(shape), F32, kind="ExternalInput").ap()

    x_d = inp("x", [T, D])
    rwkv_norm_g = inp("rwkv_norm_g", [D])
    rwkv_mu = inp("rwkv_mu", [6, D])
    w_rkv = inp("rwkv_w_rkv", [3, D, D])
    w0_d = inp("rwkv_w0", [D])
    w1_d = inp("rwkv_w1", [D, 64])
    w2_d = inp("rwkv_w2", [64, D])
    a0_d = inp("rwkv_a0", [D])
    a1_d = inp("rwkv_a1", [D, 64])
    a2_d = inp("rwkv_a2", [64, D])
    g1_d = inp("rwkv_g1", [D, 160])
    g2_d = inp("rwkv_g2", [160, D])
    kk_d = inp("rwkv_k_k", [D])
    ka_d = inp("rwkv_k_a", [D])
    rk_d = inp("rwkv_r_k", [D])
    lnw_d = inp("rwkv_lnx_w", [D])
    lnb_d = inp("rwkv_lnx_b", [D])
    wo_d = inp("rwkv_w_o", [D, D])
    kv_norm_g = inp("kv_norm_g", [D])
    kv_w = inp("kv_w", [D, 2 * D + H])
    kv_f_bias = inp("kv_f_bias", [H])
    k_norm_g = inp("k_norm_g", [D])
    attn_norm_g = inp("attn_norm_g", [D])
    attn_w_q = inp("attn_w_q", [D, D])
    q_norm_g = inp("q_norm_g", [D])
    attn_w_o = inp("attn_w_o", [D, D])
    mlp_norm_g = inp("mlp_norm_g", [2, D])
    mlp_w_in = inp("mlp_w_in", [2, D, FF])
    mlp_w_out = inp("mlp_w_out", [2, FF, D])
    out_d = dt("out", [T, D], F32, kind="ExternalOutput").ap()
    x1_d = dt("x1_scratch", [T, D], F32).ap()
    kT_d = dt("kT_scratch", [H, 70, T], BF16).ap()
    qT_d = dt("qT_scratch", [H, 70, T], BF16).ap()

    kb = KB(nc)
    g = G(kb)
    NC = ALU

    PS = [nc.alloc_psum_tensor(f"ps{i}", [128, 512], F32) for i in range(8)]
    PSU = [Unit(f"ps{i}", excl=True) for i in range(8)]
    pT = PS[2][:].bitcast(BF16)
    pTU = [PSU[2], PSU[2]]

    ident_f = kb.sb("ident_f", [128, 128], F32)
    ident_b = kb.sb("ident_b", [128, 128], BF16)
    tri_f = kb.sb("tri_f", [128, 128], F32)
    ones_f = kb.sb("ones_f", [128, 128], F32)
    maskE = kb.sb("maskE", [128, 256], BF16)
    mask_ls = kb.sb("mask_ls", [128, 128], BF16)
    tri_b = kb.sb("tri_b", [128, 128], BF16)
    ones_b = kb.sb("ones_b", [128, 128], BF16)
    maskneg_b = kb.sb("maskneg_b", [128, 128], BF16)
    maskE2 = kb.sb("maskE2", [128, 512], BF16)
    mask2 = kb.sb("mask2", [128, 256], BF16)
    maskneg = kb.sb("maskneg", [128, 128], F32)
    ctmp = kb.sb("ctmp", [128, 128], F32)
    cU = Unit("consts")
    ctU = Unit("ctmp")

    def sel(out, in_, cm, step, base, r, w):
        kb.op("pool", lambda E: E.affine_select(out=out, in_=in_, pattern=[[step, 128]], compare_op=ALU.is_ge,
                                                fill=0.0, base=base, channel_multiplier=cm), r, w)
    g.memset("pool", ones_f[:], 1.0, [cU])
    sel(ctmp[:], ones_f[:], 1, -1, 0, [cU], [ctU])
    sel(ident_f[:], ctmp[:], -1, 1, 0, [ctU], [cU])
    g.cp("pool", ident_b[:], ident_f[:], [cU], [cU])
    sel(tri_f[:], ones_f[:], -1, 1, 0, [cU], [cU])
    g.cp("pool", tri_b[:], tri_f[:], [cU], [cU])
    g.cp("pool", ones_b[:], ones_f[:], [cU], [cU])
    g.cp("pool", maskE[:, 128:256], tri_f[:], [cU], [cU])
    sel(ctmp[:], ones_f[:], -1, 1, -1, [cU], [ctU])
    g.cp("pool", maskE[:, 0:128], ctmp[:], [ctU], [cU])
    sel(ctmp[:], ones_f[:], 1, -1, -1, [cU, ctU], [ctU])
    g.cp("pool", mask_ls[:], ctmp[:], [ctU], [cU])
    g.ts("pool", maskneg[:], tri_f[:], -1.0, 1e4, ALU.add, ALU.mult, [cU], [cU])
    g.cp("pool", maskneg_b[:], maskneg[:], [cU], [cU])
    g.cp("pool", maskE2[:, 0:256], maskE[:], [cU], [cU])
    g.cp("pool", maskE2[:, 256:512], maskE[:], [cU], [cU])
    g.cp("pool", mask2[:, 0:128], mask_ls[:], [cU], [cU])
    g.cp("pool", mask2[:, 128:256], ones_f[:], [cU], [cU])

    persist_off = kb.sb_off

    def load_featcols(vecs, name):
        rows = 8 * len(vecs)
        stage = kb.sb(name + "_st", [64, 128], F32)
        dstt = kb.sb(name, [128, rows], F32)
        su, du = Unit(name + "_st"), Unit(name)
        for vi, vec in enumerate(vecs):
            kb.dma("sp", stage[vi * 8:(vi + 1) * 8, :], vec.rearrange("(c p) -> c p", p=128), writes=[su])
        g.tr(PS[2][:, 0:rows], stage[0:rows, :], ident_f[0:rows, 0:rows], [su, cU], [PSU[2]])
        g.cp("dve", dstt[:], PS[2][:, 0:rows], [PSU[2]], [du])
        return dstt, du

    def rstd_of(xin_ap, junk_ap, ss, rstd, rU, wU_junk, wU_s):
        wj = list(wU_junk) if isinstance(wU_junk, (list, tuple)) else [wU_junk]
        g.act(junk_ap, xin_ap, AF.Square, rU, wj + [wU_s], accum_out=ss)
        g.ts("dve", rstd, ss, 1.0 / D, NORM_EPS, ALU.mult, ALU.add, [wU_s], [wU_s])
        g.act(rstd, rstd, AF.Ln, [wU_s], [wU_s])
        g.act(rstd, rstd, AF.Exp, [wU_s], [wU_s], scale=-0.5)

    def phase_rwkv():
        kb.sb_off = persist_off
        W3 = [kb.sb(f"W{p}", [128, 8, D], BF16) for p in range(3)]
        Wo = kb.sb("Wo", [128, 8, D], BF16)
        W1 = kb.sb("W1", [128, 8, 64], BF16)
        A1 = kb.sb("A1", [128, 8, 64], BF16)
        G1 = kb.sb("G1", [128, 8, 160], BF16)
        W2 = kb.sb("W2", [64, D], BF16)
        A2 = kb.sb("A2", [64, D], BF16)
        G2a = kb.sb("G2a", [128, D], BF16)
        G2b = kb.sb("G2b", [32, D], BF16)
        gmu, gmuU = load_featcols([rwkv_mu[m] for m in range(6)] + [rwkv_norm_g], "gmu")
        muT = gmu[:, 0:48].rearrange("p (m c) -> p m c", m=6)
        gT = gmu[:, 48:56]
        w0r = kb.sb("w0r", [128, D], F32)
        a0r = kb.sb("a0r", [128, D], BF16)
        kkr_ = kb.sb("kkrow", [128, D], BF16)
        kar_ = kb.sb("karow", [128, D], BF16)
        rkr_ = kb.sb("rkrow", [128, D], BF16)
        lnwr = kb.sb("lnwrow", [128, D], BF16)
        lnbr = kb.sb("lnbrow", [128, D], BF16)
        wU = Unit("rw_weights")
        rowf = kb.sb("tA", [128, D], F32)
        tA = rowf
        rowU = Unit("tA")

        for p in range(3):
            for c in range(8):
                kb.dma("pool", W3[p][:, c, :], w_rkv[p, c * 128:(c + 1) * 128, :], writes=[wU], multi=True)
        for c in range(8):
            kb.dma("pool", Wo[:, c, :], wo_d[c * 128:(c + 1) * 128, :], writes=[wU], multi=True)
        kb.dma("pool", W1[:], w1_d.rearrange("(c p) e -> p c e", p=128), writes=[wU], multi=True)
        kb.dma("pool", A1[:], a1_d.rearrange("(c p) e -> p c e", p=128), writes=[wU], multi=True)
        kb.dma("pool", G1[:], g1_d.rearrange("(c p) e -> p c e", p=128), writes=[wU], multi=True)
        kb.dma("pool", W2[:], w2_d, writes=[wU], multi=True)
        kb.dma("pool", A2[:], a2_d, writes=[wU], multi=True)
        kb.dma("pool", G2a[:], g2_d[0:128, :], writes=[wU], multi=True)
        kb.dma("pool", G2b[:], g2_d[128:160, :], writes=[wU], multi=True)
        kb.dma("sp", w0r[:], w0_d.partition_broadcast(128), writes=[wU])
        for src, dst in ((a0_d, a0r), (kk_d, kkr_), (ka_d, kar_), (rk_d, rkr_), (lnw_d, lnwr), (lnb_d, lnbr)):
            kb.dma("sp", rowf[:], src.partition_broadcast(128), writes=[rowU])
            g.cp("dve", dst[:], rowf[:], [rowU], [wU])
        for c in range(8):
            for Wt in (W3[0], W3[1], W3[2], W1, A1, G1):
                g.act(Wt[:, c, :], Wt[:, c, :], AF.Copy, [wU, gmuU], [wU], scale=gT[:, c:c + 1])

        ck(1)
        xin2 = [kb.sb(f"xin{i}", [128, D], F32) for i in range(2)]
        xinU = units(2, "xin")
        r_sb = kb.sb("r_sb", [128, D], F32)
        k_sb = kb.sb("k_sb", [128, D], F32)
        a_sb = kb.sb("a_sb", [128, D], F32)
        sgh = kb.sb("sgh", [128, D], BF16)
        sgl = kb.sb("sgl", [128, D], BF16)
        kkt = kb.sb("kkt", [128, D], F32)
        kp = kb.sb("kp", [128, D], F32)
        b_sb = kb.sb("b_sb", [128, D], F32)
        Ea_off = kb.sb_off
        Ea = kb.sb("Ea", [128, D], F32)
        Eb_off = kb.sb_off
        Eb = kb.sb("Eb", [128, D], F32)
        y_sb = kb.sb("y_sb", [128, D], F32)
        sg = y_sb
        xn = kb.sb("xn", [128, D], BF16)
        v_bf = kb.sb("v_bf", [128, D], BF16)
        g_bf = kb.sb("g_bf", [128, D], BF16)
        At = kb.sb("At", [128, D], BF16)
        Bt = kb.sb("Bt", [128, D], BF16)
        Kt = kb.sb("Kt", [128, D], BF16)
        Rt = kb.sb("Rt", [128, D], BF16)
        Bh = kb.sb("Bh", [128, D], BF16)
        Kh = kb.sb("Kh", [128, D], BF16)
        yg = xn
        hTx = [kb.sb(f"hTx{i}", [128, 8, 129], BF16) for i in range(2)]
        xxT = kb.sb("xxT", [128, 8, 128], BF16)
        xp = [kb.sb(f"xp{i}", [128, 8, 128], BF16) for i in range(6)]
        ygT = xxT
        twT = kb.sb("twT", [64, 128], BF16)
        t1T = kb.sb("t1T", [64, 128], BF16)
        sgT1 = kb.sb("sgT1", [128, 128], BF16)
        sgT2 = kb.sb("sgT2", [32, 128], BF16)
        small = kb.sb("small", [128, 16 * 12], F32)
        ss = small[:, 0:1]
        rstd = small[:, 1:2]
        ss16 = small[:, 16:32]
        rn16 = small[:, 32:48]
        bs16 = small[:, 48:64]
        s1 = small[:, 64:80]
        s2 = small[:, 80:96]
        mean = small[:, 96:112]
        var = small[:, 112:128]
        rs = small[:, 128:144]
        m2 = small[:, 144:160]
        PCf = kb.sb("PCf", [64, 16], F32)
        S_f = kb.sb("S_f", [64, D], F32)
        S_bf = [kb.sb(f"S_bf{i}", [64, D], BF16) for i in range(2)]
        G4 = 4
        FM = [kb.sb(f"FM{i}", [64, 512], BF16) for i in range(G4)]
        E12 = [kb.sb(f"E12_{i}", [128, 512], BF16, off=Eb_off + i * 1024) for i in range(G4)]
        BN0 = [kb.sb(f"BN0_{i}", [128, 256], BF16) for i in range(G4)]
        AB = [[kb.sb(f"AB{i}_{k}", [128, 256], BF16, off=Ea_off + (i * 2 + k) * 512) for k in range(2)] for i in range(G4)]
        Nj = [[kb.sb(f"Nj{i}_{k}", [128, 128], BF16) for k in range(2)] for i in range(G4)]
        GT = [kb.sb(f"GT{i}", [64, 64], BF16) for i in range(G4)]
        QhT = [kb.sb(f"QhT{i}", [64, 128], BF16) for i in range(G4)]

        U = {n: Unit(n) for n in ["xin", "r", "k", "a", "sg", "tA", "tB", "cum", "kk", "kp", "b", "Ea", "Eb", "y",
                                  "xn", "v", "g", "At", "Bt", "Kt", "Rt", "Bh", "Kh", "yg", "xxT", "ygT", "twT",
                                  "t1T", "sgT1", "sgT2", "ss", "ss16", "bs16", "gn", "PCf", "x1d", "sgh", "sgl"]}
        U["tA"] = rowU
        U["yg"] = U["xn"]
        U["ygT"] = U["xxT"]
        hU = units(2, "hTx")
        xpU = units(6, "xp")
        SfU = units(16, "Sf")
        SbU = [units(16, "Sb0_"), units(16, "Sb1_")]
        FMU = units(G4, "FM")
        E12U = units(G4, "E12")
        BN0U = units(G4, "BN0")
        ABU = [units(2, f"AB{i}_") for i in range(G4)]
        NjU = [units(2, f"Nj{i}_") for i in range(G4)]
        GTU = units(G4, "GT")
        QhU = units(G4, "QhT")
        LU = PSU[7]
        L2U = PSU[7]
        P7 = PS[7]
        pbig = [0]
        BIGB = [0, 1, 3, 4, 5, 6]

        def bigps():
            i = BIGB[pbig[0] % 6]
            pbig[0] += 1
            return PS[i], PSU[i]

        HS = 10
        cols = (slice(0, HS * 64), slice(HS * 64, D))
        hds = (slice(0, HS), slice(HS, H))
        nh = (HS, H - HS)
        ENG = ("dve", "pool")
        UH = {n: (Unit(n + "_D"), Unit(n + "_P")) for n in ["kk", "tA", "kp", "b", "Rt", "Bt", "Kt", "At", "Bh", "Kh", "y", "yg"]}
        xnU = list(UH["yg"])

        def uh(n, h):
            return UH[n][0 if h < HS else 1]

        def R(spec, i):
            return [UH[x][i] if isinstance(x, str) else x for x in spec]

        def v3h(t_, i):
            return t_[:, cols[i]].rearrange("p (h n) -> p h n", n=N)

        def bch(s_, i):
            return s_[:, hds[i]].unsqueeze(2).broadcast_to([128, nh[i], N])

        def tt2(out, in0, in1, op, r, w):
            for i in (0, 1):
                g.tt(ENG[i], out[:, cols[i]], in0[:, cols[i]], in1[:, cols[i]], op, R(r, i), R(w, i))

        def tt2b(out, in0, s16, op, r, w):
            for i in (0, 1):
                g.tt(ENG[i], v3h(out, i), v3h(in0, i), bch(s16, i), op, R(r, i), R(w, i))

        def stt2(out, in0, scalar, in1, op0, op1, r, w):
            g.stt(out[:, cols[0]], in0[:, cols[0]], scalar, in1[:, cols[0]], op0, op1, R(r, 0), R(w, 0))
            neutral = 1.0 if op1 == ALU.mult else 0.0
            kb.op("pool", lambda E: E.tensor_scalar(out=out[:, cols[1]], in0=in0[:, cols[1]], scalar1=scalar, scalar2=neutral,
                                                    op0=op0, op1=(ALU.mult if op1 == ALU.mult else ALU.add)), R(r, 1), R(w, 1))
            g.tt("pool", out[:, cols[1]], out[:, cols[1]], in1[:, cols[1]], op1, R(r, 1) + R(w, 1), R(w, 1))

        g.memset("dve", S_f[:], 0.0, SfU)
        g.memset("pool", S_bf[0][:], 0.0, SbU[0])
        g.memset("pool", hTx[0][:, :, 0:1], 0.0, [hU[0]])

        for b in range(NB):
            cur = b % 2
            nxt = 1 - cur
            hc, hUc = hTx[cur], hU[cur]
            xin, xiU = xin2[b % 2], xinU[b % 2]
            if b == 0:
                kb.dma("sp", xin[:], x_d[0:128, :], reads=[], writes=[xiU])
            if b + 1 < NB:
                kb.dma("sp", xin2[(b + 1) % 2][:], x_d[(b + 1) * 128:(b + 2) * 128, :], reads=[], writes=[xinU[(b + 1) % 2]])
            rstd_of(xin[:], xn[:], ss, rstd, [xiU], xnU, U["ss"])
            g.ts("dve", xn[:], xin[:], rstd, None, ALU.mult, None, [xiU, U["ss"]], xnU)
            for c in range(8):
                g.tr(pT[:, c * 128:(c + 1) * 128], xn[:, c * 128:(c + 1) * 128], ident_b[:], xnU + [cU], [pTU[c // 4]])
            g.cp("act", hc[:, :, 1:129], pT.rearrange("p (c t) -> p c t", c=8), pTU, [hUc])
            g.cp("pool", hTx[nxt][:, :, 0:1], hc[:, :, 128:129], [hUc], [hU[nxt]])
            g.tt("dve", xxT[:], hc[:, :, 0:128], hc[:, :, 1:129], ALU.subtract, [hUc], [U["xxT"]])
            xnv = xn[:].rearrange("p (c t) -> p c t", c=8)
            for p in range(6):
                mu_bc = muT[:, p, :].unsqueeze(2).broadcast_to([128, 8, 128])
                g.tt("dve", xnv, xxT[:], mu_bc, ALU.mult, [U["xxT"], gmuU], xnU)
                g.tt("dve", xp[p][:], xnv, hc[:, :, 1:129], ALU.add, xnU + [hUc], [xpU[p]])
            ck(2)

            def proj(xpi, Wt, evac):
                for half in range(2):
                    ps, psu = bigps()
                    for c in range(8):
                        g.mm(ps[:], xp[xpi][:, c, :], Wt[:, c, half * 512:(half + 1) * 512], c == 0, c == 7,
                             [xpU[xpi], wU], [psu])
                    evac(ps, psu, half)
            hs = lambda half: slice(half * 512, (half + 1) * 512)
            proj(0, W3[0], lambda ps, psu, half: g.cp("act", r_sb[:, hs(half)], ps[:], [psu], [U["r"]]))
            proj(2, W3[1], lambda ps, psu, half: g.cp("act", k_sb[:, hs(half)], ps[:], [psu], [U["k"]]))
            proj(3, W3[2], lambda ps, psu, half: g.cp("act", v_bf[:, hs(half)], ps[:], [psu], [U["v"]]))
            ck(3)
            for c in range(8):
                g.mm(P7[0:64, 256:384], W1[:, c, :], xp[1][:, c, :], c == 0, c == 7, [xpU[1], wU], [LU])
            g.act(twT[:], P7[0:64, 256:384], AF.Tanh, [LU], [U["twT"]])
            for half in range(2):
                ps, psu = bigps()
                g.mm(ps[:], twT[:], W2[:, hs(half)], True, True, [U["twT"], wU], [psu])
                g.tt("dve", tA[:, hs(half)], ps[:], w0r[:, hs(half)], ALU.add, [psu, wU], [rowU] + list(UH["tA"]))
            g.act(sg[:], tA[:], AF.Sigmoid, [rowU] + list(UH["tA"]), list(UH["y"]))
            for c in range(8):
                g.mm(P7[0:64, 256:384], A1[:, c, :], xp[4][:, c, :], c == 0, c == 7, [xpU[4], wU], [LU])
            g.cp("act", t1T[:], P7[0:64, 256:384], [LU], [U["t1T"]])
            for half in range(2):
                ps, psu = bigps()
                g.mm(ps[:], t1T[:], A2[:, hs(half)], True, True, [U["t1T"], wU], [psu])
                g.tt("dve", a_sb[:, hs(half)], ps[:], a0r[:, hs(half)], ALU.add, [psu, wU], [U["a"]])
            g.act(a_sb[:], a_sb[:], AF.Sigmoid, [U["a"]], [U["a"]])
            for c in range(8):
                g.mm(P7[:, 256:384], G1[:, c, 0:128], xp[5][:, c, :], c == 0, c == 7, [xpU[5], wU], [LU])
            for c in range(8):
                g.mm(P7[0:32, 384:512], G1[:, c, 128:160], xp[5][:, c, :], c == 0, c == 7, [xpU[5], wU], [L2U])
            g.act(sgT1[:], P7[:, 256:384], AF.Sigmoid, [LU], [U["sgT1"]])
            g.act(sgT2[:], P7[0:32, 384:512], AF.Sigmoid, [L2U], [U["sgT2"]])
            for half in range(2):
                ps, psu = bigps()
                g.mm(ps[:], sgT1[:], G2a[:, hs(half)], True, False, [U["sgT1"], wU], [psu])
                g.mm(ps[:], sgT2[:], G2b[:, hs(half)], False, True, [U["sgT2"], wU], [psu])
                g.cp("act", g_bf[:, hs(half)], ps[:], [psu], [U["g"]])
            ck(4)
            g.cp("act", sgh[:], sg[:], list(UH["y"]), [U["sgh"]])
            g.tt("pool", tA[:], sg[:], sgh[:], ALU.subtract, list(UH["y"]) + [U["sgh"]], [rowU] + list(UH["tA"]))
            g.cp("act", sgl[:], tA[:], [rowU] + list(UH["tA"]), [U["sgl"]])

            def trimm(maskT, half):
                ps, psu = bigps()
                g.mm(ps[:], maskT, sgh[:, hs(half)], True, False, [U["sgh"], cU], [psu])
                g.mm(ps[:], maskT, sgl[:, hs(half)], False, True, [U["sgl"], cU], [psu])
                return ps, psu
            kb.gate("act", [uu for q_ in range(G4) for uu in (E12U[q_], ABU[q_][0], ABU[q_][1])])
            for half in range(2):
                ps, psu = trimm(tri_b[:], half)
                g.act(Ea[:, hs(half)], ps[:], AF.Exp, [psu], [U["Ea"]], scale=-C_DEC)
                g.act(Eb[:, hs(half)], ps[:], AF.Exp, [psu], [U["Eb"]], scale=C_DEC)
            ck(41)
            ps, psu = bigps()
            for h in range(H):
                g.mm(ps[0:64, h:h + 1], sgh[:, h * 64:(h + 1) * 64], tri_b[:, 127:128], True, False, [U["sgh"], cU], [psu])
                g.mm(ps[0:64, h:h + 1], sgl[:, h * 64:(h + 1) * 64], tri_b[:, 127:128], False, True, [U["sgl"], cU], [psu])
            g.act(PCf[:], ps[0:64, 0:16], AF.Exp, [psu], [U["PCf"]], scale=-C_DEC)
            ck(5)
            v3 = lambda t_: t_[:].rearrange("p (h n) -> p h n", h=H)
            bc = lambda s_: s_.unsqueeze(2).broadcast_to([128, H, N])
            tt2(kkt, k_sb, kkr_, ALU.mult, [U["k"], wU], ["kk"])
            tt2(tA, kkt, kkt, ALU.mult, ["kk"], ["tA"])
            g.red(ss16, v3(tA), list(UH["tA"]), [U["ss16"]])
            g.ts("dve", rn16, ss16, 1e-24, None, ALU.max, None, [U["ss16"]], [U["ss16"]])
            g.act(rn16, rn16, AF.Ln, [U["ss16"]], [U["ss16"]])
            g.act(rn16, rn16, AF.Exp, [U["ss16"]], [U["ss16"]], scale=-0.5)
            tt2b(kkt, kkt, rn16, ALU.mult, ["kk", U["ss16"]], ["kk"])
            tt2(Rt, r_sb, Ea, ALU.mult, [U["r"], U["Ea"]], ["Rt"])
            stt2(tA, a_sb, -1.0, kar_, ALU.add, ALU.mult, [U["a"], wU], ["tA"])
            stt2(kp, tA, 1.0, k_sb, ALU.add, ALU.mult, ["tA", U["k"]], ["kp"])
            tt2(b_sb, kkt, a_sb, ALU.mult, ["kk", U["a"]], ["b"])
            tt2(Bt, b_sb, Eb, ALU.mult, ["b", U["Eb"]], ["Bt"])
            tt2(Kt, kp, Eb, ALU.mult, ["kp", U["Eb"]], ["Kt"])
            for half in range(2):
                ps, psu = trimm(mask_ls[:], half)
                g.act(Eb[:, hs(half)], ps[:], AF.Exp, [psu], [U["Eb"]], scale=-C_DEC)
            for half in range(2):
                ps, psu = trimm(maskE[:, 0:128], half)
                g.act(Ea[:, hs(half)], ps[:], AF.Exp, [psu], [U["Ea"]], scale=-C_DEC)
            stt2(At, kkt, -1.0, Ea, ALU.mult, ALU.mult, ["kk", U["Ea"]], ["At"])
            tt2(Bh, b_sb, Eb, ALU.mult, ["b", U["Eb"]], ["Bh"])
            tt2(Kh, kp, Eb, ALU.mult, ["kp", U["Eb"]], ["Kh"])
            tt2(tA, r_sb, kp, ALU.mult, [U["r"], "kp"], ["tA"])
            tt2(tA, tA, rkr_, ALU.mult, ["tA", wU], ["tA"])
            g.red(bs16, v3(tA), list(UH["tA"]), [U["bs16"]])
            tt2b(kkt, v_bf, bs16, ALU.mult, [U["v"], U["bs16"], "kk"], ["kk"])
            tt2(kkt, kkt, lnbr, ALU.add, ["kk", wU], ["kk"])

            ck(6)
            kb.gate("dve", [U["Ea"], U["Eb"]])
            kb.gate("act", [U["Ea"], U["Eb"]])
            for h0 in range(0, H, G4):
                hs4 = list(range(h0, h0 + G4))

                def ctx(h):
                    q = h - h0
                    return q, PS[3 + q], PSU[3 + q], slice(h * 64, (h + 1) * 64)
                for h in hs4:
                    q, bk, bu, hsl = ctx(h)
                    bkb = bk[:].bitcast(BF16)
                    for qi, (src, su) in enumerate(((At, uh("At", h)), (Rt, uh("Rt", h)), (Bt, uh("Bt", h)), (Kt, uh("Kt", h)))):
                        g.tr(bkb[0:64, qi * 128:(qi + 1) * 128], src[:, hsl], ident_b[:], [su, cU], [bu])
                    g.cp("act", FM[q][:], bkb[0:64, 0:512], [bu], [FMU[q]])
                for h in hs4:
                    q, bk, bu, hsl = ctx(h)
                    fm = FM[q]
                    g.mm(bk[:, 0:256], fm[:, 256:384], fm[:, 0:256], True, True, [FMU[q]], [bu])
                    g.mm(bk[:, 256:512], fm[:, 384:512], fm[:, 0:256], True, True, [FMU[q]], [bu])
                    g.tt("dve", E12[q][:], bk[:], maskE2[:], ALU.mult, [bu, cU], [E12U[q]])
                for h in hs4:
                    q, bk, bu, hsl = ctx(h)
                    fm = FM[q]
                    g.mm(bk[:, 0:128], fm[:, 0:128], fm[:, 256:384], True, True, [FMU[q]], [bu])
                    g.mm(bk[:, 128:192], fm[:, 0:128], ident_b[0:64, 0:64], True, True, [FMU[q], cU], [bu])
                    g.mm(bk[:, 192:256], E12[q][:, 256:384], v_bf[:, hsl], True, True, [E12U[q], U["v"]], [bu])
                    g.tt("dve", BN0[q][:], bk[:, 0:256], mask2[:], ALU.mult, [bu, cU], [BN0U[q]])
                cur_ab = {}
                for h in hs4:
                    q = h - h0
                    cur_ab[q] = (E12[q][:, 0:128], BN0[q][:, 0:128], [E12U[q], BN0U[q]], BN0[q][:, 128:256], [BN0U[q]])
                for j in range(7):
                    for h in hs4:
                        q, bk, bu, hsl = ctx(h)
                        A_cur, B_cur, abu, N_cur, nu = cur_ab[q]
                        no = (j + 1) % 2
                        g.mm(bk[:, 0:128], A_cur, N_cur, True, True, abu + nu, [bu])
                        if j < 6:
                            g.mm(bk[:, 128:256], B_cur, A_cur, True, True, abu, [bu])
                            g.mm(bk[:, 256:384], A_cur, B_cur, True, True, abu, [bu])
                        g.tt("dve", Nj[q][no][:], bk[:, 0:128], N_cur, ALU.add, [bu] + nu, [NjU[q][no]])
                        if j < 6:
                            g.cp("act", AB[q][no][:], bk[:, 128:384], [bu], [ABU[q][no]])
                            cur_ab[q] = (AB[q][no][:, 0:128], AB[q][no][:, 128:256], [ABU[q][no]], Nj[q][no][:], [NjU[q][no]])
                        else:
                            cur_ab[q] = (None, None, None, Nj[q][no][:], [NjU[q][no]])
                for h in hs4:
                    q, bk, bu, hsl = ctx(h)
                    XZ, XZu = cur_ab[q][3], cur_ab[q][4]
                    X_ = XZ[:, 0:64]
                    g.mm(bk[0:64, 0:64], X_, Bh[:, hsl], True, True, XZu + [uh("Bh", h)], [bu])
                    g.mm(bk[0:64, 128:256], X_, E12[q][:, 128:256], True, True, XZu + [E12U[q]], [bu])
                    g.cp("act", GT[q][:], bk[0:64, 0:64], [bu], [GTU[q]])
                    g.tt("dve", QhT[q][:], bk[0:64, 128:256], FM[q][:, 128:256], ALU.add, [bu, FMU[q]], [QhU[q]])
                for h in hs4:
                    q, bk, bu, hsl = ctx(h)
                    XZ, XZu = cur_ab[q][3], cur_ab[q][4]
                    Z_ = XZ[:, 64:128]
                    MrbT, MrkT = E12[q][:, 128:256], E12[q][:, 384:512]
                    Sb_cur, Sb_cu = S_bf[cur][:, hsl], SbU[cur][h]
                    Yr = bk[:, 256:320]
                    g.mm(Yr, MrbT, Z_, True, False, [E12U[q]] + XZu, [bu])
                    g.mm(Yr, MrkT, v_bf[:, hsl], False, False, [E12U[q], U["v"]], [bu])
                    g.mm(Yr, QhT[q][:], Sb_cur, False, True, [QhU[q], Sb_cu], [bu])
                    Hr = bk[0:64, 320:384]
                    g.mm(Hr, Bh[:, hsl], Z_, True, False, [uh("Bh", h)] + XZu, [bu])
                    g.mm(Hr, Kh[:, hsl], v_bf[:, hsl], False, False, [uh("Kh", h), U["v"]], [bu])
                    g.mm(Hr, GT[q][:], Sb_cur, False, True, [GTU[q], Sb_cu], [bu])
                    g.cp("act", y_sb[:, hsl], Yr, [bu], [uh("y", h)])
                    g.stt(S_f[:, hsl], S_f[:, hsl], PCf[:, h:h + 1], Hr, ALU.mult, ALU.add,
                          [SfU[h], U["PCf"], bu], [SfU[h]])
                    g.cp("pool", S_bf[nxt][:, hsl], S_f[:, hsl], [SfU[h]], [SbU[nxt][h]])
                ck(7)

            ck(8)
            yU = list(UH["y"])
            g.red(s1, v3(y_sb), yU, [U["gn"]])
            tt2(tA, y_sb, y_sb, ALU.mult, ["y"], ["tA"])
            g.red(s2, v3(tA), list(UH["tA"]), [U["gn"]])
            g.ts("dve", mean, s1, 1.0 / N, None, ALU.mult, None, [U["gn"]], [U["gn"]])
            g.tt("dve", m2, mean, mean, ALU.mult, [U["gn"]], [U["gn"]])
            g.stt(var, s2, 1.0 / N, m2, ALU.mult, ALU.subtract, [U["gn"]], [U["gn"]])
            g.ts("dve", var, var, GN_EPS, None, ALU.add, None, [U["gn"]], [U["gn"]])
            g.act(rs, var, AF.Ln, [U["gn"]], [U["gn"]])
            g.act(rs, rs, AF.Exp, [U["gn"]], [U["gn"]], scale=-0.5)
            g.stt(m2, mean, -1.0, rs, ALU.mult, ALU.mult, [U["gn"]], [U["gn"]])
            tt2b(tA, y_sb, rs, ALU.mult, ["y", U["gn"]], ["tA"])
            tt2b(tA, tA, m2, ALU.add, ["tA", U["gn"]], ["tA"])
            tt2(tA, tA, lnwr, ALU.mult, ["tA", wU], ["tA"])
            tt2(tA, tA, kkt, ALU.add, ["tA", "kk"], ["tA"])
            tt2(yg, tA, g_bf, ALU.mult, ["tA", U["g"]], ["yg"])
            for c in range(8):
                g.tr(pT[:, c * 128:(c + 1) * 128], yg[:, c * 128:(c + 1) * 128], ident_b[:], xnU + [cU], [pTU[c // 4]])
            g.cp("act", ygT[:], pT.rearrange("p (c t) -> p c t", c=8), pTU, [U["ygT"]])
            for half in range(2):
                ps, psu = bigps()
                for c in range(8):
                    g.mm(ps[:], ygT[:, c, :], Wo[:, c, hs(half)], c == 0, c == 7, [U["ygT"], wU], [psu])
                g.tt("dve", xin[:, hs(half)], ps[:], xin[:, hs(half)], ALU.add, [psu, xiU], [xiU])
            kb.dma("sp", x1_d[b * 128:(b + 1) * 128, :], xin[:], reads=[xiU], writes=[U["x1d"]])
        return U["x1d"]

    try:
        x1U = phase_rwkv()
    except _Stop:
        x1U = Unit("x1dummy")
    kb.fence()

    kb.sb_off = persist_off
    x_sb = kb.sb("x_sb", [128, NB, D], F32)
    xU = units(NB, "x")
    resid_off = kb.sb_off
    for b in range(NB):
        kb.dma("sp", x_sb[:, b, :], x1_d[b * 128:(b + 1) * 128, :], reads=[x1U], writes=[xU[b]])

    outU = Unit("out")

    def dump_x():
        for b in range(NB):
            kb.dma("sp", out_d[b * 128:(b + 1) * 128, :], x_sb[:, b, :], reads=[xU[b]], writes=[outU])
        kb.final_wait("sp", [outU])
        kb.fence()
        kb.build()
        return nc

    if stop_after <= 1:
        return dump_x()

    SB_TOP = 229344
    KVW_OFF = SB_TOP - 8 * (2 * D + H) * 2
    WQ_OFF = KVW_OFF - 8 * D * 2
    MLPW_OFF = SB_TOP - 4 * 8192

    def mlp_weights(layer, off=None):
        if off is None:
            WI = [kb.sb(f"WI{layer}_{i}", [128, 8, 512], BF16) for i in range(2)]
            WO = [kb.sb(f"WO{layer}_{i}", [128, 4, D], BF16) for i in range(2)]
        else:
            WI = [kb.sb(f"WI{layer}_{i}", [128, 8, 512], BF16, off=off + i * 8192) for i in range(2)]
            WO = [kb.sb(f"WO{layer}_{i}", [128, 4, D], BF16, off=off + 16384 + i * 8192) for i in range(2)]
        WIU, WOU = units(2, "WI"), units(2, "WO")
        done = set()

        def load_w(e):
            if e in done or e >= 8:
                return
            done.add(e)
            i = e % 2
            for c in range(8):
                kb.dma("pool", WI[i][:, c, :], mlp_w_in[layer, c * 128:(c + 1) * 128, e * 512:(e + 1) * 512], writes=[WIU[i]], multi=True)
            for fc in range(4):
                kb.dma("pool", WO[i][:, fc, :], mlp_w_out[layer, e * 512 + fc * 128: e * 512 + (fc + 1) * 128, :], writes=[WOU[i]], multi=True)
        return WI, WO, WIU, WOU, load_w

    def phase_mlp(layer, final, wts=None, mid_hook=None):
        kb.sb_off = resid_off
        hnT = kb.sb("hnT", [128, 8, T], BF16)
        hid = kb.sb("hid", [128, 4, T], BF16)
        WI, WO, WIU, WOU, load_w = wts if wts is not None else mlp_weights(layer)
        gT, gU = load_featcols([mlp_norm_g[layer]], f"mgT{layer}")
        xn = kb.sb("mxn", [128, D], BF16)
        junk = kb.sb("mjunk", [128, D], BF16)
        rl = [kb.sb(f"mrl{i}", [128, 512], F32) for i in range(2)]
        small = kb.sb("msmall", [128, 32], F32)
        ss16a, rstd16 = small[:, 0:NB], small[:, 16:16 + NB]
        hnU = units(NB, "hnT")
        NT = T // 512
        hidU = [units(NT, f"hid{fc}_") for fc in range(4)]
        xnU = Unit("mxn")
        jU = Unit("mjunk")
        sU = Unit("msmall")
        rlU = units(2, "mrl")
        load_w(0)
        for b in range(NB):
            g.act(junk[:], x_sb[:, b, :], AF.Square, [xU[b]], [jU, sU], accum_out=ss16a[:, b:b + 1])
        g.ts("dve", rstd16, ss16a, 1.0 / D, NORM_EPS, ALU.mult, ALU.add, [sU], [sU])
        g.act(rstd16, rstd16, AF.Ln, [sU], [sU])
        g.act(rstd16, rstd16, AF.Exp, [sU], [sU], scale=-0.5)

        def pre(b):
            g.ts("dve", xn[:], x_sb[:, b, :], rstd16[:, b:b + 1], None, ALU.mult, None, [xU[b], sU], [xnU])
            for c in range(8):
                g.tr(pT[:, c * 128:(c + 1) * 128], xn[:, c * 128:(c + 1) * 128], ident_b[:], [xnU, cU], [PSU[2]])
            for c in range(8):
                if c % 2:
                    g.ts("dve", hnT[:, c, b * 128:(b + 1) * 128], pT[:, c * 128:(c + 1) * 128], gT[:, c:c + 1], None, ALU.mult, None,
                         [PSU[2], gU], [hnU[b]])
                else:
                    g.act(hnT[:, c, b * 128:(b + 1) * 128], pT[:, c * 128:(c + 1) * 128], AF.Copy, [PSU[2], gU], [hnU[b]],
                          scale=gT[:, c:c + 1])
        pb = [0]

        def hidden(e, fc, tt):
            i = e % 2
            pi = pb[0]
            pb[0] ^= 1
            ps, psu = PS[pi], PSU[pi]
            for c in range(8):
                g.mm(ps[:], WI[i][:, c, fc * 128:(fc + 1) * 128], hnT[:, c, tt * 512:(tt + 1) * 512], c == 0, c == 7,
                     [WIU[i]] + hnU[tt * 4:(tt + 1) * 4], [psu])
            ri = (fc * 4 + tt) % 2
            g.act(rl[ri][:], ps[:], AF.Relu, [psu], [rlU[ri]])
            g.tt("pool", hid[:, fc, tt * 512:(tt + 1) * 512], rl[ri][:], rl[ri][:], ALU.mult, [rlU[ri]], [hidU[fc][tt]])

        for b in range(4):
            pre(b)
        for e in range(8):
            i = e % 2
            load_w(e + 1)
            if mid_hook is not None:
                mid_hook(e)
            if e == 0:
                for tt in range(NT):
                    if tt + 1 < NT:
                        for b in range(4 * (tt + 1), 4 * (tt + 2)):
                            pre(b)
                    for fc in range(4):
                        hidden(e, fc, tt)
            else:
                for fc in range(4):
                    for tt in range(NT):
                        hidden(e, fc, tt)
            for b in range(NB):
                for half in range(2):
                    pi = 3 + pb[0]
                    pb[0] ^= 1
                    ps, psu = PS[pi], PSU[pi]
                    for fc in range(4):
                        g.mm(ps[:], hid[:, fc, b * 128:(b + 1) * 128], WO[i][:, fc, half * 512:(half + 1) * 512], fc == 0, fc == 3,
                             [hidU[fc][b // 4], WOU[i]], [psu])
                    g.tt("dve", x_sb[:, b, half * 512:(half + 1) * 512], ps[:], x_sb[:, b, half * 512:(half + 1) * 512], ALU.add,
                         [psu, xU[b]], [xU[b]])
                if final and e == 7:
                    kb.dma("sp", out_d[b * 128:(b + 1) * 128, :], x_sb[:, b, :], reads=[xU[b]], writes=[outU])
        assert kb.sb_off <= (WQ_OFF if layer == 0 else MLPW_OFF), kb.sb_off

    KVW = kb.sb("KVW", [128, 8, 2 * D + H], BF16, off=KVW_OFF)
    WQ = kb.sb("WQ", [128, 8, D], BF16, off=WQ_OFF)
    qkvwU = Unit("qkv_w")

    def prefetch_qkv_w(e):
        todo = []
        for c in range(8):
            todo.append((KVW[:, c, 0:1024], kv_w[c * 128:(c + 1) * 128, 0:1024]))
            todo.append((KVW[:, c, 1024:2 * D + H], kv_w[c * 128:(c + 1) * 128, 1024:2 * D + H]))
            todo.append((WQ[:, c, :], attn_w_q[c * 128:(c + 1) * 128, :]))
        per = 4
        if 1 <= e <= 6:
            for o, i_ in todo[(e - 1) * per:(e - 1) * per + per]:
                kb.dma("pool", o, i_, writes=[qkvwU], multi=True)

    phase_mlp(0, False, mid_hook=prefetch_qkv_w)
    kb.fence()
    if stop_after <= 2:
        return dump_x()

    kb.sb_off = resid_off
    V_all = kb.sb("V_all", [128, NB, H, 65], BF16)
    VU = units(NB, "V")
    oU = units(NB, "o")
    attn_off = kb.sb_off

    def phase_qkv():
        kb.sb_off = attn_off
        gkq, gkqU = load_featcols([kv_norm_g, attn_norm_g], "gkq")
        gTk = gkq[:, 0:8]
        gTq = gkq[:, 8:16]
        kgr = kb.sb("kgr", [128, D], F32)
        qgr = kb.sb("qgr", [128, D], F32)
        fbr = kb.sb("fbr", [128, H], F32)
        xn = kb.sb("qxn", [128, D], BF16)
        junk = kb.sb("qjunk", [128, D], BF16)
        hT = [kb.sb(f"qhT{i}", [128, 8, 128], BF16) for i in range(2)]
        k_sb = kb.sb("k_sb", [128, D], F32)
        q_sb = kb.sb("q_sb", [128, D], F32)
        sqk = kb.sb("sqk", [128, D], F32)
        sqq = kb.sb("sqq", [128, D], F32)
        k_aug = kb.sb("k_aug", [128, H, 70], BF16)
        q_aug = kb.sb("q_aug", [128, H, 70], BF16)
        kTb = kb.sb("kTb", [70, H, 128], BF16)
        qTb = kb.sb("qTb", [70, H, 128], BF16)
        small = kb.sb("qsmall", [128, 16 * 12], F32)
        carry = kb.sb("carry", [128, H], F32)
        cbf = kb.sb("cbf", [128, 3, H], BF16)
        lfb = kb.sb("lfb", [128, 3, H], BF16)
        ss16a, rstd16 = small[:, 0:NB], small[:, 16:16 + NB]
        lf, cc, r1 = small[:, 48:64], small[:, 64:80], small[:, 80:96]
        ssk, rnk, ssq, rnq = small[:, 96:112], small[:, 112:128], small[:, 128:144], small[:, 144:160]
        wU = qkvwU
        U = {n: Unit(n) for n in ["xn", "junk", "k", "q", "sqk", "sqq", "k_aug", "q_aug", "kTb", "qTb", "s", "ssk", "ssq",
                                  "lf", "cc", "carry", "cbf", "kTd", "qTd", "lfb"]}
        hTU = units(2, "qhT")
        kb.dma("sp", kgr[:], k_norm_g.partition_broadcast(128), writes=[wU])
        kb.dma("sp", qgr[:], q_norm_g.partition_broadcast(128), writes=[wU])
        kb.dma("sp", fbr[:], kv_f_bias.partition_broadcast(128), writes=[wU])
        for c in range(8):
            g.act(KVW[:, c, :], KVW[:, c, :], AF.Copy, [wU, gkqU], [wU], scale=gTk[:, c:c + 1])
            g.ts("dve", WQ[:, c, :], WQ[:, c, :], gTq[:, c:c + 1], None, ALU.mult, None, [wU, gkqU], [wU])
        g.ts("dve", qgr[:], qgr[:], 0.125, None, ALU.mult, None, [wU], [wU])
        ck(10)
        g.memset("pool", carry[:], 0.0, [U["carry"]])
        g.memset("pool", k_aug[:], 1.0, [U["k_aug"]])
        g.memset("pool", q_aug[:], 1.0, [U["q_aug"]])
        g.memset("pool", V_all[:], 1.0, VU)
        v3 = lambda ap: ap.rearrange("p (h n) -> p h n", h=H)
        bc = lambda s_: s_.unsqueeze(2).broadcast_to([128, H, N])
        for b in range(NB):
            g.act(junk[:], x_sb[:, b, :], AF.Square, [xU[b]], [U["junk"], U["s"]], accum_out=ss16a[:, b:b + 1])
        g.ts("dve", rstd16, ss16a, 1.0 / D, NORM_EPS, ALU.mult, ALU.add, [U["s"]], [U["s"]])
        g.act(rstd16, rstd16, AF.Ln, [U["s"]], [U["s"]])
        g.act(rstd16, rstd16, AF.Exp, [U["s"]], [U["s"]], scale=-0.5)

        def stageA(b):
            g.ts("dve", xn[:], x_sb[:, b, :], rstd16[:, b:b + 1], None, ALU.mult, None, [xU[b], U["s"]], [U["xn"]])
            for c in range(8):
                g.tr(pT[:, c * 128:(c + 1) * 128], xn[:, c * 128:(c + 1) * 128], ident_b[:], [U["xn"], cU], [PSU[2]])
            g.cp("act", hT[b % 2][:], pT.rearrange("p (c t) -> p c t", c=8), [PSU[2]], [hTU[b % 2]])

        def stageBv(b):
            h_, hu = hT[b % 2], hTU[b % 2]
            for half in range(2):
                for c in range(8):
                    g.mm(PS[3 + half][:], h_[:, c, :], KVW[:, c, D + half * 512: D + (half + 1) * 512], c == 0, c == 7,
                         [hu, wU], [PSU[3 + half]])

        def stageB(b):
            h_, hu = hT[b % 2], hTU[b % 2]
            for half in range(2):
                for c in range(8):
                    g.mm(PS[half][:], h_[:, c, :], KVW[:, c, half * 512:(half + 1) * 512], c == 0, c == 7, [hu, wU], [PSU[half]])
            for half in range(2):
                for c in range(8):
                    g.mm(PS[5 + half][:], h_[:, c, :], WQ[:, c, half * 512:(half + 1) * 512], c == 0, c == 7, [hu, wU], [PSU[5 + half]])
            for c in range(8):
                g.mm(PS[7][:, 0:16], h_[:, c, :], KVW[:, c, 2 * D:2 * D + H], c == 0, c == 7, [hu, wU], [PSU[7]])

        def stageC1(b):
            for half in range(2):
                g.cp("act", k_sb[:, half * 512:(half + 1) * 512], PS[half][:], [PSU[half]], [U["k"]])
            for half in range(2):
                g.cp("act", V_all[:, b, half * 8:(half + 1) * 8, 0:64], PS[3 + half][:].rearrange("p (h n) -> p h n", h=8),
                     [PSU[3 + half]], [VU[b]])
            for half in range(2):
                g.cp("act", q_sb[:, half * 512:(half + 1) * 512], PS[5 + half][:], [PSU[5 + half]], [U["q"]])
            g.tt("dve", lf, PS[7][:, 0:16], fbr[:], ALU.add, [PSU[7], wU], [U["lf"]])

        def headnorm(src, su, sq, squ, ss_, rn_, ssu, dst_aug, gain_row, du):
            g.tt("pool", sq[:], src[:], src[:], ALU.mult, [su], [squ])
            g.red(ss_, v3(sq[:]), [squ], [ssu])
            g.ts("dve", rn_, ss_, 1.0 / N, NORM_EPS, ALU.mult, ALU.add, [ssu], [ssu])
            g.act(rn_, rn_, AF.Ln, [ssu], [ssu])
            g.act(rn_, rn_, AF.Exp, [ssu], [ssu], scale=-0.5)
            g.tt("dve", v3(sq[:]), v3(src[:]), bc(rn_), ALU.mult, [su, ssu], [squ])
            g.tt("pool", dst_aug[:, :, 0:64], v3(sq[:]), v3(gain_row[:]), ALU.mult, [squ, wU], [du])

        def stageC2(b):
            g.act(lf, lf, AF.Exp, [U["lf"]], [U["lf"]], scale=-1.0)
            g.act(lf, lf, AF.Ln, [U["lf"]], [U["lf"]], bias=1.0)
            g.ts("dve", lf, lf, -1.0, None, ALU.mult, None, [U["lf"]], [U["lf"]])
            g.cp("dve", lfb[:, 0, :], lf, [U["lf"]], [U["lfb"]])
            g.tt("dve", r1, lf, lfb[:, 0, :], ALU.subtract, [U["lf"], U["lfb"]], [U["cc"]])
            g.cp("dve", lfb[:, 1, :], r1, [U["cc"]], [U["lfb"]])
            g.tt("dve", r1, r1, lfb[:, 1, :], ALU.subtract, [U["cc"], U["lfb"]], [U["cc"]])
            g.cp("dve", lfb[:, 2, :], r1, [U["cc"]], [U["lfb"]])
            for pc in range(3):
                g.mm(PS[2][:, 0:16], tri_b[:], lfb[:, pc, :], pc == 0, pc == 2, [U["lfb"], cU], [PSU[2]])
            for pc in range(3):
                g.mm(PS[2][:, 16:32], ones_b[:], lfb[:, pc, :], pc == 0, pc == 2, [U["lfb"], cU], [PSU[2]])
            g.tt("dve", cc, PS[2][:, 0:16], carry[:], ALU.add, [PSU[2], U["carry"]], [U["cc"]])
            g.tt("dve", carry[:], PS[2][:, 16:32], carry[:], ALU.add, [PSU[2], U["carry"]], [U["carry"]])
            g.cp("dve", cbf[:, 0, :], cc, [U["cc"]], [U["cbf"]])
            g.tt("dve", r1, cc, cbf[:, 0, :], ALU.subtract, [U["cc"], U["cbf"]], [U["cc"]])
            g.cp("dve", cbf[:, 1, :], r1, [U["cc"]], [U["cbf"]])
            g.tt("dve", r1, r1, cbf[:, 1, :], ALU.subtract, [U["cc"], U["cbf"]], [U["cc"]])
            g.cp("dve", cbf[:, 2, :], r1, [U["cc"]], [U["cbf"]])
            headnorm(k_sb, U["k"], sqk, U["sqk"], ssk, rnk, U["ssk"], k_aug, kgr, U["k_aug"])
            headnorm(q_sb, U["q"], sqq, U["sqq"], ssq, rnq, U["ssq"], q_aug, qgr, U["q_aug"])
            g.ts("dve", k_aug[:, :, 67:70], cbf[:].rearrange("p s h -> p h s"), -1.0, None, ALU.mult, None, [U["cbf"]], [U["k_aug"]])
            g.cp("dve", q_aug[:, :, 64:67], cbf[:].rearrange("p s h -> p h s"), [U["cbf"]], [U["q_aug"]])
            tbank = [2, 3, 4, 2]
            gi_ = 0
            for aug, au, Tb, Tu, dst, du in ((k_aug, U["k_aug"], kTb, U["kTb"], kT_d, U["kTd"]),
                                             (q_aug, U["q_aug"], qTb, U["qTb"], qT_d, U["qTd"])):
                for hh in range(2):
                    bkI = tbank[gi_]
                    gi_ += 1
                    pTx = PS[bkI][:].bitcast(BF16)
                    for h8 in range(8):
                        h = hh * 8 + h8
                        g.tr(pTx[0:70, h8 * 128:(h8 + 1) * 128], aug[:, h, :], ident_b[:], [au, cU], [PSU[bkI]])
                    g.cp("act" if hh == 0 else "dve", Tb[:, hh * 8:(hh + 1) * 8, :], pTx[0:70, :].rearrange("p (h t) -> p h t", h=8),
                         [PSU[bkI]], [Tu])
                kb.dma("sp", dst[:, :, b * 128:(b + 1) * 128].rearrange("h r t -> r h t"), Tb[:], reads=[Tu], writes=[du])

        assert kb.sb_off <= WQ_OFF, kb.sb_off
        stageA(0)
        stageB(0)
        stageBv(0)
        for b in range(NB):
            if b + 1 < NB:
                stageA(b + 1)
            stageC1(b)
            if b + 1 < NB:
                stageB(b + 1)
            stageC2(b)
            if b + 1 < NB:
                stageBv(b + 1)
            ck(15 if b == 0 else -1)
        return U["kTd"], U["qTd"]

    try:
        kTdU, qTdU = phase_qkv()
    except _Stop:
        kb.fence()
        return dump_x()
    kb.fence()

    def phase_attn():
        kb.sb_off = attn_off
        o_all = kb.sb("o_all", [128, NB, D], BF16)
        WOa = kb.sb("WOa", [128, 8, D], BF16)
        kTh = [kb.sb(f"kTh{i}", [70, T], BF16) for i in range(2)]
        qTh = [kb.sb(f"qTh{i}", [70, T], BF16) for i in range(2)]
        rec = kb.sb("rec", [128, 4], F32)
        oT = kb.sb("oT", [128, 8, 128], BF16)
        wU = Unit("woa")
        khU, qhU = units(2, "kTh"), units(2, "qTh")
        recU, oTU = Unit("rec"), Unit("oT")
        for c in range(8):
            kb.dma("pool", WOa[:, c, :], attn_w_o[c * 128:(c + 1) * 128, :], writes=[wU], multi=True)
        mlp1_w[4](0)
        assert kb.sb_off + 3 * 1024 + 64 <= MLPW_OFF, kb.sb_off
        NBUF = 3
        psS = [PS[3], PS[4], PS[7]]
        psSU = [PSU[3], PSU[4], PSU[7]]
        psO = [PS[5], PS[6]]
        psOU = [PSU[5], PSU[6]]
        PT = [kb.sb(f"PTb{i}", [128, 4, 128], BF16) for i in range(NBUF)]
        PTU = units(NBUF, "PTb")
        groups = []
        oi = 0
        for h in range(H):
            for i in range(NB):
                for j0 in range(0, i + 1, 4):
                    groups.append((h, i, list(range(j0, min(j0 + 4, i + 1))), oi % 2))
                oi += 1
        loaded = set()

        def load_head(h):
            if h < H and h not in loaded:
                loaded.add(h)
                hb = h % 2
                kb.dma("sp", kTh[hb][:], kT_d[h], reads=[kTdU], writes=[khU[hb]])
                kb.dma("sp", qTh[hb][:], qT_d[h], reads=[qTdU], writes=[qhU[hb]])

        def emit_qk(gidx):
            h, i, js, _ = groups[gidx]
            hb = h % 2
            if i == 0:
                load_head(h)
                load_head(h + 1)
            ps, psu = psS[gidx % NBUF], psSU[gidx % NBUF]
            for jj, j in enumerate(js):
                diag = j == i
                g.mm(ps[:, jj * 128:(jj + 1) * 128], kTh[hb][:, j * 128:(j + 1) * 128], qTh[hb][:, i * 128:(i + 1) * 128],
                     True, not diag, [khU[hb], qhU[hb]], [psu])
                if diag:
                    g.mm(ps[:, jj * 128:(jj + 1) * 128], ident_b[:], maskneg_b[:], False, True, [cU], [psu])

        def emit_rest(gidx):
            h, i, js, ob = groups[gidx]
            ps, psu = psS[gidx % NBUF], psSU[gidx % NBUF]
            pt, ptu = PT[gidx % NBUF], PTU[gidx % NBUF]
            po, pou = psO[ob], psOU[ob]
            n = len(js)
            g.act(pt[:, 0:n, :], ps[:, 0:n * 128].rearrange("p (j t) -> p j t", j=n), AF.Exp, [psu], [ptu])
            for jj, j in enumerate(js):
                g.mm(po[:, 0:65], pt[:, jj, :], V_all[:, j, h, :], j == 0, j == i, [ptu, VU[j]], [pou])
            if js[-1] == i:
                kb.op("dve", lambda E, po=po: E.reciprocal(out=rec[:, 0:1], in_=po[:, 64:65]), [pou], [recU])
                g.ts("dve", o_all[:, i, h * 64:(h + 1) * 64], po[:, 0:64], rec[:, 0:1], None, ALU.mult, None, [pou, recU], [oU[i]])

        emit_qk(0)
        for gidx in range(len(groups)):
            if gidx + 1 < len(groups):
                emit_qk(gidx + 1)
            emit_rest(gidx)
        ck(17)
        pb = 0
        for b in range(NB):
            for c in range(8):
                g.tr(pT[:, c * 128:(c + 1) * 128], o_all[:, b, c * 128:(c + 1) * 128], ident_b[:], [oU[b], cU], [pTU[c // 4]])
            g.cp("act", oT[:], pT.rearrange("p (c t) -> p c t", c=8), pTU, [oTU])
            for half in range(2):
                ps, psu = PS[pb], PSU[pb]
                pb ^= 1
                for c in range(8):
                    g.mm(ps[:], oT[:, c, :], WOa[:, c, half * 512:(half + 1) * 512], c == 0, c == 7, [oTU, wU], [psu])
                g.tt("dve", x_sb[:, b, half * 512:(half + 1) * 512], ps[:], x_sb[:, b, half * 512:(half + 1) * 512], ALU.add,
                     [psu, xU[b]], [xU[b]])

    mlp1_w = mlp_weights(1, off=MLPW_OFF)
    try:
        phase_attn()
    except _Stop:
        pass
    kb.fence()
    if stop_after <= 3:
        return dump_x()

    phase_mlp(1, True, wts=mlp1_w)
    kb.final_wait("sp", [outU])
    kb.fence()
    kb.build()
    return nc


_NC_CACHE = {}


def _prep(inputs):
    f = lambda a: np.ascontiguousarray(np.asarray(a, dtype=np.float32))
    common = {
        "rwkv_norm_g": f(inputs["rwkv_norm_g"][0]),
        "rwkv_mu": f(inputs["rwkv_mu"][0]),
        "rwkv_w_rkv": f(inputs["rwkv_w_rkv"][0]),
        "rwkv_w0": f(inputs["rwkv_w0"][0]),
        "rwkv_w1": f(inputs["rwkv_w1"][0]),
        "rwkv_w2": f(inputs["rwkv_w2"][0]),
        "rwkv_a0": f(inputs["rwkv_a0"][0]),
        "rwkv_a1": f(inputs["rwkv_a1"][0]),
        "rwkv_a2": f(inputs["rwkv_a2"][0]),
        "rwkv_g1": f(inputs["rwkv_g1"][0]),
        "rwkv_g2": f(inputs["rwkv_g2"][0]),
        "rwkv_k_k": f(inputs["rwkv_k_k"][0]),
        "rwkv_k_a": f(inputs["rwkv_k_a"][0]),
        "rwkv_r_k": f(inputs["rwkv_r_k"][0]).reshape(D),
        "rwkv_lnx_w": f(inputs["rwkv_lnx_w"][0]),
        "rwkv_lnx_b": f(inputs["rwkv_lnx_b"][0]),
        "rwkv_w_o": f(inputs["rwkv_w_o"][0]),
        "kv_norm_g": f(inputs["kv_norm_g"]),
        "kv_w": f(inputs["kv_w"]),
        "kv_f_bias": f(inputs["kv_f_bias"]),
        "k_norm_g": f(inputs["k_norm_g"]).reshape(D),
        "attn_norm_g": f(inputs["attn_norm_g"][0]),
        "attn_w_q": f(inputs["attn_w_q"][0]),
        "q_norm_g": f(inputs["q_norm_g"][0]).reshape(D),
        "attn_w_o": f(inputs["attn_w_o"][0]),
        "mlp_norm_g": f(inputs["mlp_norm_g"]),
        "mlp_w_in": f(inputs["mlp_w_in"]),
        "mlp_w_out": f(inputs["mlp_w_out"]),
    }
    x = f(inputs["x"])
    return [dict(common, x=x[b]) for b in range(8)]


def kernel(_stop_after=99, **inputs):
    if _stop_after not in _NC_CACHE:
        _NC_CACHE[_stop_after] = build_nc(_stop_after)
    nc = _NC_CACHE[_stop_after]
    in_maps = _prep(inputs)
    res = run_bass_kernel_spmd(nc, in_maps, core_ids=list(range(8)))
    return np.stack([np.asarray(r["out"], dtype=np.float32) for r in res.results], axis=0)
```

```python
import numpy as np
import concourse.bass as bass
import concourse.mybir as mybir
from concourse.bass_utils import run_bass_kernel_spmd

F32 = mybir.dt.float32
BF16 = mybir.dt.bfloat16
AF = mybir.ActivationFunctionType
ALU = mybir.AluOpType
AX = mybir.AxisListType

D = 1024
T = 2048
H = 16
N = 64
NB = 16
FF = 4096
C_DEC = 0.6065306597126334
NORM_EPS = 1e-6
GN_EPS = 64e-5


DBG = 0


class _Stop(Exception):
    pass


def ck(k):
    if DBG == k:
        raise _Stop()


class Unit:
    __slots__ = ("w", "r", "name", "excl", "ws", "base", "mopen")

    def __init__(self, name="", excl=False):
        self.w = None
        self.r = {}
        self.ws = []
        self.base = []
        self.mopen = False
        self.name = name
        self.excl = excl


def units(n, name=""):
    return [Unit(f"{name}{i}") for i in range(n)]


class KB:
    SEM_ROLL = 3000

    def __init__(self, nc, n_dma_sems=6, same_engine_sync=True):
        self.nc = nc
        self.eng = {"pe": nc.tensor, "act": nc.scalar, "dve": nc.vector,
                    "pool": nc.gpsimd, "sp": nc.sync}
        self.q = {e: [] for e in self.eng}
        self.nsem = 0
        self.sem = {e: self._newsem(e) for e in self.eng}
        self.cnt = {e: 0 for e in self.eng}
        self.seen = {e: {} for e in self.eng}
        self.same = same_engine_sync
        self.dq = {}
        for e in ("sp", "act", "pool"):
            self.dq[e] = {"sems": [self._newsem(f"d{e}") for _ in range(n_dma_sems)],
                          "cnt": [0] * n_dma_sems, "i": 0}
        self.sb_off = 16512
        self.sb_top = 229344
        self.ninst = 0
        self.ntens = 0

    def _newsem(self, tag):
        self.nsem += 1
        return self.nc.alloc_semaphore(f"sem_{tag}_{self.nsem}")

    def sb(self, name, shape, dtype, off=None):
        esz = 2 if dtype == BF16 else 4
        n = int(np.prod(shape[1:])) * esz
        if off is None:
            off = self.sb_off
            self.sb_off = (off + n + 31) // 32 * 32
        assert off + n <= self.sb_top, (name, off, n, self.sb_top)
        self.ntens += 1
        return self.nc.alloc_sbuf_tensor_at(f"{name}_{self.ntens}", list(shape), dtype, offset=off)

    def _collect(self, e, reads, writes, multi=False):
        deps = {}

        def add(tok):
            s, v = tok
            k = id(s)
            if k not in deps or deps[k][1] < v:
                deps[k] = (s, v)
        own_sem = self.sem[e]
        for u in reads:
            if u.w is not None:
                add(u.w)
            for tok in u.ws:
                add(tok)
            if u.excl:
                for tok in u.r.values():
                    if tok[0] is not own_sem:
                        add(tok)
        for u in writes:
            if multi and u.mopen:
                for tok in u.base:
                    add(tok)
                continue
            base = []
            if u.w is not None:
                base.append(u.w)
            base.extend(u.ws)
            base.extend(u.r.values())
            for tok in base:
                add(tok)
            if multi:
                u.base = base
        waits = []
        own = id(self.sem[e])
        for k, (s, v) in deps.items():
            if k == own and (e == "pe" or not self.same):
                continue
            if self.seen[e].get(k, 0) < v:
                waits.append((s, v))
                self.seen[e][k] = v
        return waits

    def _mark(self, tok, reads, writes, multi=False):
        s = tok[0]
        for u in writes:
            if multi and u.mopen:
                u.ws.append(tok)
                continue
            u.w = tok
            u.ws = []
            u.r = {}
            u.mopen = multi
        for u in reads:
            if u.w is tok:
                continue
            u.mopen = False
            u.r[id(s)] = tok

    def op(self, e, fn, reads=(), writes=()):
        waits = self._collect(e, reads, writes)
        if self.cnt[e] >= self.SEM_ROLL:
            self.sem[e] = self._newsem(e)
            self.cnt[e] = 0
        self.cnt[e] += 1
        tok = (self.sem[e], self.cnt[e])
        self.q[e].append((waits, fn, tok[0], 1))
        self._mark(tok, reads, writes)
        self.ninst += 1
        return tok

    def dma(self, e, out, in_, reads=(), writes=(), multi=False, **kw):
        d = self.dq[e]
        i = d["i"]
        d["i"] = (i + 1) % len(d["sems"])
        s = d["sems"][i]
        waits = self._collect(e, reads, writes, multi=multi)
        if d["cnt"][i] > 0 and self.seen[e].get(id(s), 0) < d["cnt"][i]:
            waits.append((s, d["cnt"][i]))
            self.seen[e][id(s)] = d["cnt"][i]
        if d["cnt"][i] >= 16 * 180:
            s = self._newsem(f"d{e}")
            d["sems"][i] = s
            d["cnt"][i] = 0
        d["cnt"][i] += 16
        tok = (s, d["cnt"][i])
        self.q[e].append((waits, lambda E: E.dma_start(out=out, in_=in_, **kw), s, 16))
        self._mark(tok, reads, writes, multi=multi)
        self.ninst += 1
        return tok

    def gate(self, e, us):
        waits = self._collect(e, (), us)
        if waits:
            self.q[e].append((waits, None, None, 0))

    def final_wait(self, e, us):
        deps = {}
        for u in us:
            if u.w is not None:
                s, v = u.w
                if id(s) not in deps or deps[id(s)][1] < v:
                    deps[id(s)] = (s, v)
        self.q[e].append((list(deps.values()), None, None, 0))

    def fence(self):
        toks = []
        for e in self.eng:
            if self.cnt[e] > 0:
                toks.append((self.sem[e], self.cnt[e]))
        for e, d in self.dq.items():
            for s, c in zip(d["sems"], d["cnt"]):
                if c > 0:
                    toks.append((s, c))
        for e in self.eng:
            waits = []
            for s, v in toks:
                if self.seen[e].get(id(s), 0) < v:
                    waits.append((s, v))
                    self.seen[e][id(s)] = v
            if waits:
                self.q[e].append((waits, None, None, 0))

    def build(self):
        nc = self.nc
        with nc.Block() as block:
            def mk(ename):
                def body(E):
                    for waits, fn, s, inc in self.q[ename]:
                        for ws, wv in waits:
                            E.wait_ge(ws, wv)
                        if fn is not None:
                            fn(E).then_inc(s, inc)
                return body
            block.tensor(mk("pe"))
            block.scalar(mk("act"))
            block.vector(mk("dve"))
            block.gpsimd(mk("pool"))
            block.sync(mk("sp"))


class G:
    def __init__(self, kb):
        self.kb = kb

    def mm(self, out, lhsT, rhs, start, stop, r, w):
        self.kb.op("pe", lambda E: E.matmul(out, lhsT=lhsT, rhs=rhs, start=start, stop=stop), r, w)

    def tr(self, out, in_, ident, r, w):
        self.kb.op("pe", lambda E: E.transpose(out=out, in_=in_, identity=ident), r, w)

    def tt(self, e, out, in0, in1, op, r, w):
        self.kb.op(e, lambda E: E.tensor_tensor(out=out, in0=in0, in1=in1, op=op), r, w)

    def ts(self, e, out, in0, s1, s2, op0, op1, r, w):
        if op1 is None:
            self.kb.op(e, lambda E: E.tensor_scalar(out=out, in0=in0, scalar1=s1, scalar2=None, op0=op0), r, w)
        else:
            self.kb.op(e, lambda E: E.tensor_scalar(out=out, in0=in0, scalar1=s1, scalar2=s2, op0=op0, op1=op1), r, w)

    def stt(self, out, in0, scalar, in1, op0, op1, r, w):
        self.kb.op("dve", lambda E: E.scalar_tensor_tensor(out=out, in0=in0, scalar=scalar, in1=in1, op0=op0, op1=op1), r, w)

    def act(self, out, in_, func, r, w, bias=None, scale=None, accum_out=None):
        kw = {}
        if bias is not None:
            kw["bias"] = bias
        if scale is not None:
            kw["scale"] = scale
        if accum_out is not None:
            kw["accum_out"] = accum_out
        self.kb.op("act", lambda E: E.activation(out=out, in_=in_, func=func, **kw), r, w)

    def cp(self, e, out, in_, r, w):
        if e == "act":
            self.kb.op(e, lambda E: E.activation(out=out, in_=in_, func=AF.Copy), r, w)
        else:
            self.kb.op(e, lambda E: E.tensor_copy(out=out, in_=in_), r, w)

    def red(self, out, in_, r, w):
        self.kb.op("dve", lambda E: E.tensor_reduce(out=out, in_=in_, axis=AX.X, op=ALU.add), r, w)

    def memset(self, e, ap, val, w):
        self.kb.op(e, lambda E: E.memset(ap, val), (), w)


def build_nc(stop_after=99):
    nc = bass.Bass("TRN2", target_bir_lowering=False)
    dt = nc.dram_tensor

    def inp(name, shape):
        return dt(name, list(shape), F32, kind="ExternalInput").ap()

    x_d = inp("x", [T, D])
    rwkv_norm_g = inp("rwkv_norm_g", [D])
    rwkv_mu = inp("rwkv_mu", [6, D])
    w_rkv = inp("rwkv_w_rkv", [3, D, D])
    w0_d = inp("rwkv_w0", [D])
    w1_d = inp("rwkv_w1", [D, 64])
    w2_d = inp("rwkv_w2", [64, D])
    a0_d = inp("rwkv_a0", [D])
    a1_d = inp("rwkv_a1", [D, 64])
    a2_d = inp("rwkv_a2", [64, D])
    g1_d = inp("rwkv_g1", [D, 160])
    g2_d = inp("rwkv_g2", [160, D])
    kk_d = inp("rwkv_k_k", [D])
    ka_d = inp("rwkv_k_a", [D])
    rk_d = inp("rwkv_r_k", [D])
    lnw_d = inp("rwkv_lnx_w", [D])
    lnb_d = inp("rwkv_lnx_b", [D])
    wo_d = inp("rwkv_w_o", [D, D])
    kv_norm_g = inp("kv_norm_g", [D])
    kv_w = inp("kv_w", [D, 2 * D + H])
    kv_f_bias = inp("kv_f_bias", [H])
    k_norm_g = inp("k_norm_g", [D])
    attn_norm_g = inp("attn_norm_g", [D])
    attn_w_q = inp("attn_w_q", [D, D])
    q_norm_g = inp("q_norm_g", [D])
    attn_w_o = inp("attn_w_o", [D, D])
    mlp_norm_g = inp("mlp_norm_g", [2, D])
    mlp_w_in = inp("mlp_w_in", [2, D, FF])
    mlp_w_out = inp("mlp_w_out", [2, FF, D])
    out_d = dt("out", [T, D], F32, kind="ExternalOutput").ap()
    x1_d = dt("x1_scratch", [T, D], F32).ap()
    kT_d = dt("kT_scratch", [H, 70, T], BF16).ap()
    qT_d = dt("qT_scratch", [H, 70, T], BF16).ap()

    kb = KB(nc)
    g = G(kb)
    NC = ALU

    PS = [nc.alloc_psum_tensor(f"ps{i}", [128, 512], F32) for i in range(8)]
    PSU = [Unit(f"ps{i}", excl=True) for i in range(8)]
    pT = PS[2][:].bitcast(BF16)
    pTU = [PSU[2], PSU[2]]

    ident_f = kb.sb("ident_f", [128, 128], F32)
    ident_b = kb.sb("ident_b", [128, 128], BF16)
    tri_f = kb.sb("tri_f", [128, 128], F32)
    ones_f = kb.sb("ones_f", [128, 128], F32)
    maskE = kb.sb("maskE", [128, 256], BF16)
    mask_ls = kb.sb("mask_ls", [128, 128], BF16)
    tri_b = kb.sb("tri_b", [128, 128], BF16)
    ones_b = kb.sb("ones_b", [128, 128], BF16)
    maskneg_b = kb.sb("maskneg_b", [128, 128], BF16)
    maskE2 = kb.sb("maskE2", [128, 512], BF16)
    mask2 = kb.sb("mask2", [128, 256], BF16)
    maskneg = kb.sb("maskneg", [128, 128], F32)
    ctmp = kb.sb("ctmp", [128, 128], F32)
    cU = Unit("consts")
    ctU = Unit("ctmp")

    def sel(out, in_, cm, step, base, r, w):
        kb.op("pool", lambda E: E.affine_select(out=out, in_=in_, pattern=[[step, 128]], compare_op=ALU.is_ge,
                                                fill=0.0, base=base, channel_multiplier=cm), r, w)
    g.memset("pool", ones_f[:], 1.0, [cU])
    sel(ctmp[:], ones_f[:], 1, -1, 0, [cU], [ctU])
    sel(ident_f[:], ctmp[:], -1, 1, 0, [ctU], [cU])
    g.cp("pool", ident_b[:], ident_f[:], [cU], [cU])
    sel(tri_f[:], ones_f[:], -1, 1, 0, [cU], [cU])
    g.cp("pool", tri_b[:], tri_f[:], [cU], [cU])
    g.cp("pool", ones_b[:], ones_f[:], [cU], [cU])
    g.cp("pool", maskE[:, 128:256], tri_f[:], [cU], [cU])
    sel(ctmp[:], ones_f[:], -1, 1, -1, [cU], [ctU])
    g.cp("pool", maskE[:, 0:128], ctmp[:], [ctU], [cU])
    sel(ctmp[:], ones_f[:], 1, -1, -1, [cU, ctU], [ctU])
    g.cp("pool", mask_ls[:], ctmp[:], [ctU], [cU])
    g.ts("pool", maskneg[:], tri_f[:], -1.0, 1e4, ALU.add, ALU.mult, [cU], [cU])
    g.cp("pool", maskneg_b[:], maskneg[:], [cU], [cU])
    g.cp("pool", maskE2[:, 0:256], maskE[:], [cU], [cU])
    g.cp("pool", maskE2[:, 256:512], maskE[:], [cU], [cU])
    g.cp("pool", mask2[:, 0:128], mask_ls[:], [cU], [cU])
    g.cp("pool", mask2[:, 128:256], ones_f[:], [cU], [cU])

    persist_off = kb.sb_off

    def load_featcols(vecs, name):
        rows = 8 * len(vecs)
        stage = kb.sb(name + "_st", [64, 128], F32)
        dstt = kb.sb(name, [128, rows], F32)
        su, du = Unit(name + "_st"), Unit(name)
        for vi, vec in enumerate(vecs):
            kb.dma("sp", stage[vi * 8:(vi + 1) * 8, :], vec.rearrange("(c p) -> c p", p=128), writes=[su])
        g.tr(PS[2][:, 0:rows], stage[0:rows, :], ident_f[0:rows, 0:rows], [su, cU], [PSU[2]])
        g.cp("dve", dstt[:], PS[2][:, 0:rows], [PSU[2]], [du])
        return dstt, du

    def rstd_of(xin_ap, junk_ap, ss, rstd, rU, wU_junk, wU_s):
        wj = list(wU_junk) if isinstance(wU_junk, (list, tuple)) else [wU_junk]
        g.act(junk_ap, xin_ap, AF.Square, rU, wj + [wU_s], accum_out=ss)
        g.ts("dve", rstd, ss, 1.0 / D, NORM_EPS, ALU.mult, ALU.add, [wU_s], [wU_s])
        g.act(rstd, rstd, AF.Ln, [wU_s], [wU_s])
        g.act(rstd, rstd, AF.Exp, [wU_s], [wU_s], scale=-0.5)

    def phase_rwkv():
        kb.sb_off = persist_off
        W3 = [kb.sb(f"W{p}", [128, 8, D], BF16) for p in range(3)]
        Wo = kb.sb("Wo", [128, 8, D], BF16)
        W1 = kb.sb("W1", [128, 8, 64], BF16)
        A1 = kb.sb("A1", [128, 8, 64], BF16)
        G1 = kb.sb("G1", [128, 8, 160], BF16)
        W2 = kb.sb("W2", [64, D], BF16)
        A2 = kb.sb("A2", [64, D], BF16)
        G2a = kb.sb("G2a", [128, D], BF16)
        G2b = kb.sb("G2b", [32, D], BF16)
        gmu, gmuU = load_featcols([rwkv_mu[m] for m in range(6)] + [rwkv_norm_g], "gmu")
        muT = gmu[:, 0:48].rearrange("p (m c) -> p m c", m=6)
        gT = gmu[:, 48:56]
        w0r = kb.sb("w0r", [128, D], F32)
        a0r = kb.sb("a0r", [128, D], BF16)
        kkr_ = kb.sb("kkrow", [128, D], BF16)
        kar_ = kb.sb("karow", [128, D], BF16)
        rkr_ = kb.sb("rkrow", [128, D], BF16)
        lnwr = kb.sb("lnwrow", [128, D], BF16)
        lnbr = kb.sb("lnbrow", [128, D], BF16)
        wU = Unit("rw_weights")
        rowf = kb.sb("tA", [128, D], F32)
        tA = rowf
        rowU = Unit("tA")

        for p in range(3):
            for c in range(8):
                kb.dma("pool", W3[p][:, c, :], w_rkv[p, c * 128:(c + 1) * 128, :], writes=[wU], multi=True)
        for c in range(8):
            kb.dma("pool", Wo[:, c, :], wo_d[c * 128:(c + 1) * 128, :], writes=[wU], multi=True)
        kb.dma("pool", W1[:], w1_d.rearrange("(c p) e -> p c e", p=128), writes=[wU], multi=True)
        kb.dma("pool", A1[:], a1_d.rearrange("(c p) e -> p c e", p=128), writes=[wU], multi=True)
        kb.dma("pool", G1[:], g1_d.rearrange("(c p) e -> p c e", p=128), writes=[wU], multi=True)
        kb.dma("pool", W2[:], w2_d, writes=[wU], multi=True)
        kb.dma("pool", A2[:], a2_d, writes=[wU], multi=True)
        kb.dma("pool", G2a[:], g2_d[0:128, :], writes=[wU], multi=True)
        kb.dma("pool", G2b[:], g2_d[128:160, :], writes=[wU], multi=True)
        kb.dma("sp", w0r[:], w0_d.partition_broadcast(128), writes=[wU])
        for src, dst in ((a0_d, a0r), (kk_d, kkr_), (ka_d, kar_), (rk_d, rkr_), (lnw_d, lnwr), (lnb_d, lnbr)):
            kb.dma("sp", rowf[:], src.partition_broadcast(128), writes=[rowU])
            g.cp("dve", dst[:], rowf[:], [rowU], [wU])
        for c in range(8):
            for Wt in (W3[0], W3[1], W3[2], W1, A1, G1):
                g.act(Wt[:, c, :], Wt[:, c, :], AF.Copy, [wU, gmuU], [wU], scale=gT[:, c:c + 1])

        ck(1)
        xin2 = [kb.sb(f"xin{i}", [128, D], F32) for i in range(2)]
        xinU = units(2, "xin")
        r_sb = kb.sb("r_sb", [128, D], F32)
        k_sb = kb.sb("k_sb", [128, D], F32)
        a_sb = kb.sb("a_sb", [128, D], F32)
        sgh = kb.sb("sgh", [128, D], BF16)
        sgl = kb.sb("sgl", [128, D], BF16)
        kkt = kb.sb("kkt", [128, D], F32)
        kp = kb.sb("kp", [128, D], F32)
        b_sb = kb.sb("b_sb", [128, D], F32)
        Ea_off = kb.sb_off
        Ea = kb.sb("Ea", [128, D], F32)
        Eb_off = kb.sb_off
        Eb = kb.sb("Eb", [128, D], F32)
        y_sb = kb.sb("y_sb", [128, D], F32)
        sg = y_sb
        xn = kb.sb("xn", [128, D], BF16)
        v_bf = kb.sb("v_bf", [128, D], BF16)
        g_bf = kb.sb("g_bf", [128, D], BF16)
        At = kb.sb("At", [128, D], BF16)
        Bt = kb.sb("Bt", [128, D], BF16)
        Kt = kb.sb("Kt", [128, D], BF16)
        Rt = kb.sb("Rt", [128, D], BF16)
        Bh = kb.sb("Bh", [128, D], BF16)
        Kh = kb.sb("Kh", [128, D], BF16)
        yg = xn
        hTx = [kb.sb(f"hTx{i}", [128, 8, 129], BF16) for i in range(2)]
        xxT = kb.sb("xxT", [128, 8, 128], BF16)
        xp = [kb.sb(f"xp{i}", [128, 8, 128], BF16) for i in range(6)]
        ygT = xxT
        twT = kb.sb("twT", [64, 128], BF16)
        t1T = kb.sb("t1T", [64, 128], BF16)
        sgT1 = kb.sb("sgT1", [128, 128], BF16)
        sgT2 = kb.sb("sgT2", [32, 128], BF16)
        small = kb.sb("small", [128, 16 * 12], F32)
        ss = small[:, 0:1]
        rstd = small[:, 1:2]
        ss16 = small[:, 16:32]
        rn16 = small[:, 32:48]
        bs16 = small[:, 48:64]
        s1 = small[:, 64:80]
        s2 = small[:, 80:96]
        mean = small[:, 96:112]
        var = small[:, 112:128]
        rs = small[:, 128:144]
        m2 = small[:, 144:160]
        PCf = kb.sb("PCf", [64, 16], F32)
        S_f = kb.sb("S_f", [64, D], F32)
        S_bf = [kb.sb(f"S_bf{i}", [64, D], BF16) for i in range(2)]
        G4 = 4
        FM = [kb.sb(f"FM{i}", [64, 512], BF16) for i in range(G4)]
        E12 = [kb.sb(f"E12_{i}", [128, 512], BF16, off=Eb_off + i * 1024) for i in range(G4)]
        BN0 = [kb.sb(f"BN0_{i}", [128, 256], BF16) for i in range(G4)]
        AB = [[kb.sb(f"AB{i}_{k}", [128, 256], BF16, off=Ea_off + (i * 2 + k) * 512) for k in range(2)] for i in range(G4)]
        Nj = [[kb.sb(f"Nj{i}_{k}", [128, 128], BF16) for k in range(2)] for i in range(G4)]
        GT = [kb.sb(f"GT{i}", [64, 64], BF16) for i in range(G4)]
        QhT = [kb.sb(f"QhT{i}", [64, 128], BF16) for i in range(G4)]

        U = {n: Unit(n) for n in ["xin", "r", "k", "a", "sg", "tA", "tB", "cum", "kk", "kp", "b", "Ea", "Eb", "y",
                                  "xn", "v", "g", "At", "Bt", "Kt", "Rt", "Bh", "Kh", "yg", "xxT", "ygT", "twT",
                                  "t1T", "sgT1", "sgT2", "ss", "ss16", "bs16", "gn", "PCf", "x1d", "sgh", "sgl"]}
        U["tA"] = rowU
        U["yg"] = U["xn"]
        U["ygT"] = U["xxT"]
        hU = units(2, "hTx")
        xpU = units(6, "xp")
        SfU = units(16, "Sf")
        SbU = [units(16, "Sb0_"), units(16, "Sb1_")]
        FMU = units(G4, "FM")
        E12U = units(G4, "E12")
        BN0U = units(G4, "BN0")
        ABU = [units(2, f"AB{i}_") for i in range(G4)]
        NjU = [units(2, f"Nj{i}_") for i in range(G4)]
        GTU = units(G4, "GT")
        QhU = units(G4, "QhT")
        LU = PSU[7]
        L2U = PSU[7]
        P7 = PS[7]
        pbig = [0]
        BIGB = [0, 1, 3, 4, 5, 6]

        def bigps():
            i = BIGB[pbig[0] % 6]
            pbig[0] += 1
            return PS[i], PSU[i]

        HS = 10
        cols = (slice(0, HS * 64), slice(HS * 64, D))
        hds = (slice(0, HS), slice(HS, H))
        nh = (HS, H - HS)
        ENG = ("dve", "pool")
        UH = {n: (Unit(n + "_D"), Unit(n + "_P")) for n in ["kk", "tA", "kp", "b", "Rt", "Bt", "Kt", "At", "Bh", "Kh", "y", "yg"]}
        xnU = list(UH["yg"])

        def uh(n, h):
            return UH[n][0 if h < HS else 1]

        def R(spec, i):
            return [UH[x][i] if isinstance(x, str) else x for x in spec]

        def v3h(t_, i):
            return t_[:, cols[i]].rearrange("p (h n) -> p h n", n=N)

        def bch(s_, i):
            return s_[:, hds[i]].unsqueeze(2).broadcast_to([128, nh[i], N])

        def tt2(out, in0, in1, op, r, w):
            for i in (0, 1):
                g.tt(ENG[i], out[:, cols[i]], in0[:, cols[i]], in1[:, cols[i]], op, R(r, i), R(w, i))

        def tt2b(out, in0, s16, op, r, w):
            for i in (0, 1):
                g.tt(ENG[i], v3h(out, i), v3h(in0, i), bch(s16, i), op, R(r, i), R(w, i))

        def stt2(out, in0, scalar, in1, op0, op1, r, w):
            g.stt(out[:, cols[0]], in0[:, cols[0]], scalar, in1[:, cols[0]], op0, op1, R(r, 0), R(w, 0))
            neutral = 1.0 if op1 == ALU.mult else 0.0
            kb.op("pool", lambda E: E.tensor_scalar(out=out[:, cols[1]], in0=in0[:, cols[1]], scalar1=scalar, scalar2=neutral,
                                                    op0=op0, op1=(ALU.mult if op1 == ALU.mult else ALU.add)), R(r, 1), R(w, 1))
            g.tt("pool", out[:, cols[1]], out[:, cols[1]], in1[:, cols[1]], op1, R(r, 1) + R(w, 1), R(w, 1))

        g.memset("dve", S_f[:], 0.0, SfU)
        g.memset("pool", S_bf[0][:], 0.0, SbU[0])
        g.memset("pool", hTx[0][:, :, 0:1], 0.0, [hU[0]])

        for b in range(NB):
            cur = b % 2
            nxt = 1 - cur
            hc, hUc = hTx[cur], hU[cur]
            xin, xiU = xin2[b % 2], xinU[b % 2]
            if b == 0:
                kb.dma("sp", xin[:], x_d[0:128, :], reads=[], writes=[xiU])
            if b + 1 < NB:
                kb.dma("sp", xin2[(b + 1) % 2][:], x_d[(b + 1) * 128:(b + 2) * 128, :], reads=[], writes=[xinU[(b + 1) % 2]])
            rstd_of(xin[:], xn[:], ss, rstd, [xiU], xnU, U["ss"])
            g.ts("dve", xn[:], xin[:], rstd, None, ALU.mult, None, [xiU, U["ss"]], xnU)
            for c in range(8):
                g.tr(pT[:, c * 128:(c + 1) * 128], xn[:, c * 128:(c + 1) * 128], ident_b[:], xnU + [cU], [pTU[c // 4]])
            g.cp("act", hc[:, :, 1:129], pT.rearrange("p (c t) -> p c t", c=8), pTU, [hUc])
            g.cp("pool", hTx[nxt][:, :, 0:1], hc[:, :, 128:129], [hUc], [hU[nxt]])
            g.tt("dve", xxT[:], hc[:, :, 0:128], hc[:, :, 1:129], ALU.subtract, [hUc], [U["xxT"]])
            xnv = xn[:].rearrange("p (c t) -> p c t", c=8)
            for p in range(6):
                mu_bc = muT[:, p, :].unsqueeze(2).broadcast_to([128, 8, 128])
                g.tt("dve", xnv, xxT[:], mu_bc, ALU.mult, [U["xxT"], gmuU], xnU)
                g.tt("dve", xp[p][:], xnv, hc[:, :, 1:129], ALU.add, xnU + [hUc], [xpU[p]])
            ck(2)

            def proj(xpi, Wt, evac):
                for half in range(2):
                    ps, psu = bigps()
                    for c in range(8):
                        g.mm(ps[:], xp[xpi][:, c, :], Wt[:, c, half * 512:(half + 1) * 512], c == 0, c == 7,
                             [xpU[xpi], wU], [psu])
                    evac(ps, psu, half)
            hs = lambda half: slice(half * 512, (half + 1) * 512)
            proj(0, W3[0], lambda ps, psu, half: g.cp("act", r_sb[:, hs(half)], ps[:], [psu], [U["r"]]))
            proj(2, W3[1], lambda ps, psu, half: g.cp("act", k_sb[:, hs(half)], ps[:], [psu], [U["k"]]))
            proj(3, W3[2], lambda ps, psu, half: g.cp("act", v_bf[:, hs(half)], ps[:], [psu], [U["v"]]))
            ck(3)
            for c in range(8):
                g.mm(P7[0:64, 256:384], W1[:, c, :], xp[1][:, c, :], c == 0, c == 7, [xpU[1], wU], [LU])
            g.act(twT[:], P7[0:64, 256:384], AF.Tanh, [LU], [U["twT"]])
            for half in range(2):
                ps, psu = bigps()
                g.mm(ps[:], twT[:], W2[:, hs(half)], True, True, [U["twT"], wU], [psu])
                g.tt("dve", tA[:, hs(half)], ps[:], w0r[:, hs(half)], ALU.add, [psu, wU], [rowU] + list(UH["tA"]))
            g.act(sg[:], tA[:], AF.Sigmoid, [rowU] + list(UH["tA"]), list(UH["y"]))
            for c in range(8):
                g.mm(P7[0:64, 256:384], A1[:, c, :], xp[4][:, c, :], c == 0, c == 7, [xpU[4], wU], [LU])
            g.cp("act", t1T[:], P7[0:64, 256:384], [LU], [U["t1T"]])
            for half in range(2):
                ps, psu = bigps()
                g.mm(ps[:], t1T[:], A2[:, hs(half)], True, True, [U["t1T"], wU], [psu])
                g.tt("dve", a_sb[:, hs(half)], ps[:], a0r[:, hs(half)], ALU.add, [psu, wU], [U["a"]])
            g.act(a_sb[:], a_sb[:], AF.Sigmoid, [U["a"]], [U["a"]])
            for c in range(8):
                g.mm(P7[:, 256:384], G1[:, c, 0:128], xp[5][:, c, :], c == 0, c == 7, [xpU[5], wU], [LU])
            for c in range(8):
                g.mm(P7[0:32, 384:512], G1[:, c, 128:160], xp[5][:, c, :], c == 0, c == 7, [xpU[5], wU], [L2U])
            g.act(sgT1[:], P7[:, 256:384], AF.Sigmoid, [LU], [U["sgT1"]])
            g.act(sgT2[:], P7[0:32, 384:512], AF.Sigmoid, [L2U], [U["sgT2"]])
            for half in range(2):
                ps, psu = bigps()
                g.mm(ps[:], sgT1[:], G2a[:, hs(half)], True, False, [U["sgT1"], wU], [psu])
                g.mm(ps[:], sgT2[:], G2b[:, hs(half)], False, True, [U["sgT2"], wU], [psu])
                g.cp("act", g_bf[:, hs(half)], ps[:], [psu], [U["g"]])
            ck(4)
            g.cp("act", sgh[:], sg[:], list(UH["y"]), [U["sgh"]])
            g.tt("pool", tA[:], sg[:], sgh[:], ALU.subtract, list(UH["y"]) + [U["sgh"]], [rowU] + list(UH["tA"]))
            g.cp("act", sgl[:], tA[:], [rowU] + list(UH["tA"]), [U["sgl"]])

            def trimm(maskT, half):
                ps, psu = bigps()
                g.mm(ps[:], maskT, sgh[:, hs(half)], True, False, [U["sgh"], cU], [psu])
                g.mm(ps[:], maskT, sgl[:, hs(half)], False, True, [U["sgl"], cU], [psu])
                return ps, psu
            kb.gate("act", [uu for q_ in range(G4) for uu in (E12U[q_], ABU[q_][0], ABU[q_][1])])
            for half in range(2):
                ps, psu = trimm(tri_b[:], half)
                g.act(Ea[:, hs(half)], ps[:], AF.Exp, [psu], [U["Ea"]], scale=-C_DEC)
                g.act(Eb[:, hs(half)], ps[:], AF.Exp, [psu], [U["Eb"]], scale=C_DEC)
            ck(41)
            ps, psu = bigps()
            for h in range(H):
                g.mm(ps[0:64, h:h + 1], sgh[:, h * 64:(h + 1) * 64], tri_b[:, 127:128], True, False, [U["sgh"], cU], [psu])
                g.mm(ps[0:64, h:h + 1], sgl[:, h * 64:(h + 1) * 64], tri_b[:, 127:128], False, True, [U["sgl"], cU], [psu])
            g.act(PCf[:], ps[0:64, 0:16], AF.Exp, [psu], [U["PCf"]], scale=-C_DEC)
            ck(5)
            v3 = lambda t_: t_[:].rearrange("p (h n) -> p h n", h=H)
            bc = lambda s_: s_.unsqueeze(2).broadcast_to([128, H, N])
            tt2(kkt, k_sb, kkr_, ALU.mult, [U["k"], wU], ["kk"])
            tt2(tA, kkt, kkt, ALU.mult, ["kk"], ["tA"])
            g.red(ss16, v3(tA), list(UH["tA"]), [U["ss16"]])
            g.ts("dve", rn16, ss16, 1e-24, None, ALU.max, None, [U["ss16"]], [U["ss16"]])
            g.act(rn16, rn16, AF.Ln, [U["ss16"]], [U["ss16"]])
            g.act(rn16, rn16, AF.Exp, [U["ss16"]], [U["ss16"]], scale=-0.5)
            tt2b(kkt, kkt, rn16, ALU.mult, ["kk", U["ss16"]], ["kk"])
            tt2(Rt, r_sb, Ea, ALU.mult, [U["r"], U["Ea"]], ["Rt"])
            stt2(tA, a_sb, -1.0, kar_, ALU.add, ALU.mult, [U["a"], wU], ["tA"])
            stt2(kp, tA, 1.0, k_sb, ALU.add, ALU.mult, ["tA", U["k"]], ["kp"])
            tt2(b_sb, kkt, a_sb, ALU.mult, ["kk", U["a"]], ["b"])
            tt2(Bt, b_sb, Eb, ALU.mult, ["b", U["Eb"]], ["Bt"])
            tt2(Kt, kp, Eb, ALU.mult, ["kp", U["Eb"]], ["Kt"])
            for half in range(2):
                ps, psu = trimm(mask_ls[:], half)
                g.act(Eb[:, hs(half)], ps[:], AF.Exp, [psu], [U["Eb"]], scale=-C_DEC)
            for half in range(2):
                ps, psu = trimm(maskE[:, 0:128], half)
                g.act(Ea[:, hs(half)], ps[:], AF.Exp, [psu], [U["Ea"]], scale=-C_DEC)
            stt2(At, kkt, -1.0, Ea, ALU.mult, ALU.mult, ["kk", U["Ea"]], ["At"])
            tt2(Bh, b_sb, Eb, ALU.mult, ["b", U["Eb"]], ["Bh"])
            tt2(Kh, kp, Eb, ALU.mult, ["kp", U["Eb"]], ["Kh"])
            tt2(tA, r_sb, kp, ALU.mult, [U["r"], "kp"], ["tA"])
            tt2(tA, tA, rkr_, ALU.mult, ["tA", wU], ["tA"])
            g.red(bs16, v3(tA), list(UH["tA"]), [U["bs16"]])
            tt2b(kkt, v_bf, bs16, ALU.mult, [U["v"], U["bs16"], "kk"], ["kk"])
            tt2(kkt, kkt, lnbr, ALU.add, ["kk", wU], ["kk"])

            ck(6)
            kb.gate("dve", [U["Ea"], U["Eb"]])
            kb.gate("act", [U["Ea"], U["Eb"]])
            for h0 in range(0, H, G4):
                hs4 = list(range(h0, h0 + G4))

                def ctx(h):
                    q = h - h0
                    return q, PS[3 + q], PSU[3 + q], slice(h * 64, (h + 1) * 64)
                for h in hs4:
                    q, bk, bu, hsl = ctx(h)
                    bkb = bk[:].bitcast(BF16)
                    for qi, (src, su) in enumerate(((At, uh("At", h)), (Rt, uh("Rt", h)), (Bt, uh("Bt", h)), (Kt, uh("Kt", h)))):
                        g.tr(bkb[0:64, qi * 128:(qi + 1) * 128], src[:, hsl], ident_b[:], [su, cU], [bu])
                    g.cp("act", FM[q][:], bkb[0:64, 0:512], [bu], [FMU[q]])
                for h in hs4:
                    q, bk, bu, hsl = ctx(h)
                    fm = FM[q]
                    g.mm(bk[:, 0:256], fm[:, 256:384], fm[:, 0:256], True, True, [FMU[q]], [bu])
                    g.mm(bk[:, 256:512], fm[:, 384:512], fm[:, 0:256], True, True, [FMU[q]], [bu])
                    g.tt("dve", E12[q][:], bk[:], maskE2[:], ALU.mult, [bu, cU], [E12U[q]])
                for h in hs4:
                    q, bk, bu, hsl = ctx(h)
                    fm = FM[q]
                    g.mm(bk[:, 0:128], fm[:, 0:128], fm[:, 256:384], True, True, [FMU[q]], [bu])
                    g.mm(bk[:, 128:192], fm[:, 0:128], ident_b[0:64, 0:64], True, True, [FMU[q], cU], [bu])
                    g.mm(bk[:, 192:256], E12[q][:, 256:384], v_bf[:, hsl], True, True, [E12U[q], U["v"]], [bu])
                    g.tt("dve", BN0[q][:], bk[:, 0:256], mask2[:], ALU.mult, [bu, cU], [BN0U[q]])
                cur_ab = {}
                for h in hs4:
                    q = h - h0
                    cur_ab[q] = (E12[q][:, 0:128], BN0[q][:, 0:128], [E12U[q], BN0U[q]], BN0[q][:, 128:256], [BN0U[q]])
                for j in range(7):
                    for h in hs4:
                        q, bk, bu, hsl = ctx(h)
                        A_cur, B_cur, abu, N_cur, nu = cur_ab[q]
                        no = (j + 1) % 2
                        g.mm(bk[:, 0:128], A_cur, N_cur, True, True, abu + nu, [bu])
                        if j < 6:
                            g.mm(bk[:, 128:256], B_cur, A_cur, True, True, abu, [bu])
                            g.mm(bk[:, 256:384], A_cur, B_cur, True, True, abu, [bu])
                        g.tt("dve", Nj[q][no][:], bk[:, 0:128], N_cur, ALU.add, [bu] + nu, [NjU[q][no]])
                        if j < 6:
                            g.cp("act", AB[q][no][:], bk[:, 128:384], [bu], [ABU[q][no]])
                            cur_ab[q] = (AB[q][no][:, 0:128], AB[q][no][:, 128:256], [ABU[q][no]], Nj[q][no][:], [NjU[q][no]])
                        else:
                            cur_ab[q] = (None, None, None, Nj[q][no][:], [NjU[q][no]])
                for h in hs4:
                    q, bk, bu, hsl = ctx(h)
                    XZ, XZu = cur_ab[q][3], cur_ab[q][4]
                    X_ = XZ[:, 0:64]
                    g.mm(bk[0:64, 0:64], X_, Bh[:, hsl], True, True, XZu + [uh("Bh", h)], [bu])
                    g.mm(bk[0:64, 128:256], X_, E12[q][:, 128:256], True, True, XZu + [E12U[q]], [bu])
                    g.cp("act", GT[q][:], bk[0:64, 0:64], [bu], [GTU[q]])
                    g.tt("dve", QhT[q][:], bk[0:64, 128:256], FM[q][:, 128:256], ALU.add, [bu, FMU[q]], [QhU[q]])
                for h in hs4:
                    q, bk, bu, hsl = ctx(h)
                    XZ, XZu = cur_ab[q][3], cur_ab[q][4]
                    Z_ = XZ[:, 64:128]
                    MrbT, MrkT = E12[q][:, 128:256], E12[q][:, 384:512]
                    Sb_cur, Sb_cu = S_bf[cur][:, hsl], SbU[cur][h]
                    Yr = bk[:, 256:320]
                    g.mm(Yr, MrbT, Z_, True, False, [E12U[q]] + XZu, [bu])
                    g.mm(Yr, MrkT, v_bf[:, hsl], False, False, [E12U[q], U["v"]], [bu])
                    g.mm(Yr, QhT[q][:], Sb_cur, False, True, [QhU[q], Sb_cu], [bu])
                    Hr = bk[0:64, 320:384]
                    g.mm(Hr, Bh[:, hsl], Z_, True, False, [uh("Bh", h)] + XZu, [bu])
                    g.mm(Hr, Kh[:, hsl], v_bf[:, hsl], False, False, [uh("Kh", h), U["v"]], [bu])
                    g.mm(Hr, GT[q][:], Sb_cur, False, True, [GTU[q], Sb_cu], [bu])
                    g.cp("act", y_sb[:, hsl], Yr, [bu], [uh("y", h)])
                    g.stt(S_f[:, hsl], S_f[:, hsl], PCf[:, h:h + 1], Hr, ALU.mult, ALU.add,
                          [SfU[h], U["PCf"], bu], [SfU[h]])
                    g.cp("pool", S_bf[nxt][:, hsl], S_f[:, hsl], [SfU[h]], [SbU[nxt][h]])
                ck(7)

            ck(8)
            yU = list(UH["y"])
            g.red(s1, v3(y_sb), yU, [U["gn"]])
            tt2(tA, y_sb, y_sb, ALU.mult, ["y"], ["tA"])
            g.red(s2, v3(tA), list(UH["tA"]), [U["gn"]])
            g.ts("dve", mean, s1, 1.0 / N, None, ALU.mult, None, [U["gn"]], [U["gn"]])
            g.tt("dve", m2, mean, mean, ALU.mult, [U["gn"]], [U["gn"]])
            g.stt(var, s2, 1.0 / N, m2, ALU.mult, ALU.subtract, [U["gn"]], [U["gn"]])
            g.ts("dve", var, var, GN_EPS, None, ALU.add, None, [U["gn"]], [U["gn"]])
            g.act(rs, var, AF.Ln, [U["gn"]], [U["gn"]])
            g.act(rs, rs, AF.Exp, [U["gn"]], [U["gn"]], scale=-0.5)
            g.stt(m2, mean, -1.0, rs, ALU.mult, ALU.mult, [U["gn"]], [U["gn"]])
            tt2b(tA, y_sb, rs, ALU.mult, ["y", U["gn"]], ["tA"])
            tt2b(tA, tA, m2, ALU.add, ["tA", U["gn"]], ["tA"])
            tt2(tA, tA, lnwr, ALU.mult, ["tA", wU], ["tA"])
            tt2(tA, tA, kkt, ALU.add, ["tA", "kk"], ["tA"])
            tt2(yg, tA, g_bf, ALU.mult, ["tA", U["g"]], ["yg"])
            for c in range(8):
                g.tr(pT[:, c * 128:(c + 1) * 128], yg[:, c * 128:(c + 1) * 128], ident_b[:], xnU + [cU], [pTU[c // 4]])
            g.cp("act", ygT[:], pT.rearrange("p (c t) -> p c t", c=8), pTU, [U["ygT"]])
            for half in range(2):
                ps, psu = bigps()
                for c in range(8):
                    g.mm(ps[:], ygT[:, c, :], Wo[:, c, hs(half)], c == 0, c == 7, [U["ygT"], wU], [psu])
                g.tt("dve", xin[:, hs(half)], ps[:], xin[:, hs(half)], ALU.add, [psu, xiU], [xiU])
            kb.dma("sp", x1_d[b * 128:(b + 1) * 128, :], xin[:], reads=[xiU], writes=[U["x1d"]])
        return U["x1d"]

    try:
        x1U = phase_rwkv()
    except _Stop:
        x1U = Unit("x1dummy")
    kb.fence()

    kb.sb_off = persist_off
    x_sb = kb.sb("x_sb", [128, NB, D], F32)
    xU = units(NB, "x")
    resid_off = kb.sb_off
    for b in range(NB):
        kb.dma("sp", x_sb[:, b, :], x1_d[b * 128:(b + 1) * 128, :], reads=[x1U], writes=[xU[b]])

    outU = Unit("out")

    def dump_x():
        for b in range(NB):
            kb.dma("sp", out_d[b * 128:(b + 1) * 128, :], x_sb[:, b, :], reads=[xU[b]], writes=[outU])
        kb.final_wait("sp", [outU])
        kb.fence()
        kb.build()
        return nc

    if stop_after <= 1:
        return dump_x()

    SB_TOP = 229344
    KVW_OFF = SB_TOP - 8 * (2 * D + H) * 2
    WQ_OFF = KVW_OFF - 8 * D * 2
    MLPW_OFF = SB_TOP - 4 * 8192

    def mlp_weights(layer, off=None):
        if off is None:
            WI = [kb.sb(f"WI{layer}_{i}", [128, 8, 512], BF16) for i in range(2)]
            WO = [kb.sb(f"WO{layer}_{i}", [128, 4, D], BF16) for i in range(2)]
        else:
            WI = [kb.sb(f"WI{layer}_{i}", [128, 8, 512], BF16, off=off + i * 8192) for i in range(2)]
            WO = [kb.sb(f"WO{layer}_{i}", [128, 4, D], BF16, off=off + 16384 + i * 8192) for i in range(2)]
        WIU, WOU = units(2, "WI"), units(2, "WO")
        done = set()

        def load_w(e):
            if e in done or e >= 8:
                return
            done.add(e)
            i = e % 2
            for c in range(8):
                kb.dma("pool", WI[i][:, c, :], mlp_w_in[layer, c * 128:(c + 1) * 128, e * 512:(e + 1) * 512], writes=[WIU[i]], multi=True)
            for fc in range(4):
                kb.dma("pool", WO[i][:, fc, :], mlp_w_out[layer, e * 512 + fc * 128: e * 512 + (fc + 1) * 128, :], writes=[WOU[i]], multi=True)
        return WI, WO, WIU, WOU, load_w

    def phase_mlp(layer, final, wts=None, mid_hook=None):
        kb.sb_off = resid_off
        hnT = kb.sb("hnT", [128, 8, T], BF16)
        hid = kb.sb("hid", [128, 4, T], BF16)
        WI, WO, WIU, WOU, load_w = wts if wts is not None else mlp_weights(layer)
        gT, gU = load_featcols([mlp_norm_g[layer]], f"mgT{layer}")
        xn = kb.sb("mxn", [128, D], BF16)
        junk = kb.sb("mjunk", [128, D], BF16)
        rl = [kb.sb(f"mrl{i}", [128, 512], F32) for i in range(2)]
        small = kb.sb("msmall", [128, 32], F32)
        ss16a, rstd16 = small[:, 0:NB], small[:, 16:16 + NB]
        hnU = units(NB, "hnT")
        NT = T // 512
        hidU = [units(NT, f"hid{fc}_") for fc in range(4)]
        xnU = Unit("mxn")
        jU = Unit("mjunk")
        sU = Unit("msmall")
        rlU = units(2, "mrl")
        load_w(0)
        for b in range(NB):
            g.act(junk[:], x_sb[:, b, :], AF.Square, [xU[b]], [jU, sU], accum_out=ss16a[:, b:b + 1])
        g.ts("dve", rstd16, ss16a, 1.0 / D, NORM_EPS, ALU.mult, ALU.add, [sU], [sU])
        g.act(rstd16, rstd16, AF.Ln, [sU], [sU])
        g.act(rstd16, rstd16, AF.Exp, [sU], [sU], scale=-0.5)

        def pre(b):
            g.ts("dve", xn[:], x_sb[:, b, :], rstd16[:, b:b + 1], None, ALU.mult, None, [xU[b], sU], [xnU])
            for c in range(8):
                g.tr(pT[:, c * 128:(c + 1) * 128], xn[:, c * 128:(c + 1) * 128], ident_b[:], [xnU, cU], [PSU[2]])
            for c in range(8):
                if c % 2:
                    g.ts("dve", hnT[:, c, b * 128:(b + 1) * 128], pT[:, c * 128:(c + 1) * 128], gT[:, c:c + 1], None, ALU.mult, None,
                         [PSU[2], gU], [hnU[b]])
                else:
                    g.act(hnT[:, c, b * 128:(b + 1) * 128], pT[:, c * 128:(c + 1) * 128], AF.Copy, [PSU[2], gU], [hnU[b]],
                          scale=gT[:, c:c + 1])
        pb = [0]

        def hidden(e, fc, tt):
            i = e % 2
            pi = pb[0]
            pb[0] ^= 1
            ps, psu = PS[pi], PSU[pi]
            for c in range(8):
                g.mm(ps[:], WI[i][:, c, fc * 128:(fc + 1) * 128], hnT[:, c, tt * 512:(tt + 1) * 512], c == 0, c == 7,
                     [WIU[i]] + hnU[tt * 4:(tt + 1) * 4], [psu])
            ri = (fc * 4 + tt) % 2
            g.act(rl[ri][:], ps[:], AF.Relu, [psu], [rlU[ri]])
            g.tt("pool", hid[:, fc, tt * 512:(tt + 1) * 512], rl[ri][:], rl[ri][:], ALU.mult, [rlU[ri]], [hidU[fc][tt]])

        for b in range(4):
            pre(b)
        for e in range(8):
            i = e % 2
            load_w(e + 1)
            if mid_hook is not None:
                mid_hook(e)
            if e == 0:
                for tt in range(NT):
                    if tt + 1 < NT:
                        for b in range(4 * (tt + 1), 4 * (tt + 2)):
                            pre(b)
                    for fc in range(4):
                        hidden(e, fc, tt)
            else:
                for fc in range(4):
                    for tt in range(NT):
                        hidden(e, fc, tt)
            for b in range(NB):
                for half in range(2):
                    pi = 3 + pb[0]
                    pb[0] ^= 1
                    ps, psu = PS[pi], PSU[pi]
                    for fc in range(4):
                        g.mm(ps[:], hid[:, fc, b * 128:(b + 1) * 128], WO[i][:, fc, half * 512:(half + 1) * 512], fc == 0, fc == 3,
                             [hidU[fc][b // 4], WOU[i]], [psu])
                    g.tt("dve", x_sb[:, b, half * 512:(half + 1) * 512], ps[:], x_sb[:, b, half * 512:(half + 1) * 512], ALU.add,
                         [psu, xU[b]], [xU[b]])
                if final and e == 7:
                    kb.dma("sp", out_d[b * 128:(b + 1) * 128, :], x_sb[:, b, :], reads=[xU[b]], writes=[outU])
        assert kb.sb_off <= (WQ_OFF if layer == 0 else MLPW_OFF), kb.sb_off

    KVW = kb.sb("KVW", [128, 8, 2 * D + H], BF16, off=KVW_OFF)
    WQ = kb.sb("WQ", [128, 8, D], BF16, off=WQ_OFF)
    qkvwU = Unit("qkv_w")

    def prefetch_qkv_w(e):
        todo = []
        for c in range(8):
            todo.append((KVW[:, c, 0:1024], kv_w[c * 128:(c + 1) * 128, 0:1024]))
            todo.append((KVW[:, c, 1024:2 * D + H], kv_w[c * 128:(c + 1) * 128, 1024:2 * D + H]))
            todo.append((WQ[:, c, :], attn_w_q[c * 128:(c + 1) * 128, :]))
        per = 4
        if 1 <= e <= 6:
            for o, i_ in todo[(e - 1) * per:(e - 1) * per + per]:
                kb.dma("pool", o, i_, writes=[qkvwU], multi=True)

    phase_mlp(0, False, mid_hook=prefetch_qkv_w)
    kb.fence()
    if stop_after <= 2:
        return dump_x()

    kb.sb_off = resid_off
    V_all = kb.sb("V_all", [128, NB, H, 65], BF16)
    VU = units(NB, "V")
    oU = units(NB, "o")
    attn_off = kb.sb_off

    def phase_qkv():
        kb.sb_off = attn_off
        gkq, gkqU = load_featcols([kv_norm_g, attn_norm_g], "gkq")
        gTk = gkq[:, 0:8]
        gTq = gkq[:, 8:16]
        kgr = kb.sb("kgr", [128, D], F32)
        qgr = kb.sb("qgr", [128, D], F32)
        fbr = kb.sb("fbr", [128, H], F32)
        xn = kb.sb("qxn", [128, D], BF16)
        junk = kb.sb("qjunk", [128, D], BF16)
        hT = [kb.sb(f"qhT{i}", [128, 8, 128], BF16) for i in range(2)]
        k_sb = kb.sb("k_sb", [128, D], F32)
        q_sb = kb.sb("q_sb", [128, D], F32)
        sqk = kb.sb("sqk", [128, D], F32)
        sqq = kb.sb("sqq", [128, D], F32)
        k_aug = kb.sb("k_aug", [128, H, 70], BF16)
        q_aug = kb.sb("q_aug", [128, H, 70], BF16)
        kTb = kb.sb("kTb", [70, H, 128], BF16)
        qTb = kb.sb("qTb", [70, H, 128], BF16)
        small = kb.sb("qsmall", [128, 16 * 12], F32)
        carry = kb.sb("carry", [128, H], F32)
        cbf = kb.sb("cbf", [128, 3, H], BF16)
        lfb = kb.sb("lfb", [128, 3, H], BF16)
        ss16a, rstd16 = small[:, 0:NB], small[:, 16:16 + NB]
        lf, cc, r1 = small[:, 48:64], small[:, 64:80], small[:, 80:96]
        ssk, rnk, ssq, rnq = small[:, 96:112], small[:, 112:128], small[:, 128:144], small[:, 144:160]
        wU = qkvwU
        U = {n: Unit(n) for n in ["xn", "junk", "k", "q", "sqk", "sqq", "k_aug", "q_aug", "kTb", "qTb", "s", "ssk", "ssq",
                                  "lf", "cc", "carry", "cbf", "kTd", "qTd", "lfb"]}
        hTU = units(2, "qhT")
        kb.dma("sp", kgr[:], k_norm_g.partition_broadcast(128), writes=[wU])
        kb.dma("sp", qgr[:], q_norm_g.partition_broadcast(128), writes=[wU])
        kb.dma("sp", fbr[:], kv_f_bias.partition_broadcast(128), writes=[wU])
        for c in range(8):
            g.act(KVW[:, c, :], KVW[:, c, :], AF.Copy, [wU, gkqU], [wU], scale=gTk[:, c:c + 1])
            g.ts("dve", WQ[:, c, :], WQ[:, c, :], gTq[:, c:c + 1], None, ALU.mult, None, [wU, gkqU], [wU])
        g.ts("dve", qgr[:], qgr[:], 0.125, None, ALU.mult, None, [wU], [wU])
        ck(10)
        g.memset("pool", carry[:], 0.0, [U["carry"]])
        g.memset("pool", k_aug[:], 1.0, [U["k_aug"]])
        g.memset("pool", q_aug[:], 1.0, [U["q_aug"]])
        g.memset("pool", V_all[:], 1.0, VU)
        v3 = lambda ap: ap.rearrange("p (h n) -> p h n", h=H)
        bc = lambda s_: s_.unsqueeze(2).broadcast_to([128, H, N])
        for b in range(NB):
            g.act(junk[:], x_sb[:, b, :], AF.Square, [xU[b]], [U["junk"], U["s"]], accum_out=ss16a[:, b:b + 1])
        g.ts("dve", rstd16, ss16a, 1.0 / D, NORM_EPS, ALU.mult, ALU.add, [U["s"]], [U["s"]])
        g.act(rstd16, rstd16, AF.Ln, [U["s"]], [U["s"]])
        g.act(rstd16, rstd16, AF.Exp, [U["s"]], [U["s"]], scale=-0.5)

        def stageA(b):
            g.ts("dve", xn[:], x_sb[:, b, :], rstd16[:, b:b + 1], None, ALU.mult, None, [xU[b], U["s"]], [U["xn"]])
            for c in range(8):
                g.tr(pT[:, c * 128:(c + 1) * 128], xn[:, c * 128:(c + 1) * 128], ident_b[:], [U["xn"], cU], [PSU[2]])
            g.cp("act", hT[b % 2][:], pT.rearrange("p (c t) -> p c t", c=8), [PSU[2]], [hTU[b % 2]])

        def stageBv(b):
            h_, hu = hT[b % 2], hTU[b % 2]
            for half in range(2):
                for c in range(8):
                    g.mm(PS[3 + half][:], h_[:, c, :], KVW[:, c, D + half * 512: D + (half + 1) * 512], c == 0, c == 7,
                         [hu, wU], [PSU[3 + half]])

        def stageB(b):
            h_, hu = hT[b % 2], hTU[b % 2]
            for half in range(2):
                for c in range(8):
                    g.mm(PS[half][:], h_[:, c, :], KVW[:, c, half * 512:(half + 1) * 512], c == 0, c == 7, [hu, wU], [PSU[half]])
            for half in range(2):
                for c in range(8):
                    g.mm(PS[5 + half][:], h_[:, c, :], WQ[:, c, half * 512:(half + 1) * 512], c == 0, c == 7, [hu, wU], [PSU[5 + half]])
            for c in range(8):
                g.mm(PS[7][:, 0:16], h_[:, c, :], KVW[:, c, 2 * D:2 * D + H], c == 0, c == 7, [hu, wU], [PSU[7]])

        def stageC1(b):
            for half in range(2):
                g.cp("act", k_sb[:, half * 512:(half + 1) * 512], PS[half][:], [PSU[half]], [U["k"]])
            for half in range(2):
                g.cp("act", V_all[:, b, half * 8:(half + 1) * 8, 0:64], PS[3 + half][:].rearrange("p (h n) -> p h n", h=8),
                     [PSU[3 + half]], [VU[b]])
            for half in range(2):
                g.cp("act", q_sb[:, half * 512:(half + 1) * 512], PS[5 + half][:], [PSU[5 + half]], [U["q"]])
            g.tt("dve", lf, PS[7][:, 0:16], fbr[:], ALU.add, [PSU[7], wU], [U["lf"]])

        def headnorm(src, su, sq, squ, ss_, rn_, ssu, dst_aug, gain_row, du):
            g.tt("pool", sq[:], src[:], src[:], ALU.mult, [su], [squ])
            g.red(ss_, v3(sq[:]), [squ], [ssu])
            g.ts("dve", rn_, ss_, 1.0 / N, NORM_EPS, ALU.mult, ALU.add, [ssu], [ssu])
            g.act(rn_, rn_, AF.Ln, [ssu], [ssu])
            g.act(rn_, rn_, AF.Exp, [ssu], [ssu], scale=-0.5)
            g.tt("dve", v3(sq[:]), v3(src[:]), bc(rn_), ALU.mult, [su, ssu], [squ])
            g.tt("pool", dst_aug[:, :, 0:64], v3(sq[:]), v3(gain_row[:]), ALU.mult, [squ, wU], [du])

        def stageC2(b):
            g.act(lf, lf, AF.Exp, [U["lf"]], [U["lf"]], scale=-1.0)
            g.act(lf, lf, AF.Ln, [U["lf"]], [U["lf"]], bias=1.0)
            g.ts("dve", lf, lf, -1.0, None, ALU.mult, None, [U["lf"]], [U["lf"]])
            g.cp("dve", lfb[:, 0, :], lf, [U["lf"]], [U["lfb"]])
            g.tt("dve", r1, lf, lfb[:, 0, :], ALU.subtract, [U["lf"], U["lfb"]], [U["cc"]])
            g.cp("dve", lfb[:, 1, :], r1, [U["cc"]], [U["lfb"]])
            g.tt("dve", r1, r1, lfb[:, 1, :], ALU.subtract, [U["cc"], U["lfb"]], [U["cc"]])
            g.cp("dve", lfb[:, 2, :], r1, [U["cc"]], [U["lfb"]])
            for pc in range(3):
                g.mm(PS[2][:, 0:16], tri_b[:], lfb[:, pc, :], pc == 0, pc == 2, [U["lfb"], cU], [PSU[2]])
            for pc in range(3):
                g.mm(PS[2][:, 16:32], ones_b[:], lfb[:, pc, :], pc == 0, pc == 2, [U["lfb"], cU], [PSU[2]])
            g.tt("dve", cc, PS[2][:, 0:16], carry[:], ALU.add, [PSU[2], U["carry"]], [U["cc"]])
            g.tt("dve", carry[:], PS[2][:, 16:32], carry[:], ALU.add, [PSU[2], U["carry"]], [U["carry"]])
            g.cp("dve", cbf[:, 0, :], cc, [U["cc"]], [U["cbf"]])
            g.tt("dve", r1, cc, cbf[:, 0, :], ALU.subtract, [U["cc"], U["cbf"]], [U["cc"]])
            g.cp("dve", cbf[:, 1, :], r1, [U["cc"]], [U["cbf"]])
            g.tt("dve", r1, r1, cbf[:, 1, :], ALU.subtract, [U["cc"], U["cbf"]], [U["cc"]])
            g.cp("dve", cbf[:, 2, :], r1, [U["cc"]], [U["cbf"]])
            headnorm(k_sb, U["k"], sqk, U["sqk"], ssk, rnk, U["ssk"], k_aug, kgr, U["k_aug"])
            headnorm(q_sb, U["q"], sqq, U["sqq"], ssq, rnq, U["ssq"], q_aug, qgr, U["q_aug"])
            g.ts("dve", k_aug[:, :, 67:70], cbf[:].rearrange("p s h -> p h s"), -1.0, None, ALU.mult, None, [U["cbf"]], [U["k_aug"]])
            g.cp("dve", q_aug[:, :, 64:67], cbf[:].rearrange("p s h -> p h s"), [U["cbf"]], [U["q_aug"]])
            tbank = [2, 3, 4, 2]
            gi_ = 0
            for aug, au, Tb, Tu, dst, du in ((k_aug, U["k_aug"], kTb, U["kTb"], kT_d, U["kTd"]),
                                             (q_aug, U["q_aug"], qTb, U["qTb"], qT_d, U["qTd"])):
                for hh in range(2):
                    bkI = tbank[gi_]
                    gi_ += 1
                    pTx = PS[bkI][:].bitcast(BF16)
                    for h8 in range(8):
                        h = hh * 8 + h8
                        g.tr(pTx[0:70, h8 * 128:(h8 + 1) * 128], aug[:, h, :], ident_b[:], [au, cU], [PSU[bkI]])
                    g.cp("act" if hh == 0 else "dve", Tb[:, hh * 8:(hh + 1) * 8, :], pTx[0:70, :].rearrange("p (h t) -> p h t", h=8),
                         [PSU[bkI]], [Tu])
                kb.dma("sp", dst[:, :, b * 128:(b + 1) * 128].rearrange("h r t -> r h t"), Tb[:], reads=[Tu], writes=[du])

        assert kb.sb_off <= WQ_OFF, kb.sb_off
        stageA(0)
        stageB(0)
        stageBv(0)
        for b in range(NB):
            if b + 1 < NB:
                stageA(b + 1)
            stageC1(b)
            if b + 1 < NB:
                stageB(b + 1)
            stageC2(b)
            if b + 1 < NB:
                stageBv(b + 1)
            ck(15 if b == 0 else -1)
        return U["kTd"], U["qTd"]

    try:
        kTdU, qTdU = phase_qkv()
    except _Stop:
        kb.fence()
        return dump_x()
    kb.fence()

    def phase_attn():
        kb.sb_off = attn_off
        o_all = kb.sb("o_all", [128, NB, D], BF16)
        WOa = kb.sb("WOa", [128, 8, D], BF16)
        kTh = [kb.sb(f"kTh{i}", [70, T], BF16) for i in range(2)]
        qTh = [kb.sb(f"qTh{i}", [70, T], BF16) for i in range(2)]
        rec = kb.sb("rec", [128, 4], F32)
        oT = kb.sb("oT", [128, 8, 128], BF16)
        wU = Unit("woa")
        khU, qhU = units(2, "kTh"), units(2, "qTh")
        recU, oTU = Unit("rec"), Unit("oT")
        for c in range(8):
            kb.dma("pool", WOa[:, c, :], attn_w_o[c * 128:(c + 1) * 128, :], writes=[wU], multi=True)
        mlp1_w[4](0)
        assert kb.sb_off + 3 * 1024 + 64 <= MLPW_OFF, kb.sb_off
        NBUF = 3
        psS = [PS[3], PS[4], PS[7]]
        psSU = [PSU[3], PSU[4], PSU[7]]
        psO = [PS[5], PS[6]]
        psOU = [PSU[5], PSU[6]]
        PT = [kb.sb(f"PTb{i}", [128, 4, 128], BF16) for i in range(NBUF)]
        PTU = units(NBUF, "PTb")
        groups = []
        oi = 0
        for h in range(H):
            for i in range(NB):
                for j0 in range(0, i + 1, 4):
                    groups.append((h, i, list(range(j0, min(j0 + 4, i + 1))), oi % 2))
                oi += 1
        loaded = set()

        def load_head(h):
            if h < H and h not in loaded:
                loaded.add(h)
                hb = h % 2
                kb.dma("sp", kTh[hb][:], kT_d[h], reads=[kTdU], writes=[khU[hb]])
                kb.dma("sp", qTh[hb][:], qT_d[h], reads=[qTdU], writes=[qhU[hb]])

        def emit_qk(gidx):
            h, i, js, _ = groups[gidx]
            hb = h % 2
            if i == 0:
                load_head(h)
                load_head(h + 1)
            ps, psu = psS[gidx % NBUF], psSU[gidx % NBUF]
            for jj, j in enumerate(js):
                diag = j == i
                g.mm(ps[:, jj * 128:(jj + 1) * 128], kTh[hb][:, j * 128:(j + 1) * 128], qTh[hb][:, i * 128:(i + 1) * 128],
                     True, not diag, [khU[hb], qhU[hb]], [psu])
                if diag:
                    g.mm(ps[:, jj * 128:(jj + 1) * 128], ident_b[:], maskneg_b[:], False, True, [cU], [psu])

        def emit_rest(gidx):
            h, i, js, ob = groups[gidx]
            ps, psu = psS[gidx % NBUF], psSU[gidx % NBUF]
            pt, ptu = PT[gidx % NBUF], PTU[gidx % NBUF]
            po, pou = psO[ob], psOU[ob]
            n = len(js)
            g.act(pt[:, 0:n, :], ps[:, 0:n * 128].rearrange("p (j t) -> p j t", j=n), AF.Exp, [psu], [ptu])
            for jj, j in enumerate(js):
                g.mm(po[:, 0:65], pt[:, jj, :], V_all[:, j, h, :], j == 0, j == i, [ptu, VU[j]], [pou])
            if js[-1] == i:
                kb.op("dve", lambda E, po=po: E.reciprocal(out=rec[:, 0:1], in_=po[:, 64:65]), [pou], [recU])
                g.ts("dve", o_all[:, i, h * 64:(h + 1) * 64], po[:, 0:64], rec[:, 0:1], None, ALU.mult, None, [pou, recU], [oU[i]])

        emit_qk(0)
        if len(groups) > 1:
            emit_qk(1)
        for gidx in range(len(groups)):
            if gidx + 2 < len(groups):
                emit_qk(gidx + 2)
            emit_rest(gidx)
        ck(17)
        pb = 0
        for b in range(NB):
            for c in range(8):
                g.tr(pT[:, c * 128:(c + 1) * 128], o_all[:, b, c * 128:(c + 1) * 128], ident_b[:], [oU[b], cU], [pTU[c // 4]])
            g.cp("act", oT[:], pT.rearrange("p (c t) -> p c t", c=8), pTU, [oTU])
            for half in range(2):
                ps, psu = PS[pb], PSU[pb]
                pb ^= 1
                for c in range(8):
                    g.mm(ps[:], oT[:, c, :], WOa[:, c, half * 512:(half + 1) * 512], c == 0, c == 7, [oTU, wU], [psu])
                g.tt("dve", x_sb[:, b, half * 512:(half + 1) * 512], ps[:], x_sb[:, b, half * 512:(half + 1) * 512], ALU.add,
                     [psu, xU[b]], [xU[b]])

    mlp1_w = mlp_weights(1, off=MLPW_OFF)
    try:
        phase_attn()
    except _Stop:
        pass
    kb.fence()
    if stop_after <= 3:
        return dump_x()

    phase_mlp(1, True, wts=mlp1_w)
    kb.final_wait("sp", [outU])
    kb.fence()
    kb.build()
    return nc


_NC_CACHE = {}


def _prep(inputs):
    f = lambda a: np.ascontiguousarray(np.asarray(a, dtype=np.float32))
    common = {
        "rwkv_norm_g": f(inputs["rwkv_norm_g"][0]),
        "rwkv_mu": f(inputs["rwkv_mu"][0]),
        "rwkv_w_rkv": f(inputs["rwkv_w_rkv"][0]),
        "rwkv_w0": f(inputs["rwkv_w0"][0]),
        "rwkv_w1": f(inputs["rwkv_w1"][0]),
        "rwkv_w2": f(inputs["rwkv_w2"][0]),
        "rwkv_a0": f(inputs["rwkv_a0"][0]),
        "rwkv_a1": f(inputs["rwkv_a1"][0]),
        "rwkv_a2": f(inputs["rwkv_a2"][0]),
        "rwkv_g1": f(inputs["rwkv_g1"][0]),
        "rwkv_g2": f(inputs["rwkv_g2"][0]),
        "rwkv_k_k": f(inputs["rwkv_k_k"][0]),
        "rwkv_k_a": f(inputs["rwkv_k_a"][0]),
        "rwkv_r_k": f(inputs["rwkv_r_k"][0]).reshape(D),
        "rwkv_lnx_w": f(inputs["rwkv_lnx_w"][0]),
        "rwkv_lnx_b": f(inputs["rwkv_lnx_b"][0]),
        "rwkv_w_o": f(inputs["rwkv_w_o"][0]),
        "kv_norm_g": f(inputs["kv_norm_g"]),
        "kv_w": f(inputs["kv_w"]),
        "kv_f_bias": f(inputs["kv_f_bias"]),
        "k_norm_g": f(inputs["k_norm_g"]).reshape(D),
        "attn_norm_g": f(inputs["attn_norm_g"][0]),
        "attn_w_q": f(inputs["attn_w_q"][0]),
        "q_norm_g": f(inputs["q_norm_g"][0]).reshape(D),
        "attn_w_o": f(inputs["attn_w_o"][0]),
        "mlp_norm_g": f(inputs["mlp_norm_g"]),
        "mlp_w_in": f(inputs["mlp_w_in"]),
        "mlp_w_out": f(inputs["mlp_w_out"]),
    }
    x = f(inputs["x"])
    return [dict(common, x=x[b]) for b in range(8)]


def kernel(_stop_after=99, **inputs):
    if _stop_after not in _NC_CACHE:
        _NC_CACHE[_stop_after] = build_nc(_stop_after)
    nc = _NC_CACHE[_stop_after]
    in_maps = _prep(inputs)
    res = run_bass_kernel_spmd(nc, in_maps, core_ids=list(range(8)))
    return np.stack([np.asarray(r["out"], dtype=np.float32) for r in res.results], axis=0)
```
